# Optimizing a Trainium2 kernel written in Bass

```python
import jax, jax.numpy as jnp
from jax import lax
import numpy as np

D_MODEL = 1024
BATCH = 4
SEQ = 8192
DEPTH = 4

GRID_W = 64
CTX_LEN = 256
N_MIXERS = 3
EPS = 1e-6
CONV_WIDTH = 31
FNET_GROUPS = 4
FNET_GROUP_DIM = D_MODEL // FNET_GROUPS
HEAD_DIM = 128
N_Q_HEADS = D_MODEL // HEAD_DIM
N_KV_HEADS = 2
Q_PER_KV = N_Q_HEADS // N_KV_HEADS
Q_DIM = N_Q_HEADS * HEAD_DIM
KV_DIM = N_KV_HEADS * HEAD_DIM
ROPE_THETA = 10000.0
Q_BLOCK = 128
N_EXPERT_GROUPS = 4
EXPERTS_PER_GROUP = 8
N_EXPERTS = N_EXPERT_GROUPS * EXPERTS_PER_GROUP
TOP_K_INNER = 2
D_EXPERT = D_MODEL // 2
MOE_BLOCK = 256

kernel_name = 'hybrid_conv_fourier_gqa_hmoe_dit'


def _rmsnorm(x, g):
    xf = x.astype(jnp.float32)
    y = xf * lax.rsqrt(jnp.mean(xf * xf, axis=-1, keepdims=True) + EPS)
    return (y * g.astype(jnp.float32)).astype(x.dtype)


def _modulate(h, shift, scale):
    return h * (1 + scale) + shift


def _conformer_conv(h, w_in, b_in, w_dw, b_dw, g_norm, w_out, b_out):
    a, b = jnp.split(h @ w_in + b_in, 2, axis=-1)
    v = a * jax.nn.sigmoid(b)
    pad = CONV_WIDTH // 2
    v = lax.conv_general_dilated(v, w_dw[:, None, :], window_strides=(1,), padding=[(pad, pad)],
                                 dimension_numbers=('NWC', 'WIO', 'NWC'),
                                 feature_group_count=D_MODEL) + b_dw
    v = jax.nn.silu(_rmsnorm(v, g_norm))
    return v @ w_out + b_out


def _fourier_mix(h, w_out, b_out):
    B_, L, D = h.shape
    hg = h.astype(jnp.float32).reshape(B_, L, FNET_GROUPS, FNET_GROUP_DIM)
    f = jnp.fft.fft2(hg, axes=(1, 3), norm='ortho').real
    return f.reshape(B_, L, D).astype(h.dtype) @ w_out + b_out


def _axial_rope_tables(rows):
    row = jnp.repeat(jnp.arange(rows, dtype=jnp.float32), GRID_W)
    col = jnp.tile(jnp.arange(GRID_W, dtype=jnp.float32), rows)
    n_pairs_axis = HEAD_DIM // 4
    inv = ROPE_THETA ** (-jnp.arange(n_pairs_axis, dtype=jnp.float32) / n_pairs_axis)
    ang = jnp.concatenate([row[:, None] * inv, col[:, None] * inv], axis=-1)
    return jnp.cos(ang), jnp.sin(ang)


def _rope(x, cos, sin):
    xf = x.astype(jnp.float32).reshape(x.shape[:-1] + (HEAD_DIM // 2, 2))
    x0, x1 = xf[..., 0], xf[..., 1]
    shp = (cos.shape[0],) + (1,) * (x.ndim - 3) + (cos.shape[1],)
    c_, s_ = cos.reshape(shp), sin.reshape(shp)
    out = jnp.stack([x0 * c_ - x1 * s_, x0 * s_ + x1 * c_], axis=-1).reshape(x.shape)
    return out.astype(x.dtype)


def _attn_q(h, w_q, gq):
    B_, L, _ = h.shape
    return _rmsnorm((h @ w_q).reshape(B_, L, N_KV_HEADS, Q_PER_KV, HEAD_DIM), gq)


def _attn_kv(h, w_kv, gk):
    B_, L, _ = h.shape
    k, v = jnp.split(h @ w_kv, 2, axis=-1)
    k = _rmsnorm(k.reshape(B_, L, N_KV_HEADS, HEAD_DIM), gk)
    return k, v.reshape(B_, L, N_KV_HEADS, HEAD_DIM)


def _sdpa(q, k, v):
    s = jnp.einsum('bqkgd,bskd->bkgqs', q, k, preferred_element_type=jnp.float32) * (HEAD_DIM ** -0.5)
    p = jax.nn.softmax(s, axis=-1)
    return jnp.einsum('bkgqs,bskd->bqkgd', p.astype(v.dtype), v)


def _hier_moe(t, w_group, b_group, w_expert, b_expert, w_gate, w_up, w_down):
    T_, D = t.shape
    tf = t.astype(jnp.float32)
    g_prob = jax.nn.softmax(tf @ w_group.astype(jnp.float32) + b_group.astype(jnp.float32), axis=-1)
    g_top, g_idx = lax.top_k(g_prob, 1)
    e_logits = (tf @ w_expert.astype(jnp.float32) + b_expert.astype(jnp.float32)).reshape(T_, N_EXPERT_GROUPS, EXPERTS_PER_GROUP)
    e_logits = e_logits[jnp.arange(T_), g_idx[:, 0]]
    e_top, e_idx = lax.top_k(jax.nn.softmax(e_logits, axis=-1), TOP_K_INNER)
    gate = g_top * e_top / jnp.sum(e_top, axis=-1, keepdims=True)
    expert = g_idx * EXPERTS_PER_GROUP + e_idx
    A = T_ * TOP_K_INNER
    e_flat = expert.reshape(A)
    order = jnp.argsort(e_flat)
    e_sorted = e_flat[order]
    counts = jnp.zeros((N_EXPERTS,), jnp.int32).at[e_flat].add(1)
    padded = (counts + MOE_BLOCK - 1) // MOE_BLOCK * MOE_BLOCK
    start = jnp.cumsum(counts) - counts
    pad_end = jnp.cumsum(padded)
    pad_start = pad_end - padded
    dest = pad_start[e_sorted] + jnp.arange(A, dtype=jnp.int32) - start[e_sorted]
    n_blocks = -(-A // MOE_BLOCK) + N_EXPERTS
    P = n_blocks * MOE_BLOCK
    tok = order // TOP_K_INNER
    buf = jnp.zeros((P, D), t.dtype).at[dest].set(t[tok])
    blk_expert = jnp.minimum(jnp.searchsorted(pad_end, jnp.arange(n_blocks, dtype=jnp.int32) * MOE_BLOCK, side='right'), N_EXPERTS - 1)

    def run(args):
        xb, e = args
        return (jax.nn.silu(xb @ w_gate[e]) * (xb @ w_up[e])) @ w_down[e]

    yb = lax.map(run, (buf.reshape(n_blocks, MOE_BLOCK, D), blk_expert)).reshape(P, D)
    y_sorted = yb[dest] * gate.reshape(A)[order][:, None].astype(t.dtype)
    return jnp.zeros((T_, D), t.dtype).at[tok].add(y_sorted)


def setup_inputs(seed: int = 0) -> dict:
    key = jax.random.key(seed)
    ks = iter(jax.random.split(key, 32))
    nrm = lambda shape, scale: jax.random.normal(next(ks), shape, jnp.float32) * scale
    D = D_MODEL
    n_conv = len(range(0, DEPTH, N_MIXERS))
    n_four = len(range(1, DEPTH, N_MIXERS))
    n_attn = len(range(2, DEPTH, N_MIXERS))
    return {
        'x': nrm((BATCH, SEQ, D), 1.0),
        'c': nrm((BATCH, D), 1.0),
        'ctx': nrm((BATCH, CTX_LEN, D), 1.0),
        'c_ctx': nrm((D,), 1.0),
        'norm_g': 1.0 + nrm((DEPTH, 2, D), 0.05),
        'w_mod': nrm((DEPTH, D, 6 * D), 0.5 * D ** -0.5),
        'b_mod': nrm((DEPTH, 6 * D), 0.02),
        'conv_w_in': nrm((n_conv, D, 2 * D), D ** -0.5),
        'conv_b_in': nrm((n_conv, 2 * D), 0.02),
        'conv_w_dw': nrm((n_conv, CONV_WIDTH, D), CONV_WIDTH ** -0.5),
        'conv_b_dw': nrm((n_conv, D), 0.02),
        'conv_norm_g': 1.0 + nrm((n_conv, D), 0.05),
        'conv_w_out': nrm((n_conv, D, D), D ** -0.5),
        'conv_b_out': nrm((n_conv, D), 0.02),
        'fnet_w_out': nrm((n_four, D, D), D ** -0.5),
        'fnet_b_out': nrm((n_four, D), 0.02),
        'attn_w_qkv': nrm((n_attn, D, Q_DIM + 2 * KV_DIM), D ** -0.5),
        'attn_q_norm_g': 1.0 + nrm((n_attn, HEAD_DIM), 0.05),
        'attn_k_norm_g': 1.0 + nrm((n_attn, HEAD_DIM), 0.05),
        'attn_w_out': nrm((n_attn, Q_DIM, D), Q_DIM ** -0.5),
        'moe_w_group': nrm((DEPTH, D, N_EXPERT_GROUPS), D ** -0.5),
        'moe_b_group': nrm((DEPTH, N_EXPERT_GROUPS), 0.01),
        'moe_w_expert': nrm((DEPTH, D, N_EXPERTS), D ** -0.5),
        'moe_b_expert': nrm((DEPTH, N_EXPERTS), 0.01),
        'moe_w_gate': nrm((DEPTH, N_EXPERTS, D, D_EXPERT), D ** -0.5),
        'moe_w_up': nrm((DEPTH, N_EXPERTS, D, D_EXPERT), D ** -0.5),
        'moe_w_down': nrm((DEPTH, N_EXPERTS, D_EXPERT, D), D_EXPERT ** -0.5),
    }


def reference(x, c, ctx, c_ctx, norm_g, w_mod, b_mod,
              conv_w_in, conv_b_in, conv_w_dw, conv_b_dw, conv_norm_g, conv_w_out, conv_b_out,
              fnet_w_out, fnet_b_out,
              attn_w_qkv, attn_q_norm_g, attn_k_norm_g, attn_w_out,
              moe_w_group, moe_b_group, moe_w_expert, moe_b_expert, moe_w_gate, moe_w_up, moe_w_down):
    B_, L, D = x.shape
    C = ctx.shape[1]
    ROWS = L // GRID_W
    cos, sin = _axial_rope_tables(ROWS)
    last_reader = max([i for i in range(DEPTH) if i % N_MIXERS == 2], default=-1)
    silu_c = jax.nn.silu(c)
    silu_cc = jax.nn.silu(c_ctx)
    xc = ctx
    for i in range(DEPTH):
        m, j = i % N_MIXERS, i // N_MIXERS
        ctx_on = i <= last_reader
        ctx_full = i < last_reader
        sh1, sc1, g1, sh2, sc2, g2 = jnp.split((silu_c @ w_mod[i] + b_mod[i])[:, None, :], 6, axis=-1)
        hl = _modulate(_rmsnorm(x, norm_g[i, 0]), sh1, sc1)
        if ctx_on:
            nc = 6 if ctx_full else 2
            cmod = jnp.split(silu_cc @ w_mod[i][:, :nc * D] + b_mod[i][:nc * D], nc, axis=-1)
            hc = _modulate(_rmsnorm(xc, norm_g[i, 0]), cmod[0], cmod[1])
        if m == 0:
            cp = (conv_w_in[j], conv_b_in[j], conv_w_dw[j], conv_b_dw[j], conv_norm_g[j], conv_w_out[j], conv_b_out[j])
            x = x + g1 * _conformer_conv(hl, *cp)
            if ctx_full:
                xc = xc + cmod[2] * _conformer_conv(hc, *cp)
        elif m == 1:
            x = x + g1 * _fourier_mix(hl, fnet_w_out[j], fnet_b_out[j])
            if ctx_full:
                xc = xc + cmod[2] * _fourier_mix(hc, fnet_w_out[j], fnet_b_out[j])
        else:
            w_q, w_kv = attn_w_qkv[j][:, :Q_DIM], attn_w_qkv[j][:, Q_DIM:]
            q = _rope(_attn_q(hl, w_q, attn_q_norm_g[j]), cos, sin)
            k, v = _attn_kv(hl, w_kv, attn_k_norm_g[j])
            k = _rope(k, cos, sin)
            kc, vc = _attn_kv(hc, w_kv, attn_k_norm_g[j])
            k_all = jnp.concatenate([kc, k], axis=1)
            v_all = jnp.concatenate([vc, v], axis=1)
            n_blk = L // Q_BLOCK
            qb = q.reshape(B_, n_blk, Q_BLOCK, N_KV_HEADS, Q_PER_KV, HEAD_DIM).swapaxes(0, 1)
            o = lax.map(lambda qq: _sdpa(qq, k_all, v_all), qb)
            o = o.swapaxes(0, 1).reshape(B_, L, Q_DIM)
            x = x + g1 * (o @ attn_w_out[j])
            if ctx_full:
                qc = _attn_q(hc, w_q, attn_q_norm_g[j])
                oc = _sdpa(qc, kc, vc).reshape(B_, C, Q_DIM)
                xc = xc + cmod[2] * (oc @ attn_w_out[j])
        mp = (moe_w_group[i], moe_b_group[i], moe_w_expert[i], moe_b_expert[i], moe_w_gate[i], moe_w_up[i], moe_w_down[i])
        h2 = _modulate(_rmsnorm(x, norm_g[i, 1]), sh2, sc2).reshape(B_ * L, D)
        if ctx_full:
            h2c = _modulate(_rmsnorm(xc, norm_g[i, 1]), cmod[3], cmod[4]).reshape(B_ * C, D)
            y = _hier_moe(jnp.concatenate([h2, h2c], axis=0), *mp)
            x = x + g2 * y[:B_ * L].reshape(B_, L, D)
            xc = xc + cmod[5] * y[B_ * L:].reshape(B_, C, D)
        else:
            x = x + g2 * _hier_moe(h2, *mp).reshape(B_, L, D)
    return x
```

```python
import numpy as np
import concourse.bass as bass
import concourse.mybir as mybir
from concourse.bass_utils import run_bass_kernel_spmd
from contextlib import ExitStack

F32 = mybir.dt.float32
BF16 = mybir.dt.bfloat16
I32 = mybir.dt.int32
AF = mybir.ActivationFunctionType
ALU = mybir.AluOpType
AX = mybir.AxisListType

ENG = ["pe", "act", "dve", "pool", "sp"]
D = 1024
EPS = 1e-6
NE = 32
BLK = 256
SUB = BLK // 128


class Prog:
    SEM_LIMIT = 20000
    ND = 8

    def __init__(self, nc, es):
        self.nc, self.es = nc, es
        self.q = {e: [] for e in ENG}
        self.cur = {}
        self.waited = {e: {} for e in ENG}
        self.last_w = {}
        self.readers = {}
        self.nsem = 0
        self.dsem = {e: [] for e in ENG}
        self.dn = {e: 0 for e in ENG}
        self.sems = {}
        self.ninst = 0
        self.nbar = 0
        self.full = {}

    def new_sem(self):
        self.nsem += 1
        self.sems[self.nsem] = self.es.enter_context(self.nc.semaphore(f"s{self.nsem}"))
        return self.nsem

    def sb(self, name, shape, dt):
        return self.es.enter_context(self.nc.sbuf_tensor("sb_" + name, list(shape), dt))

    def ps(self, name, shape, dt):
        return self.es.enter_context(self.nc.psum_tensor("ps_" + name, list(shape), dt))

    def _deps(self, eng, reads, writes, skip_same):
        deps = {}

        def add(ev):
            if ev is None:
                return
            s, v, e = ev
            if skip_same and e == eng:
                return
            if deps.get(s, 0) < v:
                deps[s] = v
        for k in reads:
            add(self.last_w.get(k))
        for k in writes:
            add(self.last_w.get(k))
            for ev in self.readers.get(k, ()):
                add(ev)
        out = []
        for s, v in deps.items():
            if self.waited[eng].get(s, 0) < v:
                self.waited[eng][s] = v
                out.append((s, v))
        return out

    def _commit(self, ev, reads, writes):
        for k in writes:
            self.last_w[k] = ev
            self.readers[k] = []
        for k in reads:
            self.readers.setdefault(k, []).append(ev)

    def op(self, eng, fn, reads=(), writes=(), skip_same=False):
        waits = self._deps(eng, reads, writes, skip_same)
        c = self.cur.get(eng)
        if c is None or c[1] >= self.SEM_LIMIT:
            if c is not None:
                self.full[eng] = (c[0], c[1])
            c = [self.new_sem(), 0]
            self.cur[eng] = c
        c[1] += 1
        ev = (c[0], c[1], eng)
        self.q[eng].append((waits, fn, c[0], 1))
        self._commit(ev, reads, writes)
        self.ninst += 1
        return ev

    def dma(self, eng, fn, reads=(), writes=()):
        waits = self._deps(eng, reads, writes, False)
        pool = self.dsem[eng]
        i = self.dn[eng] % self.ND
        self.dn[eng] += 1
        if len(pool) <= i:
            pool.append([self.new_sem(), 0])
        d = pool[i]
        if d[1] > 0 and self.waited[eng].get(d[0], 0) < 16 * d[1]:
            self.waited[eng][d[0]] = 16 * d[1]
            waits.append((d[0], 16 * d[1]))
        d[1] += 1
        ev = (d[0], 16 * d[1], "dma_" + eng)
        self.q[eng].append((waits, fn, d[0], 16))
        self._commit(ev, reads, writes)
        self.ninst += 1
        return ev

    def wait_keys(self, eng, keys):
        waits = self._deps(eng, keys, (), False)
        self.q[eng].append((waits, None, None, 0))

    def barrier(self):
        evs = []
        for e in ENG:
            c = self.cur.get(e)
            if c is not None:
                evs.append((c[0], c[1]))
            if e in self.full:
                evs.append(self.full[e])
            for d in self.dsem[e]:
                if d[1] > 0:
                    evs.append((d[0], 16 * d[1]))
        for e in ENG:
            waits = []
            for s, v in evs:
                if self.waited[e].get(s, 0) < v:
                    self.waited[e][s] = v
                    waits.append((s, v))
            self.q[e].append((waits, None, None, 0))
        self.last_w = {}
        self.readers = {}

    def finish(self):
        nc = self.nc
        engobj = {"pe": "tensor", "act": "scalar", "dve": "vector", "pool": "gpsimd", "sp": "sync"}
        with nc.Block() as block:
            for e in ENG:
                if not self.q[e]:
                    continue

                def body(engine, e=e):
                    for waits, fn, s, inc in self.q[e]:
                        for (ws, wv) in waits:
                            engine.wait_ge(self.sems[ws], wv)
                        if fn is not None:
                            fn(engine).then_inc(self.sems[s], inc)
                getattr(block, engobj[e])(body)

    def act(self, out, in_, func, r, w, bias=None, scale=None, accum=None):
        kw = {}
        if bias is not None:
            kw["bias"] = bias
        if scale is not None:
            kw["scale"] = scale
        if accum is not None:
            kw["accum_out"] = accum
        return self.op("act", lambda e: e.activation(out=out, in_=in_, func=func, **kw), r, w)

    def tt(self, eng, out, in0, in1, op, r, w):
        return self.op(eng, lambda e: e.tensor_tensor(out=out, in0=in0, in1=in1, op=op), r, w)

    def ts(self, eng, out, in0, s1, s2, op0, op1, r, w):
        if op1 is None:
            return self.op(eng, lambda e: e.tensor_scalar(out=out, in0=in0, scalar1=s1, scalar2=None, op0=op0), r, w)
        return self.op(eng, lambda e: e.tensor_scalar(out=out, in0=in0, scalar1=s1, scalar2=s2, op0=op0, op1=op1), r, w)

    def stt(self, out, in0, scalar, in1, op0, op1, r, w):
        return self.op("dve", lambda e: e.scalar_tensor_tensor(out=out, in0=in0, scalar=scalar, in1=in1, op0=op0, op1=op1), r, w)

    def cp(self, eng, out, in_, r, w):
        return self.op(eng, lambda e: e.tensor_copy(out=out, in_=in_), r, w)

    def red(self, out, in_, op, r, w):
        return self.op("dve", lambda e: e.tensor_reduce(out=out, in_=in_, axis=AX.X, op=op), r, w)

    def mm(self, out, lhsT, rhs, start, stop, r, w):
        return self.op("pe", lambda e: e.matmul(out, lhsT=lhsT, rhs=rhs, start=start, stop=stop), r, w, skip_same=True)

    def tr(self, out, in_, ident, r, w):
        return self.op("pe", lambda e: e.transpose(out=out, in_=in_, identity=ident), r, w, skip_same=True)

    def ld(self, q, out, in_, r, w, slow=False):
        if slow:
            return self.dma(q, lambda e: e.dma_start(out=out, in_=in_, allow_slow_non_contiguous=True), r, w)
        return self.dma(q, lambda e: e.dma_start(out=out, in_=in_), r, w)


class Arena:
    def __init__(self, P, name, kbytes):
        self.t = P.sb(name, [128, kbytes * 256], F32)
        self.n = kbytes * 256
        self.off = 0
        self.base = 0

    def reset(self):
        self.off = self.base

    def get(self, shape, dt):
        n = int(np.prod(shape[1:]))
        words = n if dt in (F32, I32) else (n + 1) // 2
        assert self.off + words <= self.n, ("arena overflow", self.off + words, self.n)
        v = self.t[0:shape[0], self.off:self.off + words]
        self.off += words
        if dt != F32:
            v = v.bitcast(dt)
            if dt == BF16 and n % 2:
                v = v[:, 0:n]
        if len(shape) == 3:
            v = v.rearrange("p (a b) -> p a b", a=shape[1])
        elif len(shape) == 4:
            v = v.rearrange("p (a b c) -> p a b c", a=shape[1], b=shape[2])
        return v


def dram(nc, name, shape, dt, kind):
    return nc.dram_tensor(name, list(shape), dt, kind=kind).ap()


def moe_phase(nc, P, C, ar, tiles, NT, NB, dd, psum):
    ar.reset()
    assert len(tiles) == NT
    tp, pab, pmm = psum["big"], psum["a"], psum["b"]
    wr = ar.get([128, 8, 36], F32)
    brb = ar.get([128, 36], F32)
    lg = ar.get([128, NT, 36], F32)
    iotab = ar.get([128, NB], F32)
    iotap = ar.get([128, 1], F32)
    P.ld("sp", wr, dd["wr"].rearrange("(c p) n -> p c n", p=128), [], ["wr"], slow=True)
    P.ld("sp", brb, dd["br"].partition_broadcast(128).rearrange("p o n -> p (o n)"), [], ["brb"])
    P.ld("sp", iotab, dd["iotab"][:, 0:NB], [], ["iotab"])
    P.ld("sp", iotap, dd["iotap"], [], ["iotap"])
    g = lambda shape, dt=F32: ar.get(shape, dt)
    ga1 = g([128, NT]); ga2 = g([128, NT]); d1 = g([128, NT], I32); d2 = g([128, NT], I32); widx = g([128, NB], I32)
    h2b = [ar.get([128, D], BF16) for _ in range(2)]
    mark = ar.off
    xt = [ar.get([128, D], F32) for _ in range(4)]
    h2 = [ar.get([128, D], F32) for _ in range(2)]
    h2T = [ar.get([128, 8, 128], F32) for _ in range(2)]
    junk = ar.get([128, D], BF16)
    ssq = ar.get([128, 4], F32)
    rsq = ar.get([128, 4], F32)
    gmax = g([128, NT]); gmask = g([128, NT, 4]); gex = g([128, NT, 4]); gsum = g([128, NT]); gtop = g([128, NT])
    pen = g([128, NT, 4]); em = g([128, NT, 32]); m1 = g([128, NT]); oh1 = g([128, NT, 32]); em2 = g([128, NT, 32])
    m2 = g([128, NT]); oh2 = g([128, NT, 32]); dd_ = g([128, NT])
    S = em; pre = g([128, NT, 32]); tot = g([128, NT, 32]); base = g([128, NT, 32]); tmp32 = em2
    cnt = g([128, 32]); padded = g([128, 32]); pst = [g([128, 32]) for _ in range(2)]; pend = g([128, 32]); pstart = g([128, 32])
    d1f = g([128, NT]); d2f = g([128, NT])
    cmpb = g([128, NB, 32]); bef = g([128, NB]); wif = g([128, NB])
    tpf = tp.rearrange("p a b -> p (a b)")
    trp = [tpf[:, 0:1024].rearrange("p (j q) -> p j q", j=8), tpf[:, 1024:2048].rearrange("p (j q) -> p j q", j=8)]
    lgp = [pab[0].rearrange("p a b -> p (a b)"), pab[1].rearrange("p a b -> p (a b)")]

    def A0(i):
        w, xd, t = tiles[i]
        P.ld("sp", xt[i % 4], xd[t * 128:(t + 1) * 128, :], [], [("xt", i % 4)])

    def A1(i):
        xk = ("xt", i % 4)
        k = i % 4
        P.act(junk, xt[k], AF.Square, [xk], ["junk", ("ssq", k)], accum=ssq[:, k:k + 1])
        P.act(rsq[:, k:k + 1], ssq[:, k:k + 1], AF.Sqrt, [("ssq", k), "epsc"], [("rsq", k)], bias=C.G.epsc[:, 0:1], scale=1.0 / D)
        P.op("dve", lambda e: e.reciprocal(out=rsq[:, k:k + 1], in_=rsq[:, k:k + 1]), [("rsq", k)], [("rsq", k)])

    def B1(i):
        w, xd, t = tiles[i]
        k = i % 4
        hb, hk = h2[i % 2], ("h2", i % 2)
        P.stt(hb, xt[k], rsq[:, k:k + 1], C.bc[("A2", w)], ALU.mult, ALU.mult, [("xt", k), ("rsq", k), ("A2", w)], [hk])
        P.tt("pool", hb, hb, C.bc[("sh2", w)], ALU.add, [hk, ("sh2", w)], [hk])
        bb, bk = h2b[i % 2], ("h2b", i % 2)
        P.act(bb, hb, AF.Copy, [hk], [bk])
        P.ld("sp", dd["H2s"][i * 128:(i + 1) * 128, :], bb, [bk], [])

    def C1(i):
        hb, hk = h2[i % 2], ("h2", i % 2)
        tb, tk = trp[i % 2], ("trp", i % 2)
        for j in range(8):
            P.tr(tb[:, j, :], hb[:, j * 128:(j + 1) * 128], C.ident[:], [hk, "ident"], [tk])
        P.cp("dve", h2T[i % 2], tb, [tk], [("h2T", i % 2)])

    def D1(i):
        hT, hTk = h2T[i % 2], ("h2T", i % 2)
        lp, lk = lgp[i % 2], ("lgp", i % 2)
        for j in range(8):
            P.mm(lp[:, 0:36], hT[:, j, :], wr[:, j, :], j == 0, j == 7, [hTk, "wr"], [lk])
        P.tt("dve", lg[:, i, :], lp[:, 0:36], brb, ALU.add, [lk, "brb"], ["lg"])

    A0(0)
    if NT > 1:
        A0(1)
    for s_ in range(NT + 3):
        if s_ + 2 < NT:
            A0(s_ + 2)
        if s_ < NT:
            A1(s_)
        if 0 <= s_ - 1 < NT:
            B1(s_ - 1)
        if 0 <= s_ - 2 < NT:
            C1(s_ - 2)
        if 0 <= s_ - 3 < NT:
            D1(s_ - 3)

    glv, elv = lg[:, :, 0:4], lg[:, :, 4:36]
    bc3 = lambda a, n: a.unsqueeze(2).to_broadcast([128, NT, n])
    P.red(gmax, glv, ALU.max, ["lg"], ["gmax"])
    P.tt("dve", gmask, glv, bc3(gmax, 4), ALU.is_equal, ["lg", "gmax"], ["gmask"])
    P.tt("dve", gex, glv, bc3(gmax, 4), ALU.subtract, ["lg", "gmax"], ["gex"])
    P.act(gex, gex, AF.Exp, ["gex"], ["gex"])
    P.red(gsum, gex, ALU.add, ["gex"], ["gsum"])
    P.op("dve", lambda e: e.reciprocal(out=gtop, in_=gsum), ["gsum"], ["gtop"])
    P.ts("dve", pen, gmask, 1.0, 1e30, ALU.subtract, ALU.mult, ["gmask"], ["pen"])
    P.tt("dve", em.rearrange("p t (a b) -> p t a b", a=4), elv.rearrange("p t (a b) -> p t a b", a=4),
         pen.unsqueeze(3).to_broadcast([128, NT, 4, 8]), ALU.add, ["lg", "pen"], ["em"])
    P.red(m1, em, ALU.max, ["em"], ["m1"])
    P.tt("dve", oh1, em, bc3(m1, 32), ALU.is_equal, ["em", "m1"], ["oh1"])
    P.ts("dve", em2, oh1, -1e30, None, ALU.mult, None, ["oh1"], ["em2"])
    P.tt("dve", em2, em2, em, ALU.add, ["em2", "em"], ["em2"])
    P.red(m2, em2, ALU.max, ["em2"], ["m2"])
    P.tt("dve", oh2, em2, bc3(m2, 32), ALU.is_equal, ["em2", "m2"], ["oh2"])
    P.tt("dve", dd_, m2, m1, ALU.subtract, ["m1", "m2"], ["dd"])
    P.act(dd_, dd_, AF.Exp, ["dd"], ["dd"])
    P.ts("dve", dd_, dd_, 1.0, None, ALU.add, None, ["dd"], ["dd"])
    P.op("dve", lambda e: e.reciprocal(out=dd_, in_=dd_), ["dd"], ["dd"])
    P.tt("dve", ga1, gtop, dd_, ALU.mult, ["gtop", "dd"], ["ga1"])
    P.tt("dve", ga2, gtop, ga1, ALU.subtract, ["gtop", "ga1"], ["ga2"])
    P.tt("dve", S, oh1, oh2, ALU.add, ["oh1", "oh2"], ["S", "em"])
    Sf = S.rearrange("p t e -> p (t e)")
    pref = pre.rearrange("p t e -> p (t e)")
    totf = tot.rearrange("p t e -> p (t e)")
    NW = NT * 32
    c0 = 0
    k = 0
    while c0 < NW:
        wdt = min(512, NW - c0)
        for (lh, dst, dk) in ((C.ltri, pref, "pre"), (C.onesf, totf, "tot")):
            pp, pk = pmm[k % 2], ("pmm", k % 2)
            k += 1
            P.mm(pp[:, 0:wdt], lh[:], Sf[:, c0:c0 + wdt], True, True, ["S", "ltri", "onesf"], [pk])
            P.cp("dve", dst[:, c0:c0 + wdt], pp[:, 0:wdt], [pk], [dk])
        c0 += wdt
    P.op("dve", lambda e: e.memset(base[:, 0, :], 0.0), [], ["base"])
    for i in range(1, NT):
        P.tt("dve", base[:, i, :], base[:, i - 1, :], tot[:, i - 1, :], ALU.add, ["base", "tot"], ["base"])
    P.tt("dve", cnt, base[:, NT - 1, :], tot[:, NT - 1, :], ALU.add, ["base", "tot"], ["cnt"])
    P.tt("dve", pre, pre, base, ALU.add, ["pre", "base"], ["pre"])
    cmp2 = cmpb.rearrange("p b e -> p (b e)").rearrange("p (e b) -> p e b", e=32)
    P.tt("dve", cmp2, cnt.unsqueeze(2).to_broadcast([128, 32, NB]), iotab.unsqueeze(1).to_broadcast([128, 32, NB]), ALU.is_gt, ["cnt", "iotab"], ["cmpb"])
    P.red(padded, cmp2, ALU.add, ["cmpb"], ["padded"])
    P.ts("dve", padded, padded, float(BLK), None, ALU.mult, None, ["padded"], ["padded"])
    P.cp("dve", pst[0], padded, ["padded"], [("pst", 0)])
    cur = 0
    sh = 1
    while sh < 32:
        a, b = pst[cur], pst[1 - cur]
        P.cp("dve", b[:, 0:sh], a[:, 0:sh], [("pst", cur)], [("pst", 1 - cur)])
        P.tt("dve", b[:, sh:32], a[:, sh:32], a[:, 0:32 - sh], ALU.add, [("pst", cur)], [("pst", 1 - cur)])
        cur = 1 - cur
        sh *= 2
    P.cp("dve", pend, pst[cur], [("pst", cur)], ["pend"])
    P.tt("dve", pstart, pend, padded, ALU.subtract, ["pend", "padded"], ["pstart"])
    P.tt("dve", pre, pre, pstart.unsqueeze(1).to_broadcast([128, NT, 32]), ALU.add, ["pre", "pstart"], ["pre"])
    for (oh, ohk, df, dfk, di_, dik) in ((oh1, "oh1", d1f, "d1f", d1, "d1"), (oh2, "oh2", d2f, "d2f", d2, "d2")):
        P.tt("dve", tmp32, pre, oh, ALU.mult, ["pre", ohk], ["tmp32", "em2"])
        P.red(df, tmp32, ALU.add, ["tmp32"], [dfk])
        P.cp("dve", di_, df, [dfk], [dik])
    P.tt("dve", cmpb, pend.unsqueeze(1).to_broadcast([128, NB, 32]), iotab.unsqueeze(2).to_broadcast([128, NB, 32]), ALU.is_le, ["pend", "iotab"], ["cmpb"])
    P.red(bef, cmpb, ALU.add, ["cmpb"], ["bef"])
    P.ts("dve", bef, bef, float(NE - 1), None, ALU.min, None, ["bef"], ["bef"])
    P.ts("dve", wif, bef, 128.0, iotap[:, 0:1], ALU.mult, ALU.add, ["bef", "iotap"], ["wif"])
    P.cp("dve", widx, wif, ["wif"], ["widx"])

    P.barrier()
    ar.off = mark
    hb4 = [ar.get([128, D], BF16) for _ in range(4)]
    for i in range(NT):
        bb, bk = hb4[i % 4], ("hb4", i % 4)
        P.ld("sp", bb, dd["H2s"][i * 128:(i + 1) * 128, :], [], [bk])
        for (di_, dik) in ((d1, "d1"), (d2, "d2")):
            P.dma("pool", lambda e, bb=bb, di_=di_, i=i: e.indirect_dma_start(
                out=dd["Xs"], out_offset=bass.IndirectOffsetOnAxis(ap=di_[:, i:i + 1], axis=0), in_=bb, in_offset=None),
                [bk, dik], [])
    P.barrier()

    ar.off = mark
    NWB = 3
    wgb = [ar.get([128, 8, 512], BF16) for _ in range(NWB)]
    wub = [ar.get([128, 8, 512], BF16) for _ in range(NWB)]
    wdb = [ar.get([128, 4, D], BF16) for _ in range(NWB)]
    Xb = [ar.get([128, D], BF16) for _ in range(4)]
    XbT = [ar.get([128, 8, 128], BF16) for _ in range(2)]
    sgl = [ar.get([128, 512], F32) for _ in range(2)]
    actb = [ar.get([128, 512], BF16) for _ in range(2)]
    actT = [ar.get([128, 4, 128], BF16) for _ in range(2)]
    Yb = [ar.get([128, D], F32) for _ in range(4)]
    tpf2 = tp.rearrange("p a b -> p (a b)")
    tpb = tpf2.bitcast(BF16)
    xTp = tpb[:, 0:1024].rearrange("p (j q) -> p j q", j=8)
    aTp1 = tpb[:, 1024:1536].rearrange("p (j q) -> p j q", j=4)
    gup = [[pab[0].rearrange("p a b -> p (a b)"), pab[1].rearrange("p a b -> p (a b)")], [tpf2[:, 1024:1536], tpf2[:, 1536:2048]]]
    dn = pmm
    NS = NB * SUB

    def wload(b, which):
        wb = b % NWB
        for (buf, src, nm) in which:
            dst = buf[wb].rearrange("p a b -> p (a b)")
            P.dma("pool", lambda e, dst=dst, src=src, b=b: e.indirect_dma_start(
                out=dst, out_offset=None, in_=src, in_offset=bass.IndirectOffsetOnAxis(ap=widx[:, b:b + 1], axis=0)),
                ["widx"], [(nm, wb)])
    WGU = ((wgb, dd["wg"], "wg"), (wub, dd["wu"], "wu"))
    WD = ((wdb, dd["wd"], "wd"),)

    def S0(n):
        r0 = (n // SUB) * BLK + (n % SUB) * 128
        P.ld("sp", Xb[n % 4], dd["Xs"][r0:r0 + 128, :], [], [("Xb", n % 4)])

    def S1(n):
        xb_, xk = Xb[n % 4], ("Xb", n % 4)
        for c in range(8):
            P.tr(xTp[:, c, :], xb_[:, c::8], C.identb[:], [xk, "identb"], ["xTp"])
        P.cp("dve", XbT[n % 2], xTp, ["xTp"], [("XbT", n % 2)])

    def S2a(n):
        wb = (n // SUB) % NWB
        xT, xTk = XbT[n % 2], ("XbT", n % 2)
        g_, u_ = gup[n % 2]
        gk_, uk_ = ("gp", n % 2), ("up", n % 2)
        for c in range(8):
            P.mm(g_, xT[:, c, :], wgb[wb][:, c, :], c == 0, c == 7, [xTk, ("wg", wb)], [gk_])
        for c in range(8):
            P.mm(u_, xT[:, c, :], wub[wb][:, c, :], c == 0, c == 7, [xTk, ("wu", wb)], [uk_])
        P.act(sgl[n % 2], g_, AF.Silu, [gk_], [("sgl", n % 2)])
        P.tt("dve", actb[n % 2], u_, sgl[n % 2], ALU.mult, [uk_, ("sgl", n % 2)], [("actb", n % 2)])

    def S2b(n):
        ab, ak = actb[n % 2], ("actb", n % 2)
        ap_, apk = aTp1, "aTp"
        for c in range(4):
            P.tr(ap_[:, c, :], ab[:, c::4], C.identb[:], [ak, "identb"], [apk])
        P.act(actT[n % 2], ap_, AF.Copy, [apk], [("actT", n % 2)])

    def S3(n):
        wb = (n // SUB) % NWB
        r0 = (n // SUB) * BLK + (n % SUB) * 128
        aT, aTk = actT[n % 2], ("actT", n % 2)
        yb, yk = Yb[n % 4], ("Yb", n % 4)
        for half in range(2):
            pp, pk = dn[half], ("dn", half)
            for c in range(4):
                P.mm(pp[:], aT[:, c, :], wdb[wb][:, c, half * 512:(half + 1) * 512], c == 0, c == 3, [aTk, ("wd", wb)], [pk])
            if half == 0:
                P.act(yb[:, 0:512], pp[:], AF.Copy, [pk], [yk])
            else:
                P.cp("dve", yb[:, 512:1024], pp[:], [pk], [yk])
        P.ld("sp", dd["Ys"][r0:r0 + 128, :], yb, [yk], [])

    for b0 in range(NWB):
        wload(b0, WGU)
        wload(b0, WD)
    S0(0)
    S0(1)
    for step in range(NS + 3):
        if step + 2 < NS:
            S0(step + 2)
        if step < NS:
            S1(step)
        if 0 <= step - 1 < NS:
            S2a(step - 1)
            n2 = step - 1
            if n2 % SUB == SUB - 1 and n2 // SUB + NWB < NB:
                wload(n2 // SUB + NWB, WGU)
        if 0 <= step - 2 < NS:
            S2b(step - 2)
        if 0 <= step - 3 < NS:
            S3(step - 3)
            n3 = step - 3
            if n3 % SUB == SUB - 1 and n3 // SUB + NWB < NB:
                wload(n3 // SUB + NWB, WD)
    P.barrier()

    ar.off = mark
    xt = [ar.get([128, D], F32) for _ in range(4)]
    Y1 = [ar.get([128, D], F32) for _ in range(4)]
    Y2 = [ar.get([128, D], F32) for _ in range(4)]

    def L3(i):
        w, xd, t = tiles[i]
        k = i % 4
        P.ld("sp", xt[k], xd[t * 128:(t + 1) * 128, :], [], [("xt", k)])
        for (Y, nm, di_) in ((Y1, "Y1", d1), (Y2, "Y2", d2)):
            P.dma("pool", lambda e, yb=Y[k], di_=di_, i=i: e.indirect_dma_start(
                out=yb, out_offset=None, in_=dd["Ys"], in_offset=bass.IndirectOffsetOnAxis(ap=di_[:, i:i + 1], axis=0)),
                [], [(nm, k)])

    def C3(i):
        w, xd, t = tiles[i]
        k = i % 4
        ya, yak = Y1[k], ("Y1", k)
        yb2, ybk = Y2[k], ("Y2", k)
        xb_, xk = xt[k], ("xt", k)
        P.ts("dve", ya, ya, ga1[:, i:i + 1], None, ALU.mult, None, [yak], [yak])
        P.stt(ya, yb2, ga2[:, i:i + 1], ya, ALU.mult, ALU.add, [yak, ybk], [yak])
        P.tt("pool", ya, ya, C.bc[("g2", w)], ALU.mult, [yak, ("g2", w)], [yak])
        P.tt("dve", xb_, ya, xb_, ALU.add, [yak, xk], [xk])
        P.ld("sp", xd[t * 128:(t + 1) * 128, :], xb_, [xk], [])

    L3(0)
    if NT > 1:
        L3(1)
    for i in range(NT):
        if i + 2 < NT:
            L3(i + 2)
        C3(i)


class Ctx:
    pass


def setup_consts(nc, P, G):
    I = lambda n, s, dt=F32: dram(nc, n, s, dt, "ExternalInput")
    G.d_ident = I("ident", [128, 128])
    G.d_ltri = I("ltri", [128, 128])
    G.d_cvec = I("cvec", [2, D])
    G.ident = P.sb("ident", [128, 128], F32)
    G.identb = P.sb("identb", [128, 128], BF16)
    G.onesb = P.sb("onesb", [128, 128], BF16)
    G.onesf = P.sb("onesf", [128, 128], F32)
    G.ltri = P.sb("ltri", [128, 128], F32)
    G.negh = P.sb("negh", [128, 512], F32)
    G.epsc = P.sb("epsc", [128, 1], F32)
    P.ld("sp", G.ident[:], G.d_ident, [], ["ident"])
    P.ld("sp", G.ltri[:], G.d_ltri, [], ["ltri"])
    P.cp("dve", G.identb[:], G.ident[:], ["ident"], ["identb"])
    P.op("dve", lambda e: e.memset(G.onesb[:], 1.0), [], ["onesb"])
    P.op("dve", lambda e: e.memset(G.onesf[:], 1.0), [], ["onesf"])
    P.op("dve", lambda e: e.memset(G.negh[:], -0.5), [], ["negh"])
    P.op("dve", lambda e: e.memset(G.epsc[:], EPS), [], ["epsc"])
    G.tp = P.ps("tp", [128, 8, 256], F32)
    G.pm = [P.ps(f"pm{i}", [128, 512], F32) for i in range(2)]
    G.pb6 = P.ps("pb6", [128, 512], F32)
    G.pb7 = P.ps("pb7", [128, 512], F32)


class Layer:
    def __init__(self, nc, P, G, ar, L, nwhich, ctx_bc, bc1):
        self.nc, self.P, self.G = nc, P, G
        self.nwhich, self.ctx_bc, self.bc1 = nwhich, ctx_bc, bc1
        self.ident, self.identb, self.onesb, self.onesf, self.ltri, self.negh = G.ident, G.identb, G.onesb, G.onesf, G.ltri, G.negh
        I = lambda n, s, dt=F32: dram(nc, n, s, dt, "ExternalInput")
        self.d_wmod = I(f"w_mod{L}", [D, 6 * D])
        self.d_bmod = I(f"b_mod{L}", [1, 6 * D])
        self.d_ng = I(f"ng{L}", [2, D])
        self.d_cvec = G.d_cvec
        P.barrier()
        ar.base = 0
        ar.off = 0
        self.bc = {}
        for wch in range(nwhich if ctx_bc else 1):
            for nm in ["g1", "A2", "sh2", "g2"]:
                self.bc[(nm, wch)] = ar.get([128, D], F32)
        if bc1:
            self.bc[("A1", 0)] = ar.get([128, D], F32)
            self.bc[("sh1", 0)] = ar.get([128, D], F32)
        self.fm = ar.get([128, 2, 2, 8], F32)
        ar.base = ar.off
        self.mod_psum = G.pm
        self.mod_phase(ar)

    def mod_phase(self, ar):
        P = self.P
        ar.reset()
        nw = self.nwhich
        cfm = ar.get([128, 2, 8], F32)
        scb = ar.get([128, 2, 8], F32)
        Lb = ar.get([128, 2 * 8, 128], F32)
        modbc = ar.get([128, nw, 6 * D], F32)
        bmb = ar.get([128, 6 * D], F32)
        ngb = ar.get([128, 2, D], F32)
        wm = [ar.get([128, 8, 512], F32) for _ in range(2)]
        tmp = ar.get([128, 8, 128], F32)
        pm = self.G.pm
        P.ld("sp", cfm, self.d_cvec.rearrange("r (j p) -> p r j", p=128), [], ["cfm"], slow=True)
        P.ld("sp", bmb, self.d_bmod.partition_broadcast(128).rearrange("p o n -> p (o n)"), [], ["bmb"])
        P.ld("sp", ngb, self.d_ng.partition_broadcast(128), [], ["ngb"])
        P.act(scb, cfm, AF.Silu, ["cfm"], ["scb"])
        for wch in range(nw):
            for j in range(8):
                P.cp("dve", Lb[:, wch * 8 + j, :], scb[:, wch, j:j + 1].to_broadcast([128, 128]), ["scb"], [("Lb", wch, j)])
        wmv = self.d_wmod.rearrange("(j p) n -> p j n", p=128)
        for n in range(12):
            w = wm[n % 2]
            P.ld("sp", w, wmv[:, :, n * 512:(n + 1) * 512], [], [("wm", n % 2)])
            for wch in range(nw):
                pp = pm[(n * nw + wch) % 2]
                pk = ("pm", (n * nw + wch) % 2)
                for j in range(8):
                    P.mm(pp[:], Lb[:, wch * 8 + j, :], w[:, j, :], j == 0, j == 7, [("Lb", wch, j), ("wm", n % 2)], [pk])
                P.tt("dve", modbc[:, wch, n * 512:(n + 1) * 512], pp[:], bmb[:, n * 512:(n + 1) * 512], ALU.add, [pk, "bmb"], [("mod", wch)])
        for wch in range(nw):
            m = lambda i: modbc[:, wch, i * D:(i + 1) * D]
            mk = ("mod", wch)
            if wch == 0 or self.ctx_bc:
                P.cp("pool", self.bc[("g1", wch)], m(2), [mk], [("g1", wch)])
                P.cp("pool", self.bc[("sh2", wch)], m(3), [mk], [("sh2", wch)])
                P.cp("pool", self.bc[("g2", wch)], m(5), [mk], [("g2", wch)])
                P.stt(self.bc[("A2", wch)], m(4), 1.0, ngb[:, 1, :], ALU.add, ALU.mult, [mk, "ngb"], [("A2", wch)])
            P.stt(m(1), m(1), 1.0, ngb[:, 0, :], ALU.add, ALU.mult, [mk, "ngb"], [mk])
            if self.bc1 and wch == 0:
                P.cp("pool", self.bc[("A1", 0)], m(1), [mk], [("A1", 0)])
                P.cp("pool", self.bc[("sh1", 0)], m(0), [mk], [("sh1", 0)])
            for k, src in enumerate([m(1), m(0)]):
                P.tt("dve", tmp, src.rearrange("p (j q) -> p j q", j=8), self.ident[:].unsqueeze(1).to_broadcast([128, 8, 128]), ALU.mult, [mk, "ident"], ["fmtmp"])
                P.red(self.fm[:, wch, k, :], tmp, ALU.add, ["fmtmp"], [("fm", wch)])
        P.barrier()

    def norm_tile(self, xt, xk, xh, xhk, ss_name):
        P = self.P
        junk, ss, rs = self.nt_junk, self.nt_ss, self.nt_rs
        P.act(junk[:], xt, AF.Square, [xk], ["nt_junk", "nt_ss"], accum=ss[:])
        P.act(rs[:], ss[:], AF.Sqrt, ["nt_ss", "epsc"], ["nt_rs"], bias=self.G.epsc[:, 0:1], scale=1.0 / D)
        P.op("dve", lambda e: e.reciprocal(out=rs[:], in_=rs[:]), ["nt_rs"], ["nt_rs"])
        P.ts("dve", xh, xt, rs[:, 0:1], None, ALU.mult, None, [xk, "nt_rs"], [xhk])

    def alloc_norm(self, ar):
        self.nt_junk = ar.get([128, D], BF16)
        self.nt_ss = ar.get([128, 1], F32)
        self.nt_rs = ar.get([128, 1], F32)


def moe_inputs(nc, L):
    I = lambda n, s, dt=F32: dram(nc, n, s, dt, "ExternalInput")
    return dict(wr=I(f"wr{L}", [D, 36]), br=I(f"br{L}", [1, 36]), wg=I(f"wg{L}", [NE * 128, 4096]),
                wu=I(f"wu{L}", [NE * 128, 4096]), wd=I(f"wd{L}", [NE * 128, 4096]))


def run_moe(nc, P, G, C, ar, tiles, mi):
    NT = len(tiles)
    NB = (2 * NT * 128 + BLK - 1) // BLK + NE
    dd = dict(mi)
    dd.update(iotab=G.d_iotab, iotap=G.d_iotap, Xs=G.d_Xs, Ys=G.d_Ys, H2s=G.d_H2s)
    pab = [G.pm[0].rearrange("p (a b) -> p a b", a=2), G.pm[1].rearrange("p (a b) -> p a b", a=2)]
    P.barrier()
    moe_phase(nc, P, C, ar, tiles, NT, NB, dd, psum=dict(big=G.tp, a=pab, b=[G.pb6, G.pb7]))
    P.barrier()


def emit_conv(nc, P, G, C, ar, L, wins, ctxseg):
    I = lambda n, s, dt=F32: dram(nc, n, s, dt, "ExternalInput")
    d_win = I(f"w_in{L}", [D, 2 * D])
    d_binfm = I(f"b_in_fm{L}", [128, 16])
    d_wdwfm = I(f"w_dw_fm{L}", [128, 8 * 31])
    d_bdwfm = I(f"b_dw_fm{L}", [128, 8])
    d_gnfm = I(f"gn_fm{L}", [128, 8])
    d_wout = I(f"w_out{L}", [D, D])
    d_bout = I(f"b_out{L}", [1, D])
    tp, pm = G.tp, G.pm
    NTm = 32
    VW = (NTm + 2) * 128
    ar.off = ar.base
    binfm = ar.get([128, 16], F32)
    wdwfm = ar.get([128, 8, 31], F32)
    bdwfm = ar.get([128, 8], F32)
    gnfm = ar.get([128, 8], F32)
    hmask = ar.get([128, 2], F32)
    bog = [ar.get([128, D], F32) for _ in range(C.nwhich)]
    ar.base = ar.off
    P.ld("sp", binfm, d_binfm, [], ["binfm"])
    P.ld("sp", wdwfm, d_wdwfm.rearrange("p (j t) -> p j t", j=8), [], ["wdwfm"])
    P.ld("sp", bdwfm, d_bdwfm, [], ["bdwfm"])
    P.ld("sp", gnfm, d_gnfm, [], ["gnfm"])
    if any(w_.get("mask") is not None for w_ in wins):
        P.ld("sp", hmask, [w_["mask"] for w_ in wins if w_.get("mask") is not None][0], [], ["hmask"])
    for w in range(C.nwhich):
        P.ld("sp", bog[w], d_bout.partition_broadcast(128).rearrange("p o n -> p (o n)"), [], [("bog", w)])
        P.tt("pool", bog[w], bog[w], C.bc[("g1", w)], ALU.mult, [("bog", w), ("g1", w)], [("bog", w)])
    P.barrier()
    for wi, win_ in enumerate(wins):
        segs = [dict(w=0, x=win_["x"], nt=NTm + 2, own0=1, nown=NTm, out=win_["out"], halo=True)]
        if ctxseg is not None and wi == 0:
            segs.append(dict(w=1, x=ctxseg["x"], nt=2, own0=0, nown=2, out=ctxseg["out"], halo=False))
        has_ctx = len(segs) > 1
        ar.reset()
        vT = ar.get([128, 8, VW], BF16)
        vTc = ar.get([128, 8, 16 + 256 + 16], BF16)
        c2mark = ar.off
        win = ar.get([128, 8, 2 * D], BF16)
        hT = [ar.get([128, 8, 256], BF16) for _ in range(2)]
        xt = [ar.get([128, D], F32) for _ in range(2)]
        xh_ = [ar.get([128, D], F32) for _ in range(2)]
        sig = [ar.get([128, 256], F32) for _ in range(2)]
        C.alloc_norm(ar)
        pab = [pm[0].rearrange("p (a b) -> p a b", a=2), pm[1].rearrange("p (a b) -> p a b", a=2)]
        P.ld("pool", win, d_win.rearrange("(c p) n -> p c n", p=128), [], ["win"])
        if has_ctx:
            P.op("pool", lambda e: e.memset(vTc, 0.0), [], ["vTc"])
        gi = 0
        ti = 0
        for sg in segs:
            w = sg["w"]
            for g in range(sg["nt"] // 2):
                hb = hT[gi % 2]
                hk = ("hT", gi % 2)
                for tl in range(2):
                    t = g * 2 + tl
                    xb_, xk = xt[ti % 2], ("xt", ti % 2)
                    xhb, xhk = xh_[ti % 2], ("xh", ti % 2)
                    ti += 1
                    P.ld("sp", xb_, sg["x"][t * 128:(t + 1) * 128, :], [], [xk])
                    C.norm_tile(xb_, xk, xhb, xhk, None)
                    for j in range(8):
                        P.tr(tp[:, j, tl * 128:(tl + 1) * 128], xhb[:, j * 128:(j + 1) * 128], C.ident[:], [xhk, "ident"], [("tp", j // 2)])
                for j in range(8):
                    P.act(hb[:, j, :], tp[:, j, :], AF.Identity, [("tp", j // 2), ("fm", w)], [hk],
                          bias=C.fm[:, w, 1, j:j + 1], scale=C.fm[:, w, 0, j:j + 1])
                for jo in range(8):
                    pb_ = pab[jo % 2]
                    pk = ("pab", jo % 2)
                    for half in range(2):
                        for c in range(8):
                            P.mm(pb_[:, half, :], win[:, c, half * D + jo * 128: half * D + (jo + 1) * 128], hb[:, c, :], c == 0, c == 7, ["win", hk], [pk])
                    sg_ = sig[jo % 2]
                    P.act(sg_, pb_[:, 1, :], AF.Sigmoid, [pk, "binfm"], [("sig", jo % 2)], bias=binfm[:, 8 + jo:9 + jo])
                    if sg["halo"]:
                        dst = vT[:, jo, g * 256:(g + 1) * 256]
                        dk = "vT"
                    else:
                        dst = vTc[:, jo, 16 + g * 256:16 + (g + 1) * 256]
                        dk = "vTc"
                    P.stt(dst, pb_[:, 0, :], binfm[:, jo:jo + 1], sg_, ALU.add, ALU.mult, [pk, ("sig", jo % 2), "binfm"], [dk])
                if sg["halo"] and g == 0:
                    if win_.get("mask") is not None:
                        P.ts("pool", vT[:, :, 0:128], vT[:, :, 0:128], hmask[:, 0:1], None, ALU.mult, None, ["vT", "hmask"], ["vT"])
                    elif win_["zlo"]:
                        P.op("pool", lambda e: e.memset(vT[:, :, 0:128], 0.0), [], ["vT"])
                if sg["halo"] and g == sg["nt"] // 2 - 1:
                    if win_.get("mask") is not None:
                        P.ts("pool", vT[:, :, VW - 128:VW], vT[:, :, VW - 128:VW], hmask[:, 1:2], None, ALU.mult, None, ["vT", "hmask"], ["vT"])
                    elif win_["zhi"]:
                        P.op("pool", lambda e: e.memset(vT[:, :, VW - 128:VW], 0.0), [], ["vT"])
                gi += 1
        P.barrier()
        ar.off = c2mark
        wout = ar.get([128, 8, D], BF16)
        Dg = [ar.get([128, 31, 128], BF16) for _ in range(2)]
        vc = ar.get([128, 8, 256], F32)
        sq = ar.get([128, 8, 256], BF16)
        t1 = ar.get([128, 256], F32)
        rsb = ar.get([128, 256], F32)
        tmpv = [ar.get([128, 256], F32) for _ in range(2)]
        uT = ar.get([128, 8, 256], BF16)
        xt2 = [ar.get([128, D], F32) for _ in range(2)]
        xn = [ar.get([128, D], F32) for _ in range(2)]
        cv = [tp[:, 0:2, :].rearrange("p a b -> p (a b)"), tp[:, 2:4, :].rearrange("p a b -> p (a b)")]
        ssb = tp[:, 4:6, :].rearrange("p a b -> p (a b)")
        po = [pm[0], pm[1]]
        P.ld("pool", wout, d_wout.rearrange("(c p) n -> p c n", p=128), [], ["wout"])
        di = 0
        xi = 0
        pi = 0
        for sg in segs:
            w = sg["w"]
            ntok = sg["nown"] * 128
            W = 256
            for tb in range(ntok // W):
                for j in range(8):
                    dg, dgk = Dg[di % 2], ("Dg", di % 2)
                    di += 1
                    P.tt("pool", dg, C.identb[:].unsqueeze(1).to_broadcast([128, 31, 128]),
                         wdwfm[:, j, :].unsqueeze(2).to_broadcast([128, 31, 128]), ALU.mult, ["identb", "wdwfm"], [dgk])
                    cvb, cvk = cv[j % 2], ("cv", j % 2)
                    for tau in range(31):
                        if sg["halo"]:
                            c0 = 128 + tb * W + tau - 15
                            rhs = vT[:, j, c0:c0 + W]
                            rk = "vT"
                        else:
                            c0 = 16 + tb * W + tau - 15
                            rhs = vTc[:, j, c0:c0 + W]
                            rk = "vTc"
                        P.mm(cvb[:, 0:W], dg[:, tau, :], rhs, tau == 0, tau == 30, [dgk, rk], [cvk])
                    P.act(vc[:, j, 0:W], cvb[:, 0:W], AF.Identity, [cvk, "bdwfm"], [("vc", j)], bias=bdwfm[:, j:j + 1])
                    P.act(sq[:, j, 0:W], cvb[:, 0:W], AF.Square, [cvk, "bdwfm"], [("sq", j)], bias=bdwfm[:, j:j + 1])
                for j in range(8):
                    P.mm(ssb[:, 0:W], C.onesb[:], sq[:, j, 0:W], j == 0, j == 7, [("sq", j), "onesb"], ["ssb"])
                P.act(t1[:, 0:W], ssb[:, 0:W], AF.Sqrt, ["ssb", "epsc"], ["t1"], bias=G.epsc[:, 0:1], scale=1.0 / D)
                P.op("dve", lambda e, W=W: e.reciprocal(out=rsb[:, 0:W], in_=t1[:, 0:W]), ["t1"], ["rsb"])
                for j in range(8):
                    tv, tvk = tmpv[j % 2], ("tmpv", j % 2)
                    P.tt("dve", tv[:, 0:W], vc[:, j, 0:W], rsb[:, 0:W], ALU.mult, [("vc", j), "rsb"], [tvk])
                    P.act(uT[:, j, 0:W], tv[:, 0:W], AF.Silu, [tvk, "gnfm"], [("uT", j)], scale=gnfm[:, j:j + 1])
                for s in range(W // 128):
                    t = sg["own0"] + (tb * W) // 128 + s
                    xb_, xk = xt2[xi % 2], ("xt2", xi % 2)
                    xnb, xnk = xn[xi % 2], ("xn", xi % 2)
                    xi += 1
                    P.ld("sp", xb_, sg["x"][t * 128:(t + 1) * 128, :], [], [xk])
                    P.tt("pool", xb_, xb_, bog[w], ALU.add, [xk, ("bog", w)], [xk])
                    for half in range(2):
                        pp, pk = po[pi % 2], ("po", pi % 2)
                        pi += 1
                        for j in range(8):
                            P.mm(pp[:], uT[:, j, s * 128:(s + 1) * 128], wout[:, j, half * 512:(half + 1) * 512], j == 0, j == 7, [("uT", j), "wout"], [pk])
                        P.tt("dve", xnb[:, half * 512:(half + 1) * 512], pp[:], C.bc[("g1", w)][:, half * 512:(half + 1) * 512], ALU.mult, [pk, ("g1", w)], [xnk])
                        P.tt("pool", xnb[:, half * 512:(half + 1) * 512], xnb[:, half * 512:(half + 1) * 512], xb_[:, half * 512:(half + 1) * 512], ALU.add, [xnk, xk], [xnk])
                    to = t - sg["own0"]
                    P.ld("sp", sg["out"][to * 128:(to + 1) * 128, :], xnb, [xnk], [])
        P.barrier()


def emit_fft(nc, P, G, C, ar, L, d_x1, d_xc1, d_x2g, d_xc2):
    I = lambda n, s, dt=F32: dram(nc, n, s, dt, "ExternalInput")
    NPASS = 8
    KP = 64 // NPASS
    CB = 512 // (KP * 2)
    d_wa = I("wa", [64, NPASS, KP * 2])
    d_mb = I("mb", [128, 64 * 4 * 128])
    d_cd = I("cd", [128, 2 * 512])
    d_cdn = I("cdn", [128, 2 * 512])
    d_wout = I(f"w_out{L}", [D, D])
    d_bout = I(f"b_out{L}", [1, D])
    d_hl = G.d_hl
    tp, pm, pb6, pb7 = G.tp, G.pm, G.pb6, G.pb7
    tpf = tp.rearrange("p a b -> p (a b)")
    ar.off = ar.base
    bog = [ar.get([128, D], F32) for _ in range(2)]
    ar.base = ar.off
    for w in range(2):
        P.ld("sp", bog[w], d_bout.partition_broadcast(128).rearrange("p o n -> p (o n)"), [], [("bog", w)])
        P.tt("pool", bog[w], bog[w], C.bc[("g1", w)], ALU.mult, [("bog", w), ("g1", w)], [("bog", w)])
    P.barrier()
    ar.reset()
    xt = [ar.get([128, D], F32) for _ in range(2)]
    xh_ = [ar.get([128, D], F32) for _ in range(2)]
    hb = [ar.get([128, D], BF16) for _ in range(2)]
    C.alloc_norm(ar)
    for t in range(64):
        xb_, xk = xt[t % 2], ("xt", t % 2)
        xhb, xhk = xh_[t % 2], ("xh", t % 2)
        P.ld("sp", xb_, d_x1[t * 128:(t + 1) * 128, :], [], [xk])
        C.norm_tile(xb_, xk, xhb, xhk, None)
        P.tt("dve", xhb, xhb, C.bc[("A1", 0)], ALU.mult, [xhk, ("A1", 0)], [xhk])
        P.tt("pool", hb[t % 2], xhb, C.bc[("sh1", 0)], ALU.add, [xhk, ("sh1", 0)], [("hb", t % 2)])
        P.ld("sp", d_hl[t * 128:(t + 1) * 128, :], hb[t % 2], [("hb", t % 2)], [])
    P.barrier()
    ar.reset()
    wout = ar.get([128, 8, D], BF16)
    cd = ar.get([128, 2, 512], BF16)
    cdn = ar.get([128, 2, 512], BF16)
    wa = ar.get([64, NPASS, KP * 2], BF16)
    fT = ar.get([128, 8, KP * 128], BF16)
    fTc = ar.get([128, 8, 256], BF16)
    XA = ar.get([64, 128, 128], BF16)
    Yg = ar.get([128, 2, KP, 256], BF16)
    MB4 = [ar.get([128, 4, 4, 128], BF16) for _ in range(2)]
    ZT4 = [ar.get([128, 2, 2, 512], BF16) for _ in range(2)]
    xt = [ar.get([128, D], F32) for _ in range(2)]
    xn = [ar.get([128, 512], F32) for _ in range(2)]
    xh_ = [ar.get([128, D], F32) for _ in range(1)]
    hTc = ar.get([128, 8, 256], BF16)
    Hcs = ar.get([128, 2, 512], BF16)
    C.alloc_norm(ar)
    P.ld("pool", wout, d_wout.rearrange("(c p) n -> p c n", p=128), [], ["wout"])
    P.ld("pool", cd, d_cd.rearrange("p (a b) -> p a b", a=2), [], ["cd"])
    P.ld("pool", cdn, d_cdn.rearrange("p (a b) -> p a b", a=2), [], ["cdn"])
    P.ld("pool", wa, d_wa, [], ["wa"])
    hlv = d_hl.rearrange("(t1 t2) c -> t1 t2 c", t2=128)
    mbv = d_mb.rearrange("p (k a q) -> p k a q", k=64, a=4)
    x1v = d_x1.rearrange("(k2 k1) d -> k1 k2 d", k1=64)
    pA = [tpf[:, 0:512], tpf[:, 512:1024]]
    pZ = [tpf[:, 1024:1280], tpf[:, 1536:1792]]
    pF = [pm[0], pm[1]]
    pO2 = [pb6, pb7]

    def out_proj(src, ntile, xsrc, w, outap):
        for tl in range(ntile):
            xb_, xk = xt[tl % 2], ("xt", tl % 2)
            P.ld("sp", xb_, xsrc(tl), [], [xk])
            P.tt("pool", xb_, xb_, bog[w], ALU.add, [xk, ("bog", w)], [xk])
            for half in range(2):
                pp, pk = pO2[half], ("pO2", half)
                tb, tk = xn[half], ("xn", half)
                for c in range(8):
                    P.mm(pp[:], src[:, c, tl * 128:(tl + 1) * 128], wout[:, c, half * 512:(half + 1) * 512], c == 0, c == 7, ["fT", "wout"], [pk])
                P.tt("dve", tb, pp[:], C.bc[("g1", w)][:, half * 512:(half + 1) * 512], ALU.mult, [pk, ("g1", w)], [tk])
                P.tt("pool", xb_[:, half * 512:(half + 1) * 512], xb_[:, half * 512:(half + 1) * 512], tb, ALU.add, [tk, xk], [xk])
            P.ld("sp", outap[tl * 128:(tl + 1) * 128, :], xb_, [xk], [])

    an = 0
    zn = 0
    fn_ = 0
    mn = 0
    for hh in range(NPASS):
        for g in range(4):
            for nch in range(2):
                ch0 = g * 256 + nch * 128
                for q4 in range(4):
                    P.ld("sp", XA[:, q4 * 32:(q4 + 1) * 32, :], hlv[:, q4 * 32:(q4 + 1) * 32, ch0:ch0 + 128], [], ["XA"])
                for cb in range(128 // CB):
                    pa, pak = pA[an % 2], ("pA", an % 2)
                    an += 1
                    for cc in range(CB):
                        ch = cb * CB + cc
                        P.mm(pa[:, cc * KP * 2:(cc + 1) * KP * 2], XA[:, :, ch], wa[:, hh, :], True, True, ["XA", "wa"], [pak])
                    dst = Yg[:, :, :, nch * 128 + cb * CB: nch * 128 + (cb + 1) * CB].rearrange("p r k c -> p c k r")
                    srcv = pa.rearrange("p (c k r) -> p c k r", c=CB, k=KP)
                    if cb % 2 == 0:
                        P.cp("dve", dst, srcv, [pak], ["Yg"])
                    else:
                        P.act(dst, srcv, AF.Copy, [pak], ["Yg"])
            for kb in range(KP // 4):
                mb_, mbk = MB4[mn % 2], ("MB4", mn % 2)
                mn += 1
                k0 = hh * KP + kb * 4
                P.ld("pool", mb_, mbv[:, k0:k0 + 4, :, :], [], [mbk])
                zt, ztk = ZT4[fn_ % 2], ("ZT4", fn_ % 2)
                for q in range(4):
                    kl = kb * 4 + q
                    for nch in range(2):
                        pz, pzk = pZ[zn % 2], ("pZ", zn % 2)
                        zn += 1
                        P.mm(pz, Yg[:, 0, kl, nch * 128:(nch + 1) * 128], mb_[:, q, 0:2, :].rearrange("p a b -> p (a b)"), True, False, ["Yg", mbk], [pzk])
                        P.mm(pz, Yg[:, 1, kl, nch * 128:(nch + 1) * 128], mb_[:, q, 2:4, :].rearrange("p a b -> p (a b)"), False, True, ["Yg", mbk], [pzk])
                        P.cp("dve", zt[:, nch, :, q * 128:(q + 1) * 128], pz.rearrange("p (a b) -> p a b", a=2), [pzk], [ztk])
                for mch in range(2):
                    pf, pfk = pF[mch], ("pF", mch)
                    i4 = 0
                    for nch in range(2):
                        for ri in range(2):
                            P.mm(pf[:], cd[:, nch, ri * 256 + mch * 128: ri * 256 + (mch + 1) * 128], zt[:, nch, ri, :], i4 == 0, i4 == 3, ["cd", ztk], [pfk])
                            i4 += 1
                    P.act(fT[:, 2 * g + mch, kb * 512:(kb + 1) * 512], pf[:], AF.Copy, [pfk], ["fT"])
                fn_ += 1
        out_proj(fT, KP, lambda tl, hh=hh: x1v[hh * KP + tl], 0, d_x2g[hh * KP * 128:(hh + 1) * KP * 128, :])
    for tl in range(2):
        xb_, xk = xt[tl % 2], ("xt", tl % 2)
        xhb, xhk = xh_[0], ("xh", 0)
        P.ld("sp", xb_, d_xc1[tl * 128:(tl + 1) * 128, :], [], [xk])
        C.norm_tile(xb_, xk, xhb, xhk, None)
        for j in range(8):
            P.tr(tp[:, j, tl * 128:(tl + 1) * 128], xhb[:, j * 128:(j + 1) * 128], C.ident[:], [xhk, "ident"], [("tpc", j // 2)])
    for j in range(8):
        P.act(hTc[:, j, :], tp[:, j, :], AF.Identity, [("tpc", j // 2), ("fm", 1)], ["hTc"], bias=C.fm[:, 1, 1, j:j + 1], scale=C.fm[:, 1, 0, j:j + 1])
    for g in range(4):
        for tl in range(2):
            pf, pfk = pF[tl], ("pF", tl)
            for nch in range(2):
                P.mm(pf[:], hTc[:, 2 * g + nch, tl * 128:(tl + 1) * 128], cd[:, nch, :], nch == 0, nch == 1, ["hTc", "cd"], [pfk])
            P.cp("dve", Hcs[:, tl, :], pf[:], [pfk], ["Hcs"])
        for mch in range(2):
            pf, pfk = pF[mch], ("pF", mch)
            i4 = 0
            for tl in range(2):
                for ri in range(2):
                    P.mm(pf[:, 0:256], Hcs[:, tl, ri * 256 + mch * 128: ri * 256 + (mch + 1) * 128], cdn[:, tl, ri * 256:(ri + 1) * 256], i4 == 0, i4 == 3, ["Hcs", "cdn"], [pfk])
                    i4 += 1
            P.act(fTc[:, 2 * g + mch, :], pf[:, 0:256], AF.Copy, [pfk], ["fT"])
    out_proj(fTc, 2, lambda tl: d_xc1[tl * 128:(tl + 1) * 128, :], 1, d_xc2)
    P.barrier()


def emit_attn(nc, P, G, C, ar, d_x2g, d_xc2, d_x3h):
    I = lambda n, s, dt=F32: dram(nc, n, s, dt, "ExternalInput")
    NQT = 34
    NKT = 66
    d_cosg = I("cosg", [128, 8192])
    d_sing = I("sing", [128, 8192])
    d_cosq = I("cosq", [128, NQT * 128])
    d_sinq = I("sinq", [128, NQT * 128])
    d_qidx = I("qidx", [128, NQT], I32)
    d_wqkv = I("w_qkv", [D, 1536])
    d_gq = I("gq_fm", [128, 1])
    d_gk = I("gk_fm", [128, 1])
    d_wo = I("w_o", [D, D])
    d_prot = I("prot", [128, 128])
    SCALE = 128.0 ** -0.5
    tp, pm, pb6, pb7 = G.tp, G.pm, G.pb6, G.pb7
    tpf = tp.rearrange("p a b -> p (a b)")
    ar.off = ar.base
    gq = ar.get([128, 1], F32)
    gk = ar.get([128, 1], F32)
    prot = ar.get([128, 128], F32)
    protb = ar.get([128, 128], BF16)
    qidx = ar.get([128, NQT], I32)
    ar.base = ar.off
    P.ld("sp", gq, d_gq, [], ["gq"])
    P.ld("sp", gk, d_gk, [], ["gk"])
    P.ld("sp", prot, d_prot, [], ["prot"])
    P.ld("sp", qidx, d_qidx, [], ["qidx"])
    P.cp("dve", protb, prot, ["prot"], ["protb"])
    P.barrier()
    ar.reset()
    KT = ar.get([128, 2, NKT * 128], BF16)
    Vx = ar.get([128, NKT, 2, 130], BF16)
    wqkv = ar.get([128, 8, 1536], BF16)
    wo = ar.get([128, 8, D], BF16)
    hT = ar.get([128, 8, 512], BF16)
    xt = [ar.get([128, D], F32) for _ in range(2)]
    xh_ = [ar.get([128, D], F32) for _ in range(2)]
    C.alloc_norm(ar)
    sqb = ar.get([128, 512], BF16)
    t1 = ar.get([128, 512], F32)
    rs = ar.get([128, 512], F32)
    kn = ar.get([128, 512], F32)
    knb = ar.get([128, 512], BF16)
    cs = ar.get([128, 512], F32)
    sn = ar.get([128, 512], F32)
    t2 = ar.get([128, 512], F32)
    QT = [ar.get([128, 512], BF16) for _ in range(2)]
    PT = [ar.get([128, 512], BF16) for _ in range(3)]
    rec = ar.get([128, 4], F32)
    Ob = [ar.get([128, 128], BF16) for _ in range(2)]
    OT = ar.get([128, 8, 512], BF16)
    oacc = [ar.get([128, 4, 130], F32)] * 2
    print('attn arena words', ar.off, 'of', ar.n)
    pb4, pb5 = pm
    pS = [pb4, pb5]
    pO = [tpf[:, i * 512:i * 512 + 130] for i in range(4)]
    pb7b = pb7[:, 0:64].bitcast(BF16)
    P.ld("pool", wqkv, d_wqkv.rearrange("(c p) n -> p c n", p=128), [], ["wqkv"])
    P.ld("pool", wo, d_wo.rearrange("(c p) n -> p c n", p=128), [], ["wo"])
    P.op("pool", lambda e: e.memset(Vx, 1.0), [], ["Vx"])
    cnt = {"x": 0}

    def load_tile(dst, dk, src):
        if src[0] == "rows":
            P.ld("sp", dst, src[1], [], [dk])
        else:
            t = src[1]
            P.dma("pool", lambda e: e.indirect_dma_start(out=dst, out_offset=None, in_=d_x2g,
                                                         in_offset=bass.IndirectOffsetOnAxis(ap=qidx[:, t:t + 1], axis=0)), ["qidx"], [dk])

    def pro_group(srcs, w, c0):
        ntl = len(srcs)
        for tl in range(ntl):
            n = cnt["x"]
            cnt["x"] += 1
            xb_, xk = xt[n % 2], ("xt", n % 2)
            xhb, xhk = xh_[n % 2], ("xh", n % 2)
            load_tile(xb_, xk, srcs[tl])
            C.norm_tile(xb_, xk, xhb, xhk, None)
            for j in range(8):
                P.tr(tp[:, j, tl * 128:(tl + 1) * 128], xhb[:, j * 128:(j + 1) * 128], C.ident[:], [xhk, "ident"], [("bank", j // 2)])
        for j in range(8):
            P.act(hT[:, j, c0:c0 + ntl * 128], tp[:, j, 0:ntl * 128], AF.Identity, [("bank", j // 2), ("fm", w)], ["hT"],
                  bias=C.fm[:, w, 1, j:j + 1], scale=C.fm[:, w, 0, j:j + 1])

    def qk_steps(mm_fn, ps, psk, W, gfm, gk_, rope, out, outk):
        mm_fn()
        yield
        P.act(sqb[:, 0:W], ps[:, 0:W], AF.Square, [psk], ["sqb"])
        yield
        P.mm(pb7[:, 0:W], C.onesb[:], sqb[:, 0:W], True, True, ["sqb", "onesb"], ["pb7"])
        yield
        P.act(t1[:, 0:W], pb7[:, 0:W], AF.Sqrt, ["pb7", "epsc"], ["t1"], bias=G.epsc[:, 0:1], scale=1.0 / 128)
        yield
        P.op("dve", lambda e: e.reciprocal(out=rs[:, 0:W], in_=t1[:, 0:W]), ["t1"], ["rs"])
        yield
        P.stt(kn[:, 0:W], ps[:, 0:W], gfm[:, 0:1], rs[:, 0:W], ALU.mult, ALU.mult, [psk, gk_, "rs"], ["kn"])
        yield
        if rope:
            P.act(knb[:, 0:W], kn[:, 0:W], AF.Copy, ["kn"], ["knb"])
            yield
            P.mm(pb7[:, 0:W], protb, knb[:, 0:W], True, True, ["knb", "protb"], ["pb7"])
            yield
            P.tt("dve", t2[:, 0:W], pb7[:, 0:W], sn[:, 0:W], ALU.mult, ["pb7", "sn"], ["t2"])
            yield
            P.tt("pool", kn[:, 0:W], kn[:, 0:W], cs[:, 0:W], ALU.mult, ["kn", "cs"], ["kn"])
            yield
            P.tt("dve", out, kn[:, 0:W], t2[:, 0:W], ALU.add, ["kn", "t2"], [outk])
        else:
            P.act(out, kn[:, 0:W], AF.Copy, ["kn"], [outk])
        yield

    def run_all(gen):
        if gen is not None:
            for _ in gen:
                pass

    def step(gen):
        if gen is None:
            return None
        try:
            next(gen)
            return gen
        except StopIteration:
            return None

    rows = lambda ap, t: ("rows", ap[t * 128:(t + 1) * 128, :])
    groups = [(d_xc2, 0, 2, 1, 0, False, None)]
    for g in range(16):
        groups.append((d_x2g, g * 4, 4, 0, 2 + g * 4, True, g))
    for (xap, t0, ntl, w, kt0, rope, g) in groups:
        W = ntl * 128
        for r in range(0, ntl, 2):
            pro_group([rows(xap, t0 + r), rows(xap, t0 + r + 1)], w, r * 128)
        if rope:
            P.ld("sp", cs[:, 0:W], d_cosg[:, g * 512:g * 512 + W], [], ["cs"])
            P.ld("sp", sn[:, 0:W], d_sing[:, g * 512:g * 512 + W], [], ["sn"])
        for j in range(2):
            def kmm(j=j, W=W):
                for c in range(8):
                    P.mm(pb6[:, 0:W], wqkv[:, c, 1024 + j * 128:1024 + (j + 1) * 128], hT[:, c, 0:W], c == 0, c == 7, ["wqkv", "hT"], ["pb6"])
            run_all(qk_steps(kmm, pb6, "pb6", W, gk, "gk", rope, KT[:, j, kt0 * 128:kt0 * 128 + W], "KT"))
        for tl in range(ntl):
            pv = pS[tl % 2]
            pvk = ("pS", tl % 2)
            for c in range(8):
                P.mm(pv[:, 0:256], hT[:, c, tl * 128:(tl + 1) * 128], wqkv[:, c, 1280:1536], c == 0, c == 7, ["wqkv", "hT"], [pvk])
            P.cp("dve", Vx[:, kt0 + tl, :, 0:128], pv[:, 0:256].rearrange("p (a b) -> p a b", a=2), [pvk], ["Vx"])

    pn = 0
    qchunks = [(i * 4, 4) for i in range(8)] + [(32, 2)]
    for (qt0, nqt) in qchunks:
        W = nqt * 128
        for r in range(0, nqt, 2):
            pro_group([("idx", qt0 + r), ("idx", qt0 + r + 1)], 0, r * 128)
        P.ld("sp", cs[:, 0:W], d_cosq[:, qt0 * 128:qt0 * 128 + W], [], ["cs"])
        P.ld("sp", sn[:, 0:W], d_sinq[:, qt0 * 128:qt0 * 128 + W], [], ["sn"])
        def qgen(h, W=W):
            def qmm():
                for c in range(8):
                    P.mm(pb6[:, 0:W], wqkv[:, c, h * 128:(h + 1) * 128], hT[:, c, 0:W], c == 0, c == 7, ["wqkv", "hT"], ["pb6"])
            return qk_steps(qmm, pb6, "pb6", W, gq, "gq", True, QT[h % 2][:, 0:W], ("QT", h % 2))

        def fin_steps(h, nqt=nqt):
            oa, oak = oacc[0], "oacc"
            for qs in range(nqt):
                P.op("dve", lambda e, qs=qs: e.reciprocal(out=rec[:, qs:qs + 1], in_=oa[:, qs, 128:129]), [oak], ["rec"])
                ob, obk = Ob[qs % 2], ("Ob", qs % 2)
                P.ts("dve", ob, oa[:, qs, 0:128], rec[:, qs:qs + 1], None, ALU.mult, None, [oak, "rec"], [obk])
                yield
                P.tr(pb7b, ob, C.identb[:], [obk, "identb"], ["pb7"])
                yield
                P.act(OT[:, h, qs * 128:(qs + 1) * 128], pb7b, AF.Copy, ["pb7"], ["OT"])
                yield

        run_all(qgen(0))
        deferred = None
        for h in range(8):
            j = h // 4
            qt, qk_ = QT[h % 2], ("QT", h % 2)
            gen = qgen(h + 1) if h + 1 < 8 else None

            def S(kt):
                P.mm(pS[kt % 2][:, 0:W], KT[:, j, kt * 128:(kt + 1) * 128], qt[:, 0:W], True, True, ["KT", qk_], [("pS", kt % 2)])
            S(0)
            for kt in range(NKT):
                if kt + 1 < NKT:
                    S(kt + 1)
                pt, ptk = PT[pn % 3], ("PT", pn % 3)
                pn += 1
                P.act(pt[:, 0:W], pS[kt % 2][:, 0:W], AF.Exp, [("pS", kt % 2)], [ptk], scale=SCALE)
                for qs in range(nqt):
                    P.mm(pO[qs], pt[:, qs * 128:(qs + 1) * 128], Vx[:, kt, j, :], kt == 0, kt == NKT - 1, [ptk, "Vx"], [("bank", qs)])
                if kt % 2 == 1:
                    if deferred is not None:
                        deferred = step(deferred)
                    else:
                        gen = step(gen)
            run_all(deferred)
            run_all(gen)
            oa, oak = oacc[0], "oacc"
            for qs in range(nqt):
                P.cp("dve", oa[:, qs, :], pO[qs], [("bank", qs)], [oak])
            deferred = fin_steps(h)
        run_all(deferred)
        for qs in range(nqt):
            t = qt0 + qs
            n = cnt["x"]
            cnt["x"] += 1
            xb_, xk = xt[n % 2], ("xt", n % 2)
            load_tile(xb_, xk, ("idx", t))
            xnb, xnk = xh_[n % 2], ("xh", n % 2)
            for half in range(2):
                pp, pk = pS[half], ("pS", half)
                for h in range(8):
                    P.mm(pp[:], OT[:, h, qs * 128:(qs + 1) * 128], wo[:, h, half * 512:(half + 1) * 512], h == 0, h == 7, ["OT", "wo"], [pk])
                P.tt("dve", xnb[:, half * 512:(half + 1) * 512], pp[:], C.bc[("g1", 0)][:, half * 512:(half + 1) * 512], ALU.mult, [pk, ("g1", 0)], [xnk])
                P.tt("pool", xnb[:, half * 512:(half + 1) * 512], xnb[:, half * 512:(half + 1) * 512], xb_[:, half * 512:(half + 1) * 512], ALU.add, [xnk, xk], [xnk])
            P.ld("sp", d_x3h[t * 128:(t + 1) * 128, :], xnb, [xnk], [])
    P.barrier()


def build_fused():
    nc = bass.Bass("TRN2", target_bir_lowering=False)
    I = lambda n, s, dt=F32: dram(nc, n, s, dt, "ExternalInput")
    N = lambda n, s, dt=F32: dram(nc, n, s, dt, "Internal")
    NBM = (2 * 66 * 128 + BLK - 1) // BLK + NE
    d_xpad = I("xpad", [66 * 128, D])
    d_xc = I("xc", [256, D])
    d_hmask = I("hmask", [128, 2])
    d_xo = dram(nc, "xo", [32 * 128, D], F32, "ExternalOutput")
    d_x1 = N("x1", [8192, D])
    d_xc1 = N("xc1", [256, D])
    d_x2g = N("x2g", [8192, D])
    d_xc2 = N("xc2", [256, D])
    d_x3h = N("x3h", [34 * 128, D])
    with ExitStack() as es:
        P = Prog(nc, es)
        G = Ctx()
        setup_consts(nc, P, G)
        G.d_iotab = I("iota_b", [128, NBM])
        G.d_iotap = I("iota_p", [128, 1])
        G.d_Xs = N("Xs", [NBM * BLK, D], BF16)
        G.d_Ys = N("Ys", [NBM * BLK, D])
        G.d_hl = N("hl", [8192, D], BF16)
        G.d_H2s = N("H2s", [66 * 128, D], BF16)
        ar = Arena(P, "arena", 182)
        ar.base = 0
        tl = lambda ap, a, b, w=0: [(w, ap, t) for t in range(a, b)]
        C = Layer(nc, P, G, ar, 0, 2, True, False)
        mi = moe_inputs(nc, 0)
        wins = [dict(x=d_xpad[0:34 * 128, :], out=d_x1[0:4096, :], zlo=True, zhi=False),
                dict(x=d_xpad[32 * 128:66 * 128, :], out=d_x1[4096:8192, :], zlo=False, zhi=True)]
        emit_conv(nc, P, G, C, ar, 0, wins, dict(x=d_xc, out=d_xc1))
        run_moe(nc, P, G, C, ar, tl(d_x1, 0, 64) + tl(d_xc1, 0, 2, 1), mi)
        C = Layer(nc, P, G, ar, 1, 2, True, True)
        mi = moe_inputs(nc, 1)
        emit_fft(nc, P, G, C, ar, 1, d_x1, d_xc1, d_x2g, d_xc2)
        run_moe(nc, P, G, C, ar, tl(d_x2g, 0, 64) + tl(d_xc2, 0, 2, 1), mi)
        C = Layer(nc, P, G, ar, 2, 2, False, False)
        mi = moe_inputs(nc, 2)
        emit_attn(nc, P, G, C, ar, d_x2g, d_xc2, d_x3h)
        run_moe(nc, P, G, C, ar, tl(d_x3h, 0, 34), mi)
        C = Layer(nc, P, G, ar, 3, 1, False, False)
        mi = moe_inputs(nc, 3)
        emit_conv(nc, P, G, C, ar, 3, [dict(x=d_x3h, out=d_xo, zlo=False, zhi=False, mask=d_hmask)], None)
        run_moe(nc, P, G, C, ar, tl(d_xo, 0, 32), mi)
        P.barrier()
        P.finish()
        print("fused program instructions:", P.ninst, "sems:", P.nsem)
    return nc


_CACHE = {}


def _fm(v, n):
    return np.ascontiguousarray(v.reshape(n, 128).T)


def _rope_tables():
    rows = 8192 // 64
    row = np.repeat(np.arange(rows, dtype=np.float32), 64)
    col = np.tile(np.arange(64, dtype=np.float32), rows)
    inv = (np.float32(10000.0) ** (-np.arange(32, dtype=np.float32) / np.float32(32))).astype(np.float32)
    ang = np.concatenate([row[:, None] * inv, col[:, None] * inv], axis=-1).astype(np.float32)
    c, s_ = np.cos(ang).astype(np.float32), np.sin(ang).astype(np.float32)
    cosf = np.repeat(c, 2, axis=1).T
    sgn = np.tile(np.array([-1.0, 1.0], np.float32), 64)
    sinf = (np.repeat(s_, 2, axis=1) * sgn[None, :]).T
    return np.ascontiguousarray(cosf), np.ascontiguousarray(sinf)


def _fft_tables():
    k1 = np.arange(64, dtype=np.float64)
    t1 = np.arange(64, dtype=np.float64)
    a = 2 * np.pi * np.outer(t1, k1) / 64.0
    wa = np.stack([np.cos(a), -np.sin(a)], -1) / 8.0
    wa = wa.reshape(64, 8, 8 * 2)
    t2 = np.arange(128, dtype=np.float64)
    k2 = np.arange(128, dtype=np.float64)
    kk = k1[:, None] + 64.0 * k2[None, :]
    th = 2 * np.pi * t2[:, None, None] * kk[None, :, :] / 8192.0
    mr, mi = np.cos(th), -np.sin(th)
    mb = np.stack([mr, mi, -mi, mr], 2) / np.sqrt(128.0)
    n = np.arange(256, dtype=np.float64)
    ph = 2 * np.pi * np.outer(n, n) / 256.0
    cs = np.concatenate([np.cos(ph), np.sin(ph)], 1) / 16.0
    csn = np.concatenate([np.cos(ph), -np.sin(ph)], 1) / 16.0
    cd = cs.reshape(2, 128, 512).transpose(1, 0, 2).reshape(128, 1024)
    cdn = csn.reshape(2, 128, 512).transpose(1, 0, 2).reshape(128, 1024)
    f = lambda a_: np.ascontiguousarray(a_.astype(np.float32))
    return f(wa), f(mb.reshape(128, -1)), f(cd), f(cdn)


def kernel(**inp):
    inp = {k: np.asarray(v) for k, v in inp.items()}
    if "nc" not in _CACHE:
        _CACHE["nc"] = build_fused()
    nc = _CACHE["nc"]
    NBM = (2 * 66 * 128 + BLK - 1) // BLK + NE
    sh = dict(ident=np.eye(128, dtype=np.float32), ltri=np.triu(np.ones((128, 128), np.float32), 1),
              iota_b=np.tile((np.arange(NBM, dtype=np.float32) * BLK)[None, :], (128, 1)),
              iota_p=np.arange(128, dtype=np.float32).reshape(128, 1))
    for L in range(4):
        sh[f"w_mod{L}"] = inp["w_mod"][L]
        sh[f"b_mod{L}"] = inp["b_mod"][L][None, :]
        sh[f"ng{L}"] = inp["norm_g"][L]
        sh[f"wr{L}"] = np.ascontiguousarray(np.concatenate([inp["moe_w_group"][L], inp["moe_w_expert"][L]], axis=1))
        sh[f"br{L}"] = np.concatenate([inp["moe_b_group"][L], inp["moe_b_expert"][L]])[None, :]
        sh[f"wg{L}"] = inp["moe_w_gate"][L].reshape(NE * 128, 4096)
        sh[f"wu{L}"] = inp["moe_w_up"][L].reshape(NE * 128, 4096)
        sh[f"wd{L}"] = inp["moe_w_down"][L].reshape(NE * 128, 4096)
    for L, j in ((0, 0), (3, 1)):
        sh[f"w_in{L}"] = inp["conv_w_in"][j]
        sh[f"b_in_fm{L}"] = _fm(inp["conv_b_in"][j], 16)
        sh[f"w_dw_fm{L}"] = np.ascontiguousarray(inp["conv_w_dw"][j].T.reshape(8, 128, 31).transpose(1, 0, 2).reshape(128, 8 * 31))
        sh[f"b_dw_fm{L}"] = _fm(inp["conv_b_dw"][j], 8)
        sh[f"gn_fm{L}"] = _fm(inp["conv_norm_g"][j], 8)
        sh[f"w_out{L}"] = inp["conv_w_out"][j]
        sh[f"b_out{L}"] = inp["conv_b_out"][j][None, :]
    sh["w_out1"] = inp["fnet_w_out"][0]
    sh["b_out1"] = inp["fnet_b_out"][0][None, :]
    sh["wa"], sh["mb"], sh["cd"], sh["cdn"] = _fft_tables()
    cosf, sinf = _rope_tables()
    gtok = (np.arange(64)[:, None] + 64 * np.arange(128)[None, :]).reshape(-1)
    sh["cosg"] = np.ascontiguousarray(cosf[:, gtok])
    sh["sing"] = np.ascontiguousarray(sinf[:, gtok])
    prot = np.zeros((128, 128), np.float32)
    for i in range(64):
        prot[2 * i, 2 * i + 1] = 1
        prot[2 * i + 1, 2 * i] = 1
    sh.update(w_qkv=inp["attn_w_qkv"][0], gq_fm=inp["attn_q_norm_g"][0].reshape(128, 1), gk_fm=inp["attn_k_norm_g"][0].reshape(128, 1),
              w_o=inp["attn_w_out"][0], prot=prot)
    zpad = np.zeros((128, D), np.float32)
    in_maps = []
    for core in range(8):
        b, s = core // 2, core % 2
        m = dict(sh)
        m["xpad"] = np.ascontiguousarray(np.concatenate([zpad, inp["x"][b], zpad], axis=0))
        m["xc"] = np.ascontiguousarray(inp["ctx"][b])
        m["cvec"] = np.stack([inp["c"][b], inp["c_ctx"]])
        tok = np.clip(4096 * s - 128 + np.arange(34 * 128), 0, 8191)
        m["cosq"] = np.ascontiguousarray(cosf[:, tok])
        m["sinq"] = np.ascontiguousarray(sinf[:, tok])
        grow = (tok % 64) * 128 + tok // 64
        m["qidx"] = np.ascontiguousarray(grow.reshape(34, 128).T.astype(np.int32))
        hm = np.ones((128, 2), np.float32)
        if s == 0:
            hm[:, 0] = 0
        else:
            hm[:, 1] = 0
        m["hmask"] = hm
        in_maps.append(m)
    res = run_bass_kernel_spmd(nc, in_maps, core_ids=list(range(8)))
    out = np.empty_like(inp["x"])
    for core in range(8):
        b, s = core // 2, core % 2
        out[b, s * 4096:(s + 1) * 4096] = res.results[core]["xo"]
    return out
```

```python
import numpy as np
import concourse.bass as bass
import concourse.mybir as mybir
from concourse.bass_utils import run_bass_kernel_spmd
from contextlib import ExitStack

F32 = mybir.dt.float32
BF16 = mybir.dt.bfloat16
I32 = mybir.dt.int32
AF = mybir.ActivationFunctionType
ALU = mybir.AluOpType
AX = mybir.AxisListType

ENG = ["pe", "act", "dve", "pool", "sp"]
D = 1024
EPS = 1e-6
NE = 32
BLK = 256
SUB = BLK // 128


class Prog:
    SEM_LIMIT = 20000
    ND = 8

    def __init__(self, nc, es):
        self.nc, self.es = nc, es
        self.q = {e: [] for e in ENG}
        self.cur = {}
        self.waited = {e: {} for e in ENG}
        self.last_w = {}
        self.readers = {}
        self.nsem = 0
        self.dsem = {e: [] for e in ENG}
        self.dn = {e: 0 for e in ENG}
        self.sems = {}
        self.ninst = 0
        self.nbar = 0
        self.full = {}

    def new_sem(self):
        self.nsem += 1
        self.sems[self.nsem] = self.es.enter_context(self.nc.semaphore(f"s{self.nsem}"))
        return self.nsem

    def sb(self, name, shape, dt):
        return self.es.enter_context(self.nc.sbuf_tensor("sb_" + name, list(shape), dt))

    def ps(self, name, shape, dt):
        return self.es.enter_context(self.nc.psum_tensor("ps_" + name, list(shape), dt))

    def _deps(self, eng, reads, writes, skip_same):
        deps = {}

        def add(ev):
            if ev is None:
                return
            s, v, e = ev
            if skip_same and e == eng:
                return
            if deps.get(s, 0) < v:
                deps[s] = v
        for k in reads:
            add(self.last_w.get(k))
        for k in writes:
            add(self.last_w.get(k))
            for ev in self.readers.get(k, ()):
                add(ev)
        out = []
        for s, v in deps.items():
            if self.waited[eng].get(s, 0) < v:
                self.waited[eng][s] = v
                out.append((s, v))
        return out

    def _commit(self, ev, reads, writes):
        for k in writes:
            self.last_w[k] = ev
            self.readers[k] = []
        for k in reads:
            self.readers.setdefault(k, []).append(ev)

    def op(self, eng, fn, reads=(), writes=(), skip_same=False):
        waits = self._deps(eng, reads, writes, skip_same)
        c = self.cur.get(eng)
        if c is None or c[1] >= self.SEM_LIMIT:
            if c is not None:
                self.full[eng] = (c[0], c[1])
            c = [self.new_sem(), 0]
            self.cur[eng] = c
        c[1] += 1
        ev = (c[0], c[1], eng)
        self.q[eng].append((waits, fn, c[0], 1))
        self._commit(ev, reads, writes)
        self.ninst += 1
        return ev

    def dma(self, eng, fn, reads=(), writes=()):
        waits = self._deps(eng, reads, writes, False)
        pool = self.dsem[eng]
        i = self.dn[eng] % self.ND
        self.dn[eng] += 1
        if len(pool) <= i:
            pool.append([self.new_sem(), 0])
        d = pool[i]
        if d[1] > 0 and self.waited[eng].get(d[0], 0) < 16 * d[1]:
            self.waited[eng][d[0]] = 16 * d[1]
            waits.append((d[0], 16 * d[1]))
        d[1] += 1
        ev = (d[0], 16 * d[1], "dma_" + eng)
        self.q[eng].append((waits, fn, d[0], 16))
        self._commit(ev, reads, writes)
        self.ninst += 1
        return ev

    def wait_keys(self, eng, keys):
        waits = self._deps(eng, keys, (), False)
        self.q[eng].append((waits, None, None, 0))

    def barrier(self):
        evs = []
        for e in ENG:
            c = self.cur.get(e)
            if c is not None:
                evs.append((c[0], c[1]))
            if e in self.full:
                evs.append(self.full[e])
            for d in self.dsem[e]:
                if d[1] > 0:
                    evs.append((d[0], 16 * d[1]))
        for e in ENG:
            waits = []
            for s, v in evs:
                if self.waited[e].get(s, 0) < v:
                    self.waited[e][s] = v
                    waits.append((s, v))
            self.q[e].append((waits, None, None, 0))
        self.last_w = {}
        self.readers = {}

    def finish(self):
        nc = self.nc
        engobj = {"pe": "tensor", "act": "scalar", "dve": "vector", "pool": "gpsimd", "sp": "sync"}
        with nc.Block() as block:
            for e in ENG:
                if not self.q[e]:
                    continue

                def body(engine, e=e):
                    for waits, fn, s, inc in self.q[e]:
                        for (ws, wv) in waits:
                            engine.wait_ge(self.sems[ws], wv)
                        if fn is not None:
                            fn(engine).then_inc(self.sems[s], inc)
                getattr(block, engobj[e])(body)

    def act(self, out, in_, func, r, w, bias=None, scale=None, accum=None):
        kw = {}
        if bias is not None:
            kw["bias"] = bias
        if scale is not None:
            kw["scale"] = scale
        if accum is not None:
            kw["accum_out"] = accum
        return self.op("act", lambda e: e.activation(out=out, in_=in_, func=func, **kw), r, w)

    def tt(self, eng, out, in0, in1, op, r, w):
        return self.op(eng, lambda e: e.tensor_tensor(out=out, in0=in0, in1=in1, op=op), r, w)

    def ts(self, eng, out, in0, s1, s2, op0, op1, r, w):
        if op1 is None:
            return self.op(eng, lambda e: e.tensor_scalar(out=out, in0=in0, scalar1=s1, scalar2=None, op0=op0), r, w)
        return self.op(eng, lambda e: e.tensor_scalar(out=out, in0=in0, scalar1=s1, scalar2=s2, op0=op0, op1=op1), r, w)

    def stt(self, out, in0, scalar, in1, op0, op1, r, w):
        return self.op("dve", lambda e: e.scalar_tensor_tensor(out=out, in0=in0, scalar=scalar, in1=in1, op0=op0, op1=op1), r, w)

    def cp(self, eng, out, in_, r, w):
        return self.op(eng, lambda e: e.tensor_copy(out=out, in_=in_), r, w)

    def red(self, out, in_, op, r, w):
        return self.op("dve", lambda e: e.tensor_reduce(out=out, in_=in_, axis=AX.X, op=op), r, w)

    def mm(self, out, lhsT, rhs, start, stop, r, w):
        return self.op("pe", lambda e: e.matmul(out, lhsT=lhsT, rhs=rhs, start=start, stop=stop), r, w, skip_same=True)

    def tr(self, out, in_, ident, r, w):
        return self.op("pe", lambda e: e.transpose(out=out, in_=in_, identity=ident), r, w, skip_same=True)

    def ld(self, q, out, in_, r, w, slow=False):
        if slow:
            return self.dma(q, lambda e: e.dma_start(out=out, in_=in_, allow_slow_non_contiguous=True), r, w)
        return self.dma(q, lambda e: e.dma_start(out=out, in_=in_), r, w)


class Arena:
    def __init__(self, P, name, kbytes):
        self.t = P.sb(name, [128, kbytes * 256], F32)
        self.n = kbytes * 256
        self.off = 0
        self.base = 0

    def reset(self):
        self.off = self.base

    def get(self, shape, dt):
        n = int(np.prod(shape[1:]))
        words = n if dt in (F32, I32) else (n + 1) // 2
        assert self.off + words <= self.n, ("arena overflow", self.off + words, self.n)
        v = self.t[0:shape[0], self.off:self.off + words]
        self.off += words
        if dt != F32:
            v = v.bitcast(dt)
            if dt == BF16 and n % 2:
                v = v[:, 0:n]
        if len(shape) == 3:
            v = v.rearrange("p (a b) -> p a b", a=shape[1])
        elif len(shape) == 4:
            v = v.rearrange("p (a b c) -> p a b c", a=shape[1], b=shape[2])
        return v


def dram(nc, name, shape, dt, kind):
    return nc.dram_tensor(name, list(shape), dt, kind=kind).ap()


def moe_phase(nc, P, C, ar, tiles, NT, NB, dd, psum):
    ar.reset()
    assert len(tiles) == NT
    tp, pab, pmm = psum["big"], psum["a"], psum["b"]
    wr = ar.get([128, 8, 36], F32)
    brb = ar.get([128, 36], F32)
    lg = ar.get([128, NT, 36], F32)
    iotab = ar.get([128, NB], F32)
    iotap = ar.get([128, 1], F32)
    P.ld("sp", wr, dd["wr"].rearrange("(c p) n -> p c n", p=128), [], ["wr"], slow=True)
    P.ld("sp", brb, dd["br"].partition_broadcast(128).rearrange("p o n -> p (o n)"), [], ["brb"])
    P.ld("sp", iotab, dd["iotab"][:, 0:NB], [], ["iotab"])
    P.ld("sp", iotap, dd["iotap"], [], ["iotap"])
    g = lambda shape, dt=F32: ar.get(shape, dt)
    ga1 = g([128, NT]); ga2 = g([128, NT]); d1 = g([128, NT], I32); d2 = g([128, NT], I32); widx = g([128, NB], I32)
    h2b = [ar.get([128, D], BF16) for _ in range(2)]
    mark = ar.off
    xt = [ar.get([128, D], F32) for _ in range(4)]
    h2 = [ar.get([128, D], F32) for _ in range(2)]
    h2T = [ar.get([128, 8, 128], F32) for _ in range(2)]
    junk = ar.get([128, D], BF16)
    ssq = ar.get([128, 4], F32)
    rsq = ar.get([128, 4], F32)
    gmax = g([128, NT]); gmask = g([128, NT, 4]); gex = g([128, NT, 4]); gsum = g([128, NT]); gtop = g([128, NT])
    pen = g([128, NT, 4]); em = g([128, NT, 32]); m1 = g([128, NT]); oh1 = g([128, NT, 32]); em2 = g([128, NT, 32])
    m2 = g([128, NT]); oh2 = g([128, NT, 32]); dd_ = g([128, NT])
    S = em; pre = g([128, NT, 32]); tot = g([128, NT, 32]); base = g([128, NT, 32]); tmp32 = em2
    cnt = g([128, 32]); padded = g([128, 32]); pst = [g([128, 32]) for _ in range(2)]; pend = g([128, 32]); pstart = g([128, 32])
    d1f = g([128, NT]); d2f = g([128, NT])
    cmpb = g([128, NB, 32]); bef = g([128, NB]); wif = g([128, NB])
    tpf = tp.rearrange("p a b -> p (a b)")
    trp = [tpf[:, 0:1024].rearrange("p (j q) -> p j q", j=8), tpf[:, 1024:2048].rearrange("p (j q) -> p j q", j=8)]
    lgp = [pab[0].rearrange("p a b -> p (a b)"), pab[1].rearrange("p a b -> p (a b)")]

    def A0(i):
        w, xd, t = tiles[i]
        P.ld("sp", xt[i % 4], xd[t * 128:(t + 1) * 128, :], [], [("xt", i % 4)])

    def A1(i):
        xk = ("xt", i % 4)
        k = i % 4
        P.act(junk, xt[k], AF.Square, [xk], ["junk", ("ssq", k)], accum=ssq[:, k:k + 1])
        P.act(rsq[:, k:k + 1], ssq[:, k:k + 1], AF.Sqrt, [("ssq", k), "epsc"], [("rsq", k)], bias=C.G.epsc[:, 0:1], scale=1.0 / D)
        P.op("dve", lambda e: e.reciprocal(out=rsq[:, k:k + 1], in_=rsq[:, k:k + 1]), [("rsq", k)], [("rsq", k)])

    def B1(i):
        w, xd, t = tiles[i]
        k = i % 4
        hb, hk = h2[i % 2], ("h2", i % 2)
        P.stt(hb, xt[k], rsq[:, k:k + 1], C.bc[("A2", w)], ALU.mult, ALU.mult, [("xt", k), ("rsq", k), ("A2", w)], [hk])
        P.tt("pool", hb, hb, C.bc[("sh2", w)], ALU.add, [hk, ("sh2", w)], [hk])
        bb, bk = h2b[i % 2], ("h2b", i % 2)
        P.act(bb, hb, AF.Copy, [hk], [bk])
        P.ld("sp", dd["H2s"][i * 128:(i + 1) * 128, :], bb, [bk], [])

    def C1(i):
        hb, hk = h2[i % 2], ("h2", i % 2)
        tb, tk = trp[i % 2], ("trp", i % 2)
        for j in range(8):
            P.tr(tb[:, j, :], hb[:, j * 128:(j + 1) * 128], C.ident[:], [hk, "ident"], [tk])
        P.cp("dve", h2T[i % 2], tb, [tk], [("h2T", i % 2)])

    def D1(i):
        hT, hTk = h2T[i % 2], ("h2T", i % 2)
        lp, lk = lgp[i % 2], ("lgp", i % 2)
        for j in range(8):
            P.mm(lp[:, 0:36], hT[:, j, :], wr[:, j, :], j == 0, j == 7, [hTk, "wr"], [lk])
        P.tt("dve", lg[:, i, :], lp[:, 0:36], brb, ALU.add, [lk, "brb"], ["lg"])

    A0(0)
    if NT > 1:
        A0(1)
    for s_ in range(NT + 3):
        if s_ + 2 < NT:
            A0(s_ + 2)
        if s_ < NT:
            A1(s_)
        if 0 <= s_ - 1 < NT:
            B1(s_ - 1)
        if 0 <= s_ - 2 < NT:
            C1(s_ - 2)
        if 0 <= s_ - 3 < NT:
            D1(s_ - 3)

    glv, elv = lg[:, :, 0:4], lg[:, :, 4:36]
    bc3 = lambda a, n: a.unsqueeze(2).to_broadcast([128, NT, n])
    P.red(gmax, glv, ALU.max, ["lg"], ["gmax"])
    P.tt("dve", gmask, glv, bc3(gmax, 4), ALU.is_equal, ["lg", "gmax"], ["gmask"])
    P.tt("dve", gex, glv, bc3(gmax, 4), ALU.subtract, ["lg", "gmax"], ["gex"])
    P.act(gex, gex, AF.Exp, ["gex"], ["gex"])
    P.red(gsum, gex, ALU.add, ["gex"], ["gsum"])
    P.op("dve", lambda e: e.reciprocal(out=gtop, in_=gsum), ["gsum"], ["gtop"])
    P.ts("dve", pen, gmask, 1.0, 1e30, ALU.subtract, ALU.mult, ["gmask"], ["pen"])
    P.tt("dve", em.rearrange("p t (a b) -> p t a b", a=4), elv.rearrange("p t (a b) -> p t a b", a=4),
         pen.unsqueeze(3).to_broadcast([128, NT, 4, 8]), ALU.add, ["lg", "pen"], ["em"])
    P.red(m1, em, ALU.max, ["em"], ["m1"])
    P.tt("dve", oh1, em, bc3(m1, 32), ALU.is_equal, ["em", "m1"], ["oh1"])
    P.ts("dve", em2, oh1, -1e30, None, ALU.mult, None, ["oh1"], ["em2"])
    P.tt("dve", em2, em2, em, ALU.add, ["em2", "em"], ["em2"])
    P.red(m2, em2, ALU.max, ["em2"], ["m2"])
    P.tt("dve", oh2, em2, bc3(m2, 32), ALU.is_equal, ["em2", "m2"], ["oh2"])
    P.tt("dve", dd_, m2, m1, ALU.subtract, ["m1", "m2"], ["dd"])
    P.act(dd_, dd_, AF.Exp, ["dd"], ["dd"])
    P.ts("dve", dd_, dd_, 1.0, None, ALU.add, None, ["dd"], ["dd"])
    P.op("dve", lambda e: e.reciprocal(out=dd_, in_=dd_), ["dd"], ["dd"])
    P.tt("dve", ga1, gtop, dd_, ALU.mult, ["gtop", "dd"], ["ga1"])
    P.tt("dve", ga2, gtop, ga1, ALU.subtract, ["gtop", "ga1"], ["ga2"])
    P.tt("dve", S, oh1, oh2, ALU.add, ["oh1", "oh2"], ["S", "em"])
    Sf = S.rearrange("p t e -> p (t e)")
    pref = pre.rearrange("p t e -> p (t e)")
    totf = tot.rearrange("p t e -> p (t e)")
    NW = NT * 32
    c0 = 0
    k = 0
    while c0 < NW:
        wdt = min(512, NW - c0)
        for (lh, dst, dk) in ((C.ltri, pref, "pre"), (C.onesf, totf, "tot")):
            pp, pk = pmm[k % 2], ("pmm", k % 2)
            k += 1
            P.mm(pp[:, 0:wdt], lh[:], Sf[:, c0:c0 + wdt], True, True, ["S", "ltri", "onesf"], [pk])
            P.cp("dve", dst[:, c0:c0 + wdt], pp[:, 0:wdt], [pk], [dk])
        c0 += wdt
    P.op("dve", lambda e: e.memset(base[:, 0, :], 0.0), [], ["base"])
    for i in range(1, NT):
        P.tt("dve", base[:, i, :], base[:, i - 1, :], tot[:, i - 1, :], ALU.add, ["base", "tot"], ["base"])
    P.tt("dve", cnt, base[:, NT - 1, :], tot[:, NT - 1, :], ALU.add, ["base", "tot"], ["cnt"])
    P.tt("dve", pre, pre, base, ALU.add, ["pre", "base"], ["pre"])
    cmp2 = cmpb.rearrange("p b e -> p (b e)").rearrange("p (e b) -> p e b", e=32)
    P.tt("dve", cmp2, cnt.unsqueeze(2).to_broadcast([128, 32, NB]), iotab.unsqueeze(1).to_broadcast([128, 32, NB]), ALU.is_gt, ["cnt", "iotab"], ["cmpb"])
    P.red(padded, cmp2, ALU.add, ["cmpb"], ["padded"])
    P.ts("dve", padded, padded, float(BLK), None, ALU.mult, None, ["padded"], ["padded"])
    P.cp("dve", pst[0], padded, ["padded"], [("pst", 0)])
    cur = 0
    sh = 1
    while sh < 32:
        a, b = pst[cur], pst[1 - cur]
        P.cp("dve", b[:, 0:sh], a[:, 0:sh], [("pst", cur)], [("pst", 1 - cur)])
        P.tt("dve", b[:, sh:32], a[:, sh:32], a[:, 0:32 - sh], ALU.add, [("pst", cur)], [("pst", 1 - cur)])
        cur = 1 - cur
        sh *= 2
    P.cp("dve", pend, pst[cur], [("pst", cur)], ["pend"])
    P.tt("dve", pstart, pend, padded, ALU.subtract, ["pend", "padded"], ["pstart"])
    P.tt("dve", pre, pre, pstart.unsqueeze(1).to_broadcast([128, NT, 32]), ALU.add, ["pre", "pstart"], ["pre"])
    for (oh, ohk, df, dfk, di_, dik) in ((oh1, "oh1", d1f, "d1f", d1, "d1"), (oh2, "oh2", d2f, "d2f", d2, "d2")):
        P.tt("dve", tmp32, pre, oh, ALU.mult, ["pre", ohk], ["tmp32", "em2"])
        P.red(df, tmp32, ALU.add, ["tmp32"], [dfk])
        P.cp("dve", di_, df, [dfk], [dik])
    P.tt("dve", cmpb, pend.unsqueeze(1).to_broadcast([128, NB, 32]), iotab.unsqueeze(2).to_broadcast([128, NB, 32]), ALU.is_le, ["pend", "iotab"], ["cmpb"])
    P.red(bef, cmpb, ALU.add, ["cmpb"], ["bef"])
    P.ts("dve", bef, bef, float(NE - 1), None, ALU.min, None, ["bef"], ["bef"])
    P.ts("dve", wif, bef, 128.0, iotap[:, 0:1], ALU.mult, ALU.add, ["bef", "iotap"], ["wif"])
    P.cp("dve", widx, wif, ["wif"], ["widx"])

    P.barrier()
    ar.off = mark
    hb4 = [ar.get([128, D], BF16) for _ in range(4)]
    for i in range(NT):
        bb, bk = hb4[i % 4], ("hb4", i % 4)
        P.ld("sp", bb, dd["H2s"][i * 128:(i + 1) * 128, :], [], [bk])
        for (di_, dik) in ((d1, "d1"), (d2, "d2")):
            P.dma("pool", lambda e, bb=bb, di_=di_, i=i: e.indirect_dma_start(
                out=dd["Xs"], out_offset=bass.IndirectOffsetOnAxis(ap=di_[:, i:i + 1], axis=0), in_=bb, in_offset=None),
                [bk, dik], [])
    P.barrier()

    ar.off = mark
    NWB = 3
    wgb = [ar.get([128, 8, 512], BF16) for _ in range(NWB)]
    wub = [ar.get([128, 8, 512], BF16) for _ in range(NWB)]
    wdb = [ar.get([128, 4, D], BF16) for _ in range(NWB)]
    Xb = [ar.get([128, D], BF16) for _ in range(4)]
    XbT = [ar.get([128, 8, 128], BF16) for _ in range(2)]
    sgl = [ar.get([128, 512], F32) for _ in range(2)]
    actb = [ar.get([128, 512], BF16) for _ in range(2)]
    actT = [ar.get([128, 4, 128], BF16) for _ in range(2)]
    Yb = [ar.get([128, D], F32) for _ in range(4)]
    tpf2 = tp.rearrange("p a b -> p (a b)")
    tpb = tpf2.bitcast(BF16)
    xTp = tpb[:, 0:1024].rearrange("p (j q) -> p j q", j=8)
    aTp1 = tpb[:, 1024:1536].rearrange("p (j q) -> p j q", j=4)
    gup = [[pab[0].rearrange("p a b -> p (a b)"), pab[1].rearrange("p a b -> p (a b)")], [tpf2[:, 1024:1536], tpf2[:, 1536:2048]]]
    dn = pmm
    NS = NB * SUB

    def wload(b, which):
        wb = b % NWB
        for (buf, src, nm) in which:
            dst = buf[wb].rearrange("p a b -> p (a b)")
            P.dma("pool", lambda e, dst=dst, src=src, b=b: e.indirect_dma_start(
                out=dst, out_offset=None, in_=src, in_offset=bass.IndirectOffsetOnAxis(ap=widx[:, b:b + 1], axis=0)),
                ["widx"], [(nm, wb)])
    WGU = ((wgb, dd["wg"], "wg"), (wub, dd["wu"], "wu"))
    WD = ((wdb, dd["wd"], "wd"),)

    def S0(n):
        r0 = (n // SUB) * BLK + (n % SUB) * 128
        P.ld("sp", Xb[n % 4], dd["Xs"][r0:r0 + 128, :], [], [("Xb", n % 4)])

    def S1(n):
        xb_, xk = Xb[n % 4], ("Xb", n % 4)
        for c in range(8):
            P.tr(xTp[:, c, :], xb_[:, c::8], C.identb[:], [xk, "identb"], ["xTp"])
        P.cp("dve", XbT[n % 2], xTp, ["xTp"], [("XbT", n % 2)])

    def S2a(n):
        wb = (n // SUB) % NWB
        xT, xTk = XbT[n % 2], ("XbT", n % 2)
        g_, u_ = gup[n % 2]
        gk_, uk_ = ("gp", n % 2), ("up", n % 2)
        for c in range(8):
            P.mm(g_, xT[:, c, :], wgb[wb][:, c, :], c == 0, c == 7, [xTk, ("wg", wb)], [gk_])
        for c in range(8):
            P.mm(u_, xT[:, c, :], wub[wb][:, c, :], c == 0, c == 7, [xTk, ("wu", wb)], [uk_])
        P.act(sgl[n % 2], g_, AF.Silu, [gk_], [("sgl", n % 2)])
        P.tt("dve", actb[n % 2], u_, sgl[n % 2], ALU.mult, [uk_, ("sgl", n % 2)], [("actb", n % 2)])

    def S2b(n):
        ab, ak = actb[n % 2], ("actb", n % 2)
        ap_, apk = aTp1, "aTp"
        for c in range(4):
            P.tr(ap_[:, c, :], ab[:, c::4], C.identb[:], [ak, "identb"], [apk])
        P.act(actT[n % 2], ap_, AF.Copy, [apk], [("actT", n % 2)])

    def S3(n):
        wb = (n // SUB) % NWB
        r0 = (n // SUB) * BLK + (n % SUB) * 128
        aT, aTk = actT[n % 2], ("actT", n % 2)
        yb, yk = Yb[n % 4], ("Yb", n % 4)
        for half in range(2):
            pp, pk = dn[half], ("dn", half)
            for c in range(4):
                P.mm(pp[:], aT[:, c, :], wdb[wb][:, c, half * 512:(half + 1) * 512], c == 0, c == 3, [aTk, ("wd", wb)], [pk])
            if half == 0:
                P.act(yb[:, 0:512], pp[:], AF.Copy, [pk], [yk])
            else:
                P.cp("dve", yb[:, 512:1024], pp[:], [pk], [yk])
        P.ld("sp", dd["Ys"][r0:r0 + 128, :], yb, [yk], [])

    for b0 in range(NWB):
        wload(b0, WGU)
        wload(b0, WD)
    S0(0)
    S0(1)
    for step in range(NS + 3):
        if step + 2 < NS:
            S0(step + 2)
        if step < NS:
            S1(step)
        if 0 <= step - 1 < NS:
            S2a(step - 1)
            n2 = step - 1
            if n2 % SUB == SUB - 1 and n2 // SUB + NWB < NB:
                wload(n2 // SUB + NWB, WGU)
        if 0 <= step - 2 < NS:
            S2b(step - 2)
        if 0 <= step - 3 < NS:
            S3(step - 3)
            n3 = step - 3
            if n3 % SUB == SUB - 1 and n3 // SUB + NWB < NB:
                wload(n3 // SUB + NWB, WD)
    P.barrier()

    ar.off = mark
    xt = [ar.get([128, D], F32) for _ in range(4)]
    Y1 = [ar.get([128, D], F32) for _ in range(4)]
    Y2 = [ar.get([128, D], F32) for _ in range(4)]

    def L3(i):
        w, xd, t = tiles[i]
        k = i % 4
        P.ld("sp", xt[k], xd[t * 128:(t + 1) * 128, :], [], [("xt", k)])
        for (Y, nm, di_) in ((Y1, "Y1", d1), (Y2, "Y2", d2)):
            P.dma("pool", lambda e, yb=Y[k], di_=di_, i=i: e.indirect_dma_start(
                out=yb, out_offset=None, in_=dd["Ys"], in_offset=bass.IndirectOffsetOnAxis(ap=di_[:, i:i + 1], axis=0)),
                [], [(nm, k)])

    def C3(i):
        w, xd, t = tiles[i]
        k = i % 4
        ya, yak = Y1[k], ("Y1", k)
        yb2, ybk = Y2[k], ("Y2", k)
        xb_, xk = xt[k], ("xt", k)
        P.ts("dve", ya, ya, ga1[:, i:i + 1], None, ALU.mult, None, [yak], [yak])
        P.stt(ya, yb2, ga2[:, i:i + 1], ya, ALU.mult, ALU.add, [yak, ybk], [yak])
        P.tt("pool", ya, ya, C.bc[("g2", w)], ALU.mult, [yak, ("g2", w)], [yak])
        P.tt("dve", xb_, ya, xb_, ALU.add, [yak, xk], [xk])
        P.ld("sp", xd[t * 128:(t + 1) * 128, :], xb_, [xk], [])

    L3(0)
    if NT > 1:
        L3(1)
    for i in range(NT):
        if i + 2 < NT:
            L3(i + 2)
        C3(i)


class Ctx:
    pass


def setup_consts(nc, P, G):
    I = lambda n, s, dt=F32: dram(nc, n, s, dt, "ExternalInput")
    G.d_ident = I("ident", [128, 128])
    G.d_ltri = I("ltri", [128, 128])
    G.d_cvec = I("cvec", [2, D])
    G.ident = P.sb("ident", [128, 128], F32)
    G.identb = P.sb("identb", [128, 128], BF16)
    G.onesb = P.sb("onesb", [128, 128], BF16)
    G.onesf = P.sb("onesf", [128, 128], F32)
    G.ltri = P.sb("ltri", [128, 128], F32)
    G.negh = P.sb("negh", [128, 512], F32)
    G.epsc = P.sb("epsc", [128, 1], F32)
    P.ld("sp", G.ident[:], G.d_ident, [], ["ident"])
    P.ld("sp", G.ltri[:], G.d_ltri, [], ["ltri"])
    P.cp("dve", G.identb[:], G.ident[:], ["ident"], ["identb"])
    P.op("dve", lambda e: e.memset(G.onesb[:], 1.0), [], ["onesb"])
    P.op("dve", lambda e: e.memset(G.onesf[:], 1.0), [], ["onesf"])
    P.op("dve", lambda e: e.memset(G.negh[:], -0.5), [], ["negh"])
    P.op("dve", lambda e: e.memset(G.epsc[:], EPS), [], ["epsc"])
    G.tp = P.ps("tp", [128, 8, 256], F32)
    G.pm = [P.ps(f"pm{i}", [128, 512], F32) for i in range(2)]
    G.pb6 = P.ps("pb6", [128, 512], F32)
    G.pb7 = P.ps("pb7", [128, 512], F32)


class Layer:
    def __init__(self, nc, P, G, ar, L, nwhich, ctx_bc, bc1):
        self.nc, self.P, self.G = nc, P, G
        self.nwhich, self.ctx_bc, self.bc1 = nwhich, ctx_bc, bc1
        self.ident, self.identb, self.onesb, self.onesf, self.ltri, self.negh = G.ident, G.identb, G.onesb, G.onesf, G.ltri, G.negh
        I = lambda n, s, dt=F32: dram(nc, n, s, dt, "ExternalInput")
        self.d_wmod = I(f"w_mod{L}", [D, 6 * D])
        self.d_bmod = I(f"b_mod{L}", [1, 6 * D])
        self.d_ng = I(f"ng{L}", [2, D])
        self.d_cvec = G.d_cvec
        P.barrier()
        ar.base = 0
        ar.off = 0
        self.bc = {}
        for wch in range(nwhich if ctx_bc else 1):
            for nm in ["g1", "A2", "sh2", "g2"]:
                self.bc[(nm, wch)] = ar.get([128, D], F32)
        if bc1:
            self.bc[("A1", 0)] = ar.get([128, D], F32)
            self.bc[("sh1", 0)] = ar.get([128, D], F32)
        self.fm = ar.get([128, 2, 2, 8], F32)
        ar.base = ar.off
        self.mod_psum = G.pm
        self.mod_phase(ar)

    def mod_phase(self, ar):
        P = self.P
        ar.reset()
        nw = self.nwhich
        cfm = ar.get([128, 2, 8], F32)
        scb = ar.get([128, 2, 8], F32)
        Lb = ar.get([128, 2 * 8, 128], F32)
        modbc = ar.get([128, nw, 6 * D], F32)
        bmb = ar.get([128, 6 * D], F32)
        ngb = ar.get([128, 2, D], F32)
        wm = [ar.get([128, 8, 512], F32) for _ in range(2)]
        tmp = ar.get([128, 8, 128], F32)
        pm = self.G.pm
        P.ld("sp", cfm, self.d_cvec.rearrange("r (j p) -> p r j", p=128), [], ["cfm"], slow=True)
        P.ld("sp", bmb, self.d_bmod.partition_broadcast(128).rearrange("p o n -> p (o n)"), [], ["bmb"])
        P.ld("sp", ngb, self.d_ng.partition_broadcast(128), [], ["ngb"])
        P.act(scb, cfm, AF.Silu, ["cfm"], ["scb"])
        for wch in range(nw):
            for j in range(8):
                P.cp("dve", Lb[:, wch * 8 + j, :], scb[:, wch, j:j + 1].to_broadcast([128, 128]), ["scb"], [("Lb", wch, j)])
        wmv = self.d_wmod.rearrange("(j p) n -> p j n", p=128)
        for n in range(12):
            w = wm[n % 2]
            P.ld("sp", w, wmv[:, :, n * 512:(n + 1) * 512], [], [("wm", n % 2)])
            for wch in range(nw):
                pp = pm[(n * nw + wch) % 2]
                pk = ("pm", (n * nw + wch) % 2)
                for j in range(8):
                    P.mm(pp[:], Lb[:, wch * 8 + j, :], w[:, j, :], j == 0, j == 7, [("Lb", wch, j), ("wm", n % 2)], [pk])
                P.tt("dve", modbc[:, wch, n * 512:(n + 1) * 512], pp[:], bmb[:, n * 512:(n + 1) * 512], ALU.add, [pk, "bmb"], [("mod", wch)])
        for wch in range(nw):
            m = lambda i: modbc[:, wch, i * D:(i + 1) * D]
            mk = ("mod", wch)
            if wch == 0 or self.ctx_bc:
                P.cp("pool", self.bc[("g1", wch)], m(2), [mk], [("g1", wch)])
                P.cp("pool", self.bc[("sh2", wch)], m(3), [mk], [("sh2", wch)])
                P.cp("pool", self.bc[("g2", wch)], m(5), [mk], [("g2", wch)])
                P.stt(self.bc[("A2", wch)], m(4), 1.0, ngb[:, 1, :], ALU.add, ALU.mult, [mk, "ngb"], [("A2", wch)])
            P.stt(m(1), m(1), 1.0, ngb[:, 0, :], ALU.add, ALU.mult, [mk, "ngb"], [mk])
            if self.bc1 and wch == 0:
                P.cp("pool", self.bc[("A1", 0)], m(1), [mk], [("A1", 0)])
                P.cp("pool", self.bc[("sh1", 0)], m(0), [mk], [("sh1", 0)])
            for k, src in enumerate([m(1), m(0)]):
                P.tt("dve", tmp, src.rearrange("p (j q) -> p j q", j=8), self.ident[:].unsqueeze(1).to_broadcast([128, 8, 128]), ALU.mult, [mk, "ident"], ["fmtmp"])
                P.red(self.fm[:, wch, k, :], tmp, ALU.add, ["fmtmp"], [("fm", wch)])
        P.barrier()

    def norm_tile(self, xt, xk, xh, xhk, ss_name):
        P = self.P
        junk, ss, rs = self.nt_junk, self.nt_ss, self.nt_rs
        P.act(junk[:], xt, AF.Square, [xk], ["nt_junk", "nt_ss"], accum=ss[:])
        P.act(rs[:], ss[:], AF.Sqrt, ["nt_ss", "epsc"], ["nt_rs"], bias=self.G.epsc[:, 0:1], scale=1.0 / D)
        P.op("dve", lambda e: e.reciprocal(out=rs[:], in_=rs[:]), ["nt_rs"], ["nt_rs"])
        P.ts("dve", xh, xt, rs[:, 0:1], None, ALU.mult, None, [xk, "nt_rs"], [xhk])

    def alloc_norm(self, ar):
        self.nt_junk = ar.get([128, D], BF16)
        self.nt_ss = ar.get([128, 1], F32)
        self.nt_rs = ar.get([128, 1], F32)


def moe_inputs(nc, L):
    I = lambda n, s, dt=F32: dram(nc, n, s, dt, "ExternalInput")
    return dict(wr=I(f"wr{L}", [D, 36]), br=I(f"br{L}", [1, 36]), wg=I(f"wg{L}", [NE * 128, 4096]),
                wu=I(f"wu{L}", [NE * 128, 4096]), wd=I(f"wd{L}", [NE * 128, 4096]))


def run_moe(nc, P, G, C, ar, tiles, mi):
    NT = len(tiles)
    NB = (2 * NT * 128 + BLK - 1) // BLK + NE
    dd = dict(mi)
    dd.update(iotab=G.d_iotab, iotap=G.d_iotap, Xs=G.d_Xs, Ys=G.d_Ys, H2s=G.d_H2s)
    pab = [G.pm[0].rearrange("p (a b) -> p a b", a=2), G.pm[1].rearrange("p (a b) -> p a b", a=2)]
    P.barrier()
    moe_phase(nc, P, C, ar, tiles, NT, NB, dd, psum=dict(big=G.tp, a=pab, b=[G.pb6, G.pb7]))
    P.barrier()


def emit_conv(nc, P, G, C, ar, L, wins, ctxseg):
    I = lambda n, s, dt=F32: dram(nc, n, s, dt, "ExternalInput")
    d_win = I(f"w_in{L}", [D, 2 * D])
    d_binfm = I(f"b_in_fm{L}", [128, 16])
    d_wdwfm = I(f"w_dw_fm{L}", [128, 8 * 31])
    d_bdwfm = I(f"b_dw_fm{L}", [128, 8])
    d_gnfm = I(f"gn_fm{L}", [128, 8])
    d_wout = I(f"w_out{L}", [D, D])
    d_bout = I(f"b_out{L}", [1, D])
    tp, pm = G.tp, G.pm
    NTm = 32
    VW = (NTm + 2) * 128
    ar.off = ar.base
    binfm = ar.get([128, 16], F32)
    wdwfm = ar.get([128, 8, 31], F32)
    bdwfm = ar.get([128, 8], F32)
    gnfm = ar.get([128, 8], F32)
    hmask = ar.get([128, 2], F32)
    bog = [ar.get([128, D], F32) for _ in range(C.nwhich)]
    ar.base = ar.off
    P.ld("sp", binfm, d_binfm, [], ["binfm"])
    P.ld("sp", wdwfm, d_wdwfm.rearrange("p (j t) -> p j t", j=8), [], ["wdwfm"])
    P.ld("sp", bdwfm, d_bdwfm, [], ["bdwfm"])
    P.ld("sp", gnfm, d_gnfm, [], ["gnfm"])
    if any(w_.get("mask") is not None for w_ in wins):
        P.ld("sp", hmask, [w_["mask"] for w_ in wins if w_.get("mask") is not None][0], [], ["hmask"])
    for w in range(C.nwhich):
        P.ld("sp", bog[w], d_bout.partition_broadcast(128).rearrange("p o n -> p (o n)"), [], [("bog", w)])
        P.tt("pool", bog[w], bog[w], C.bc[("g1", w)], ALU.mult, [("bog", w), ("g1", w)], [("bog", w)])
    P.barrier()
    for wi, win_ in enumerate(wins):
        segs = [dict(w=0, x=win_["x"], nt=NTm + 2, own0=1, nown=NTm, out=win_["out"], halo=True)]
        if ctxseg is not None and wi == 0:
            segs.append(dict(w=1, x=ctxseg["x"], nt=2, own0=0, nown=2, out=ctxseg["out"], halo=False))
        has_ctx = len(segs) > 1
        ar.reset()
        vT = ar.get([128, 8, VW], BF16)
        vTc = ar.get([128, 8, 16 + 256 + 16], BF16)
        c2mark = ar.off
        win = ar.get([128, 8, 2 * D], BF16)
        hT = [ar.get([128, 8, 256], BF16) for _ in range(2)]
        xt = [ar.get([128, D], F32) for _ in range(2)]
        xh_ = [ar.get([128, D], F32) for _ in range(2)]
        sig = [ar.get([128, 256], F32) for _ in range(2)]
        C.alloc_norm(ar)
        pab = [pm[0].rearrange("p (a b) -> p a b", a=2), pm[1].rearrange("p (a b) -> p a b", a=2)]
        P.ld("pool", win, d_win.rearrange("(c p) n -> p c n", p=128), [], ["win"])
        if has_ctx:
            P.op("pool", lambda e: e.memset(vTc, 0.0), [], ["vTc"])
        gi = 0
        ti = 0
        for sg in segs:
            w = sg["w"]
            for g in range(sg["nt"] // 2):
                hb = hT[gi % 2]
                hk = ("hT", gi % 2)
                for tl in range(2):
                    t = g * 2 + tl
                    xb_, xk = xt[ti % 2], ("xt", ti % 2)
                    xhb, xhk = xh_[ti % 2], ("xh", ti % 2)
                    ti += 1
                    P.ld("sp", xb_, sg["x"][t * 128:(t + 1) * 128, :], [], [xk])
                    C.norm_tile(xb_, xk, xhb, xhk, None)
                    for j in range(8):
                        P.tr(tp[:, j, tl * 128:(tl + 1) * 128], xhb[:, j * 128:(j + 1) * 128], C.ident[:], [xhk, "ident"], [("tp", j // 2)])
                for j in range(8):
                    P.act(hb[:, j, :], tp[:, j, :], AF.Identity, [("tp", j // 2), ("fm", w)], [hk],
                          bias=C.fm[:, w, 1, j:j + 1], scale=C.fm[:, w, 0, j:j + 1])
                for jo in range(8):
                    pb_ = pab[jo % 2]
                    pk = ("pab", jo % 2)
                    for half in range(2):
                        for c in range(8):
                            P.mm(pb_[:, half, :], win[:, c, half * D + jo * 128: half * D + (jo + 1) * 128], hb[:, c, :], c == 0, c == 7, ["win", hk], [pk])
                    sg_ = sig[jo % 2]
                    P.act(sg_, pb_[:, 1, :], AF.Sigmoid, [pk, "binfm"], [("sig", jo % 2)], bias=binfm[:, 8 + jo:9 + jo])
                    if sg["halo"]:
                        dst = vT[:, jo, g * 256:(g + 1) * 256]
                        dk = "vT"
                    else:
                        dst = vTc[:, jo, 16 + g * 256:16 + (g + 1) * 256]
                        dk = "vTc"
                    P.stt(dst, pb_[:, 0, :], binfm[:, jo:jo + 1], sg_, ALU.add, ALU.mult, [pk, ("sig", jo % 2), "binfm"], [dk])
                if sg["halo"] and g == 0:
                    if win_.get("mask") is not None:
                        P.ts("pool", vT[:, :, 0:128], vT[:, :, 0:128], hmask[:, 0:1], None, ALU.mult, None, ["vT", "hmask"], ["vT"])
                    elif win_["zlo"]:
                        P.op("pool", lambda e: e.memset(vT[:, :, 0:128], 0.0), [], ["vT"])
                if sg["halo"] and g == sg["nt"] // 2 - 1:
                    if win_.get("mask") is not None:
                        P.ts("pool", vT[:, :, VW - 128:VW], vT[:, :, VW - 128:VW], hmask[:, 1:2], None, ALU.mult, None, ["vT", "hmask"], ["vT"])
                    elif win_["zhi"]:
                        P.op("pool", lambda e: e.memset(vT[:, :, VW - 128:VW], 0.0), [], ["vT"])
                gi += 1
        P.barrier()
        ar.off = c2mark
        wout = ar.get([128, 8, D], BF16)
        Dg = [ar.get([128, 31, 128], BF16) for _ in range(2)]
        vc = ar.get([128, 8, 256], F32)
        sq = ar.get([128, 8, 256], BF16)
        t1 = ar.get([128, 256], F32)
        rsb = ar.get([128, 256], F32)
        tmpv = [ar.get([128, 256], F32) for _ in range(2)]
        uT = ar.get([128, 8, 256], BF16)
        xt2 = [ar.get([128, D], F32) for _ in range(2)]
        xn = [ar.get([128, D], F32) for _ in range(2)]
        cv = [tp[:, 0:2, :].rearrange("p a b -> p (a b)"), tp[:, 2:4, :].rearrange("p a b -> p (a b)")]
        ssb = tp[:, 4:6, :].rearrange("p a b -> p (a b)")
        po = [pm[0], pm[1]]
        P.ld("pool", wout, d_wout.rearrange("(c p) n -> p c n", p=128), [], ["wout"])
        di = 0
        xi = 0
        pi = 0
        for sg in segs:
            w = sg["w"]
            ntok = sg["nown"] * 128
            W = 256
            for tb in range(ntok // W):
                for j in range(8):
                    dg, dgk = Dg[di % 2], ("Dg", di % 2)
                    di += 1
                    P.tt("pool" if j % 4 == 3 else "dve", dg, C.identb[:].unsqueeze(1).to_broadcast([128, 31, 128]),
                         wdwfm[:, j, :].unsqueeze(2).to_broadcast([128, 31, 128]), ALU.mult, ["identb", "wdwfm"], [dgk])
                    cvb, cvk = cv[j % 2], ("cv", j % 2)
                    for tau in range(31):
                        if sg["halo"]:
                            c0 = 128 + tb * W + tau - 15
                            rhs = vT[:, j, c0:c0 + W]
                            rk = "vT"
                        else:
                            c0 = 16 + tb * W + tau - 15
                            rhs = vTc[:, j, c0:c0 + W]
                            rk = "vTc"
                        P.mm(cvb[:, 0:W], dg[:, tau, :], rhs, tau == 0, tau == 30, [dgk, rk], [cvk])
                    P.act(vc[:, j, 0:W], cvb[:, 0:W], AF.Identity, [cvk, "bdwfm"], [("vc", j)], bias=bdwfm[:, j:j + 1])
                    P.act(sq[:, j, 0:W], cvb[:, 0:W], AF.Square, [cvk, "bdwfm"], [("sq", j)], bias=bdwfm[:, j:j + 1])
                for j in range(8):
                    P.mm(ssb[:, 0:W], C.onesb[:], sq[:, j, 0:W], j == 0, j == 7, [("sq", j), "onesb"], ["ssb"])
                P.act(t1[:, 0:W], ssb[:, 0:W], AF.Sqrt, ["ssb", "epsc"], ["t1"], bias=G.epsc[:, 0:1], scale=1.0 / D)
                P.op("dve", lambda e, W=W: e.reciprocal(out=rsb[:, 0:W], in_=t1[:, 0:W]), ["t1"], ["rsb"])
                for j in range(8):
                    tv, tvk = tmpv[j % 2], ("tmpv", j % 2)
                    P.tt("dve", tv[:, 0:W], vc[:, j, 0:W], rsb[:, 0:W], ALU.mult, [("vc", j), "rsb"], [tvk])
                    P.act(uT[:, j, 0:W], tv[:, 0:W], AF.Silu, [tvk, "gnfm"], [("uT", j)], scale=gnfm[:, j:j + 1])
                for s in range(W // 128):
                    t = sg["own0"] + (tb * W) // 128 + s
                    xb_, xk = xt2[xi % 2], ("xt2", xi % 2)
                    xnb, xnk = xn[xi % 2], ("xn", xi % 2)
                    xi += 1
                    P.ld("sp", xb_, sg["x"][t * 128:(t + 1) * 128, :], [], [xk])
                    P.tt("pool", xb_, xb_, bog[w], ALU.add, [xk, ("bog", w)], [xk])
                    for half in range(2):
                        pp, pk = po[pi % 2], ("po", pi % 2)
                        pi += 1
                        for j in range(8):
                            P.mm(pp[:], uT[:, j, s * 128:(s + 1) * 128], wout[:, j, half * 512:(half + 1) * 512], j == 0, j == 7, [("uT", j), "wout"], [pk])
                        P.tt("dve", xnb[:, half * 512:(half + 1) * 512], pp[:], C.bc[("g1", w)][:, half * 512:(half + 1) * 512], ALU.mult, [pk, ("g1", w)], [xnk])
                        P.tt("pool", xnb[:, half * 512:(half + 1) * 512], xnb[:, half * 512:(half + 1) * 512], xb_[:, half * 512:(half + 1) * 512], ALU.add, [xnk, xk], [xnk])
                    to = t - sg["own0"]
                    P.ld("sp", sg["out"][to * 128:(to + 1) * 128, :], xnb, [xnk], [])
        P.barrier()


def emit_fft(nc, P, G, C, ar, L, d_x1, d_xc1, d_x2g, d_xc2):
    I = lambda n, s, dt=F32: dram(nc, n, s, dt, "ExternalInput")
    NPASS = 8
    KP = 64 // NPASS
    CB = 512 // (KP * 2)
    d_wa = I("wa", [64, NPASS, KP * 2])
    d_mb = I("mb", [128, 64 * 4 * 128])
    d_cd = I("cd", [128, 2 * 512])
    d_cdn = I("cdn", [128, 2 * 512])
    d_wout = I(f"w_out{L}", [D, D])
    d_bout = I(f"b_out{L}", [1, D])
    d_hl = G.d_hl
    tp, pm, pb6, pb7 = G.tp, G.pm, G.pb6, G.pb7
    tpf = tp.rearrange("p a b -> p (a b)")
    ar.off = ar.base
    bog = [ar.get([128, D], F32) for _ in range(2)]
    ar.base = ar.off
    for w in range(2):
        P.ld("sp", bog[w], d_bout.partition_broadcast(128).rearrange("p o n -> p (o n)"), [], [("bog", w)])
        P.tt("pool", bog[w], bog[w], C.bc[("g1", w)], ALU.mult, [("bog", w), ("g1", w)], [("bog", w)])
    P.barrier()
    ar.reset()
    xt = [ar.get([128, D], F32) for _ in range(2)]
    xh_ = [ar.get([128, D], F32) for _ in range(2)]
    hb = [ar.get([128, D], BF16) for _ in range(2)]
    C.alloc_norm(ar)
    for t in range(64):
        xb_, xk = xt[t % 2], ("xt", t % 2)
        xhb, xhk = xh_[t % 2], ("xh", t % 2)
        P.ld("sp", xb_, d_x1[t * 128:(t + 1) * 128, :], [], [xk])
        C.norm_tile(xb_, xk, xhb, xhk, None)
        P.tt("dve", xhb, xhb, C.bc[("A1", 0)], ALU.mult, [xhk, ("A1", 0)], [xhk])
        P.tt("pool", hb[t % 2], xhb, C.bc[("sh1", 0)], ALU.add, [xhk, ("sh1", 0)], [("hb", t % 2)])
        P.ld("sp", d_hl[t * 128:(t + 1) * 128, :], hb[t % 2], [("hb", t % 2)], [])
    P.barrier()
    ar.reset()
    wout = ar.get([128, 8, D], BF16)
    cd = ar.get([128, 2, 512], BF16)
    cdn = ar.get([128, 2, 512], BF16)
    wa = ar.get([64, NPASS, KP * 2], BF16)
    fT = ar.get([128, 8, KP * 128], BF16)
    fTc = ar.get([128, 8, 256], BF16)
    XA = ar.get([64, 128, 128], BF16)
    Yg = ar.get([128, 2, KP, 256], BF16)
    MB4 = [ar.get([128, 4, 4, 128], BF16) for _ in range(2)]
    ZT4 = [ar.get([128, 2, 2, 512], BF16) for _ in range(2)]
    xt = [ar.get([128, D], F32) for _ in range(2)]
    xn = [ar.get([128, 512], F32) for _ in range(2)]
    xh_ = [ar.get([128, D], F32) for _ in range(1)]
    hTc = ar.get([128, 8, 256], BF16)
    Hcs = ar.get([128, 2, 512], BF16)
    C.alloc_norm(ar)
    P.ld("pool", wout, d_wout.rearrange("(c p) n -> p c n", p=128), [], ["wout"])
    P.ld("pool", cd, d_cd.rearrange("p (a b) -> p a b", a=2), [], ["cd"])
    P.ld("pool", cdn, d_cdn.rearrange("p (a b) -> p a b", a=2), [], ["cdn"])
    P.ld("pool", wa, d_wa, [], ["wa"])
    hlv = d_hl.rearrange("(t1 t2) c -> t1 t2 c", t2=128)
    mbv = d_mb.rearrange("p (k a q) -> p k a q", k=64, a=4)
    x1v = d_x1.rearrange("(k2 k1) d -> k1 k2 d", k1=64)
    pA = [tpf[:, 0:512], tpf[:, 512:1024]]
    pZ = [tpf[:, 1024:1280], tpf[:, 1536:1792]]
    pF = [pm[0], pm[1]]
    pO2 = [pb6, pb7]

    def out_proj(src, ntile, xsrc, w, outap):
        for tl in range(ntile):
            xb_, xk = xt[tl % 2], ("xt", tl % 2)
            P.ld("sp", xb_, xsrc(tl), [], [xk])
            P.tt("pool", xb_, xb_, bog[w], ALU.add, [xk, ("bog", w)], [xk])
            for half in range(2):
                pp, pk = pO2[half], ("pO2", half)
                tb, tk = xn[half], ("xn", half)
                for c in range(8):
                    P.mm(pp[:], src[:, c, tl * 128:(tl + 1) * 128], wout[:, c, half * 512:(half + 1) * 512], c == 0, c == 7, ["fT", "wout"], [pk])
                P.tt("dve", tb, pp[:], C.bc[("g1", w)][:, half * 512:(half + 1) * 512], ALU.mult, [pk, ("g1", w)], [tk])
                P.tt("pool", xb_[:, half * 512:(half + 1) * 512], xb_[:, half * 512:(half + 1) * 512], tb, ALU.add, [tk, xk], [xk])
            P.ld("sp", outap[tl * 128:(tl + 1) * 128, :], xb_, [xk], [])

    an = 0
    zn = 0
    fn_ = 0
    mn = 0
    for hh in range(NPASS):
        for g in range(4):
            for nch in range(2):
                ch0 = g * 256 + nch * 128
                for q4 in range(4):
                    P.ld("sp", XA[:, q4 * 32:(q4 + 1) * 32, :], hlv[:, q4 * 32:(q4 + 1) * 32, ch0:ch0 + 128], [], ["XA"])
                for cb in range(128 // CB):
                    pa, pak = pA[an % 2], ("pA", an % 2)
                    an += 1
                    for cc in range(CB):
                        ch = cb * CB + cc
                        P.mm(pa[:, cc * KP * 2:(cc + 1) * KP * 2], XA[:, :, ch], wa[:, hh, :], True, True, ["XA", "wa"], [pak])
                    dst = Yg[:, :, :, nch * 128 + cb * CB: nch * 128 + (cb + 1) * CB].rearrange("p r k c -> p c k r")
                    srcv = pa.rearrange("p (c k r) -> p c k r", c=CB, k=KP)
                    if cb % 2 == 0:
                        P.cp("dve", dst, srcv, [pak], ["Yg"])
                    else:
                        P.act(dst, srcv, AF.Copy, [pak], ["Yg"])
            for kb in range(KP // 4):
                mb_, mbk = MB4[mn % 2], ("MB4", mn % 2)
                mn += 1
                k0 = hh * KP + kb * 4
                P.ld("pool", mb_, mbv[:, k0:k0 + 4, :, :], [], [mbk])
                zt, ztk = ZT4[fn_ % 2], ("ZT4", fn_ % 2)
                for q in range(4):
                    kl = kb * 4 + q
                    for nch in range(2):
                        pz, pzk = pZ[zn % 2], ("pZ", zn % 2)
                        zn += 1
                        P.mm(pz, Yg[:, 0, kl, nch * 128:(nch + 1) * 128], mb_[:, q, 0:2, :].rearrange("p a b -> p (a b)"), True, False, ["Yg", mbk], [pzk])
                        P.mm(pz, Yg[:, 1, kl, nch * 128:(nch + 1) * 128], mb_[:, q, 2:4, :].rearrange("p a b -> p (a b)"), False, True, ["Yg", mbk], [pzk])
                        P.cp("dve", zt[:, nch, :, q * 128:(q + 1) * 128], pz.rearrange("p (a b) -> p a b", a=2), [pzk], [ztk])
                for mch in range(2):
                    pf, pfk = pF[mch], ("pF", mch)
                    i4 = 0
                    for nch in range(2):
                        for ri in range(2):
                            P.mm(pf[:], cd[:, nch, ri * 256 + mch * 128: ri * 256 + (mch + 1) * 128], zt[:, nch, ri, :], i4 == 0, i4 == 3, ["cd", ztk], [pfk])
                            i4 += 1
                    P.act(fT[:, 2 * g + mch, kb * 512:(kb + 1) * 512], pf[:], AF.Copy, [pfk], ["fT"])
                fn_ += 1
        out_proj(fT, KP, lambda tl, hh=hh: x1v[hh * KP + tl], 0, d_x2g[hh * KP * 128:(hh + 1) * KP * 128, :])
    for tl in range(2):
        xb_, xk = xt[tl % 2], ("xt", tl % 2)
        xhb, xhk = xh_[0], ("xh", 0)
        P.ld("sp", xb_, d_xc1[tl * 128:(tl + 1) * 128, :], [], [xk])
        C.norm_tile(xb_, xk, xhb, xhk, None)
        for j in range(8):
            P.tr(tp[:, j, tl * 128:(tl + 1) * 128], xhb[:, j * 128:(j + 1) * 128], C.ident[:], [xhk, "ident"], [("tpc", j // 2)])
    for j in range(8):
        P.act(hTc[:, j, :], tp[:, j, :], AF.Identity, [("tpc", j // 2), ("fm", 1)], ["hTc"], bias=C.fm[:, 1, 1, j:j + 1], scale=C.fm[:, 1, 0, j:j + 1])
    for g in range(4):
        for tl in range(2):
            pf, pfk = pF[tl], ("pF", tl)
            for nch in range(2):
                P.mm(pf[:], hTc[:, 2 * g + nch, tl * 128:(tl + 1) * 128], cd[:, nch, :], nch == 0, nch == 1, ["hTc", "cd"], [pfk])
            P.cp("dve", Hcs[:, tl, :], pf[:], [pfk], ["Hcs"])
        for mch in range(2):
            pf, pfk = pF[mch], ("pF", mch)
            i4 = 0
            for tl in range(2):
                for ri in range(2):
                    P.mm(pf[:, 0:256], Hcs[:, tl, ri * 256 + mch * 128: ri * 256 + (mch + 1) * 128], cdn[:, tl, ri * 256:(ri + 1) * 256], i4 == 0, i4 == 3, ["Hcs", "cdn"], [pfk])
                    i4 += 1
            P.act(fTc[:, 2 * g + mch, :], pf[:, 0:256], AF.Copy, [pfk], ["fT"])
    out_proj(fTc, 2, lambda tl: d_xc1[tl * 128:(tl + 1) * 128, :], 1, d_xc2)
    P.barrier()


def emit_attn(nc, P, G, C, ar, d_x2g, d_xc2, d_x3h):
    I = lambda n, s, dt=F32: dram(nc, n, s, dt, "ExternalInput")
    NQT = 34
    NKT = 66
    d_cosg = I("cosg", [128, 8192])
    d_sing = I("sing", [128, 8192])
    d_cosq = I("cosq", [128, NQT * 128])
    d_sinq = I("sinq", [128, NQT * 128])
    d_qidx = I("qidx", [128, NQT], I32)
    d_wqkv = I("w_qkv", [D, 1536])
    d_gq = I("gq_fm", [128, 1])
    d_gk = I("gk_fm", [128, 1])
    d_wo = I("w_o", [D, D])
    d_prot = I("prot", [128, 128])
    SCALE = 128.0 ** -0.5
    tp, pm, pb6, pb7 = G.tp, G.pm, G.pb6, G.pb7
    tpf = tp.rearrange("p a b -> p (a b)")
    ar.off = ar.base
    gq = ar.get([128, 1], F32)
    gk = ar.get([128, 1], F32)
    prot = ar.get([128, 128], F32)
    protb = ar.get([128, 128], BF16)
    qidx = ar.get([128, NQT], I32)
    ar.base = ar.off
    P.ld("sp", gq, d_gq, [], ["gq"])
    P.ld("sp", gk, d_gk, [], ["gk"])
    P.ld("sp", prot, d_prot, [], ["prot"])
    P.ld("sp", qidx, d_qidx, [], ["qidx"])
    P.cp("dve", protb, prot, ["prot"], ["protb"])
    P.barrier()
    ar.reset()
    KT = ar.get([128, 2, NKT * 128], BF16)
    Vx = ar.get([128, NKT, 2, 130], BF16)
    wqkv = ar.get([128, 8, 1536], BF16)
    wo = ar.get([128, 8, D], BF16)
    hT = ar.get([128, 8, 512], BF16)
    xt = [ar.get([128, D], F32) for _ in range(2)]
    xh_ = [ar.get([128, D], F32) for _ in range(2)]
    C.alloc_norm(ar)
    sqb = ar.get([128, 512], BF16)
    t1 = ar.get([128, 512], F32)
    rs = ar.get([128, 512], F32)
    kn = ar.get([128, 512], F32)
    knb = ar.get([128, 512], BF16)
    cs = ar.get([128, 512], F32)
    sn = ar.get([128, 512], F32)
    t2 = ar.get([128, 512], F32)
    QT = [ar.get([128, 512], BF16) for _ in range(2)]
    PT = [ar.get([128, 512], BF16) for _ in range(3)]
    rec = ar.get([128, 4], F32)
    Ob = [ar.get([128, 128], BF16) for _ in range(2)]
    OT = ar.get([128, 8, 512], BF16)
    oacc = [ar.get([128, 4, 130], F32)] * 2
    print('attn arena words', ar.off, 'of', ar.n)
    pb4, pb5 = pm
    pS = [pb4, pb5]
    pO = [tpf[:, i * 512:i * 512 + 130] for i in range(4)]
    pb7b = pb7[:, 0:64].bitcast(BF16)
    P.ld("pool", wqkv, d_wqkv.rearrange("(c p) n -> p c n", p=128), [], ["wqkv"])
    P.ld("pool", wo, d_wo.rearrange("(c p) n -> p c n", p=128), [], ["wo"])
    P.op("pool", lambda e: e.memset(Vx, 1.0), [], ["Vx"])
    cnt = {"x": 0}

    def load_tile(dst, dk, src):
        if src[0] == "rows":
            P.ld("sp", dst, src[1], [], [dk])
        else:
            t = src[1]
            P.dma("pool", lambda e: e.indirect_dma_start(out=dst, out_offset=None, in_=d_x2g,
                                                         in_offset=bass.IndirectOffsetOnAxis(ap=qidx[:, t:t + 1], axis=0)), ["qidx"], [dk])

    def pro_group(srcs, w, c0):
        ntl = len(srcs)
        for tl in range(ntl):
            n = cnt["x"]
            cnt["x"] += 1
            xb_, xk = xt[n % 2], ("xt", n % 2)
            xhb, xhk = xh_[n % 2], ("xh", n % 2)
            load_tile(xb_, xk, srcs[tl])
            C.norm_tile(xb_, xk, xhb, xhk, None)
            for j in range(8):
                P.tr(tp[:, j, tl * 128:(tl + 1) * 128], xhb[:, j * 128:(j + 1) * 128], C.ident[:], [xhk, "ident"], [("bank", j // 2)])
        for j in range(8):
            P.act(hT[:, j, c0:c0 + ntl * 128], tp[:, j, 0:ntl * 128], AF.Identity, [("bank", j // 2), ("fm", w)], ["hT"],
                  bias=C.fm[:, w, 1, j:j + 1], scale=C.fm[:, w, 0, j:j + 1])

    def qk_steps(mm_fn, ps, psk, W, gfm, gk_, rope, out, outk):
        mm_fn()
        yield
        P.act(sqb[:, 0:W], ps[:, 0:W], AF.Square, [psk], ["sqb"])
        yield
        P.mm(pb7[:, 0:W], C.onesb[:], sqb[:, 0:W], True, True, ["sqb", "onesb"], ["pb7"])
        yield
        P.act(t1[:, 0:W], pb7[:, 0:W], AF.Sqrt, ["pb7", "epsc"], ["t1"], bias=G.epsc[:, 0:1], scale=1.0 / 128)
        yield
        P.op("dve", lambda e: e.reciprocal(out=rs[:, 0:W], in_=t1[:, 0:W]), ["t1"], ["rs"])
        yield
        P.stt(kn[:, 0:W], ps[:, 0:W], gfm[:, 0:1], rs[:, 0:W], ALU.mult, ALU.mult, [psk, gk_, "rs"], ["kn"])
        yield
        if rope:
            P.act(knb[:, 0:W], kn[:, 0:W], AF.Copy, ["kn"], ["knb"])
            yield
            P.mm(pb7[:, 0:W], protb, knb[:, 0:W], True, True, ["knb", "protb"], ["pb7"])
            yield
            P.tt("dve", t2[:, 0:W], pb7[:, 0:W], sn[:, 0:W], ALU.mult, ["pb7", "sn"], ["t2"])
            yield
            P.tt("pool", kn[:, 0:W], kn[:, 0:W], cs[:, 0:W], ALU.mult, ["kn", "cs"], ["kn"])
            yield
            P.tt("dve", out, kn[:, 0:W], t2[:, 0:W], ALU.add, ["kn", "t2"], [outk])
        else:
            P.act(out, kn[:, 0:W], AF.Copy, ["kn"], [outk])
        yield

    def run_all(gen):
        if gen is not None:
            for _ in gen:
                pass

    def step(gen):
        if gen is None:
            return None
        try:
            next(gen)
            return gen
        except StopIteration:
            return None

    rows = lambda ap, t: ("rows", ap[t * 128:(t + 1) * 128, :])
    groups = [(d_xc2, 0, 2, 1, 0, False, None)]
    for g in range(16):
        groups.append((d_x2g, g * 4, 4, 0, 2 + g * 4, True, g))
    for (xap, t0, ntl, w, kt0, rope, g) in groups:
        W = ntl * 128
        for r in range(0, ntl, 2):
            pro_group([rows(xap, t0 + r), rows(xap, t0 + r + 1)], w, r * 128)
        if rope:
            P.ld("sp", cs[:, 0:W], d_cosg[:, g * 512:g * 512 + W], [], ["cs"])
            P.ld("sp", sn[:, 0:W], d_sing[:, g * 512:g * 512 + W], [], ["sn"])
        for j in range(2):
            def kmm(j=j, W=W):
                for c in range(8):
                    P.mm(pb6[:, 0:W], wqkv[:, c, 1024 + j * 128:1024 + (j + 1) * 128], hT[:, c, 0:W], c == 0, c == 7, ["wqkv", "hT"], ["pb6"])
            run_all(qk_steps(kmm, pb6, "pb6", W, gk, "gk", rope, KT[:, j, kt0 * 128:kt0 * 128 + W], "KT"))
        for tl in range(ntl):
            pv = pS[tl % 2]
            pvk = ("pS", tl % 2)
            for c in range(8):
                P.mm(pv[:, 0:256], hT[:, c, tl * 128:(tl + 1) * 128], wqkv[:, c, 1280:1536], c == 0, c == 7, ["wqkv", "hT"], [pvk])
            P.cp("dve", Vx[:, kt0 + tl, :, 0:128], pv[:, 0:256].rearrange("p (a b) -> p a b", a=2), [pvk], ["Vx"])

    pn = 0
    qchunks = [(i * 4, 4) for i in range(8)] + [(32, 2)]
    for (qt0, nqt) in qchunks:
        W = nqt * 128
        for r in range(0, nqt, 2):
            pro_group([("idx", qt0 + r), ("idx", qt0 + r + 1)], 0, r * 128)
        P.ld("sp", cs[:, 0:W], d_cosq[:, qt0 * 128:qt0 * 128 + W], [], ["cs"])
        P.ld("sp", sn[:, 0:W], d_sinq[:, qt0 * 128:qt0 * 128 + W], [], ["sn"])
        def qgen(h, W=W):
            def qmm():
                for c in range(8):
                    P.mm(pb6[:, 0:W], wqkv[:, c, h * 128:(h + 1) * 128], hT[:, c, 0:W], c == 0, c == 7, ["wqkv", "hT"], ["pb6"])
            return qk_steps(qmm, pb6, "pb6", W, gq, "gq", True, QT[h % 2][:, 0:W], ("QT", h % 2))

        def fin_steps(h, nqt=nqt):
            oa, oak = oacc[0], "oacc"
            for qs in range(nqt):
                P.op("dve", lambda e, qs=qs: e.reciprocal(out=rec[:, qs:qs + 1], in_=oa[:, qs, 128:129]), [oak], ["rec"])
                ob, obk = Ob[qs % 2], ("Ob", qs % 2)
                P.ts("dve", ob, oa[:, qs, 0:128], rec[:, qs:qs + 1], None, ALU.mult, None, [oak, "rec"], [obk])
                yield
                P.tr(pb7b, ob, C.identb[:], [obk, "identb"], ["pb7"])
                yield
                P.act(OT[:, h, qs * 128:(qs + 1) * 128], pb7b, AF.Copy, ["pb7"], ["OT"])
                yield

        run_all(qgen(0))
        deferred = None
        for h in range(8):
            j = h // 4
            qt, qk_ = QT[h % 2], ("QT", h % 2)
            gen = qgen(h + 1) if h + 1 < 8 else None

            def S(kt):
                P.mm(pS[kt % 2][:, 0:W], KT[:, j, kt * 128:(kt + 1) * 128], qt[:, 0:W], True, True, ["KT", qk_], [("pS", kt % 2)])
            S(0)
            for kt in range(NKT):
                if kt + 1 < NKT:
                    S(kt + 1)
                pt, ptk = PT[pn % 3], ("PT", pn % 3)
                pn += 1
                P.act(pt[:, 0:W], pS[kt % 2][:, 0:W], AF.Exp, [("pS", kt % 2)], [ptk], scale=SCALE)
                for qs in range(nqt):
                    P.mm(pO[qs], pt[:, qs * 128:(qs + 1) * 128], Vx[:, kt, j, :], kt == 0, kt == NKT - 1, [ptk, "Vx"], [("bank", qs)])
                if kt % 2 == 1:
                    if deferred is not None:
                        deferred = step(deferred)
                    else:
                        gen = step(gen)
            run_all(deferred)
            run_all(gen)
            oa, oak = oacc[0], "oacc"
            for qs in range(nqt):
                P.cp("dve", oa[:, qs, :], pO[qs], [("bank", qs)], [oak])
            deferred = fin_steps(h)
        run_all(deferred)
        for qs in range(nqt):
            t = qt0 + qs
            n = cnt["x"]
            cnt["x"] += 1
            xb_, xk = xt[n % 2], ("xt", n % 2)
            load_tile(xb_, xk, ("idx", t))
            xnb, xnk = xh_[n % 2], ("xh", n % 2)
            for half in range(2):
                pp, pk = pS[half], ("pS", half)
                for h in range(8):
                    P.mm(pp[:], OT[:, h, qs * 128:(qs + 1) * 128], wo[:, h, half * 512:(half + 1) * 512], h == 0, h == 7, ["OT", "wo"], [pk])
                P.tt("dve", xnb[:, half * 512:(half + 1) * 512], pp[:], C.bc[("g1", 0)][:, half * 512:(half + 1) * 512], ALU.mult, [pk, ("g1", 0)], [xnk])
                P.tt("pool", xnb[:, half * 512:(half + 1) * 512], xnb[:, half * 512:(half + 1) * 512], xb_[:, half * 512:(half + 1) * 512], ALU.add, [xnk, xk], [xnk])
            P.ld("sp", d_x3h[t * 128:(t + 1) * 128, :], xnb, [xnk], [])
    P.barrier()


def build_fused():
    nc = bass.Bass("TRN2", target_bir_lowering=False)
    I = lambda n, s, dt=F32: dram(nc, n, s, dt, "ExternalInput")
    N = lambda n, s, dt=F32: dram(nc, n, s, dt, "Internal")
    NBM = (2 * 66 * 128 + BLK - 1) // BLK + NE
    d_xpad = I("xpad", [66 * 128, D])
    d_xc = I("xc", [256, D])
    d_hmask = I("hmask", [128, 2])
    d_xo = dram(nc, "xo", [32 * 128, D], F32, "ExternalOutput")
    d_x1 = N("x1", [8192, D])
    d_xc1 = N("xc1", [256, D])
    d_x2g = N("x2g", [8192, D])
    d_xc2 = N("xc2", [256, D])
    d_x3h = N("x3h", [34 * 128, D])
    with ExitStack() as es:
        P = Prog(nc, es)
        G = Ctx()
        setup_consts(nc, P, G)
        G.d_iotab = I("iota_b", [128, NBM])
        G.d_iotap = I("iota_p", [128, 1])
        G.d_Xs = N("Xs", [NBM * BLK, D], BF16)
        G.d_Ys = N("Ys", [NBM * BLK, D])
        G.d_hl = N("hl", [8192, D], BF16)
        G.d_H2s = N("H2s", [66 * 128, D], BF16)
        ar = Arena(P, "arena", 182)
        ar.base = 0
        tl = lambda ap, a, b, w=0: [(w, ap, t) for t in range(a, b)]
        C = Layer(nc, P, G, ar, 0, 2, True, False)
        mi = moe_inputs(nc, 0)
        wins = [dict(x=d_xpad[0:34 * 128, :], out=d_x1[0:4096, :], zlo=True, zhi=False),
                dict(x=d_xpad[32 * 128:66 * 128, :], out=d_x1[4096:8192, :], zlo=False, zhi=True)]
        emit_conv(nc, P, G, C, ar, 0, wins, dict(x=d_xc, out=d_xc1))
        run_moe(nc, P, G, C, ar, tl(d_x1, 0, 64) + tl(d_xc1, 0, 2, 1), mi)
        C = Layer(nc, P, G, ar, 1, 2, True, True)
        mi = moe_inputs(nc, 1)
        emit_fft(nc, P, G, C, ar, 1, d_x1, d_xc1, d_x2g, d_xc2)
        run_moe(nc, P, G, C, ar, tl(d_x2g, 0, 64) + tl(d_xc2, 0, 2, 1), mi)
        C = Layer(nc, P, G, ar, 2, 2, False, False)
        mi = moe_inputs(nc, 2)
        emit_attn(nc, P, G, C, ar, d_x2g, d_xc2, d_x3h)
        run_moe(nc, P, G, C, ar, tl(d_x3h, 0, 34), mi)
        C = Layer(nc, P, G, ar, 3, 1, False, False)
        mi = moe_inputs(nc, 3)
        emit_conv(nc, P, G, C, ar, 3, [dict(x=d_x3h, out=d_xo, zlo=False, zhi=False, mask=d_hmask)], None)
        run_moe(nc, P, G, C, ar, tl(d_xo, 0, 32), mi)
        P.barrier()
        P.finish()
        print("fused program instructions:", P.ninst, "sems:", P.nsem)
    return nc


_CACHE = {}


def _fm(v, n):
    return np.ascontiguousarray(v.reshape(n, 128).T)


def _rope_tables():
    rows = 8192 // 64
    row = np.repeat(np.arange(rows, dtype=np.float32), 64)
    col = np.tile(np.arange(64, dtype=np.float32), rows)
    inv = (np.float32(10000.0) ** (-np.arange(32, dtype=np.float32) / np.float32(32))).astype(np.float32)
    ang = np.concatenate([row[:, None] * inv, col[:, None] * inv], axis=-1).astype(np.float32)
    c, s_ = np.cos(ang).astype(np.float32), np.sin(ang).astype(np.float32)
    cosf = np.repeat(c, 2, axis=1).T
    sgn = np.tile(np.array([-1.0, 1.0], np.float32), 64)
    sinf = (np.repeat(s_, 2, axis=1) * sgn[None, :]).T
    return np.ascontiguousarray(cosf), np.ascontiguousarray(sinf)


def _fft_tables():
    k1 = np.arange(64, dtype=np.float64)
    t1 = np.arange(64, dtype=np.float64)
    a = 2 * np.pi * np.outer(t1, k1) / 64.0
    wa = np.stack([np.cos(a), -np.sin(a)], -1) / 8.0
    wa = wa.reshape(64, 8, 8 * 2)
    t2 = np.arange(128, dtype=np.float64)
    k2 = np.arange(128, dtype=np.float64)
    kk = k1[:, None] + 64.0 * k2[None, :]
    th = 2 * np.pi * t2[:, None, None] * kk[None, :, :] / 8192.0
    mr, mi = np.cos(th), -np.sin(th)
    mb = np.stack([mr, mi, -mi, mr], 2) / np.sqrt(128.0)
    n = np.arange(256, dtype=np.float64)
    ph = 2 * np.pi * np.outer(n, n) / 256.0
    cs = np.concatenate([np.cos(ph), np.sin(ph)], 1) / 16.0
    csn = np.concatenate([np.cos(ph), -np.sin(ph)], 1) / 16.0
    cd = cs.reshape(2, 128, 512).transpose(1, 0, 2).reshape(128, 1024)
    cdn = csn.reshape(2, 128, 512).transpose(1, 0, 2).reshape(128, 1024)
    f = lambda a_: np.ascontiguousarray(a_.astype(np.float32))
    return f(wa), f(mb.reshape(128, -1)), f(cd), f(cdn)


def kernel(**inp):
    inp = {k: np.asarray(v) for k, v in inp.items()}
    if "nc" not in _CACHE:
        _CACHE["nc"] = build_fused()
    nc = _CACHE["nc"]
    NBM = (2 * 66 * 128 + BLK - 1) // BLK + NE
    sh = dict(ident=np.eye(128, dtype=np.float32), ltri=np.triu(np.ones((128, 128), np.float32), 1),
              iota_b=np.tile((np.arange(NBM, dtype=np.float32) * BLK)[None, :], (128, 1)),
              iota_p=np.arange(128, dtype=np.float32).reshape(128, 1))
    for L in range(4):
        sh[f"w_mod{L}"] = inp["w_mod"][L]
        sh[f"b_mod{L}"] = inp["b_mod"][L][None, :]
        sh[f"ng{L}"] = inp["norm_g"][L]
        sh[f"wr{L}"] = np.ascontiguousarray(np.concatenate([inp["moe_w_group"][L], inp["moe_w_expert"][L]], axis=1))
        sh[f"br{L}"] = np.concatenate([inp["moe_b_group"][L], inp["moe_b_expert"][L]])[None, :]
        sh[f"wg{L}"] = inp["moe_w_gate"][L].reshape(NE * 128, 4096)
        sh[f"wu{L}"] = inp["moe_w_up"][L].reshape(NE * 128, 4096)
        sh[f"wd{L}"] = inp["moe_w_down"][L].reshape(NE * 128, 4096)
    for L, j in ((0, 0), (3, 1)):
        sh[f"w_in{L}"] = inp["conv_w_in"][j]
        sh[f"b_in_fm{L}"] = _fm(inp["conv_b_in"][j], 16)
        sh[f"w_dw_fm{L}"] = np.ascontiguousarray(inp["conv_w_dw"][j].T.reshape(8, 128, 31).transpose(1, 0, 2).reshape(128, 8 * 31))
        sh[f"b_dw_fm{L}"] = _fm(inp["conv_b_dw"][j], 8)
        sh[f"gn_fm{L}"] = _fm(inp["conv_norm_g"][j], 8)
        sh[f"w_out{L}"] = inp["conv_w_out"][j]
        sh[f"b_out{L}"] = inp["conv_b_out"][j][None, :]
    sh["w_out1"] = inp["fnet_w_out"][0]
    sh["b_out1"] = inp["fnet_b_out"][0][None, :]
    sh["wa"], sh["mb"], sh["cd"], sh["cdn"] = _fft_tables()
    cosf, sinf = _rope_tables()
    gtok = (np.arange(64)[:, None] + 64 * np.arange(128)[None, :]).reshape(-1)
    sh["cosg"] = np.ascontiguousarray(cosf[:, gtok])
    sh["sing"] = np.ascontiguousarray(sinf[:, gtok])
    prot = np.zeros((128, 128), np.float32)
    for i in range(64):
        prot[2 * i, 2 * i + 1] = 1
        prot[2 * i + 1, 2 * i] = 1
    sh.update(w_qkv=inp["attn_w_qkv"][0], gq_fm=inp["attn_q_norm_g"][0].reshape(128, 1), gk_fm=inp["attn_k_norm_g"][0].reshape(128, 1),
              w_o=inp["attn_w_out"][0], prot=prot)
    zpad = np.zeros((128, D), np.float32)
    in_maps = []
    for core in range(8):
        b, s = core // 2, core % 2
        m = dict(sh)
        m["xpad"] = np.ascontiguousarray(np.concatenate([zpad, inp["x"][b], zpad], axis=0))
        m["xc"] = np.ascontiguousarray(inp["ctx"][b])
        m["cvec"] = np.stack([inp["c"][b], inp["c_ctx"]])
        tok = np.clip(4096 * s - 128 + np.arange(34 * 128), 0, 8191)
        m["cosq"] = np.ascontiguousarray(cosf[:, tok])
        m["sinq"] = np.ascontiguousarray(sinf[:, tok])
        grow = (tok % 64) * 128 + tok // 64
        m["qidx"] = np.ascontiguousarray(grow.reshape(34, 128).T.astype(np.int32))
        hm = np.ones((128, 2), np.float32)
        if s == 0:
            hm[:, 0] = 0
        else:
            hm[:, 1] = 0
        m["hmask"] = hm
        in_maps.append(m)
    res = run_bass_kernel_spmd(nc, in_maps, core_ids=list(range(8)))
    out = np.empty_like(inp["x"])
    for core in range(8):
        b, s = core // 2, core % 2
        out[b, s * 4096:(s + 1) * 4096] = res.results[core]["xo"]
    return out
```

```python
import numpy as np
import concourse.bass as bass
import concourse.mybir as mybir
from concourse.bass_utils import run_bass_kernel_spmd
from contextlib import ExitStack

F32 = mybir.dt.float32
BF16 = mybir.dt.bfloat16
I32 = mybir.dt.int32
AF = mybir.ActivationFunctionType
ALU = mybir.AluOpType
AX = mybir.AxisListType

ENG = ["pe", "act", "dve", "pool", "sp"]
D = 1024
EPS = 1e-6
NE = 32
BLK = 384
SUB = BLK // 128


class Prog:
    SEM_LIMIT = 20000
    ND = 8

    def __init__(self, nc, es):
        self.nc, self.es = nc, es
        self.q = {e: [] for e in ENG}
        self.cur = {}
        self.waited = {e: {} for e in ENG}
        self.last_w = {}
        self.readers = {}
        self.nsem = 0
        self.dsem = {e: [] for e in ENG}
        self.dn = {e: 0 for e in ENG}
        self.sems = {}
        self.ninst = 0
        self.nbar = 0
        self.full = {}

    def new_sem(self):
        self.nsem += 1
        self.sems[self.nsem] = self.es.enter_context(self.nc.semaphore(f"s{self.nsem}"))
        return self.nsem

    def sb(self, name, shape, dt):
        return self.es.enter_context(self.nc.sbuf_tensor("sb_" + name, list(shape), dt))

    def ps(self, name, shape, dt):
        return self.es.enter_context(self.nc.psum_tensor("ps_" + name, list(shape), dt))

    def _deps(self, eng, reads, writes, skip_same):
        deps = {}

        def add(ev):
            if ev is None:
                return
            s, v, e = ev
            if skip_same and e == eng:
                return
            if deps.get(s, 0) < v:
                deps[s] = v
        for k in reads:
            add(self.last_w.get(k))
        for k in writes:
            add(self.last_w.get(k))
            for ev in self.readers.get(k, ()):
                add(ev)
        out = []
        for s, v in deps.items():
            if self.waited[eng].get(s, 0) < v:
                self.waited[eng][s] = v
                out.append((s, v))
        return out

    def _commit(self, ev, reads, writes):
        for k in writes:
            self.last_w[k] = ev
            self.readers[k] = []
        for k in reads:
            self.readers.setdefault(k, []).append(ev)

    def op(self, eng, fn, reads=(), writes=(), skip_same=False):
        waits = self._deps(eng, reads, writes, skip_same)
        c = self.cur.get(eng)
        if c is None or c[1] >= self.SEM_LIMIT:
            if c is not None:
                self.full[eng] = (c[0], c[1])
            c = [self.new_sem(), 0]
            self.cur[eng] = c
        c[1] += 1
        ev = (c[0], c[1], eng)
        self.q[eng].append((waits, fn, c[0], 1))
        self._commit(ev, reads, writes)
        self.ninst += 1
        return ev

    def dma(self, eng, fn, reads=(), writes=()):
        waits = self._deps(eng, reads, writes, False)
        pool = self.dsem[eng]
        i = self.dn[eng] % self.ND
        self.dn[eng] += 1
        if len(pool) <= i:
            pool.append([self.new_sem(), 0])
        d = pool[i]
        if d[1] > 0 and self.waited[eng].get(d[0], 0) < 16 * d[1]:
            self.waited[eng][d[0]] = 16 * d[1]
            waits.append((d[0], 16 * d[1]))
        d[1] += 1
        ev = (d[0], 16 * d[1], "dma_" + eng)
        self.q[eng].append((waits, fn, d[0], 16))
        self._commit(ev, reads, writes)
        self.ninst += 1
        return ev

    def wait_keys(self, eng, keys):
        waits = self._deps(eng, keys, (), False)
        self.q[eng].append((waits, None, None, 0))

    def barrier(self):
        evs = []
        for e in ENG:
            c = self.cur.get(e)
            if c is not None:
                evs.append((c[0], c[1]))
            if e in self.full:
                evs.append(self.full[e])
            for d in self.dsem[e]:
                if d[1] > 0:
                    evs.append((d[0], 16 * d[1]))
        for e in ENG:
            waits = []
            for s, v in evs:
                if self.waited[e].get(s, 0) < v:
                    self.waited[e][s] = v
                    waits.append((s, v))
            self.q[e].append((waits, None, None, 0))
        self.last_w = {}
        self.readers = {}

    def finish(self):
        nc = self.nc
        engobj = {"pe": "tensor", "act": "scalar", "dve": "vector", "pool": "gpsimd", "sp": "sync"}
        with nc.Block() as block:
            for e in ENG:
                if not self.q[e]:
                    continue

                def body(engine, e=e):
                    for waits, fn, s, inc in self.q[e]:
                        for (ws, wv) in waits:
                            engine.wait_ge(self.sems[ws], wv)
                        if fn is not None:
                            fn(engine).then_inc(self.sems[s], inc)
                getattr(block, engobj[e])(body)

    def act(self, out, in_, func, r, w, bias=None, scale=None, accum=None):
        kw = {}
        if bias is not None:
            kw["bias"] = bias
        if scale is not None:
            kw["scale"] = scale
        if accum is not None:
            kw["accum_out"] = accum
        return self.op("act", lambda e: e.activation(out=out, in_=in_, func=func, **kw), r, w)

    def tt(self, eng, out, in0, in1, op, r, w):
        return self.op(eng, lambda e: e.tensor_tensor(out=out, in0=in0, in1=in1, op=op), r, w)

    def ts(self, eng, out, in0, s1, s2, op0, op1, r, w):
        if op1 is None:
            return self.op(eng, lambda e: e.tensor_scalar(out=out, in0=in0, scalar1=s1, scalar2=None, op0=op0), r, w)
        return self.op(eng, lambda e: e.tensor_scalar(out=out, in0=in0, scalar1=s1, scalar2=s2, op0=op0, op1=op1), r, w)

    def stt(self, out, in0, scalar, in1, op0, op1, r, w):
        return self.op("dve", lambda e: e.scalar_tensor_tensor(out=out, in0=in0, scalar=scalar, in1=in1, op0=op0, op1=op1), r, w)

    def cp(self, eng, out, in_, r, w):
        return self.op(eng, lambda e: e.tensor_copy(out=out, in_=in_), r, w)

    def red(self, out, in_, op, r, w):
        return self.op("dve", lambda e: e.tensor_reduce(out=out, in_=in_, axis=AX.X, op=op), r, w)

    def mm(self, out, lhsT, rhs, start, stop, r, w):
        return self.op("pe", lambda e: e.matmul(out, lhsT=lhsT, rhs=rhs, start=start, stop=stop), r, w, skip_same=True)

    def tr(self, out, in_, ident, r, w):
        return self.op("pe", lambda e: e.transpose(out=out, in_=in_, identity=ident), r, w, skip_same=True)

    def ld(self, q, out, in_, r, w, slow=False):
        if slow:
            return self.dma(q, lambda e: e.dma_start(out=out, in_=in_, allow_slow_non_contiguous=True), r, w)
        return self.dma(q, lambda e: e.dma_start(out=out, in_=in_), r, w)


class Arena:
    def __init__(self, P, name, kbytes):
        self.t = P.sb(name, [128, kbytes * 256], F32)
        self.n = kbytes * 256
        self.off = 0
        self.base = 0

    def reset(self):
        self.off = self.base

    def get(self, shape, dt):
        n = int(np.prod(shape[1:]))
        words = n if dt in (F32, I32) else (n + 1) // 2
        assert self.off + words <= self.n, ("arena overflow", self.off + words, self.n)
        v = self.t[0:shape[0], self.off:self.off + words]
        self.off += words
        if dt != F32:
            v = v.bitcast(dt)
            if dt == BF16 and n % 2:
                v = v[:, 0:n]
        if len(shape) == 3:
            v = v.rearrange("p (a b) -> p a b", a=shape[1])
        elif len(shape) == 4:
            v = v.rearrange("p (a b c) -> p a b c", a=shape[1], b=shape[2])
        return v


def dram(nc, name, shape, dt, kind):
    return nc.dram_tensor(name, list(shape), dt, kind=kind).ap()


def moe_phase(nc, P, C, ar, tiles, NT, NB, dd, psum):
    ar.reset()
    assert len(tiles) == NT
    tp, pab, pmm = psum["big"], psum["a"], psum["b"]
    wr = ar.get([128, 8, 36], F32)
    brb = ar.get([128, 36], F32)
    lg = ar.get([128, NT, 36], F32)
    iotab = ar.get([128, NB], F32)
    iotap = ar.get([128, 1], F32)
    P.ld("sp", wr, dd["wr"].rearrange("(c p) n -> p c n", p=128), [], ["wr"], slow=True)
    P.ld("sp", brb, dd["br"].partition_broadcast(128).rearrange("p o n -> p (o n)"), [], ["brb"])
    P.ld("sp", iotab, dd["iotab"][:, 0:NB], [], ["iotab"])
    P.ld("sp", iotap, dd["iotap"], [], ["iotap"])
    g = lambda shape, dt=F32: ar.get(shape, dt)
    ga1 = g([128, NT]); ga2 = g([128, NT]); d1 = g([128, NT], I32); d2 = g([128, NT], I32); widx = g([128, NB], I32)
    h2b = [ar.get([128, D], BF16) for _ in range(2)]
    mark = ar.off
    xt = [ar.get([128, D], F32) for _ in range(4)]
    h2 = [ar.get([128, D], F32) for _ in range(2)]
    h2T = [ar.get([128, 8, 128], F32) for _ in range(2)]
    junk = ar.get([128, D], BF16)
    ssq = ar.get([128, 4], F32)
    rsq = ar.get([128, 4], F32)
    gmax = g([128, NT]); gmask = g([128, NT, 4]); gex = g([128, NT, 4]); gsum = g([128, NT]); gtop = g([128, NT])
    pen = g([128, NT, 4]); em = g([128, NT, 32]); m1 = g([128, NT]); oh1 = g([128, NT, 32]); em2 = g([128, NT, 32])
    m2 = g([128, NT]); oh2 = g([128, NT, 32]); dd_ = g([128, NT])
    S = em; pre = g([128, NT, 32]); tot = g([128, NT, 32]); base = g([128, NT, 32]); tmp32 = em2
    cnt = g([128, 32]); padded = g([128, 32]); pst = [g([128, 32]) for _ in range(2)]; pend = g([128, 32]); pstart = g([128, 32])
    d1f = g([128, NT]); d2f = g([128, NT])
    cmpb = g([128, NB, 32]); bef = g([128, NB]); wif = g([128, NB])
    tpf = tp.rearrange("p a b -> p (a b)")
    trp = [tpf[:, 0:1024].rearrange("p (j q) -> p j q", j=8), tpf[:, 1024:2048].rearrange("p (j q) -> p j q", j=8)]
    lgp = [pab[0].rearrange("p a b -> p (a b)"), pab[1].rearrange("p a b -> p (a b)")]

    def A0(i):
        w, xd, t = tiles[i]
        P.ld("sp", xt[i % 4], xd[t * 128:(t + 1) * 128, :], [], [("xt", i % 4)])

    def A1(i):
        xk = ("xt", i % 4)
        k = i % 4
        P.act(junk, xt[k], AF.Square, [xk], ["junk", ("ssq", k)], accum=ssq[:, k:k + 1])
        P.act(rsq[:, k:k + 1], ssq[:, k:k + 1], AF.Sqrt, [("ssq", k), "epsc"], [("rsq", k)], bias=C.G.epsc[:, 0:1], scale=1.0 / D)
        P.op("dve", lambda e: e.reciprocal(out=rsq[:, k:k + 1], in_=rsq[:, k:k + 1]), [("rsq", k)], [("rsq", k)])

    def B1(i):
        w, xd, t = tiles[i]
        k = i % 4
        hb, hk = h2[i % 2], ("h2", i % 2)
        P.stt(hb, xt[k], rsq[:, k:k + 1], C.bc[("A2", w)], ALU.mult, ALU.mult, [("xt", k), ("rsq", k), ("A2", w)], [hk])
        P.tt("pool", hb, hb, C.bc[("sh2", w)], ALU.add, [hk, ("sh2", w)], [hk])
        bb, bk = h2b[i % 2], ("h2b", i % 2)
        P.act(bb, hb, AF.Copy, [hk], [bk])
        P.ld("sp", dd["H2s"][i * 128:(i + 1) * 128, :], bb, [bk], [])

    def C1(i):
        hb, hk = h2[i % 2], ("h2", i % 2)
        tb, tk = trp[i % 2], ("trp", i % 2)
        for j in range(8):
            P.tr(tb[:, j, :], hb[:, j * 128:(j + 1) * 128], C.ident[:], [hk, "ident"], [tk])
        P.cp("dve", h2T[i % 2], tb, [tk], [("h2T", i % 2)])

    def D1(i):
        hT, hTk = h2T[i % 2], ("h2T", i % 2)
        lp, lk = lgp[i % 2], ("lgp", i % 2)
        for j in range(8):
            P.mm(lp[:, 0:36], hT[:, j, :], wr[:, j, :], j == 0, j == 7, [hTk, "wr"], [lk])
        P.tt("dve", lg[:, i, :], lp[:, 0:36], brb, ALU.add, [lk, "brb"], ["lg"])

    A0(0)
    if NT > 1:
        A0(1)
    for s_ in range(NT + 3):
        if s_ + 2 < NT:
            A0(s_ + 2)
        if s_ < NT:
            A1(s_)
        if 0 <= s_ - 1 < NT:
            B1(s_ - 1)
        if 0 <= s_ - 2 < NT:
            C1(s_ - 2)
        if 0 <= s_ - 3 < NT:
            D1(s_ - 3)

    glv, elv = lg[:, :, 0:4], lg[:, :, 4:36]
    bc3 = lambda a, n: a.unsqueeze(2).to_broadcast([128, NT, n])
    P.red(gmax, glv, ALU.max, ["lg"], ["gmax"])
    P.tt("dve", gmask, glv, bc3(gmax, 4), ALU.is_equal, ["lg", "gmax"], ["gmask"])
    P.tt("dve", gex, glv, bc3(gmax, 4), ALU.subtract, ["lg", "gmax"], ["gex"])
    P.act(gex, gex, AF.Exp, ["gex"], ["gex"])
    P.red(gsum, gex, ALU.add, ["gex"], ["gsum"])
    P.op("dve", lambda e: e.reciprocal(out=gtop, in_=gsum), ["gsum"], ["gtop"])
    P.ts("dve", pen, gmask, 1.0, 1e30, ALU.subtract, ALU.mult, ["gmask"], ["pen"])
    P.tt("dve", em.rearrange("p t (a b) -> p t a b", a=4), elv.rearrange("p t (a b) -> p t a b", a=4),
         pen.unsqueeze(3).to_broadcast([128, NT, 4, 8]), ALU.add, ["lg", "pen"], ["em"])
    P.red(m1, em, ALU.max, ["em"], ["m1"])
    P.tt("dve", oh1, em, bc3(m1, 32), ALU.is_equal, ["em", "m1"], ["oh1"])
    P.ts("dve", em2, oh1, -1e30, None, ALU.mult, None, ["oh1"], ["em2"])
    P.tt("dve", em2, em2, em, ALU.add, ["em2", "em"], ["em2"])
    P.red(m2, em2, ALU.max, ["em2"], ["m2"])
    P.tt("dve", oh2, em2, bc3(m2, 32), ALU.is_equal, ["em2", "m2"], ["oh2"])
    P.tt("dve", dd_, m2, m1, ALU.subtract, ["m1", "m2"], ["dd"])
    P.act(dd_, dd_, AF.Exp, ["dd"], ["dd"])
    P.ts("dve", dd_, dd_, 1.0, None, ALU.add, None, ["dd"], ["dd"])
    P.op("dve", lambda e: e.reciprocal(out=dd_, in_=dd_), ["dd"], ["dd"])
    P.tt("dve", ga1, gtop, dd_, ALU.mult, ["gtop", "dd"], ["ga1"])
    P.tt("dve", ga2, gtop, ga1, ALU.subtract, ["gtop", "ga1"], ["ga2"])
    P.tt("dve", S, oh1, oh2, ALU.add, ["oh1", "oh2"], ["S", "em"])
    Sf = S.rearrange("p t e -> p (t e)")
    pref = pre.rearrange("p t e -> p (t e)")
    totf = tot.rearrange("p t e -> p (t e)")
    NW = NT * 32
    c0 = 0
    k = 0
    while c0 < NW:
        wdt = min(512, NW - c0)
        for (lh, dst, dk) in ((C.ltri, pref, "pre"), (C.onesf, totf, "tot")):
            pp, pk = pmm[k % 2], ("pmm", k % 2)
            k += 1
            P.mm(pp[:, 0:wdt], lh[:], Sf[:, c0:c0 + wdt], True, True, ["S", "ltri", "onesf"], [pk])
            P.cp("dve", dst[:, c0:c0 + wdt], pp[:, 0:wdt], [pk], [dk])
        c0 += wdt
    P.op("dve", lambda e: e.memset(base[:, 0, :], 0.0), [], ["base"])
    for i in range(1, NT):
        P.tt("dve", base[:, i, :], base[:, i - 1, :], tot[:, i - 1, :], ALU.add, ["base", "tot"], ["base"])
    P.tt("dve", cnt, base[:, NT - 1, :], tot[:, NT - 1, :], ALU.add, ["base", "tot"], ["cnt"])
    P.tt("dve", pre, pre, base, ALU.add, ["pre", "base"], ["pre"])
    cmp2 = cmpb.rearrange("p b e -> p (b e)").rearrange("p (e b) -> p e b", e=32)
    P.tt("dve", cmp2, cnt.unsqueeze(2).to_broadcast([128, 32, NB]), iotab.unsqueeze(1).to_broadcast([128, 32, NB]), ALU.is_gt, ["cnt", "iotab"], ["cmpb"])
    P.red(padded, cmp2, ALU.add, ["cmpb"], ["padded"])
    P.ts("dve", padded, padded, float(BLK), None, ALU.mult, None, ["padded"], ["padded"])
    P.cp("dve", pst[0], padded, ["padded"], [("pst", 0)])
    cur = 0
    sh = 1
    while sh < 32:
        a, b = pst[cur], pst[1 - cur]
        P.cp("dve", b[:, 0:sh], a[:, 0:sh], [("pst", cur)], [("pst", 1 - cur)])
        P.tt("dve", b[:, sh:32], a[:, sh:32], a[:, 0:32 - sh], ALU.add, [("pst", cur)], [("pst", 1 - cur)])
        cur = 1 - cur
        sh *= 2
    P.cp("dve", pend, pst[cur], [("pst", cur)], ["pend"])
    P.tt("dve", pstart, pend, padded, ALU.subtract, ["pend", "padded"], ["pstart"])
    P.tt("dve", pre, pre, pstart.unsqueeze(1).to_broadcast([128, NT, 32]), ALU.add, ["pre", "pstart"], ["pre"])
    for (oh, ohk, df, dfk, di_, dik) in ((oh1, "oh1", d1f, "d1f", d1, "d1"), (oh2, "oh2", d2f, "d2f", d2, "d2")):
        P.tt("dve", tmp32, pre, oh, ALU.mult, ["pre", ohk], ["tmp32", "em2"])
        P.red(df, tmp32, ALU.add, ["tmp32"], [dfk])
        P.cp("dve", di_, df, [dfk], [dik])
    P.tt("dve", cmpb, pend.unsqueeze(1).to_broadcast([128, NB, 32]), iotab.unsqueeze(2).to_broadcast([128, NB, 32]), ALU.is_le, ["pend", "iotab"], ["cmpb"])
    P.red(bef, cmpb, ALU.add, ["cmpb"], ["bef"])
    P.ts("dve", bef, bef, float(NE - 1), None, ALU.min, None, ["bef"], ["bef"])
    P.ts("dve", wif, bef, 128.0, iotap[:, 0:1], ALU.mult, ALU.add, ["bef", "iotap"], ["wif"])
    P.cp("dve", widx, wif, ["wif"], ["widx"])

    P.barrier()
    ar.off = mark
    hb4 = [ar.get([128, D], BF16) for _ in range(4)]
    for i in range(NT):
        bb, bk = hb4[i % 4], ("hb4", i % 4)
        P.ld("sp", bb, dd["H2s"][i * 128:(i + 1) * 128, :], [], [bk])
        for (di_, dik) in ((d1, "d1"), (d2, "d2")):
            P.dma("pool", lambda e, bb=bb, di_=di_, i=i: e.indirect_dma_start(
                out=dd["Xs"], out_offset=bass.IndirectOffsetOnAxis(ap=di_[:, i:i + 1], axis=0), in_=bb, in_offset=None),
                [bk, dik], [])
    P.barrier()

    ar.off = mark
    NWB = 3
    wgb = [ar.get([128, 8, 512], BF16) for _ in range(NWB)]
    wub = [ar.get([128, 8, 512], BF16) for _ in range(NWB)]
    wdb = [ar.get([128, 4, D], BF16) for _ in range(NWB)]
    Xb = [ar.get([128, D], BF16) for _ in range(4)]
    XbT = [ar.get([128, 8, 128], BF16) for _ in range(2)]
    sgl = [ar.get([128, 512], F32) for _ in range(2)]
    actb = [ar.get([128, 512], BF16) for _ in range(2)]
    actT = [ar.get([128, 4, 128], BF16) for _ in range(2)]
    Yb = [ar.get([128, D], F32) for _ in range(4)]
    tpf2 = tp.rearrange("p a b -> p (a b)")
    tpb = tpf2.bitcast(BF16)
    xTp = tpb[:, 0:1024].rearrange("p (j q) -> p j q", j=8)
    aTp1 = tpb[:, 1024:1536].rearrange("p (j q) -> p j q", j=4)
    gup = [[pab[0].rearrange("p a b -> p (a b)"), pab[1].rearrange("p a b -> p (a b)")], [tpf2[:, 1024:1536], tpf2[:, 1536:2048]]]
    dn = pmm
    NS = NB * SUB

    def wload(b, which):
        wb = b % NWB
        for (buf, src, nm) in which:
            dst = buf[wb].rearrange("p a b -> p (a b)")
            P.dma("pool", lambda e, dst=dst, src=src, b=b: e.indirect_dma_start(
                out=dst, out_offset=None, in_=src, in_offset=bass.IndirectOffsetOnAxis(ap=widx[:, b:b + 1], axis=0)),
                ["widx"], [(nm, wb)])
    WGU = ((wgb, dd["wg"], "wg"), (wub, dd["wu"], "wu"))
    WD = ((wdb, dd["wd"], "wd"),)

    def S0(n):
        r0 = (n // SUB) * BLK + (n % SUB) * 128
        P.ld("sp", Xb[n % 4], dd["Xs"][r0:r0 + 128, :], [], [("Xb", n % 4)])

    def S1(n):
        xb_, xk = Xb[n % 4], ("Xb", n % 4)
        for c in range(8):
            P.tr(xTp[:, c, :], xb_[:, c::8], C.identb[:], [xk, "identb"], ["xTp"])
        P.cp("dve", XbT[n % 2], xTp, ["xTp"], [("XbT", n % 2)])

    def S2a(n):
        wb = (n // SUB) % NWB
        xT, xTk = XbT[n % 2], ("XbT", n % 2)
        g_, u_ = gup[n % 2]
        gk_, uk_ = ("gp", n % 2), ("up", n % 2)
        for c in range(8):
            P.mm(g_, xT[:, c, :], wgb[wb][:, c, :], c == 0, c == 7, [xTk, ("wg", wb)], [gk_])
        for c in range(8):
            P.mm(u_, xT[:, c, :], wub[wb][:, c, :], c == 0, c == 7, [xTk, ("wu", wb)], [uk_])
        P.act(sgl[n % 2], g_, AF.Silu, [gk_], [("sgl", n % 2)])
        P.tt("dve", actb[n % 2], u_, sgl[n % 2], ALU.mult, [uk_, ("sgl", n % 2)], [("actb", n % 2)])

    def S2b(n):
        ab, ak = actb[n % 2], ("actb", n % 2)
        ap_, apk = aTp1, "aTp"
        for c in range(4):
            P.tr(ap_[:, c, :], ab[:, c::4], C.identb[:], [ak, "identb"], [apk])
        P.act(actT[n % 2], ap_, AF.Copy, [apk], [("actT", n % 2)])

    def S3(n):
        wb = (n // SUB) % NWB
        r0 = (n // SUB) * BLK + (n % SUB) * 128
        aT, aTk = actT[n % 2], ("actT", n % 2)
        yb, yk = Yb[n % 4], ("Yb", n % 4)
        for half in range(2):
            pp, pk = dn[half], ("dn", half)
            for c in range(4):
                P.mm(pp[:], aT[:, c, :], wdb[wb][:, c, half * 512:(half + 1) * 512], c == 0, c == 3, [aTk, ("wd", wb)], [pk])
            if half == 0:
                P.act(yb[:, 0:512], pp[:], AF.Copy, [pk], [yk])
            else:
                P.cp("dve", yb[:, 512:1024], pp[:], [pk], [yk])
        P.ld("sp", dd["Ys"][r0:r0 + 128, :], yb, [yk], [])

    for b0 in range(NWB):
        wload(b0, WGU)
        wload(b0, WD)
    S0(0)
    S0(1)
    for step in range(NS + 3):
        if step + 2 < NS:
            S0(step + 2)
        if step < NS:
            S1(step)
        if 0 <= step - 1 < NS:
            S2a(step - 1)
            n2 = step - 1
            if n2 % SUB == SUB - 1 and n2 // SUB + NWB < NB:
                wload(n2 // SUB + NWB, WGU)
        if 0 <= step - 2 < NS:
            S2b(step - 2)
        if 0 <= step - 3 < NS:
            S3(step - 3)
            n3 = step - 3
            if n3 % SUB == SUB - 1 and n3 // SUB + NWB < NB:
                wload(n3 // SUB + NWB, WD)
    P.barrier()

    ar.off = mark
    xt = [ar.get([128, D], F32) for _ in range(4)]
    Y1 = [ar.get([128, D], F32) for _ in range(4)]
    Y2 = [ar.get([128, D], F32) for _ in range(4)]

    def L3(i):
        w, xd, t = tiles[i]
        k = i % 4
        P.ld("sp", xt[k], xd[t * 128:(t + 1) * 128, :], [], [("xt", k)])
        for (Y, nm, di_) in ((Y1, "Y1", d1), (Y2, "Y2", d2)):
            P.dma("pool", lambda e, yb=Y[k], di_=di_, i=i: e.indirect_dma_start(
                out=yb, out_offset=None, in_=dd["Ys"], in_offset=bass.IndirectOffsetOnAxis(ap=di_[:, i:i + 1], axis=0)),
                [], [(nm, k)])

    def C3(i):
        w, xd, t = tiles[i]
        k = i % 4
        ya, yak = Y1[k], ("Y1", k)
        yb2, ybk = Y2[k], ("Y2", k)
        xb_, xk = xt[k], ("xt", k)
        P.ts("dve", ya, ya, ga1[:, i:i + 1], None, ALU.mult, None, [yak], [yak])
        P.stt(ya, yb2, ga2[:, i:i + 1], ya, ALU.mult, ALU.add, [yak, ybk], [yak])
        P.tt("pool", ya, ya, C.bc[("g2", w)], ALU.mult, [yak, ("g2", w)], [yak])
        P.tt("dve", xb_, ya, xb_, ALU.add, [yak, xk], [xk])
        P.ld("sp", xd[t * 128:(t + 1) * 128, :], xb_, [xk], [])

    L3(0)
    if NT > 1:
        L3(1)
    for i in range(NT):
        if i + 2 < NT:
            L3(i + 2)
        C3(i)


class Ctx:
    pass


def setup_consts(nc, P, G):
    I = lambda n, s, dt=F32: dram(nc, n, s, dt, "ExternalInput")
    G.d_ident = I("ident", [128, 128])
    G.d_ltri = I("ltri", [128, 128])
    G.d_cvec = I("cvec", [2, D])
    G.ident = P.sb("ident", [128, 128], F32)
    G.identb = P.sb("identb", [128, 128], BF16)
    G.onesb = P.sb("onesb", [128, 128], BF16)
    G.onesf = P.sb("onesf", [128, 128], F32)
    G.ltri = P.sb("ltri", [128, 128], F32)
    G.negh = P.sb("negh", [128, 512], F32)
    G.epsc = P.sb("epsc", [128, 1], F32)
    P.ld("sp", G.ident[:], G.d_ident, [], ["ident"])
    P.ld("sp", G.ltri[:], G.d_ltri, [], ["ltri"])
    P.cp("dve", G.identb[:], G.ident[:], ["ident"], ["identb"])
    P.op("dve", lambda e: e.memset(G.onesb[:], 1.0), [], ["onesb"])
    P.op("dve", lambda e: e.memset(G.onesf[:], 1.0), [], ["onesf"])
    P.op("dve", lambda e: e.memset(G.negh[:], -0.5), [], ["negh"])
    P.op("dve", lambda e: e.memset(G.epsc[:], EPS), [], ["epsc"])
    G.tp = P.ps("tp", [128, 8, 256], F32)
    G.pm = [P.ps(f"pm{i}", [128, 512], F32) for i in range(2)]
    G.pb6 = P.ps("pb6", [128, 512], F32)
    G.pb7 = P.ps("pb7", [128, 512], F32)


class Layer:
    def __init__(self, nc, P, G, ar, L, nwhich, ctx_bc, bc1):
        self.nc, self.P, self.G = nc, P, G
        self.nwhich, self.ctx_bc, self.bc1 = nwhich, ctx_bc, bc1
        self.ident, self.identb, self.onesb, self.onesf, self.ltri, self.negh = G.ident, G.identb, G.onesb, G.onesf, G.ltri, G.negh
        I = lambda n, s, dt=F32: dram(nc, n, s, dt, "ExternalInput")
        self.d_wmod = I(f"w_mod{L}", [D, 6 * D])
        self.d_bmod = I(f"b_mod{L}", [1, 6 * D])
        self.d_ng = I(f"ng{L}", [2, D])
        self.d_cvec = G.d_cvec
        P.barrier()
        ar.base = 0
        ar.off = 0
        self.bc = {}
        for wch in range(nwhich if ctx_bc else 1):
            for nm in ["g1", "A2", "sh2", "g2"]:
                self.bc[(nm, wch)] = ar.get([128, D], F32)
        if bc1:
            self.bc[("A1", 0)] = ar.get([128, D], F32)
            self.bc[("sh1", 0)] = ar.get([128, D], F32)
        self.fm = ar.get([128, 2, 2, 8], F32)
        ar.base = ar.off
        self.mod_psum = G.pm
        self.mod_phase(ar)

    def mod_phase(self, ar):
        P = self.P
        ar.reset()
        nw = self.nwhich
        cfm = ar.get([128, 2, 8], F32)
        scb = ar.get([128, 2, 8], F32)
        Lb = ar.get([128, 2 * 8, 128], F32)
        modbc = ar.get([128, nw, 6 * D], F32)
        bmb = ar.get([128, 6 * D], F32)
        ngb = ar.get([128, 2, D], F32)
        wm = [ar.get([128, 8, 512], F32) for _ in range(2)]
        tmp = ar.get([128, 8, 128], F32)
        pm = self.G.pm
        P.ld("sp", cfm, self.d_cvec.rearrange("r (j p) -> p r j", p=128), [], ["cfm"], slow=True)
        P.ld("sp", bmb, self.d_bmod.partition_broadcast(128).rearrange("p o n -> p (o n)"), [], ["bmb"])
        P.ld("sp", ngb, self.d_ng.partition_broadcast(128), [], ["ngb"])
        P.act(scb, cfm, AF.Silu, ["cfm"], ["scb"])
        for wch in range(nw):
            for j in range(8):
                P.cp("dve", Lb[:, wch * 8 + j, :], scb[:, wch, j:j + 1].to_broadcast([128, 128]), ["scb"], [("Lb", wch, j)])
        wmv = self.d_wmod.rearrange("(j p) n -> p j n", p=128)
        for n in range(12):
            w = wm[n % 2]
            P.ld("sp", w, wmv[:, :, n * 512:(n + 1) * 512], [], [("wm", n % 2)])
            for wch in range(nw):
                pp = pm[(n * nw + wch) % 2]
                pk = ("pm", (n * nw + wch) % 2)
                for j in range(8):
                    P.mm(pp[:], Lb[:, wch * 8 + j, :], w[:, j, :], j == 0, j == 7, [("Lb", wch, j), ("wm", n % 2)], [pk])
                P.tt("dve", modbc[:, wch, n * 512:(n + 1) * 512], pp[:], bmb[:, n * 512:(n + 1) * 512], ALU.add, [pk, "bmb"], [("mod", wch)])
        for wch in range(nw):
            m = lambda i: modbc[:, wch, i * D:(i + 1) * D]
            mk = ("mod", wch)
            if wch == 0 or self.ctx_bc:
                P.cp("pool", self.bc[("g1", wch)], m(2), [mk], [("g1", wch)])
                P.cp("pool", self.bc[("sh2", wch)], m(3), [mk], [("sh2", wch)])
                P.cp("pool", self.bc[("g2", wch)], m(5), [mk], [("g2", wch)])
                P.stt(self.bc[("A2", wch)], m(4), 1.0, ngb[:, 1, :], ALU.add, ALU.mult, [mk, "ngb"], [("A2", wch)])
            P.stt(m(1), m(1), 1.0, ngb[:, 0, :], ALU.add, ALU.mult, [mk, "ngb"], [mk])
            if self.bc1 and wch == 0:
                P.cp("pool", self.bc[("A1", 0)], m(1), [mk], [("A1", 0)])
                P.cp("pool", self.bc[("sh1", 0)], m(0), [mk], [("sh1", 0)])
            for k, src in enumerate([m(1), m(0)]):
                P.tt("dve", tmp, src.rearrange("p (j q) -> p j q", j=8), self.ident[:].unsqueeze(1).to_broadcast([128, 8, 128]), ALU.mult, [mk, "ident"], ["fmtmp"])
                P.red(self.fm[:, wch, k, :], tmp, ALU.add, ["fmtmp"], [("fm", wch)])
        P.barrier()

    def norm_tile(self, xt, xk, xh, xhk, ss_name):
        P = self.P
        junk, ss, rs = self.nt_junk, self.nt_ss, self.nt_rs
        P.act(junk[:], xt, AF.Square, [xk], ["nt_junk", "nt_ss"], accum=ss[:])
        P.act(rs[:], ss[:], AF.Sqrt, ["nt_ss", "epsc"], ["nt_rs"], bias=self.G.epsc[:, 0:1], scale=1.0 / D)
        P.op("dve", lambda e: e.reciprocal(out=rs[:], in_=rs[:]), ["nt_rs"], ["nt_rs"])
        P.ts("dve", xh, xt, rs[:, 0:1], None, ALU.mult, None, [xk, "nt_rs"], [xhk])

    def alloc_norm(self, ar):
        self.nt_junk = ar.get([128, D], BF16)
        self.nt_ss = ar.get([128, 1], F32)
        self.nt_rs = ar.get([128, 1], F32)


def moe_inputs(nc, L):
    I = lambda n, s, dt=F32: dram(nc, n, s, dt, "ExternalInput")
    return dict(wr=I(f"wr{L}", [D, 36]), br=I(f"br{L}", [1, 36]), wg=I(f"wg{L}", [NE * 128, 4096]),
                wu=I(f"wu{L}", [NE * 128, 4096]), wd=I(f"wd{L}", [NE * 128, 4096]))


def run_moe(nc, P, G, C, ar, tiles, mi):
    NT = len(tiles)
    NB = (2 * NT * 128 + BLK - 1) // BLK + NE
    dd = dict(mi)
    dd.update(iotab=G.d_iotab, iotap=G.d_iotap, Xs=G.d_Xs, Ys=G.d_Ys, H2s=G.d_H2s)
    pab = [G.pm[0].rearrange("p (a b) -> p a b", a=2), G.pm[1].rearrange("p (a b) -> p a b", a=2)]
    P.barrier()
    moe_phase(nc, P, C, ar, tiles, NT, NB, dd, psum=dict(big=G.tp, a=pab, b=[G.pb6, G.pb7]))
    P.barrier()


def emit_conv(nc, P, G, C, ar, L, wins, ctxseg):
    I = lambda n, s, dt=F32: dram(nc, n, s, dt, "ExternalInput")
    d_win = I(f"w_in{L}", [D, 2 * D])
    d_binfm = I(f"b_in_fm{L}", [128, 16])
    d_wdwfm = I(f"w_dw_fm{L}", [128, 8 * 31])
    d_bdwfm = I(f"b_dw_fm{L}", [128, 8])
    d_gnfm = I(f"gn_fm{L}", [128, 8])
    d_wout = I(f"w_out{L}", [D, D])
    d_bout = I(f"b_out{L}", [1, D])
    tp, pm = G.tp, G.pm
    NTm = 32
    VW = (NTm + 2) * 128
    ar.off = ar.base
    binfm = ar.get([128, 16], F32)
    wdwfm = ar.get([128, 8, 31], F32)
    bdwfm = ar.get([128, 8], F32)
    gnfm = ar.get([128, 8], F32)
    hmask = ar.get([128, 2], F32)
    bog = [ar.get([128, D], F32) for _ in range(C.nwhich)]
    ar.base = ar.off
    P.ld("sp", binfm, d_binfm, [], ["binfm"])
    P.ld("sp", wdwfm, d_wdwfm.rearrange("p (j t) -> p j t", j=8), [], ["wdwfm"])
    P.ld("sp", bdwfm, d_bdwfm, [], ["bdwfm"])
    P.ld("sp", gnfm, d_gnfm, [], ["gnfm"])
    if any(w_.get("mask") is not None for w_ in wins):
        P.ld("sp", hmask, [w_["mask"] for w_ in wins if w_.get("mask") is not None][0], [], ["hmask"])
    for w in range(C.nwhich):
        P.ld("sp", bog[w], d_bout.partition_broadcast(128).rearrange("p o n -> p (o n)"), [], [("bog", w)])
        P.tt("pool", bog[w], bog[w], C.bc[("g1", w)], ALU.mult, [("bog", w), ("g1", w)], [("bog", w)])
    P.barrier()
    for wi, win_ in enumerate(wins):
        segs = [dict(w=0, x=win_["x"], nt=NTm + 2, own0=1, nown=NTm, out=win_["out"], halo=True)]
        if ctxseg is not None and wi == 0:
            segs.append(dict(w=1, x=ctxseg["x"], nt=2, own0=0, nown=2, out=ctxseg["out"], halo=False))
        has_ctx = len(segs) > 1
        ar.reset()
        vT = ar.get([128, 8, VW], BF16)
        vTc = ar.get([128, 8, 16 + 256 + 16], BF16)
        c2mark = ar.off
        win = ar.get([128, 8, 2 * D], BF16)
        hT = [ar.get([128, 8, 256], BF16) for _ in range(2)]
        xt = [ar.get([128, D], F32) for _ in range(2)]
        xh_ = [ar.get([128, D], F32) for _ in range(2)]
        sig = [ar.get([128, 256], F32) for _ in range(2)]
        C.alloc_norm(ar)
        pab = [pm[0].rearrange("p (a b) -> p a b", a=2), pm[1].rearrange("p (a b) -> p a b", a=2)]
        P.ld("pool", win, d_win.rearrange("(c p) n -> p c n", p=128), [], ["win"])
        if has_ctx:
            P.op("pool", lambda e: e.memset(vTc, 0.0), [], ["vTc"])
        gi = 0
        ti = 0
        for sg in segs:
            w = sg["w"]
            for g in range(sg["nt"] // 2):
                hb = hT[gi % 2]
                hk = ("hT", gi % 2)
                for tl in range(2):
                    t = g * 2 + tl
                    xb_, xk = xt[ti % 2], ("xt", ti % 2)
                    xhb, xhk = xh_[ti % 2], ("xh", ti % 2)
                    ti += 1
                    P.ld("sp", xb_, sg["x"][t * 128:(t + 1) * 128, :], [], [xk])
                    C.norm_tile(xb_, xk, xhb, xhk, None)
                    for j in range(8):
                        P.tr(tp[:, j, tl * 128:(tl + 1) * 128], xhb[:, j * 128:(j + 1) * 128], C.ident[:], [xhk, "ident"], [("tp", j // 2)])
                for j in range(8):
                    P.act(hb[:, j, :], tp[:, j, :], AF.Identity, [("tp", j // 2), ("fm", w)], [hk],
                          bias=C.fm[:, w, 1, j:j + 1], scale=C.fm[:, w, 0, j:j + 1])
                for jo in range(8):
                    pb_ = pab[jo % 2]
                    pk = ("pab", jo % 2)
                    for half in range(2):
                        for c in range(8):
                            P.mm(pb_[:, half, :], win[:, c, half * D + jo * 128: half * D + (jo + 1) * 128], hb[:, c, :], c == 0, c == 7, ["win", hk], [pk])
                    sg_ = sig[jo % 2]
                    P.act(sg_, pb_[:, 1, :], AF.Sigmoid, [pk, "binfm"], [("sig", jo % 2)], bias=binfm[:, 8 + jo:9 + jo])
                    if sg["halo"]:
                        dst = vT[:, jo, g * 256:(g + 1) * 256]
                        dk = "vT"
                    else:
                        dst = vTc[:, jo, 16 + g * 256:16 + (g + 1) * 256]
                        dk = "vTc"
                    P.stt(dst, pb_[:, 0, :], binfm[:, jo:jo + 1], sg_, ALU.add, ALU.mult, [pk, ("sig", jo % 2), "binfm"], [dk])
                if sg["halo"] and g == 0:
                    if win_.get("mask") is not None:
                        P.ts("pool", vT[:, :, 0:128], vT[:, :, 0:128], hmask[:, 0:1], None, ALU.mult, None, ["vT", "hmask"], ["vT"])
                    elif win_["zlo"]:
                        P.op("pool", lambda e: e.memset(vT[:, :, 0:128], 0.0), [], ["vT"])
                if sg["halo"] and g == sg["nt"] // 2 - 1:
                    if win_.get("mask") is not None:
                        P.ts("pool", vT[:, :, VW - 128:VW], vT[:, :, VW - 128:VW], hmask[:, 1:2], None, ALU.mult, None, ["vT", "hmask"], ["vT"])
                    elif win_["zhi"]:
                        P.op("pool", lambda e: e.memset(vT[:, :, VW - 128:VW], 0.0), [], ["vT"])
                gi += 1
        P.barrier()
        ar.off = c2mark
        wout = ar.get([128, 8, D], BF16)
        Dg = [ar.get([128, 31, 128], BF16) for _ in range(2)]
        vc = ar.get([128, 8, 256], F32)
        sq = ar.get([128, 8, 256], BF16)
        t1 = ar.get([128, 256], F32)
        rsb = ar.get([128, 256], F32)
        tmpv = [ar.get([128, 256], F32) for _ in range(2)]
        uT = ar.get([128, 8, 256], BF16)
        xt2 = [ar.get([128, D], F32) for _ in range(2)]
        xn = [ar.get([128, D], F32) for _ in range(2)]
        cv = [tp[:, 0:2, :].rearrange("p a b -> p (a b)"), tp[:, 2:4, :].rearrange("p a b -> p (a b)")]
        ssb = tp[:, 4:6, :].rearrange("p a b -> p (a b)")
        po = [pm[0], pm[1]]
        P.ld("pool", wout, d_wout.rearrange("(c p) n -> p c n", p=128), [], ["wout"])
        di = 0
        xi = 0
        pi = 0
        for sg in segs:
            w = sg["w"]
            ntok = sg["nown"] * 128
            W = 256
            for tb in range(ntok // W):
                for j in range(8):
                    dg, dgk = Dg[di % 2], ("Dg", di % 2)
                    di += 1
                    P.tt("pool" if j % 4 == 3 else "dve", dg, C.identb[:].unsqueeze(1).to_broadcast([128, 31, 128]),
                         wdwfm[:, j, :].unsqueeze(2).to_broadcast([128, 31, 128]), ALU.mult, ["identb", "wdwfm"], [dgk])
                    cvb, cvk = cv[j % 2], ("cv", j % 2)
                    for tau in range(31):
                        if sg["halo"]:
                            c0 = 128 + tb * W + tau - 15
                            rhs = vT[:, j, c0:c0 + W]
                            rk = "vT"
                        else:
                            c0 = 16 + tb * W + tau - 15
                            rhs = vTc[:, j, c0:c0 + W]
                            rk = "vTc"
                        P.mm(cvb[:, 0:W], dg[:, tau, :], rhs, tau == 0, tau == 30, [dgk, rk], [cvk])
                    P.act(vc[:, j, 0:W], cvb[:, 0:W], AF.Identity, [cvk, "bdwfm"], [("vc", j)], bias=bdwfm[:, j:j + 1])
                    P.act(sq[:, j, 0:W], cvb[:, 0:W], AF.Square, [cvk, "bdwfm"], [("sq", j)], bias=bdwfm[:, j:j + 1])
                for j in range(8):
                    P.mm(ssb[:, 0:W], C.onesb[:], sq[:, j, 0:W], j == 0, j == 7, [("sq", j), "onesb"], ["ssb"])
                P.act(t1[:, 0:W], ssb[:, 0:W], AF.Sqrt, ["ssb", "epsc"], ["t1"], bias=G.epsc[:, 0:1], scale=1.0 / D)
                P.op("dve", lambda e, W=W: e.reciprocal(out=rsb[:, 0:W], in_=t1[:, 0:W]), ["t1"], ["rsb"])
                for j in range(8):
                    tv, tvk = tmpv[j % 2], ("tmpv", j % 2)
                    P.tt("dve", tv[:, 0:W], vc[:, j, 0:W], rsb[:, 0:W], ALU.mult, [("vc", j), "rsb"], [tvk])
                    P.act(uT[:, j, 0:W], tv[:, 0:W], AF.Silu, [tvk, "gnfm"], [("uT", j)], scale=gnfm[:, j:j + 1])
                for s in range(W // 128):
                    t = sg["own0"] + (tb * W) // 128 + s
                    xb_, xk = xt2[xi % 2], ("xt2", xi % 2)
                    xnb, xnk = xn[xi % 2], ("xn", xi % 2)
                    xi += 1
                    P.ld("sp", xb_, sg["x"][t * 128:(t + 1) * 128, :], [], [xk])
                    P.tt("pool", xb_, xb_, bog[w], ALU.add, [xk, ("bog", w)], [xk])
                    for half in range(2):
                        pp, pk = po[pi % 2], ("po", pi % 2)
                        pi += 1
                        for j in range(8):
                            P.mm(pp[:], uT[:, j, s * 128:(s + 1) * 128], wout[:, j, half * 512:(half + 1) * 512], j == 0, j == 7, [("uT", j), "wout"], [pk])
                        P.tt("dve", xnb[:, half * 512:(half + 1) * 512], pp[:], C.bc[("g1", w)][:, half * 512:(half + 1) * 512], ALU.mult, [pk, ("g1", w)], [xnk])
                        P.tt("pool", xnb[:, half * 512:(half + 1) * 512], xnb[:, half * 512:(half + 1) * 512], xb_[:, half * 512:(half + 1) * 512], ALU.add, [xnk, xk], [xnk])
                    to = t - sg["own0"]
                    P.ld("sp", sg["out"][to * 128:(to + 1) * 128, :], xnb, [xnk], [])
        P.barrier()


def emit_fft(nc, P, G, C, ar, L, d_x1, d_xc1, d_x2g, d_xc2):
    I = lambda n, s, dt=F32: dram(nc, n, s, dt, "ExternalInput")
    NPASS = 8
    KP = 64 // NPASS
    CB = 512 // (KP * 2)
    d_wa = I("wa", [64, NPASS, KP * 2])
    d_mb = I("mb", [128, 64 * 4 * 128])
    d_cd = I("cd", [128, 2 * 512])
    d_cdn = I("cdn", [128, 2 * 512])
    d_wout = I(f"w_out{L}", [D, D])
    d_bout = I(f"b_out{L}", [1, D])
    d_hl = G.d_hl
    tp, pm, pb6, pb7 = G.tp, G.pm, G.pb6, G.pb7
    tpf = tp.rearrange("p a b -> p (a b)")
    ar.off = ar.base
    bog = [ar.get([128, D], F32) for _ in range(2)]
    ar.base = ar.off
    for w in range(2):
        P.ld("sp", bog[w], d_bout.partition_broadcast(128).rearrange("p o n -> p (o n)"), [], [("bog", w)])
        P.tt("pool", bog[w], bog[w], C.bc[("g1", w)], ALU.mult, [("bog", w), ("g1", w)], [("bog", w)])
    P.barrier()
    ar.reset()
    xt = [ar.get([128, D], F32) for _ in range(2)]
    xh_ = [ar.get([128, D], F32) for _ in range(2)]
    hb = [ar.get([128, D], BF16) for _ in range(2)]
    C.alloc_norm(ar)
    for t in range(64):
        xb_, xk = xt[t % 2], ("xt", t % 2)
        xhb, xhk = xh_[t % 2], ("xh", t % 2)
        P.ld("sp", xb_, d_x1[t * 128:(t + 1) * 128, :], [], [xk])
        C.norm_tile(xb_, xk, xhb, xhk, None)
        P.tt("dve", xhb, xhb, C.bc[("A1", 0)], ALU.mult, [xhk, ("A1", 0)], [xhk])
        P.tt("pool", hb[t % 2], xhb, C.bc[("sh1", 0)], ALU.add, [xhk, ("sh1", 0)], [("hb", t % 2)])
        P.ld("sp", d_hl[t * 128:(t + 1) * 128, :], hb[t % 2], [("hb", t % 2)], [])
    P.barrier()
    ar.reset()
    wout = ar.get([128, 8, D], BF16)
    cd = ar.get([128, 2, 512], BF16)
    cdn = ar.get([128, 2, 512], BF16)
    wa = ar.get([64, NPASS, KP * 2], BF16)
    fT = ar.get([128, 8, KP * 128], BF16)
    fTc = ar.get([128, 8, 256], BF16)
    XA = ar.get([64, 128, 128], BF16)
    Yg = ar.get([128, 2, KP, 256], BF16)
    MB4 = [ar.get([128, 4, 4, 128], BF16) for _ in range(2)]
    ZT4 = [ar.get([128, 2, 2, 512], BF16) for _ in range(2)]
    xt = [ar.get([128, D], F32) for _ in range(2)]
    xn = [ar.get([128, 512], F32) for _ in range(2)]
    xh_ = [ar.get([128, D], F32) for _ in range(1)]
    hTc = ar.get([128, 8, 256], BF16)
    Hcs = ar.get([128, 2, 512], BF16)
    C.alloc_norm(ar)
    P.ld("pool", wout, d_wout.rearrange("(c p) n -> p c n", p=128), [], ["wout"])
    P.ld("pool", cd, d_cd.rearrange("p (a b) -> p a b", a=2), [], ["cd"])
    P.ld("pool", cdn, d_cdn.rearrange("p (a b) -> p a b", a=2), [], ["cdn"])
    P.ld("pool", wa, d_wa, [], ["wa"])
    hlv = d_hl.rearrange("(t1 t2) c -> t1 t2 c", t2=128)
    mbv = d_mb.rearrange("p (k a q) -> p k a q", k=64, a=4)
    x1v = d_x1.rearrange("(k2 k1) d -> k1 k2 d", k1=64)
    pA = [tpf[:, 0:512], tpf[:, 512:1024]]
    pZ = [tpf[:, 1024:1280], tpf[:, 1536:1792]]
    pF = [pm[0], pm[1]]
    pO2 = [pb6, pb7]

    def out_proj(src, ntile, xsrc, w, outap):
        for tl in range(ntile):
            xb_, xk = xt[tl % 2], ("xt", tl % 2)
            P.ld("sp", xb_, xsrc(tl), [], [xk])
            P.tt("pool", xb_, xb_, bog[w], ALU.add, [xk, ("bog", w)], [xk])
            for half in range(2):
                pp, pk = pO2[half], ("pO2", half)
                tb, tk = xn[half], ("xn", half)
                for c in range(8):
                    P.mm(pp[:], src[:, c, tl * 128:(tl + 1) * 128], wout[:, c, half * 512:(half + 1) * 512], c == 0, c == 7, ["fT", "wout"], [pk])
                P.tt("dve", tb, pp[:], C.bc[("g1", w)][:, half * 512:(half + 1) * 512], ALU.mult, [pk, ("g1", w)], [tk])
                P.tt("pool", xb_[:, half * 512:(half + 1) * 512], xb_[:, half * 512:(half + 1) * 512], tb, ALU.add, [tk, xk], [xk])
            P.ld("sp", outap[tl * 128:(tl + 1) * 128, :], xb_, [xk], [])

    an = 0
    zn = 0
    fn_ = 0
    mn = 0
    for hh in range(NPASS):
        for g in range(4):
            for nch in range(2):
                ch0 = g * 256 + nch * 128
                for q4 in range(4):
                    P.ld("sp", XA[:, q4 * 32:(q4 + 1) * 32, :], hlv[:, q4 * 32:(q4 + 1) * 32, ch0:ch0 + 128], [], ["XA"])
                for cb in range(128 // CB):
                    pa, pak = pA[an % 2], ("pA", an % 2)
                    an += 1
                    for cc in range(CB):
                        ch = cb * CB + cc
                        P.mm(pa[:, cc * KP * 2:(cc + 1) * KP * 2], XA[:, :, ch], wa[:, hh, :], True, True, ["XA", "wa"], [pak])
                    dst = Yg[:, :, :, nch * 128 + cb * CB: nch * 128 + (cb + 1) * CB].rearrange("p r k c -> p c k r")
                    srcv = pa.rearrange("p (c k r) -> p c k r", c=CB, k=KP)
                    if cb % 2 == 0:
                        P.cp("dve", dst, srcv, [pak], ["Yg"])
                    else:
                        P.act(dst, srcv, AF.Copy, [pak], ["Yg"])
            for kb in range(KP // 4):
                mb_, mbk = MB4[mn % 2], ("MB4", mn % 2)
                mn += 1
                k0 = hh * KP + kb * 4
                P.ld("pool", mb_, mbv[:, k0:k0 + 4, :, :], [], [mbk])
                zt, ztk = ZT4[fn_ % 2], ("ZT4", fn_ % 2)
                for q in range(4):
                    kl = kb * 4 + q
                    for nch in range(2):
                        pz, pzk = pZ[zn % 2], ("pZ", zn % 2)
                        zn += 1
                        P.mm(pz, Yg[:, 0, kl, nch * 128:(nch + 1) * 128], mb_[:, q, 0:2, :].rearrange("p a b -> p (a b)"), True, False, ["Yg", mbk], [pzk])
                        P.mm(pz, Yg[:, 1, kl, nch * 128:(nch + 1) * 128], mb_[:, q, 2:4, :].rearrange("p a b -> p (a b)"), False, True, ["Yg", mbk], [pzk])
                        P.cp("dve", zt[:, nch, :, q * 128:(q + 1) * 128], pz.rearrange("p (a b) -> p a b", a=2), [pzk], [ztk])
                for mch in range(2):
                    pf, pfk = pF[mch], ("pF", mch)
                    i4 = 0
                    for nch in range(2):
                        for ri in range(2):
                            P.mm(pf[:], cd[:, nch, ri * 256 + mch * 128: ri * 256 + (mch + 1) * 128], zt[:, nch, ri, :], i4 == 0, i4 == 3, ["cd", ztk], [pfk])
                            i4 += 1
                    P.act(fT[:, 2 * g + mch, kb * 512:(kb + 1) * 512], pf[:], AF.Copy, [pfk], ["fT"])
                fn_ += 1
        out_proj(fT, KP, lambda tl, hh=hh: x1v[hh * KP + tl], 0, d_x2g[hh * KP * 128:(hh + 1) * KP * 128, :])
    for tl in range(2):
        xb_, xk = xt[tl % 2], ("xt", tl % 2)
        xhb, xhk = xh_[0], ("xh", 0)
        P.ld("sp", xb_, d_xc1[tl * 128:(tl + 1) * 128, :], [], [xk])
        C.norm_tile(xb_, xk, xhb, xhk, None)
        for j in range(8):
            P.tr(tp[:, j, tl * 128:(tl + 1) * 128], xhb[:, j * 128:(j + 1) * 128], C.ident[:], [xhk, "ident"], [("tpc", j // 2)])
    for j in range(8):
        P.act(hTc[:, j, :], tp[:, j, :], AF.Identity, [("tpc", j // 2), ("fm", 1)], ["hTc"], bias=C.fm[:, 1, 1, j:j + 1], scale=C.fm[:, 1, 0, j:j + 1])
    for g in range(4):
        for tl in range(2):
            pf, pfk = pF[tl], ("pF", tl)
            for nch in range(2):
                P.mm(pf[:], hTc[:, 2 * g + nch, tl * 128:(tl + 1) * 128], cd[:, nch, :], nch == 0, nch == 1, ["hTc", "cd"], [pfk])
            P.cp("dve", Hcs[:, tl, :], pf[:], [pfk], ["Hcs"])
        for mch in range(2):
            pf, pfk = pF[mch], ("pF", mch)
            i4 = 0
            for tl in range(2):
                for ri in range(2):
                    P.mm(pf[:, 0:256], Hcs[:, tl, ri * 256 + mch * 128: ri * 256 + (mch + 1) * 128], cdn[:, tl, ri * 256:(ri + 1) * 256], i4 == 0, i4 == 3, ["Hcs", "cdn"], [pfk])
                    i4 += 1
            P.act(fTc[:, 2 * g + mch, :], pf[:, 0:256], AF.Copy, [pfk], ["fT"])
    out_proj(fTc, 2, lambda tl: d_xc1[tl * 128:(tl + 1) * 128, :], 1, d_xc2)
    P.barrier()


def emit_attn(nc, P, G, C, ar, d_x2g, d_xc2, d_x3h):
    I = lambda n, s, dt=F32: dram(nc, n, s, dt, "ExternalInput")
    NQT = 34
    NKT = 66
    d_cosg = I("cosg", [128, 8192])
    d_sing = I("sing", [128, 8192])
    d_cosq = I("cosq", [128, NQT * 128])
    d_sinq = I("sinq", [128, NQT * 128])
    d_qidx = I("qidx", [128, NQT], I32)
    d_wqkv = I("w_qkv", [D, 1536])
    d_gq = I("gq_fm", [128, 1])
    d_gk = I("gk_fm", [128, 1])
    d_wo = I("w_o", [D, D])
    d_prot = I("prot", [128, 128])
    SCALE = 128.0 ** -0.5
    tp, pm, pb6, pb7 = G.tp, G.pm, G.pb6, G.pb7
    tpf = tp.rearrange("p a b -> p (a b)")
    ar.off = ar.base
    gq = ar.get([128, 1], F32)
    gk = ar.get([128, 1], F32)
    prot = ar.get([128, 128], F32)
    protb = ar.get([128, 128], BF16)
    qidx = ar.get([128, NQT], I32)
    ar.base = ar.off
    P.ld("sp", gq, d_gq, [], ["gq"])
    P.ld("sp", gk, d_gk, [], ["gk"])
    P.ld("sp", prot, d_prot, [], ["prot"])
    P.ld("sp", qidx, d_qidx, [], ["qidx"])
    P.cp("dve", protb, prot, ["prot"], ["protb"])
    P.barrier()
    ar.reset()
    KT = ar.get([128, 2, NKT * 128], BF16)
    Vx = ar.get([128, NKT, 2, 130], BF16)
    wqkv = ar.get([128, 8, 1536], BF16)
    wo = ar.get([128, 8, D], BF16)
    hT = ar.get([128, 8, 512], BF16)
    xt = [ar.get([128, D], F32) for _ in range(2)]
    xh_ = [ar.get([128, D], F32) for _ in range(2)]
    C.alloc_norm(ar)
    sqb = ar.get([128, 512], BF16)
    t1 = ar.get([128, 512], F32)
    rs = ar.get([128, 512], F32)
    kn = ar.get([128, 512], F32)
    knb = ar.get([128, 512], BF16)
    cs = ar.get([128, 512], F32)
    sn = ar.get([128, 512], F32)
    t2 = ar.get([128, 512], F32)
    QT = [ar.get([128, 512], BF16) for _ in range(2)]
    PT = [ar.get([128, 512], BF16) for _ in range(3)]
    rec = ar.get([128, 4], F32)
    Ob = [ar.get([128, 128], BF16) for _ in range(2)]
    OT = ar.get([128, 8, 512], BF16)
    oacc = [ar.get([128, 4, 130], F32)] * 2
    print('attn arena words', ar.off, 'of', ar.n)
    pb4, pb5 = pm
    pS = [pb4, pb5]
    pO = [tpf[:, i * 512:i * 512 + 130] for i in range(4)]
    pb7b = pb7[:, 0:64].bitcast(BF16)
    P.ld("pool", wqkv, d_wqkv.rearrange("(c p) n -> p c n", p=128), [], ["wqkv"])
    P.ld("pool", wo, d_wo.rearrange("(c p) n -> p c n", p=128), [], ["wo"])
    P.op("pool", lambda e: e.memset(Vx, 1.0), [], ["Vx"])
    cnt = {"x": 0}

    def load_tile(dst, dk, src):
        if src[0] == "rows":
            P.ld("sp", dst, src[1], [], [dk])
        else:
            t = src[1]
            P.dma("pool", lambda e: e.indirect_dma_start(out=dst, out_offset=None, in_=d_x2g,
                                                         in_offset=bass.IndirectOffsetOnAxis(ap=qidx[:, t:t + 1], axis=0)), ["qidx"], [dk])

    def pro_group(srcs, w, c0):
        ntl = len(srcs)
        for tl in range(ntl):
            n = cnt["x"]
            cnt["x"] += 1
            xb_, xk = xt[n % 2], ("xt", n % 2)
            xhb, xhk = xh_[n % 2], ("xh", n % 2)
            load_tile(xb_, xk, srcs[tl])
            C.norm_tile(xb_, xk, xhb, xhk, None)
            for j in range(8):
                P.tr(tp[:, j, tl * 128:(tl + 1) * 128], xhb[:, j * 128:(j + 1) * 128], C.ident[:], [xhk, "ident"], [("bank", j // 2)])
        for j in range(8):
            P.act(hT[:, j, c0:c0 + ntl * 128], tp[:, j, 0:ntl * 128], AF.Identity, [("bank", j // 2), ("fm", w)], ["hT"],
                  bias=C.fm[:, w, 1, j:j + 1], scale=C.fm[:, w, 0, j:j + 1])

    def qk_steps(mm_fn, ps, psk, W, gfm, gk_, rope, out, outk):
        mm_fn()
        yield
        P.act(sqb[:, 0:W], ps[:, 0:W], AF.Square, [psk], ["sqb"])
        yield
        P.mm(pb7[:, 0:W], C.onesb[:], sqb[:, 0:W], True, True, ["sqb", "onesb"], ["pb7"])
        yield
        P.act(t1[:, 0:W], pb7[:, 0:W], AF.Sqrt, ["pb7", "epsc"], ["t1"], bias=G.epsc[:, 0:1], scale=1.0 / 128)
        yield
        P.op("dve", lambda e: e.reciprocal(out=rs[:, 0:W], in_=t1[:, 0:W]), ["t1"], ["rs"])
        yield
        P.stt(kn[:, 0:W], ps[:, 0:W], gfm[:, 0:1], rs[:, 0:W], ALU.mult, ALU.mult, [psk, gk_, "rs"], ["kn"])
        yield
        if rope:
            P.act(knb[:, 0:W], kn[:, 0:W], AF.Copy, ["kn"], ["knb"])
            yield
            P.mm(pb7[:, 0:W], protb, knb[:, 0:W], True, True, ["knb", "protb"], ["pb7"])
            yield
            P.tt("dve", t2[:, 0:W], pb7[:, 0:W], sn[:, 0:W], ALU.mult, ["pb7", "sn"], ["t2"])
            yield
            P.tt("pool", kn[:, 0:W], kn[:, 0:W], cs[:, 0:W], ALU.mult, ["kn", "cs"], ["kn"])
            yield
            P.tt("dve", out, kn[:, 0:W], t2[:, 0:W], ALU.add, ["kn", "t2"], [outk])
        else:
            P.act(out, kn[:, 0:W], AF.Copy, ["kn"], [outk])
        yield

    def run_all(gen):
        if gen is not None:
            for _ in gen:
                pass

    def step(gen):
        if gen is None:
            return None
        try:
            next(gen)
            return gen
        except StopIteration:
            return None

    rows = lambda ap, t: ("rows", ap[t * 128:(t + 1) * 128, :])
    groups = [(d_xc2, 0, 2, 1, 0, False, None)]
    for g in range(16):
        groups.append((d_x2g, g * 4, 4, 0, 2 + g * 4, True, g))
    for (xap, t0, ntl, w, kt0, rope, g) in groups:
        W = ntl * 128
        for r in range(0, ntl, 2):
            pro_group([rows(xap, t0 + r), rows(xap, t0 + r + 1)], w, r * 128)
        if rope:
            P.ld("sp", cs[:, 0:W], d_cosg[:, g * 512:g * 512 + W], [], ["cs"])
            P.ld("sp", sn[:, 0:W], d_sing[:, g * 512:g * 512 + W], [], ["sn"])
        for j in range(2):
            def kmm(j=j, W=W):
                for c in range(8):
                    P.mm(pb6[:, 0:W], wqkv[:, c, 1024 + j * 128:1024 + (j + 1) * 128], hT[:, c, 0:W], c == 0, c == 7, ["wqkv", "hT"], ["pb6"])
            run_all(qk_steps(kmm, pb6, "pb6", W, gk, "gk", rope, KT[:, j, kt0 * 128:kt0 * 128 + W], "KT"))
        for tl in range(ntl):
            pv = pS[tl % 2]
            pvk = ("pS", tl % 2)
            for c in range(8):
                P.mm(pv[:, 0:256], hT[:, c, tl * 128:(tl + 1) * 128], wqkv[:, c, 1280:1536], c == 0, c == 7, ["wqkv", "hT"], [pvk])
            P.cp("dve", Vx[:, kt0 + tl, :, 0:128], pv[:, 0:256].rearrange("p (a b) -> p a b", a=2), [pvk], ["Vx"])

    pn = 0
    qchunks = [(i * 4, 4) for i in range(8)] + [(32, 2)]
    for (qt0, nqt) in qchunks:
        W = nqt * 128
        for r in range(0, nqt, 2):
            pro_group([("idx", qt0 + r), ("idx", qt0 + r + 1)], 0, r * 128)
        P.ld("sp", cs[:, 0:W], d_cosq[:, qt0 * 128:qt0 * 128 + W], [], ["cs"])
        P.ld("sp", sn[:, 0:W], d_sinq[:, qt0 * 128:qt0 * 128 + W], [], ["sn"])
        def qgen(h, W=W):
            def qmm():
                for c in range(8):
                    P.mm(pb6[:, 0:W], wqkv[:, c, h * 128:(h + 1) * 128], hT[:, c, 0:W], c == 0, c == 7, ["wqkv", "hT"], ["pb6"])
            return qk_steps(qmm, pb6, "pb6", W, gq, "gq", True, QT[h % 2][:, 0:W], ("QT", h % 2))

        def fin_steps(h, nqt=nqt):
            oa, oak = oacc[0], "oacc"
            for qs in range(nqt):
                P.op("dve", lambda e, qs=qs: e.reciprocal(out=rec[:, qs:qs + 1], in_=oa[:, qs, 128:129]), [oak], ["rec"])
                ob, obk = Ob[qs % 2], ("Ob", qs % 2)
                P.ts("dve", ob, oa[:, qs, 0:128], rec[:, qs:qs + 1], None, ALU.mult, None, [oak, "rec"], [obk])
                yield
                P.tr(pb7b, ob, C.identb[:], [obk, "identb"], ["pb7"])
                yield
                P.act(OT[:, h, qs * 128:(qs + 1) * 128], pb7b, AF.Copy, ["pb7"], ["OT"])
                yield

        run_all(qgen(0))
        deferred = None
        for h in range(8):
            j = h // 4
            qt, qk_ = QT[h % 2], ("QT", h % 2)
            gen = qgen(h + 1) if h + 1 < 8 else None

            def S(kt):
                P.mm(pS[kt % 2][:, 0:W], KT[:, j, kt * 128:(kt + 1) * 128], qt[:, 0:W], True, True, ["KT", qk_], [("pS", kt % 2)])
            S(0)
            for kt in range(NKT):
                if kt + 1 < NKT:
                    S(kt + 1)
                pt, ptk = PT[pn % 3], ("PT", pn % 3)
                pn += 1
                P.act(pt[:, 0:W], pS[kt % 2][:, 0:W], AF.Exp, [("pS", kt % 2)], [ptk], scale=SCALE)
                for qs in range(nqt):
                    P.mm(pO[qs], pt[:, qs * 128:(qs + 1) * 128], Vx[:, kt, j, :], kt == 0, kt == NKT - 1, [ptk, "Vx"], [("bank", qs)])
                if kt % 2 == 1:
                    if deferred is not None:
                        deferred = step(deferred)
                    else:
                        gen = step(gen)
            run_all(deferred)
            run_all(gen)
            oa, oak = oacc[0], "oacc"
            for qs in range(nqt):
                P.cp("dve", oa[:, qs, :], pO[qs], [("bank", qs)], [oak])
            deferred = fin_steps(h)
        run_all(deferred)
        for qs in range(nqt):
            t = qt0 + qs
            n = cnt["x"]
            cnt["x"] += 1
            xb_, xk = xt[n % 2], ("xt", n % 2)
            load_tile(xb_, xk, ("idx", t))
            xnb, xnk = xh_[n % 2], ("xh", n % 2)
            for half in range(2):
                pp, pk = pS[half], ("pS", half)
                for h in range(8):
                    P.mm(pp[:], OT[:, h, qs * 128:(qs + 1) * 128], wo[:, h, half * 512:(half + 1) * 512], h == 0, h == 7, ["OT", "wo"], [pk])
                P.tt("dve", xnb[:, half * 512:(half + 1) * 512], pp[:], C.bc[("g1", 0)][:, half * 512:(half + 1) * 512], ALU.mult, [pk, ("g1", 0)], [xnk])
                P.tt("pool", xnb[:, half * 512:(half + 1) * 512], xnb[:, half * 512:(half + 1) * 512], xb_[:, half * 512:(half + 1) * 512], ALU.add, [xnk, xk], [xnk])
            P.ld("sp", d_x3h[t * 128:(t + 1) * 128, :], xnb, [xnk], [])
    P.barrier()


def build_fused():
    nc = bass.Bass("TRN2", target_bir_lowering=False)
    I = lambda n, s, dt=F32: dram(nc, n, s, dt, "ExternalInput")
    N = lambda n, s, dt=F32: dram(nc, n, s, dt, "Internal")
    NBM = (2 * 66 * 128 + BLK - 1) // BLK + NE
    d_xpad = I("xpad", [66 * 128, D])
    d_xc = I("xc", [256, D])
    d_hmask = I("hmask", [128, 2])
    d_xo = dram(nc, "xo", [32 * 128, D], F32, "ExternalOutput")
    d_x1 = N("x1", [8192, D])
    d_xc1 = N("xc1", [256, D])
    d_x2g = N("x2g", [8192, D])
    d_xc2 = N("xc2", [256, D])
    d_x3h = N("x3h", [34 * 128, D])
    with ExitStack() as es:
        P = Prog(nc, es)
        G = Ctx()
        setup_consts(nc, P, G)
        G.d_iotab = I("iota_b", [128, NBM])
        G.d_iotap = I("iota_p", [128, 1])
        G.d_Xs = N("Xs", [NBM * BLK, D], BF16)
        G.d_Ys = N("Ys", [NBM * BLK, D])
        G.d_hl = N("hl", [8192, D], BF16)
        G.d_H2s = N("H2s", [66 * 128, D], BF16)
        ar = Arena(P, "arena", 182)
        ar.base = 0
        tl = lambda ap, a, b, w=0: [(w, ap, t) for t in range(a, b)]
        C = Layer(nc, P, G, ar, 0, 2, True, False)
        mi = moe_inputs(nc, 0)
        wins = [dict(x=d_xpad[0:34 * 128, :], out=d_x1[0:4096, :], zlo=True, zhi=False),
                dict(x=d_xpad[32 * 128:66 * 128, :], out=d_x1[4096:8192, :], zlo=False, zhi=True)]
        emit_conv(nc, P, G, C, ar, 0, wins, dict(x=d_xc, out=d_xc1))
        run_moe(nc, P, G, C, ar, tl(d_x1, 0, 64) + tl(d_xc1, 0, 2, 1), mi)
        C = Layer(nc, P, G, ar, 1, 2, True, True)
        mi = moe_inputs(nc, 1)
        emit_fft(nc, P, G, C, ar, 1, d_x1, d_xc1, d_x2g, d_xc2)
        run_moe(nc, P, G, C, ar, tl(d_x2g, 0, 64) + tl(d_xc2, 0, 2, 1), mi)
        C = Layer(nc, P, G, ar, 2, 2, False, False)
        mi = moe_inputs(nc, 2)
        emit_attn(nc, P, G, C, ar, d_x2g, d_xc2, d_x3h)
        run_moe(nc, P, G, C, ar, tl(d_x3h, 0, 34), mi)
        C = Layer(nc, P, G, ar, 3, 1, False, False)
        mi = moe_inputs(nc, 3)
        emit_conv(nc, P, G, C, ar, 3, [dict(x=d_x3h, out=d_xo, zlo=False, zhi=False, mask=d_hmask)], None)
        run_moe(nc, P, G, C, ar, tl(d_xo, 0, 32), mi)
        P.barrier()
        P.finish()
        print("fused program instructions:", P.ninst, "sems:", P.nsem)
    return nc


_CACHE = {}


def _fm(v, n):
    return np.ascontiguousarray(v.reshape(n, 128).T)


def _rope_tables():
    rows = 8192 // 64
    row = np.repeat(np.arange(rows, dtype=np.float32), 64)
    col = np.tile(np.arange(64, dtype=np.float32), rows)
    inv = (np.float32(10000.0) ** (-np.arange(32, dtype=np.float32) / np.float32(32))).astype(np.float32)
    ang = np.concatenate([row[:, None] * inv, col[:, None] * inv], axis=-1).astype(np.float32)
    c, s_ = np.cos(ang).astype(np.float32), np.sin(ang).astype(np.float32)
    cosf = np.repeat(c, 2, axis=1).T
    sgn = np.tile(np.array([-1.0, 1.0], np.float32), 64)
    sinf = (np.repeat(s_, 2, axis=1) * sgn[None, :]).T
    return np.ascontiguousarray(cosf), np.ascontiguousarray(sinf)


def _fft_tables():
    k1 = np.arange(64, dtype=np.float64)
    t1 = np.arange(64, dtype=np.float64)
    a = 2 * np.pi * np.outer(t1, k1) / 64.0
    wa = np.stack([np.cos(a), -np.sin(a)], -1) / 8.0
    wa = wa.reshape(64, 8, 8 * 2)
    t2 = np.arange(128, dtype=np.float64)
    k2 = np.arange(128, dtype=np.float64)
    kk = k1[:, None] + 64.0 * k2[None, :]
    th = 2 * np.pi * t2[:, None, None] * kk[None, :, :] / 8192.0
    mr, mi = np.cos(th), -np.sin(th)
    mb = np.stack([mr, mi, -mi, mr], 2) / np.sqrt(128.0)
    n = np.arange(256, dtype=np.float64)
    ph = 2 * np.pi * np.outer(n, n) / 256.0
    cs = np.concatenate([np.cos(ph), np.sin(ph)], 1) / 16.0
    csn = np.concatenate([np.cos(ph), -np.sin(ph)], 1) / 16.0
    cd = cs.reshape(2, 128, 512).transpose(1, 0, 2).reshape(128, 1024)
    cdn = csn.reshape(2, 128, 512).transpose(1, 0, 2).reshape(128, 1024)
    f = lambda a_: np.ascontiguousarray(a_.astype(np.float32))
    return f(wa), f(mb.reshape(128, -1)), f(cd), f(cdn)


def kernel(**inp):
    inp = {k: np.asarray(v) for k, v in inp.items()}
    if "nc" not in _CACHE:
        _CACHE["nc"] = build_fused()
    nc = _CACHE["nc"]
    NBM = (2 * 66 * 128 + BLK - 1) // BLK + NE
    sh = dict(ident=np.eye(128, dtype=np.float32), ltri=np.triu(np.ones((128, 128), np.float32), 1),
              iota_b=np.tile((np.arange(NBM, dtype=np.float32) * BLK)[None, :], (128, 1)),
              iota_p=np.arange(128, dtype=np.float32).reshape(128, 1))
    for L in range(4):
        sh[f"w_mod{L}"] = inp["w_mod"][L]
        sh[f"b_mod{L}"] = inp["b_mod"][L][None, :]
        sh[f"ng{L}"] = inp["norm_g"][L]
        sh[f"wr{L}"] = np.ascontiguousarray(np.concatenate([inp["moe_w_group"][L], inp["moe_w_expert"][L]], axis=1))
        sh[f"br{L}"] = np.concatenate([inp["moe_b_group"][L], inp["moe_b_expert"][L]])[None, :]
        sh[f"wg{L}"] = inp["moe_w_gate"][L].reshape(NE * 128, 4096)
        sh[f"wu{L}"] = inp["moe_w_up"][L].reshape(NE * 128, 4096)
        sh[f"wd{L}"] = inp["moe_w_down"][L].reshape(NE * 128, 4096)
    for L, j in ((0, 0), (3, 1)):
        sh[f"w_in{L}"] = inp["conv_w_in"][j]
        sh[f"b_in_fm{L}"] = _fm(inp["conv_b_in"][j], 16)
        sh[f"w_dw_fm{L}"] = np.ascontiguousarray(inp["conv_w_dw"][j].T.reshape(8, 128, 31).transpose(1, 0, 2).reshape(128, 8 * 31))
        sh[f"b_dw_fm{L}"] = _fm(inp["conv_b_dw"][j], 8)
        sh[f"gn_fm{L}"] = _fm(inp["conv_norm_g"][j], 8)
        sh[f"w_out{L}"] = inp["conv_w_out"][j]
        sh[f"b_out{L}"] = inp["conv_b_out"][j][None, :]
    sh["w_out1"] = inp["fnet_w_out"][0]
    sh["b_out1"] = inp["fnet_b_out"][0][None, :]
    sh["wa"], sh["mb"], sh["cd"], sh["cdn"] = _fft_tables()
    cosf, sinf = _rope_tables()
    gtok = (np.arange(64)[:, None] + 64 * np.arange(128)[None, :]).reshape(-1)
    sh["cosg"] = np.ascontiguousarray(cosf[:, gtok])
    sh["sing"] = np.ascontiguousarray(sinf[:, gtok])
    prot = np.zeros((128, 128), np.float32)
    for i in range(64):
        prot[2 * i, 2 * i + 1] = 1
        prot[2 * i + 1, 2 * i] = 1
    sh.update(w_qkv=inp["attn_w_qkv"][0], gq_fm=inp["attn_q_norm_g"][0].reshape(128, 1), gk_fm=inp["attn_k_norm_g"][0].reshape(128, 1),
              w_o=inp["attn_w_out"][0], prot=prot)
    zpad = np.zeros((128, D), np.float32)
    in_maps = []
    for core in range(8):
        b, s = core // 2, core % 2
        m = dict(sh)
        m["xpad"] = np.ascontiguousarray(np.concatenate([zpad, inp["x"][b], zpad], axis=0))
        m["xc"] = np.ascontiguousarray(inp["ctx"][b])
        m["cvec"] = np.stack([inp["c"][b], inp["c_ctx"]])
        tok = np.clip(4096 * s - 128 + np.arange(34 * 128), 0, 8191)
        m["cosq"] = np.ascontiguousarray(cosf[:, tok])
        m["sinq"] = np.ascontiguousarray(sinf[:, tok])
        grow = (tok % 64) * 128 + tok // 64
        m["qidx"] = np.ascontiguousarray(grow.reshape(34, 128).T.astype(np.int32))
        hm = np.ones((128, 2), np.float32)
        if s == 0:
            hm[:, 0] = 0
        else:
            hm[:, 1] = 0
        m["hmask"] = hm
        in_maps.append(m)
    res = run_bass_kernel_spmd(nc, in_maps, core_ids=list(range(8)))
    out = np.empty_like(inp["x"])
    for core in range(8):
        b, s = core // 2, core % 2
        out[b, s * 4096:(s + 1) * 4096] = res.results[core]["xo"]
    return out
```

```python
import numpy as np
import concourse.bass as bass
import concourse.mybir as mybir
from concourse.bass_utils import run_bass_kernel_spmd
from contextlib import ExitStack

F32 = mybir.dt.float32
BF16 = mybir.dt.bfloat16
I32 = mybir.dt.int32
AF = mybir.ActivationFunctionType
ALU = mybir.AluOpType
AX = mybir.AxisListType

ENG = ["pe", "act", "dve", "pool", "sp"]
D = 1024
EPS = 1e-6
NE = 32
BLK = 384
SUB = BLK // 128


class Prog:
    SEM_LIMIT = 20000
    ND = 8

    def __init__(self, nc, es):
        self.nc, self.es = nc, es
        self.q = {e: [] for e in ENG}
        self.cur = {}
        self.waited = {e: {} for e in ENG}
        self.last_w = {}
        self.readers = {}
        self.nsem = 0
        self.dsem = {e: [] for e in ENG}
        self.dn = {e: 0 for e in ENG}
        self.sems = {}
        self.ninst = 0
        self.nbar = 0
        self.full = {}

    def new_sem(self):
        self.nsem += 1
        self.sems[self.nsem] = self.es.enter_context(self.nc.semaphore(f"s{self.nsem}"))
        return self.nsem

    def sb(self, name, shape, dt):
        return self.es.enter_context(self.nc.sbuf_tensor("sb_" + name, list(shape), dt))

    def ps(self, name, shape, dt):
        return self.es.enter_context(self.nc.psum_tensor("ps_" + name, list(shape), dt))

    def _deps(self, eng, reads, writes, skip_same):
        deps = {}

        def add(ev):
            if ev is None:
                return
            s, v, e = ev
            if skip_same and e == eng:
                return
            if deps.get(s, 0) < v:
                deps[s] = v
        for k in reads:
            add(self.last_w.get(k))
        for k in writes:
            add(self.last_w.get(k))
            for ev in self.readers.get(k, ()):
                add(ev)
        out = []
        for s, v in deps.items():
            if self.waited[eng].get(s, 0) < v:
                self.waited[eng][s] = v
                out.append((s, v))
        return out

    def _commit(self, ev, reads, writes):
        for k in writes:
            self.last_w[k] = ev
            self.readers[k] = []
        for k in reads:
            self.readers.setdefault(k, []).append(ev)

    def op(self, eng, fn, reads=(), writes=(), skip_same=False):
        waits = self._deps(eng, reads, writes, skip_same)
        c = self.cur.get(eng)
        if c is None or c[1] >= self.SEM_LIMIT:
            if c is not None:
                self.full[eng] = (c[0], c[1])
            c = [self.new_sem(), 0]
            self.cur[eng] = c
        c[1] += 1
        ev = (c[0], c[1], eng)
        self.q[eng].append((waits, fn, c[0], 1))
        self._commit(ev, reads, writes)
        self.ninst += 1
        return ev

    def dma(self, eng, fn, reads=(), writes=()):
        waits = self._deps(eng, reads, writes, False)
        pool = self.dsem[eng]
        i = self.dn[eng] % self.ND
        self.dn[eng] += 1
        if len(pool) <= i:
            pool.append([self.new_sem(), 0])
        d = pool[i]
        if d[1] > 0 and self.waited[eng].get(d[0], 0) < 16 * d[1]:
            self.waited[eng][d[0]] = 16 * d[1]
            waits.append((d[0], 16 * d[1]))
        d[1] += 1
        ev = (d[0], 16 * d[1], "dma_" + eng)
        self.q[eng].append((waits, fn, d[0], 16))
        self._commit(ev, reads, writes)
        self.ninst += 1
        return ev

    def wait_keys(self, eng, keys):
        waits = self._deps(eng, keys, (), False)
        self.q[eng].append((waits, None, None, 0))

    def barrier(self):
        evs = []
        for e in ENG:
            c = self.cur.get(e)
            if c is not None:
                evs.append((c[0], c[1]))
            if e in self.full:
                evs.append(self.full[e])
            for d in self.dsem[e]:
                if d[1] > 0:
                    evs.append((d[0], 16 * d[1]))
        for e in ENG:
            waits = []
            for s, v in evs:
                if self.waited[e].get(s, 0) < v:
                    self.waited[e][s] = v
                    waits.append((s, v))
            self.q[e].append((waits, None, None, 0))
        self.last_w = {}
        self.readers = {}

    def finish(self):
        nc = self.nc
        engobj = {"pe": "tensor", "act": "scalar", "dve": "vector", "pool": "gpsimd", "sp": "sync"}
        with nc.Block() as block:
            for e in ENG:
                if not self.q[e]:
                    continue

                def body(engine, e=e):
                    for waits, fn, s, inc in self.q[e]:
                        for (ws, wv) in waits:
                            engine.wait_ge(self.sems[ws], wv)
                        if fn is not None:
                            fn(engine).then_inc(self.sems[s], inc)
                getattr(block, engobj[e])(body)

    def act(self, out, in_, func, r, w, bias=None, scale=None, accum=None):
        kw = {}
        if bias is not None:
            kw["bias"] = bias
        if scale is not None:
            kw["scale"] = scale
        if accum is not None:
            kw["accum_out"] = accum
        return self.op("act", lambda e: e.activation(out=out, in_=in_, func=func, **kw), r, w)

    def tt(self, eng, out, in0, in1, op, r, w):
        return self.op(eng, lambda e: e.tensor_tensor(out=out, in0=in0, in1=in1, op=op), r, w)

    def ts(self, eng, out, in0, s1, s2, op0, op1, r, w):
        if op1 is None:
            return self.op(eng, lambda e: e.tensor_scalar(out=out, in0=in0, scalar1=s1, scalar2=None, op0=op0), r, w)
        return self.op(eng, lambda e: e.tensor_scalar(out=out, in0=in0, scalar1=s1, scalar2=s2, op0=op0, op1=op1), r, w)

    def stt(self, out, in0, scalar, in1, op0, op1, r, w):
        return self.op("dve", lambda e: e.scalar_tensor_tensor(out=out, in0=in0, scalar=scalar, in1=in1, op0=op0, op1=op1), r, w)

    def cp(self, eng, out, in_, r, w):
        return self.op(eng, lambda e: e.tensor_copy(out=out, in_=in_), r, w)

    def red(self, out, in_, op, r, w):
        return self.op("dve", lambda e: e.tensor_reduce(out=out, in_=in_, axis=AX.X, op=op), r, w)

    def mm(self, out, lhsT, rhs, start, stop, r, w):
        return self.op("pe", lambda e: e.matmul(out, lhsT=lhsT, rhs=rhs, start=start, stop=stop), r, w, skip_same=True)

    def tr(self, out, in_, ident, r, w):
        return self.op("pe", lambda e: e.transpose(out=out, in_=in_, identity=ident), r, w, skip_same=True)

    def ld(self, q, out, in_, r, w, slow=False):
        if slow:
            return self.dma(q, lambda e: e.dma_start(out=out, in_=in_, allow_slow_non_contiguous=True), r, w)
        return self.dma(q, lambda e: e.dma_start(out=out, in_=in_), r, w)


class Arena:
    def __init__(self, P, name, kbytes):
        self.t = P.sb(name, [128, kbytes * 256], F32)
        self.n = kbytes * 256
        self.off = 0
        self.base = 0

    def reset(self):
        self.off = self.base

    def get(self, shape, dt):
        n = int(np.prod(shape[1:]))
        words = n if dt in (F32, I32) else (n + 1) // 2
        assert self.off + words <= self.n, ("arena overflow", self.off + words, self.n)
        v = self.t[0:shape[0], self.off:self.off + words]
        self.off += words
        if dt != F32:
            v = v.bitcast(dt)
            if dt == BF16 and n % 2:
                v = v[:, 0:n]
        if len(shape) == 3:
            v = v.rearrange("p (a b) -> p a b", a=shape[1])
        elif len(shape) == 4:
            v = v.rearrange("p (a b c) -> p a b c", a=shape[1], b=shape[2])
        return v


def dram(nc, name, shape, dt, kind):
    return nc.dram_tensor(name, list(shape), dt, kind=kind).ap()


def moe_phase(nc, P, C, ar, tiles, NT, NB, dd, psum):
    ar.reset()
    assert len(tiles) == NT
    tp, pab, pmm = psum["big"], psum["a"], psum["b"]
    wr = ar.get([128, 8, 36], F32)
    brb = ar.get([128, 36], F32)
    lg = ar.get([128, NT, 36], F32)
    iotab = ar.get([128, NB], F32)
    iotap = ar.get([128, 1], F32)
    P.ld("sp", wr, dd["wr"].rearrange("(c p) n -> p c n", p=128), [], ["wr"], slow=True)
    P.ld("sp", brb, dd["br"].partition_broadcast(128).rearrange("p o n -> p (o n)"), [], ["brb"])
    P.ld("sp", iotab, dd["iotab"][:, 0:NB], [], ["iotab"])
    P.ld("sp", iotap, dd["iotap"], [], ["iotap"])
    g = lambda shape, dt=F32: ar.get(shape, dt)
    ga1 = g([128, NT]); ga2 = g([128, NT]); d1 = g([128, NT], I32); d2 = g([128, NT], I32); widx = g([128, NB], I32)
    h2b = [ar.get([128, D], BF16) for _ in range(2)]
    mark = ar.off
    xt = [ar.get([128, D], F32) for _ in range(4)]
    h2 = [ar.get([128, D], F32) for _ in range(2)]
    h2T = [ar.get([128, 8, 128], F32) for _ in range(2)]
    junk = ar.get([128, D], BF16)
    ssq = ar.get([128, 4], F32)
    rsq = ar.get([128, 4], F32)
    gmax = g([128, NT]); gmask = g([128, NT, 4]); gex = g([128, NT, 4]); gsum = g([128, NT]); gtop = g([128, NT])
    pen = g([128, NT, 4]); em = g([128, NT, 32]); m1 = g([128, NT]); oh1 = g([128, NT, 32]); em2 = g([128, NT, 32])
    m2 = g([128, NT]); oh2 = g([128, NT, 32]); dd_ = g([128, NT])
    S = em; pre = g([128, NT, 32]); tot = g([128, NT, 32]); base = g([128, NT, 32]); tmp32 = em2
    cnt = g([128, 32]); padded = g([128, 32]); pst = [g([128, 32]) for _ in range(2)]; pend = g([128, 32]); pstart = g([128, 32])
    d1f = g([128, NT]); d2f = g([128, NT])
    cmpb = g([128, NB, 32]); bef = g([128, NB]); wif = g([128, NB])
    tpf = tp.rearrange("p a b -> p (a b)")
    trp = [tpf[:, 0:1024].rearrange("p (j q) -> p j q", j=8), tpf[:, 1024:2048].rearrange("p (j q) -> p j q", j=8)]
    lgp = [pab[0].rearrange("p a b -> p (a b)"), pab[1].rearrange("p a b -> p (a b)")]

    def A0(i):
        w, xd, t = tiles[i]
        P.ld("sp", xt[i % 4], xd[t * 128:(t + 1) * 128, :], [], [("xt", i % 4)])

    def A1(i):
        xk = ("xt", i % 4)
        k = i % 4
        P.act(junk, xt[k], AF.Square, [xk], ["junk", ("ssq", k)], accum=ssq[:, k:k + 1])
        P.act(rsq[:, k:k + 1], ssq[:, k:k + 1], AF.Sqrt, [("ssq", k), "epsc"], [("rsq", k)], bias=C.G.epsc[:, 0:1], scale=1.0 / D)
        P.op("dve", lambda e: e.reciprocal(out=rsq[:, k:k + 1], in_=rsq[:, k:k + 1]), [("rsq", k)], [("rsq", k)])

    def B1(i):
        w, xd, t = tiles[i]
        k = i % 4
        hb, hk = h2[i % 2], ("h2", i % 2)
        P.stt(hb, xt[k], rsq[:, k:k + 1], C.bc[("A2", w)], ALU.mult, ALU.mult, [("xt", k), ("rsq", k), ("A2", w)], [hk])
        P.tt("pool", hb, hb, C.bc[("sh2", w)], ALU.add, [hk, ("sh2", w)], [hk])
        bb, bk = h2b[i % 2], ("h2b", i % 2)
        P.act(bb, hb, AF.Copy, [hk], [bk])
        P.ld("sp", dd["H2s"][i * 128:(i + 1) * 128, :], bb, [bk], [])

    def C1(i):
        hb, hk = h2[i % 2], ("h2", i % 2)
        tb, tk = trp[i % 2], ("trp", i % 2)
        for j in range(8):
            P.tr(tb[:, j, :], hb[:, j * 128:(j + 1) * 128], C.ident[:], [hk, "ident"], [tk])
        P.cp("dve", h2T[i % 2], tb, [tk], [("h2T", i % 2)])

    def D1(i):
        hT, hTk = h2T[i % 2], ("h2T", i % 2)
        lp, lk = lgp[i % 2], ("lgp", i % 2)
        for j in range(8):
            P.mm(lp[:, 0:36], hT[:, j, :], wr[:, j, :], j == 0, j == 7, [hTk, "wr"], [lk])
        P.tt("dve", lg[:, i, :], lp[:, 0:36], brb, ALU.add, [lk, "brb"], ["lg"])

    A0(0)
    if NT > 1:
        A0(1)
    for s_ in range(NT + 3):
        if s_ + 2 < NT:
            A0(s_ + 2)
        if s_ < NT:
            A1(s_)
        if 0 <= s_ - 1 < NT:
            B1(s_ - 1)
        if 0 <= s_ - 2 < NT:
            C1(s_ - 2)
        if 0 <= s_ - 3 < NT:
            D1(s_ - 3)

    glv, elv = lg[:, :, 0:4], lg[:, :, 4:36]
    bc3 = lambda a, n: a.unsqueeze(2).to_broadcast([128, NT, n])
    P.red(gmax, glv, ALU.max, ["lg"], ["gmax"])
    P.tt("dve", gmask, glv, bc3(gmax, 4), ALU.is_equal, ["lg", "gmax"], ["gmask"])
    P.tt("dve", gex, glv, bc3(gmax, 4), ALU.subtract, ["lg", "gmax"], ["gex"])
    P.act(gex, gex, AF.Exp, ["gex"], ["gex"])
    P.red(gsum, gex, ALU.add, ["gex"], ["gsum"])
    P.op("dve", lambda e: e.reciprocal(out=gtop, in_=gsum), ["gsum"], ["gtop"])
    P.ts("dve", pen, gmask, 1.0, 1e30, ALU.subtract, ALU.mult, ["gmask"], ["pen"])
    P.tt("dve", em.rearrange("p t (a b) -> p t a b", a=4), elv.rearrange("p t (a b) -> p t a b", a=4),
         pen.unsqueeze(3).to_broadcast([128, NT, 4, 8]), ALU.add, ["lg", "pen"], ["em"])
    P.red(m1, em, ALU.max, ["em"], ["m1"])
    P.tt("dve", oh1, em, bc3(m1, 32), ALU.is_equal, ["em", "m1"], ["oh1"])
    P.ts("dve", em2, oh1, -1e30, None, ALU.mult, None, ["oh1"], ["em2"])
    P.tt("dve", em2, em2, em, ALU.add, ["em2", "em"], ["em2"])
    P.red(m2, em2, ALU.max, ["em2"], ["m2"])
    P.tt("dve", oh2, em2, bc3(m2, 32), ALU.is_equal, ["em2", "m2"], ["oh2"])
    P.tt("dve", dd_, m2, m1, ALU.subtract, ["m1", "m2"], ["dd"])
    P.act(dd_, dd_, AF.Exp, ["dd"], ["dd"])
    P.ts("dve", dd_, dd_, 1.0, None, ALU.add, None, ["dd"], ["dd"])
    P.op("dve", lambda e: e.reciprocal(out=dd_, in_=dd_), ["dd"], ["dd"])
    P.tt("dve", ga1, gtop, dd_, ALU.mult, ["gtop", "dd"], ["ga1"])
    P.tt("dve", ga2, gtop, ga1, ALU.subtract, ["gtop", "ga1"], ["ga2"])
    P.tt("dve", S, oh1, oh2, ALU.add, ["oh1", "oh2"], ["S", "em"])
    Sf = S.rearrange("p t e -> p (t e)")
    pref = pre.rearrange("p t e -> p (t e)")
    totf = tot.rearrange("p t e -> p (t e)")
    NW = NT * 32
    c0 = 0
    k = 0
    while c0 < NW:
        wdt = min(512, NW - c0)
        for (lh, dst, dk) in ((C.ltri, pref, "pre"), (C.onesf, totf, "tot")):
            pp, pk = pmm[k % 2], ("pmm", k % 2)
            k += 1
            P.mm(pp[:, 0:wdt], lh[:], Sf[:, c0:c0 + wdt], True, True, ["S", "ltri", "onesf"], [pk])
            P.cp("dve", dst[:, c0:c0 + wdt], pp[:, 0:wdt], [pk], [dk])
        c0 += wdt
    P.op("dve", lambda e: e.memset(base[:, 0, :], 0.0), [], ["base"])
    for i in range(1, NT):
        P.tt("dve", base[:, i, :], base[:, i - 1, :], tot[:, i - 1, :], ALU.add, ["base", "tot"], ["base"])
    P.tt("dve", cnt, base[:, NT - 1, :], tot[:, NT - 1, :], ALU.add, ["base", "tot"], ["cnt"])
    P.tt("dve", pre, pre, base, ALU.add, ["pre", "base"], ["pre"])
    cmp2 = cmpb.rearrange("p b e -> p (b e)").rearrange("p (e b) -> p e b", e=32)
    P.tt("dve", cmp2, cnt.unsqueeze(2).to_broadcast([128, 32, NB]), iotab.unsqueeze(1).to_broadcast([128, 32, NB]), ALU.is_gt, ["cnt", "iotab"], ["cmpb"])
    P.red(padded, cmp2, ALU.add, ["cmpb"], ["padded"])
    P.ts("dve", padded, padded, float(BLK), None, ALU.mult, None, ["padded"], ["padded"])
    P.cp("dve", pst[0], padded, ["padded"], [("pst", 0)])
    cur = 0
    sh = 1
    while sh < 32:
        a, b = pst[cur], pst[1 - cur]
        P.cp("dve", b[:, 0:sh], a[:, 0:sh], [("pst", cur)], [("pst", 1 - cur)])
        P.tt("dve", b[:, sh:32], a[:, sh:32], a[:, 0:32 - sh], ALU.add, [("pst", cur)], [("pst", 1 - cur)])
        cur = 1 - cur
        sh *= 2
    P.cp("dve", pend, pst[cur], [("pst", cur)], ["pend"])
    P.tt("dve", pstart, pend, padded, ALU.subtract, ["pend", "padded"], ["pstart"])
    P.tt("dve", pre, pre, pstart.unsqueeze(1).to_broadcast([128, NT, 32]), ALU.add, ["pre", "pstart"], ["pre"])
    for (oh, ohk, df, dfk, di_, dik) in ((oh1, "oh1", d1f, "d1f", d1, "d1"), (oh2, "oh2", d2f, "d2f", d2, "d2")):
        P.tt("dve", tmp32, pre, oh, ALU.mult, ["pre", ohk], ["tmp32", "em2"])
        P.red(df, tmp32, ALU.add, ["tmp32"], [dfk])
        P.cp("dve", di_, df, [dfk], [dik])
    P.tt("dve", cmpb, pend.unsqueeze(1).to_broadcast([128, NB, 32]), iotab.unsqueeze(2).to_broadcast([128, NB, 32]), ALU.is_le, ["pend", "iotab"], ["cmpb"])
    P.red(bef, cmpb, ALU.add, ["cmpb"], ["bef"])
    P.ts("dve", bef, bef, float(NE - 1), None, ALU.min, None, ["bef"], ["bef"])
    P.ts("dve", wif, bef, 128.0, iotap[:, 0:1], ALU.mult, ALU.add, ["bef", "iotap"], ["wif"])
    P.cp("dve", widx, wif, ["wif"], ["widx"])

    P.barrier()
    ar.off = mark
    hb4 = [ar.get([128, D], BF16) for _ in range(4)]
    for i in range(NT):
        bb, bk = hb4[i % 4], ("hb4", i % 4)
        P.ld("sp", bb, dd["H2s"][i * 128:(i + 1) * 128, :], [], [bk])
        for (di_, dik) in ((d1, "d1"), (d2, "d2")):
            P.dma("pool", lambda e, bb=bb, di_=di_, i=i: e.indirect_dma_start(
                out=dd["Xs"], out_offset=bass.IndirectOffsetOnAxis(ap=di_[:, i:i + 1], axis=0), in_=bb, in_offset=None),
                [bk, dik], [])
    P.barrier()

    ar.off = mark
    NWB = 3
    wgb = [ar.get([128, 8, 512], BF16) for _ in range(NWB)]
    wub = [ar.get([128, 8, 512], BF16) for _ in range(NWB)]
    wdb = [ar.get([128, 4, D], BF16) for _ in range(NWB)]
    Xb = [ar.get([128, D], BF16) for _ in range(4)]
    XbT = [ar.get([128, 8, 128], BF16) for _ in range(2)]
    sgl = [ar.get([128, 512], F32) for _ in range(2)]
    actb = [ar.get([128, 512], BF16) for _ in range(2)]
    actT = [ar.get([128, 4, 128], BF16) for _ in range(2)]
    Yb = [ar.get([128, D], F32) for _ in range(4)]
    tpf2 = tp.rearrange("p a b -> p (a b)")
    tpb = tpf2.bitcast(BF16)
    xTp = tpb[:, 0:1024].rearrange("p (j q) -> p j q", j=8)
    aTp1 = tpb[:, 1024:1536].rearrange("p (j q) -> p j q", j=4)
    gup = [[pab[0].rearrange("p a b -> p (a b)"), pab[1].rearrange("p a b -> p (a b)")], [tpf2[:, 1024:1536], tpf2[:, 1536:2048]]]
    dn = pmm
    NS = NB * SUB

    def wload(b, which):
        wb = b % NWB
        for (buf, src, nm) in which:
            dst = buf[wb].rearrange("p a b -> p (a b)")
            P.dma("pool", lambda e, dst=dst, src=src, b=b: e.indirect_dma_start(
                out=dst, out_offset=None, in_=src, in_offset=bass.IndirectOffsetOnAxis(ap=widx[:, b:b + 1], axis=0)),
                ["widx"], [(nm, wb)])
    WGU = ((wgb, dd["wg"], "wg"), (wub, dd["wu"], "wu"))
    WD = ((wdb, dd["wd"], "wd"),)

    def S0(n):
        r0 = (n // SUB) * BLK + (n % SUB) * 128
        P.ld("sp", Xb[n % 4], dd["Xs"][r0:r0 + 128, :], [], [("Xb", n % 4)])

    def S1(n):
        xb_, xk = Xb[n % 4], ("Xb", n % 4)
        for c in range(8):
            P.tr(xTp[:, c, :], xb_[:, c::8], C.identb[:], [xk, "identb"], ["xTp"])
        P.cp("dve", XbT[n % 2], xTp, ["xTp"], [("XbT", n % 2)])

    def S2a(n):
        wb = (n // SUB) % NWB
        xT, xTk = XbT[n % 2], ("XbT", n % 2)
        g_, u_ = gup[n % 2]
        gk_, uk_ = ("gp", n % 2), ("up", n % 2)
        for c in range(8):
            P.mm(g_, xT[:, c, :], wgb[wb][:, c, :], c == 0, c == 7, [xTk, ("wg", wb)], [gk_])
        for c in range(8):
            P.mm(u_, xT[:, c, :], wub[wb][:, c, :], c == 0, c == 7, [xTk, ("wu", wb)], [uk_])
        P.act(sgl[n % 2], g_, AF.Silu, [gk_], [("sgl", n % 2)])
        P.tt("dve", actb[n % 2], u_, sgl[n % 2], ALU.mult, [uk_, ("sgl", n % 2)], [("actb", n % 2)])

    def S2b(n):
        ab, ak = actb[n % 2], ("actb", n % 2)
        ap_, apk = aTp1, "aTp"
        for c in range(4):
            P.tr(ap_[:, c, :], ab[:, c::4], C.identb[:], [ak, "identb"], [apk])
        P.act(actT[n % 2], ap_, AF.Copy, [apk], [("actT", n % 2)])

    def S3(n):
        wb = (n // SUB) % NWB
        r0 = (n // SUB) * BLK + (n % SUB) * 128
        aT, aTk = actT[n % 2], ("actT", n % 2)
        yb, yk = Yb[n % 4], ("Yb", n % 4)
        for half in range(2):
            pp, pk = dn[half], ("dn", half)
            for c in range(4):
                P.mm(pp[:], aT[:, c, :], wdb[wb][:, c, half * 512:(half + 1) * 512], c == 0, c == 3, [aTk, ("wd", wb)], [pk])
            if half == 0:
                P.act(yb[:, 0:512], pp[:], AF.Copy, [pk], [yk])
            else:
                P.cp("dve", yb[:, 512:1024], pp[:], [pk], [yk])
        P.ld("sp", dd["Ys"][r0:r0 + 128, :], yb, [yk], [])

    for b0 in range(NWB):
        wload(b0, WGU)
        wload(b0, WD)
    S0(0)
    S0(1)
    for step in range(NS + 3):
        if step + 2 < NS:
            S0(step + 2)
        if step < NS:
            S1(step)
        if 0 <= step - 1 < NS:
            S2a(step - 1)
            n2 = step - 1
            if n2 % SUB == SUB - 1 and n2 // SUB + NWB < NB:
                wload(n2 // SUB + NWB, WGU)
        if 0 <= step - 2 < NS:
            S2b(step - 2)
        if 0 <= step - 3 < NS:
            S3(step - 3)
            n3 = step - 3
            if n3 % SUB == SUB - 1 and n3 // SUB + NWB < NB:
                wload(n3 // SUB + NWB, WD)
    P.barrier()

    ar.off = mark
    xt = [ar.get([128, D], F32) for _ in range(4)]
    Y1 = [ar.get([128, D], F32) for _ in range(4)]
    Y2 = [ar.get([128, D], F32) for _ in range(4)]

    def L3(i):
        w, xd, t = tiles[i]
        k = i % 4
        P.ld("sp", xt[k], xd[t * 128:(t + 1) * 128, :], [], [("xt", k)])
        for (Y, nm, di_) in ((Y1, "Y1", d1), (Y2, "Y2", d2)):
            P.dma("pool", lambda e, yb=Y[k], di_=di_, i=i: e.indirect_dma_start(
                out=yb, out_offset=None, in_=dd["Ys"], in_offset=bass.IndirectOffsetOnAxis(ap=di_[:, i:i + 1], axis=0)),
                [], [(nm, k)])

    def C3(i):
        w, xd, t = tiles[i]
        k = i % 4
        ya, yak = Y1[k], ("Y1", k)
        yb2, ybk = Y2[k], ("Y2", k)
        xb_, xk = xt[k], ("xt", k)
        P.ts("dve", ya, ya, ga1[:, i:i + 1], None, ALU.mult, None, [yak], [yak])
        P.stt(ya, yb2, ga2[:, i:i + 1], ya, ALU.mult, ALU.add, [yak, ybk], [yak])
        P.tt("pool", ya, ya, C.bc[("g2", w)], ALU.mult, [yak, ("g2", w)], [yak])
        P.tt("dve", xb_, ya, xb_, ALU.add, [yak, xk], [xk])
        P.ld("sp", xd[t * 128:(t + 1) * 128, :], xb_, [xk], [])

    L3(0)
    if NT > 1:
        L3(1)
    for i in range(NT):
        if i + 2 < NT:
            L3(i + 2)
        C3(i)


class Ctx:
    pass


def setup_consts(nc, P, G):
    I = lambda n, s, dt=F32: dram(nc, n, s, dt, "ExternalInput")
    G.d_ident = I("ident", [128, 128])
    G.d_ltri = I("ltri", [128, 128])
    G.d_cvec = I("cvec", [2, D])
    G.ident = P.sb("ident", [128, 128], F32)
    G.identb = P.sb("identb", [128, 128], BF16)
    G.onesb = P.sb("onesb", [128, 128], BF16)
    G.onesf = P.sb("onesf", [128, 128], F32)
    G.ltri = P.sb("ltri", [128, 128], F32)
    G.negh = P.sb("negh", [128, 512], F32)
    G.epsc = P.sb("epsc", [128, 1], F32)
    P.ld("sp", G.ident[:], G.d_ident, [], ["ident"])
    P.ld("sp", G.ltri[:], G.d_ltri, [], ["ltri"])
    P.cp("dve", G.identb[:], G.ident[:], ["ident"], ["identb"])
    P.op("dve", lambda e: e.memset(G.onesb[:], 1.0), [], ["onesb"])
    P.op("dve", lambda e: e.memset(G.onesf[:], 1.0), [], ["onesf"])
    P.op("dve", lambda e: e.memset(G.negh[:], -0.5), [], ["negh"])
    P.op("dve", lambda e: e.memset(G.epsc[:], EPS), [], ["epsc"])
    G.tp = P.ps("tp", [128, 8, 256], F32)
    G.pm = [P.ps(f"pm{i}", [128, 512], F32) for i in range(2)]
    G.pb6 = P.ps("pb6", [128, 512], F32)
    G.pb7 = P.ps("pb7", [128, 512], F32)


class Layer:
    def __init__(self, nc, P, G, ar, L, nwhich, ctx_bc, bc1):
        self.nc, self.P, self.G = nc, P, G
        self.nwhich, self.ctx_bc, self.bc1 = nwhich, ctx_bc, bc1
        self.ident, self.identb, self.onesb, self.onesf, self.ltri, self.negh = G.ident, G.identb, G.onesb, G.onesf, G.ltri, G.negh
        I = lambda n, s, dt=F32: dram(nc, n, s, dt, "ExternalInput")
        self.d_wmod = I(f"w_mod{L}", [D, 6 * D])
        self.d_bmod = I(f"b_mod{L}", [1, 6 * D])
        self.d_ng = I(f"ng{L}", [2, D])
        self.d_cvec = G.d_cvec
        P.barrier()
        ar.base = 0
        ar.off = 0
        self.bc = {}
        for wch in range(nwhich if ctx_bc else 1):
            for nm in ["g1", "A2", "sh2", "g2"]:
                self.bc[(nm, wch)] = ar.get([128, D], F32)
        if bc1:
            self.bc[("A1", 0)] = ar.get([128, D], F32)
            self.bc[("sh1", 0)] = ar.get([128, D], F32)
        self.fm = ar.get([128, 2, 2, 8], F32)
        ar.base = ar.off
        self.mod_psum = G.pm
        self.mod_phase(ar)

    def mod_phase(self, ar):
        P = self.P
        ar.reset()
        nw = self.nwhich
        cfm = ar.get([128, 2, 8], F32)
        scb = ar.get([128, 2, 8], F32)
        Lb = ar.get([128, 2 * 8, 128], F32)
        modbc = ar.get([128, nw, 6 * D], F32)
        bmb = ar.get([128, 6 * D], F32)
        ngb = ar.get([128, 2, D], F32)
        wm = [ar.get([128, 8, 512], F32) for _ in range(2)]
        tmp = ar.get([128, 8, 128], F32)
        pm = self.G.pm
        P.ld("sp", cfm, self.d_cvec.rearrange("r (j p) -> p r j", p=128), [], ["cfm"], slow=True)
        P.ld("sp", bmb, self.d_bmod.partition_broadcast(128).rearrange("p o n -> p (o n)"), [], ["bmb"])
        P.ld("sp", ngb, self.d_ng.partition_broadcast(128), [], ["ngb"])
        P.act(scb, cfm, AF.Silu, ["cfm"], ["scb"])
        for wch in range(nw):
            for j in range(8):
                P.cp("dve", Lb[:, wch * 8 + j, :], scb[:, wch, j:j + 1].to_broadcast([128, 128]), ["scb"], [("Lb", wch, j)])
        wmv = self.d_wmod.rearrange("(j p) n -> p j n", p=128)
        for n in range(12):
            w = wm[n % 2]
            P.ld("sp", w, wmv[:, :, n * 512:(n + 1) * 512], [], [("wm", n % 2)])
            for wch in range(nw):
                pp = pm[(n * nw + wch) % 2]
                pk = ("pm", (n * nw + wch) % 2)
                for j in range(8):
                    P.mm(pp[:], Lb[:, wch * 8 + j, :], w[:, j, :], j == 0, j == 7, [("Lb", wch, j), ("wm", n % 2)], [pk])
                P.tt("dve", modbc[:, wch, n * 512:(n + 1) * 512], pp[:], bmb[:, n * 512:(n + 1) * 512], ALU.add, [pk, "bmb"], [("mod", wch)])
        for wch in range(nw):
            m = lambda i: modbc[:, wch, i * D:(i + 1) * D]
            mk = ("mod", wch)
            if wch == 0 or self.ctx_bc:
                P.cp("pool", self.bc[("g1", wch)], m(2), [mk], [("g1", wch)])
                P.cp("pool", self.bc[("sh2", wch)], m(3), [mk], [("sh2", wch)])
                P.cp("pool", self.bc[("g2", wch)], m(5), [mk], [("g2", wch)])
                P.stt(self.bc[("A2", wch)], m(4), 1.0, ngb[:, 1, :], ALU.add, ALU.mult, [mk, "ngb"], [("A2", wch)])
            P.stt(m(1), m(1), 1.0, ngb[:, 0, :], ALU.add, ALU.mult, [mk, "ngb"], [mk])
            if self.bc1 and wch == 0:
                P.cp("pool", self.bc[("A1", 0)], m(1), [mk], [("A1", 0)])
                P.cp("pool", self.bc[("sh1", 0)], m(0), [mk], [("sh1", 0)])
            for k, src in enumerate([m(1), m(0)]):
                P.tt("dve", tmp, src.rearrange("p (j q) -> p j q", j=8), self.ident[:].unsqueeze(1).to_broadcast([128, 8, 128]), ALU.mult, [mk, "ident"], ["fmtmp"])
                P.red(self.fm[:, wch, k, :], tmp, ALU.add, ["fmtmp"], [("fm", wch)])
        P.barrier()

    def norm_tile(self, xt, xk, xh, xhk, ss_name):
        P = self.P
        junk, ss, rs = self.nt_junk, self.nt_ss, self.nt_rs
        P.act(junk[:], xt, AF.Square, [xk], ["nt_junk", "nt_ss"], accum=ss[:])
        P.act(rs[:], ss[:], AF.Sqrt, ["nt_ss", "epsc"], ["nt_rs"], bias=self.G.epsc[:, 0:1], scale=1.0 / D)
        P.op("dve", lambda e: e.reciprocal(out=rs[:], in_=rs[:]), ["nt_rs"], ["nt_rs"])
        P.ts("dve", xh, xt, rs[:, 0:1], None, ALU.mult, None, [xk, "nt_rs"], [xhk])

    def alloc_norm(self, ar):
        self.nt_junk = ar.get([128, D], BF16)
        self.nt_ss = ar.get([128, 1], F32)
        self.nt_rs = ar.get([128, 1], F32)


def moe_inputs(nc, L):
    I = lambda n, s, dt=F32: dram(nc, n, s, dt, "ExternalInput")
    return dict(wr=I(f"wr{L}", [D, 36]), br=I(f"br{L}", [1, 36]), wg=I(f"wg{L}", [NE * 128, 4096]),
                wu=I(f"wu{L}", [NE * 128, 4096]), wd=I(f"wd{L}", [NE * 128, 4096]))


def run_moe(nc, P, G, C, ar, tiles, mi):
    NT = len(tiles)
    NB = (2 * NT * 128 + BLK - 1) // BLK + NE
    dd = dict(mi)
    dd.update(iotab=G.d_iotab, iotap=G.d_iotap, Xs=G.d_Xs, Ys=G.d_Ys, H2s=G.d_H2s)
    pab = [G.pm[0].rearrange("p (a b) -> p a b", a=2), G.pm[1].rearrange("p (a b) -> p a b", a=2)]
    P.barrier()
    moe_phase(nc, P, C, ar, tiles, NT, NB, dd, psum=dict(big=G.tp, a=pab, b=[G.pb6, G.pb7]))
    P.barrier()


def emit_conv(nc, P, G, C, ar, L, wins, ctxseg):
    I = lambda n, s, dt=F32: dram(nc, n, s, dt, "ExternalInput")
    d_win = I(f"w_in{L}", [D, 2 * D])
    d_binfm = I(f"b_in_fm{L}", [128, 16])
    d_wdwfm = I(f"w_dw_fm{L}", [128, 8 * 31])
    d_bdwfm = I(f"b_dw_fm{L}", [128, 8])
    d_gnfm = I(f"gn_fm{L}", [128, 8])
    d_wout = I(f"w_out{L}", [D, D])
    d_bout = I(f"b_out{L}", [1, D])
    tp, pm = G.tp, G.pm
    NTm = 32
    VW = (NTm + 2) * 128
    ar.off = ar.base
    binfm = ar.get([128, 16], F32)
    wdwfm = ar.get([128, 8, 31], F32)
    bdwfm = ar.get([128, 8], F32)
    gnfm = ar.get([128, 8], F32)
    hmask = ar.get([128, 2], F32)
    bog = [ar.get([128, D], F32) for _ in range(C.nwhich)]
    ar.base = ar.off
    P.ld("sp", binfm, d_binfm, [], ["binfm"])
    P.ld("sp", wdwfm, d_wdwfm.rearrange("p (j t) -> p j t", j=8), [], ["wdwfm"])
    P.ld("sp", bdwfm, d_bdwfm, [], ["bdwfm"])
    P.ld("sp", gnfm, d_gnfm, [], ["gnfm"])
    if any(w_.get("mask") is not None for w_ in wins):
        P.ld("sp", hmask, [w_["mask"] for w_ in wins if w_.get("mask") is not None][0], [], ["hmask"])
    for w in range(C.nwhich):
        P.ld("sp", bog[w], d_bout.partition_broadcast(128).rearrange("p o n -> p (o n)"), [], [("bog", w)])
        P.tt("pool", bog[w], bog[w], C.bc[("g1", w)], ALU.mult, [("bog", w), ("g1", w)], [("bog", w)])
    P.barrier()
    for wi, win_ in enumerate(wins):
        segs = [dict(w=0, x=win_["x"], nt=NTm + 2, own0=1, nown=NTm, out=win_["out"], halo=True)]
        if ctxseg is not None and wi == 0:
            segs.append(dict(w=1, x=ctxseg["x"], nt=2, own0=0, nown=2, out=ctxseg["out"], halo=False))
        has_ctx = len(segs) > 1
        ar.reset()
        vT = ar.get([128, 8, VW], BF16)
        vTc = ar.get([128, 8, 16 + 256 + 16], BF16)
        c2mark = ar.off
        win = ar.get([128, 8, 2 * D], BF16)
        hT = [ar.get([128, 8, 256], BF16) for _ in range(2)]
        xt = [ar.get([128, D], F32) for _ in range(2)]
        xh_ = [ar.get([128, D], F32) for _ in range(2)]
        sig = [ar.get([128, 256], F32) for _ in range(2)]
        C.alloc_norm(ar)
        pab = [pm[0].rearrange("p (a b) -> p a b", a=2), pm[1].rearrange("p (a b) -> p a b", a=2)]
        P.ld("pool", win, d_win.rearrange("(c p) n -> p c n", p=128), [], ["win"])
        if has_ctx:
            P.op("pool", lambda e: e.memset(vTc, 0.0), [], ["vTc"])
        gi = 0
        ti = 0
        for sg in segs:
            w = sg["w"]
            for g in range(sg["nt"] // 2):
                hb = hT[gi % 2]
                hk = ("hT", gi % 2)
                for tl in range(2):
                    t = g * 2 + tl
                    xb_, xk = xt[ti % 2], ("xt", ti % 2)
                    xhb, xhk = xh_[ti % 2], ("xh", ti % 2)
                    ti += 1
                    P.ld("sp", xb_, sg["x"][t * 128:(t + 1) * 128, :], [], [xk])
                    C.norm_tile(xb_, xk, xhb, xhk, None)
                    for j in range(8):
                        P.tr(tp[:, j, tl * 128:(tl + 1) * 128], xhb[:, j * 128:(j + 1) * 128], C.ident[:], [xhk, "ident"], [("tp", j // 2)])
                for j in range(8):
                    P.act(hb[:, j, :], tp[:, j, :], AF.Identity, [("tp", j // 2), ("fm", w)], [hk],
                          bias=C.fm[:, w, 1, j:j + 1], scale=C.fm[:, w, 0, j:j + 1])
                for jo in range(8):
                    pb_ = pab[jo % 2]
                    pk = ("pab", jo % 2)
                    for half in range(2):
                        for c in range(8):
                            P.mm(pb_[:, half, :], win[:, c, half * D + jo * 128: half * D + (jo + 1) * 128], hb[:, c, :], c == 0, c == 7, ["win", hk], [pk])
                    sg_ = sig[jo % 2]
                    P.act(sg_, pb_[:, 1, :], AF.Sigmoid, [pk, "binfm"], [("sig", jo % 2)], bias=binfm[:, 8 + jo:9 + jo])
                    if sg["halo"]:
                        dst = vT[:, jo, g * 256:(g + 1) * 256]
                        dk = "vT"
                    else:
                        dst = vTc[:, jo, 16 + g * 256:16 + (g + 1) * 256]
                        dk = "vTc"
                    P.stt(dst, pb_[:, 0, :], binfm[:, jo:jo + 1], sg_, ALU.add, ALU.mult, [pk, ("sig", jo % 2), "binfm"], [dk])
                if sg["halo"] and g == 0:
                    if win_.get("mask") is not None:
                        P.ts("pool", vT[:, :, 0:128], vT[:, :, 0:128], hmask[:, 0:1], None, ALU.mult, None, ["vT", "hmask"], ["vT"])
                    elif win_["zlo"]:
                        P.op("pool", lambda e: e.memset(vT[:, :, 0:128], 0.0), [], ["vT"])
                if sg["halo"] and g == sg["nt"] // 2 - 1:
                    if win_.get("mask") is not None:
                        P.ts("pool", vT[:, :, VW - 128:VW], vT[:, :, VW - 128:VW], hmask[:, 1:2], None, ALU.mult, None, ["vT", "hmask"], ["vT"])
                    elif win_["zhi"]:
                        P.op("pool", lambda e: e.memset(vT[:, :, VW - 128:VW], 0.0), [], ["vT"])
                gi += 1
        P.barrier()
        ar.off = c2mark
        wout = ar.get([128, 8, D], BF16)
        Dg = [ar.get([128, 31, 128], BF16) for _ in range(2)]
        vc = ar.get([128, 8, 256], F32)
        sq = ar.get([128, 8, 256], BF16)
        t1 = ar.get([128, 256], F32)
        rsb = ar.get([128, 256], F32)
        tmpv = [ar.get([128, 256], F32) for _ in range(2)]
        uT = ar.get([128, 8, 256], BF16)
        xt2 = [ar.get([128, D], F32) for _ in range(2)]
        xn = [ar.get([128, D], F32) for _ in range(2)]
        cv = [tp[:, 0:2, :].rearrange("p a b -> p (a b)"), tp[:, 2:4, :].rearrange("p a b -> p (a b)")]
        ssb = tp[:, 4:6, :].rearrange("p a b -> p (a b)")
        po = [pm[0], pm[1]]
        P.ld("pool", wout, d_wout.rearrange("(c p) n -> p c n", p=128), [], ["wout"])
        di = 0
        xi = 0
        pi = 0
        for sg in segs:
            w = sg["w"]
            ntok = sg["nown"] * 128
            W = 256
            for tb in range(ntok // W):
                for j in range(8):
                    dg, dgk = Dg[di % 2], ("Dg", di % 2)
                    di += 1
                    P.tt("pool" if j % 4 == 3 else "dve", dg, C.identb[:].unsqueeze(1).to_broadcast([128, 31, 128]),
                         wdwfm[:, j, :].unsqueeze(2).to_broadcast([128, 31, 128]), ALU.mult, ["identb", "wdwfm"], [dgk])
                    cvb, cvk = cv[j % 2], ("cv", j % 2)
                    for tau in range(31):
                        if sg["halo"]:
                            c0 = 128 + tb * W + tau - 15
                            rhs = vT[:, j, c0:c0 + W]
                            rk = "vT"
                        else:
                            c0 = 16 + tb * W + tau - 15
                            rhs = vTc[:, j, c0:c0 + W]
                            rk = "vTc"
                        P.mm(cvb[:, 0:W], dg[:, tau, :], rhs, tau == 0, tau == 30, [dgk, rk], [cvk])
                    P.act(vc[:, j, 0:W], cvb[:, 0:W], AF.Identity, [cvk, "bdwfm"], [("vc", j)], bias=bdwfm[:, j:j + 1])
                    P.act(sq[:, j, 0:W], cvb[:, 0:W], AF.Square, [cvk, "bdwfm"], [("sq", j)], bias=bdwfm[:, j:j + 1])
                for j in range(8):
                    P.mm(ssb[:, 0:W], C.onesb[:], sq[:, j, 0:W], j == 0, j == 7, [("sq", j), "onesb"], ["ssb"])
                P.act(t1[:, 0:W], ssb[:, 0:W], AF.Sqrt, ["ssb", "epsc"], ["t1"], bias=G.epsc[:, 0:1], scale=1.0 / D)
                P.op("dve", lambda e, W=W: e.reciprocal(out=rsb[:, 0:W], in_=t1[:, 0:W]), ["t1"], ["rsb"])
                for j in range(8):
                    tv, tvk = tmpv[j % 2], ("tmpv", j % 2)
                    P.tt("dve", tv[:, 0:W], vc[:, j, 0:W], rsb[:, 0:W], ALU.mult, [("vc", j), "rsb"], [tvk])
                    P.act(uT[:, j, 0:W], tv[:, 0:W], AF.Silu, [tvk, "gnfm"], [("uT", j)], scale=gnfm[:, j:j + 1])
                for s in range(W // 128):
                    t = sg["own0"] + (tb * W) // 128 + s
                    xb_, xk = xt2[xi % 2], ("xt2", xi % 2)
                    xnb, xnk = xn[xi % 2], ("xn", xi % 2)
                    xi += 1
                    P.ld("sp", xb_, sg["x"][t * 128:(t + 1) * 128, :], [], [xk])
                    P.tt("pool", xb_, xb_, bog[w], ALU.add, [xk, ("bog", w)], [xk])
                    for half in range(2):
                        pp, pk = po[pi % 2], ("po", pi % 2)
                        pi += 1
                        for j in range(8):
                            P.mm(pp[:], uT[:, j, s * 128:(s + 1) * 128], wout[:, j, half * 512:(half + 1) * 512], j == 0, j == 7, [("uT", j), "wout"], [pk])
                        P.tt("dve", xnb[:, half * 512:(half + 1) * 512], pp[:], C.bc[("g1", w)][:, half * 512:(half + 1) * 512], ALU.mult, [pk, ("g1", w)], [xnk])
                        P.tt("pool", xnb[:, half * 512:(half + 1) * 512], xnb[:, half * 512:(half + 1) * 512], xb_[:, half * 512:(half + 1) * 512], ALU.add, [xnk, xk], [xnk])
                    to = t - sg["own0"]
                    P.ld("sp", sg["out"][to * 128:(to + 1) * 128, :], xnb, [xnk], [])
        P.barrier()


def emit_fft(nc, P, G, C, ar, L, d_x1, d_xc1, d_x2g, d_xc2):
    I = lambda n, s, dt=F32: dram(nc, n, s, dt, "ExternalInput")
    NPASS = 4
    KP = 64 // NPASS
    CB = 512 // (KP * 2)
    d_wa = I("wa", [64, NPASS, KP * 2])
    d_mb = I("mb", [128, 64 * 4 * 128])
    d_cd = I("cd", [128, 2 * 512])
    d_cdn = I("cdn", [128, 2 * 512])
    d_wout = I(f"w_out{L}", [D, D])
    d_bout = I(f"b_out{L}", [1, D])
    d_hl = G.d_hl
    tp, pm, pb6, pb7 = G.tp, G.pm, G.pb6, G.pb7
    tpf = tp.rearrange("p a b -> p (a b)")
    ar.off = ar.base
    bog = [ar.get([128, D], F32) for _ in range(2)]
    ar.base = ar.off
    for w in range(2):
        P.ld("sp", bog[w], d_bout.partition_broadcast(128).rearrange("p o n -> p (o n)"), [], [("bog", w)])
        P.tt("pool", bog[w], bog[w], C.bc[("g1", w)], ALU.mult, [("bog", w), ("g1", w)], [("bog", w)])
    P.barrier()
    ar.reset()
    xt = [ar.get([128, D], F32) for _ in range(2)]
    xh_ = [ar.get([128, D], F32) for _ in range(2)]
    hb = [ar.get([128, D], BF16) for _ in range(2)]
    C.alloc_norm(ar)
    for t in range(64):
        xb_, xk = xt[t % 2], ("xt", t % 2)
        xhb, xhk = xh_[t % 2], ("xh", t % 2)
        P.ld("sp", xb_, d_x1[t * 128:(t + 1) * 128, :], [], [xk])
        C.norm_tile(xb_, xk, xhb, xhk, None)
        P.tt("dve", xhb, xhb, C.bc[("A1", 0)], ALU.mult, [xhk, ("A1", 0)], [xhk])
        P.tt("pool", hb[t % 2], xhb, C.bc[("sh1", 0)], ALU.add, [xhk, ("sh1", 0)], [("hb", t % 2)])
        P.ld("sp", d_hl[t * 128:(t + 1) * 128, :], hb[t % 2], [("hb", t % 2)], [])
    P.barrier()
    ar.reset()
    wout = ar.get([128, 8, D], BF16)
    cd = ar.get([128, 2, 512], BF16)
    cdn = ar.get([128, 2, 512], BF16)
    wa = ar.get([64, NPASS, KP * 2], BF16)
    fT = ar.get([128, 8, KP * 128], BF16)
    XA = ar.get([64, 128, 128], BF16)
    yg_off = ar.off
    Yg = ar.get([128, 2, KP, 256], BF16)
    MB4 = [ar.get([128, 4, 4, 128], BF16) for _ in range(2)]
    ZT4 = [ar.get([128, 2, 2, 512], BF16) for _ in range(2)]
    xt = [ar.get([128, D], F32) for _ in range(2)]
    xn = [ar.get([128, 512], F32) for _ in range(2)]
    C.alloc_norm(ar)
    print('fft arena words', ar.off, 'of', ar.n)
    P.ld("pool", wout, d_wout.rearrange("(c p) n -> p c n", p=128), [], ["wout"])
    P.ld("pool", cd, d_cd.rearrange("p (a b) -> p a b", a=2), [], ["cd"])
    P.ld("pool", cdn, d_cdn.rearrange("p (a b) -> p a b", a=2), [], ["cdn"])
    P.ld("pool", wa, d_wa, [], ["wa"])
    hlv = d_hl.rearrange("(t1 t2) c -> t1 t2 c", t2=128)
    mbv = d_mb.rearrange("p (k a q) -> p k a q", k=64, a=4)
    x1v = d_x1.rearrange("(k2 k1) d -> k1 k2 d", k1=64)
    pA = [tpf[:, 0:512], tpf[:, 512:1024]]
    pZ = [tpf[:, 1024:1280], tpf[:, 1536:1792]]
    pF = [pm[0], pm[1]]
    pO2 = [pb6, pb7]

    def out_proj(src, ntile, xsrc, w, outap):
        for tl in range(ntile):
            xb_, xk = xt[tl % 2], ("xt", tl % 2)
            P.ld("sp", xb_, xsrc(tl), [], [xk])
            P.tt("pool", xb_, xb_, bog[w], ALU.add, [xk, ("bog", w)], [xk])
            for half in range(2):
                pp, pk = pO2[half], ("pO2", half)
                tb, tk = xn[half], ("xn", half)
                for c in range(8):
                    P.mm(pp[:], src[:, c, tl * 128:(tl + 1) * 128], wout[:, c, half * 512:(half + 1) * 512], c == 0, c == 7, ["fT", "wout"], [pk])
                P.tt("dve", tb, pp[:], C.bc[("g1", w)][:, half * 512:(half + 1) * 512], ALU.mult, [pk, ("g1", w)], [tk])
                P.tt("pool", xb_[:, half * 512:(half + 1) * 512], xb_[:, half * 512:(half + 1) * 512], tb, ALU.add, [tk, xk], [xk])
            P.ld("sp", outap[tl * 128:(tl + 1) * 128, :], xb_, [xk], [])

    an = 0
    zn = 0
    fn_ = 0
    mn = 0
    for hh in range(NPASS):
        for g in range(4):
            for nch in range(2):
                ch0 = g * 256 + nch * 128
                for q4 in range(4):
                    P.ld("sp", XA[:, q4 * 32:(q4 + 1) * 32, :], hlv[:, q4 * 32:(q4 + 1) * 32, ch0:ch0 + 128], [], ["XA"])
                for cb in range(128 // CB):
                    pa, pak = pA[an % 2], ("pA", an % 2)
                    an += 1
                    for cc in range(CB):
                        ch = cb * CB + cc
                        P.mm(pa[:, cc * KP * 2:(cc + 1) * KP * 2], XA[:, :, ch], wa[:, hh, :], True, True, ["XA", "wa"], [pak])
                    dst = Yg[:, :, :, nch * 128 + cb * CB: nch * 128 + (cb + 1) * CB].rearrange("p r k c -> p c k r")
                    srcv = pa.rearrange("p (c k r) -> p c k r", c=CB, k=KP)
                    if cb % 2 == 0:
                        P.cp("dve", dst, srcv, [pak], ["Yg"])
                    else:
                        P.act(dst, srcv, AF.Copy, [pak], ["Yg"])
            for kb in range(KP // 4):
                mb_, mbk = MB4[mn % 2], ("MB4", mn % 2)
                mn += 1
                k0 = hh * KP + kb * 4
                P.ld("pool", mb_, mbv[:, k0:k0 + 4, :, :], [], [mbk])
                zt, ztk = ZT4[fn_ % 2], ("ZT4", fn_ % 2)
                for q in range(4):
                    kl = kb * 4 + q
                    for nch in range(2):
                        pz, pzk = pZ[zn % 2], ("pZ", zn % 2)
                        zn += 1
                        P.mm(pz, Yg[:, 0, kl, nch * 128:(nch + 1) * 128], mb_[:, q, 0:2, :].rearrange("p a b -> p (a b)"), True, False, ["Yg", mbk], [pzk])
                        P.mm(pz, Yg[:, 1, kl, nch * 128:(nch + 1) * 128], mb_[:, q, 2:4, :].rearrange("p a b -> p (a b)"), False, True, ["Yg", mbk], [pzk])
                        P.cp("dve", zt[:, nch, :, q * 128:(q + 1) * 128], pz.rearrange("p (a b) -> p a b", a=2), [pzk], [ztk])
                for mch in range(2):
                    pf, pfk = pF[mch], ("pF", mch)
                    i4 = 0
                    for nch in range(2):
                        for ri in range(2):
                            P.mm(pf[:], cd[:, nch, ri * 256 + mch * 128: ri * 256 + (mch + 1) * 128], zt[:, nch, ri, :], i4 == 0, i4 == 3, ["cd", ztk], [pfk])
                            i4 += 1
                    P.act(fT[:, 2 * g + mch, kb * 512:(kb + 1) * 512], pf[:], AF.Copy, [pfk], ["fT"])
                fn_ += 1
        out_proj(fT, KP, lambda tl, hh=hh: x1v[hh * KP + tl], 0, d_x2g[hh * KP * 128:(hh + 1) * KP * 128, :])
    P.barrier()
    save_off = ar.off
    ar.off = yg_off
    fTc = ar.get([128, 8, 256], BF16)
    hTc = ar.get([128, 8, 256], BF16)
    Hcs = ar.get([128, 2, 512], BF16)
    xh_ = [ar.get([128, D], F32)]
    assert ar.off <= yg_off + (2 * KP * 256) // 2, "ctx buffers exceed Yg region"
    ar.off = save_off
    for tl in range(2):
        xb_, xk = xt[tl % 2], ("xt", tl % 2)
        xhb, xhk = xh_[0], ("xh", 0)
        P.ld("sp", xb_, d_xc1[tl * 128:(tl + 1) * 128, :], [], [xk])
        C.norm_tile(xb_, xk, xhb, xhk, None)
        for j in range(8):
            P.tr(tp[:, j, tl * 128:(tl + 1) * 128], xhb[:, j * 128:(j + 1) * 128], C.ident[:], [xhk, "ident"], [("tpc", j // 2)])
    for j in range(8):
        P.act(hTc[:, j, :], tp[:, j, :], AF.Identity, [("tpc", j // 2), ("fm", 1)], ["hTc"], bias=C.fm[:, 1, 1, j:j + 1], scale=C.fm[:, 1, 0, j:j + 1])
    for g in range(4):
        for tl in range(2):
            pf, pfk = pF[tl], ("pF", tl)
            for nch in range(2):
                P.mm(pf[:], hTc[:, 2 * g + nch, tl * 128:(tl + 1) * 128], cd[:, nch, :], nch == 0, nch == 1, ["hTc", "cd"], [pfk])
            P.cp("dve", Hcs[:, tl, :], pf[:], [pfk], ["Hcs"])
        for mch in range(2):
            pf, pfk = pF[mch], ("pF", mch)
            i4 = 0
            for tl in range(2):
                for ri in range(2):
                    P.mm(pf[:, 0:256], Hcs[:, tl, ri * 256 + mch * 128: ri * 256 + (mch + 1) * 128], cdn[:, tl, ri * 256:(ri + 1) * 256], i4 == 0, i4 == 3, ["Hcs", "cdn"], [pfk])
                    i4 += 1
            P.act(fTc[:, 2 * g + mch, :], pf[:, 0:256], AF.Copy, [pfk], ["fT"])
    out_proj(fTc, 2, lambda tl: d_xc1[tl * 128:(tl + 1) * 128, :], 1, d_xc2)
    P.barrier()


def emit_attn(nc, P, G, C, ar, d_x2g, d_xc2, d_x3h):
    I = lambda n, s, dt=F32: dram(nc, n, s, dt, "ExternalInput")
    NQT = 34
    NKT = 66
    d_cosg = I("cosg", [128, 8192])
    d_sing = I("sing", [128, 8192])
    d_cosq = I("cosq", [128, NQT * 128])
    d_sinq = I("sinq", [128, NQT * 128])
    d_qidx = I("qidx", [128, NQT], I32)
    d_wqkv = I("w_qkv", [D, 1536])
    d_gq = I("gq_fm", [128, 1])
    d_gk = I("gk_fm", [128, 1])
    d_wo = I("w_o", [D, D])
    d_prot = I("prot", [128, 128])
    SCALE = 128.0 ** -0.5
    tp, pm, pb6, pb7 = G.tp, G.pm, G.pb6, G.pb7
    tpf = tp.rearrange("p a b -> p (a b)")
    ar.off = ar.base
    gq = ar.get([128, 1], F32)
    gk = ar.get([128, 1], F32)
    prot = ar.get([128, 128], F32)
    protb = ar.get([128, 128], BF16)
    qidx = ar.get([128, NQT], I32)
    ar.base = ar.off
    P.ld("sp", gq, d_gq, [], ["gq"])
    P.ld("sp", gk, d_gk, [], ["gk"])
    P.ld("sp", prot, d_prot, [], ["prot"])
    P.ld("sp", qidx, d_qidx, [], ["qidx"])
    P.cp("dve", protb, prot, ["prot"], ["protb"])
    P.barrier()
    ar.reset()
    KT = ar.get([128, 2, NKT * 128], BF16)
    Vx = ar.get([128, NKT, 2, 130], BF16)
    wqkv = ar.get([128, 8, 1536], BF16)
    wo = ar.get([128, 8, D], BF16)
    hT = ar.get([128, 8, 512], BF16)
    xt = [ar.get([128, D], F32) for _ in range(2)]
    xh_ = [ar.get([128, D], F32) for _ in range(2)]
    C.alloc_norm(ar)
    sqb = ar.get([128, 512], BF16)
    t1 = ar.get([128, 512], F32)
    rs = ar.get([128, 512], F32)
    kn = ar.get([128, 512], F32)
    knb = ar.get([128, 512], BF16)
    cs = ar.get([128, 512], F32)
    sn = ar.get([128, 512], F32)
    t2 = ar.get([128, 512], F32)
    QT = [ar.get([128, 512], BF16) for _ in range(2)]
    PT = [ar.get([128, 512], BF16) for _ in range(3)]
    rec = ar.get([128, 4], F32)
    Ob = [ar.get([128, 128], BF16) for _ in range(2)]
    OT = ar.get([128, 8, 512], BF16)
    oacc = [ar.get([128, 4, 130], F32)] * 2
    print('attn arena words', ar.off, 'of', ar.n)
    pb4, pb5 = pm
    pS = [pb4, pb5]
    pO = [tpf[:, i * 512:i * 512 + 130] for i in range(4)]
    pb7b = pb7[:, 0:64].bitcast(BF16)
    P.ld("pool", wqkv, d_wqkv.rearrange("(c p) n -> p c n", p=128), [], ["wqkv"])
    P.ld("pool", wo, d_wo.rearrange("(c p) n -> p c n", p=128), [], ["wo"])
    P.op("pool", lambda e: e.memset(Vx, 1.0), [], ["Vx"])
    cnt = {"x": 0}

    def load_tile(dst, dk, src):
        if src[0] == "rows":
            P.ld("sp", dst, src[1], [], [dk])
        else:
            t = src[1]
            P.dma("pool", lambda e: e.indirect_dma_start(out=dst, out_offset=None, in_=d_x2g,
                                                         in_offset=bass.IndirectOffsetOnAxis(ap=qidx[:, t:t + 1], axis=0)), ["qidx"], [dk])

    def pro_group(srcs, w, c0):
        ntl = len(srcs)
        for tl in range(ntl):
            n = cnt["x"]
            cnt["x"] += 1
            xb_, xk = xt[n % 2], ("xt", n % 2)
            xhb, xhk = xh_[n % 2], ("xh", n % 2)
            load_tile(xb_, xk, srcs[tl])
            C.norm_tile(xb_, xk, xhb, xhk, None)
            for j in range(8):
                P.tr(tp[:, j, tl * 128:(tl + 1) * 128], xhb[:, j * 128:(j + 1) * 128], C.ident[:], [xhk, "ident"], [("bank", j // 2)])
        for j in range(8):
            P.act(hT[:, j, c0:c0 + ntl * 128], tp[:, j, 0:ntl * 128], AF.Identity, [("bank", j // 2), ("fm", w)], ["hT"],
                  bias=C.fm[:, w, 1, j:j + 1], scale=C.fm[:, w, 0, j:j + 1])

    def qk_steps(mm_fn, ps, psk, W, gfm, gk_, rope, out, outk):
        mm_fn()
        yield
        P.act(sqb[:, 0:W], ps[:, 0:W], AF.Square, [psk], ["sqb"])
        yield
        P.mm(pb7[:, 0:W], C.onesb[:], sqb[:, 0:W], True, True, ["sqb", "onesb"], ["pb7"])
        yield
        P.act(t1[:, 0:W], pb7[:, 0:W], AF.Sqrt, ["pb7", "epsc"], ["t1"], bias=G.epsc[:, 0:1], scale=1.0 / 128)
        yield
        P.op("dve", lambda e: e.reciprocal(out=rs[:, 0:W], in_=t1[:, 0:W]), ["t1"], ["rs"])
        yield
        P.stt(kn[:, 0:W], ps[:, 0:W], gfm[:, 0:1], rs[:, 0:W], ALU.mult, ALU.mult, [psk, gk_, "rs"], ["kn"])
        yield
        if rope:
            P.act(knb[:, 0:W], kn[:, 0:W], AF.Copy, ["kn"], ["knb"])
            yield
            P.mm(pb7[:, 0:W], protb, knb[:, 0:W], True, True, ["knb", "protb"], ["pb7"])
            yield
            P.tt("dve", t2[:, 0:W], pb7[:, 0:W], sn[:, 0:W], ALU.mult, ["pb7", "sn"], ["t2"])
            yield
            P.tt("pool", kn[:, 0:W], kn[:, 0:W], cs[:, 0:W], ALU.mult, ["kn", "cs"], ["kn"])
            yield
            P.tt("dve", out, kn[:, 0:W], t2[:, 0:W], ALU.add, ["kn", "t2"], [outk])
        else:
            P.act(out, kn[:, 0:W], AF.Copy, ["kn"], [outk])
        yield

    def run_all(gen):
        if gen is not None:
            for _ in gen:
                pass

    def step(gen):
        if gen is None:
            return None
        try:
            next(gen)
            return gen
        except StopIteration:
            return None

    rows = lambda ap, t: ("rows", ap[t * 128:(t + 1) * 128, :])
    groups = [(d_xc2, 0, 2, 1, 0, False, None)]
    for g in range(16):
        groups.append((d_x2g, g * 4, 4, 0, 2 + g * 4, True, g))
    for (xap, t0, ntl, w, kt0, rope, g) in groups:
        W = ntl * 128
        for r in range(0, ntl, 2):
            pro_group([rows(xap, t0 + r), rows(xap, t0 + r + 1)], w, r * 128)
        if rope:
            P.ld("sp", cs[:, 0:W], d_cosg[:, g * 512:g * 512 + W], [], ["cs"])
            P.ld("sp", sn[:, 0:W], d_sing[:, g * 512:g * 512 + W], [], ["sn"])
        for j in range(2):
            def kmm(j=j, W=W):
                for c in range(8):
                    P.mm(pb6[:, 0:W], wqkv[:, c, 1024 + j * 128:1024 + (j + 1) * 128], hT[:, c, 0:W], c == 0, c == 7, ["wqkv", "hT"], ["pb6"])
            run_all(qk_steps(kmm, pb6, "pb6", W, gk, "gk", rope, KT[:, j, kt0 * 128:kt0 * 128 + W], "KT"))
        for tl in range(ntl):
            pv = pS[tl % 2]
            pvk = ("pS", tl % 2)
            for c in range(8):
                P.mm(pv[:, 0:256], hT[:, c, tl * 128:(tl + 1) * 128], wqkv[:, c, 1280:1536], c == 0, c == 7, ["wqkv", "hT"], [pvk])
            P.cp("dve", Vx[:, kt0 + tl, :, 0:128], pv[:, 0:256].rearrange("p (a b) -> p a b", a=2), [pvk], ["Vx"])

    pn = 0
    qchunks = [(i * 4, 4) for i in range(8)] + [(32, 2)]
    for (qt0, nqt) in qchunks:
        W = nqt * 128
        for r in range(0, nqt, 2):
            pro_group([("idx", qt0 + r), ("idx", qt0 + r + 1)], 0, r * 128)
        P.ld("sp", cs[:, 0:W], d_cosq[:, qt0 * 128:qt0 * 128 + W], [], ["cs"])
        P.ld("sp", sn[:, 0:W], d_sinq[:, qt0 * 128:qt0 * 128 + W], [], ["sn"])
        def qgen(h, W=W):
            def qmm():
                for c in range(8):
                    P.mm(pb6[:, 0:W], wqkv[:, c, h * 128:(h + 1) * 128], hT[:, c, 0:W], c == 0, c == 7, ["wqkv", "hT"], ["pb6"])
            return qk_steps(qmm, pb6, "pb6", W, gq, "gq", True, QT[h % 2][:, 0:W], ("QT", h % 2))

        def fin_steps(h, nqt=nqt):
            oa, oak = oacc[0], "oacc"
            for qs in range(nqt):
                P.op("dve", lambda e, qs=qs: e.reciprocal(out=rec[:, qs:qs + 1], in_=oa[:, qs, 128:129]), [oak], ["rec"])
                ob, obk = Ob[qs % 2], ("Ob", qs % 2)
                P.ts("dve", ob, oa[:, qs, 0:128], rec[:, qs:qs + 1], None, ALU.mult, None, [oak, "rec"], [obk])
                yield
                P.tr(pb7b, ob, C.identb[:], [obk, "identb"], ["pb7"])
                yield
                P.act(OT[:, h, qs * 128:(qs + 1) * 128], pb7b, AF.Copy, ["pb7"], ["OT"])
                yield

        run_all(qgen(0))
        deferred = None
        for h in range(8):
            j = h // 4
            qt, qk_ = QT[h % 2], ("QT", h % 2)
            gen = qgen(h + 1) if h + 1 < 8 else None

            def S(kt):
                P.mm(pS[kt % 2][:, 0:W], KT[:, j, kt * 128:(kt + 1) * 128], qt[:, 0:W], True, True, ["KT", qk_], [("pS", kt % 2)])
            S(0)
            for kt in range(NKT):
                if kt + 1 < NKT:
                    S(kt + 1)
                pt, ptk = PT[pn % 3], ("PT", pn % 3)
                pn += 1
                P.act(pt[:, 0:W], pS[kt % 2][:, 0:W], AF.Exp, [("pS", kt % 2)], [ptk], scale=SCALE)
                for qs in range(nqt):
                    P.mm(pO[qs], pt[:, qs * 128:(qs + 1) * 128], Vx[:, kt, j, :], kt == 0, kt == NKT - 1, [ptk, "Vx"], [("bank", qs)])
                if kt % 2 == 1:
                    if deferred is not None:
                        deferred = step(deferred)
                    else:
                        gen = step(gen)
            run_all(deferred)
            run_all(gen)
            oa, oak = oacc[0], "oacc"
            for qs in range(nqt):
                P.cp("dve", oa[:, qs, :], pO[qs], [("bank", qs)], [oak])
            deferred = fin_steps(h)
        run_all(deferred)
        for qs in range(nqt):
            t = qt0 + qs
            n = cnt["x"]
            cnt["x"] += 1
            xb_, xk = xt[n % 2], ("xt", n % 2)
            load_tile(xb_, xk, ("idx", t))
            xnb, xnk = xh_[n % 2], ("xh", n % 2)
            for half in range(2):
                pp, pk = pS[half], ("pS", half)
                for h in range(8):
                    P.mm(pp[:], OT[:, h, qs * 128:(qs + 1) * 128], wo[:, h, half * 512:(half + 1) * 512], h == 0, h == 7, ["OT", "wo"], [pk])
                P.tt("dve", xnb[:, half * 512:(half + 1) * 512], pp[:], C.bc[("g1", 0)][:, half * 512:(half + 1) * 512], ALU.mult, [pk, ("g1", 0)], [xnk])
                P.tt("pool", xnb[:, half * 512:(half + 1) * 512], xnb[:, half * 512:(half + 1) * 512], xb_[:, half * 512:(half + 1) * 512], ALU.add, [xnk, xk], [xnk])
            P.ld("sp", d_x3h[t * 128:(t + 1) * 128, :], xnb, [xnk], [])
    P.barrier()


def build_fused():
    nc = bass.Bass("TRN2", target_bir_lowering=False)
    I = lambda n, s, dt=F32: dram(nc, n, s, dt, "ExternalInput")
    N = lambda n, s, dt=F32: dram(nc, n, s, dt, "Internal")
    NBM = (2 * 66 * 128 + BLK - 1) // BLK + NE
    d_xpad = I("xpad", [66 * 128, D])
    d_xc = I("xc", [256, D])
    d_hmask = I("hmask", [128, 2])
    d_xo = dram(nc, "xo", [32 * 128, D], F32, "ExternalOutput")
    d_x1 = N("x1", [8192, D])
    d_xc1 = N("xc1", [256, D])
    d_x2g = N("x2g", [8192, D])
    d_xc2 = N("xc2", [256, D])
    d_x3h = N("x3h", [34 * 128, D])
    with ExitStack() as es:
        P = Prog(nc, es)
        G = Ctx()
        setup_consts(nc, P, G)
        G.d_iotab = I("iota_b", [128, NBM])
        G.d_iotap = I("iota_p", [128, 1])
        G.d_Xs = N("Xs", [NBM * BLK, D], BF16)
        G.d_Ys = N("Ys", [NBM * BLK, D])
        G.d_hl = N("hl", [8192, D], BF16)
        G.d_H2s = N("H2s", [66 * 128, D], BF16)
        ar = Arena(P, "arena", 182)
        ar.base = 0
        tl = lambda ap, a, b, w=0: [(w, ap, t) for t in range(a, b)]
        C = Layer(nc, P, G, ar, 0, 2, True, False)
        mi = moe_inputs(nc, 0)
        wins = [dict(x=d_xpad[0:34 * 128, :], out=d_x1[0:4096, :], zlo=True, zhi=False),
                dict(x=d_xpad[32 * 128:66 * 128, :], out=d_x1[4096:8192, :], zlo=False, zhi=True)]
        emit_conv(nc, P, G, C, ar, 0, wins, dict(x=d_xc, out=d_xc1))
        run_moe(nc, P, G, C, ar, tl(d_x1, 0, 64) + tl(d_xc1, 0, 2, 1), mi)
        C = Layer(nc, P, G, ar, 1, 2, True, True)
        mi = moe_inputs(nc, 1)
        emit_fft(nc, P, G, C, ar, 1, d_x1, d_xc1, d_x2g, d_xc2)
        run_moe(nc, P, G, C, ar, tl(d_x2g, 0, 64) + tl(d_xc2, 0, 2, 1), mi)
        C = Layer(nc, P, G, ar, 2, 2, False, False)
        mi = moe_inputs(nc, 2)
        emit_attn(nc, P, G, C, ar, d_x2g, d_xc2, d_x3h)
        run_moe(nc, P, G, C, ar, tl(d_x3h, 0, 34), mi)
        C = Layer(nc, P, G, ar, 3, 1, False, False)
        mi = moe_inputs(nc, 3)
        emit_conv(nc, P, G, C, ar, 3, [dict(x=d_x3h, out=d_xo, zlo=False, zhi=False, mask=d_hmask)], None)
        run_moe(nc, P, G, C, ar, tl(d_xo, 0, 32), mi)
        P.barrier()
        P.finish()
        print("fused program instructions:", P.ninst, "sems:", P.nsem)
    return nc


_CACHE = {}


def _fm(v, n):
    return np.ascontiguousarray(v.reshape(n, 128).T)


def _rope_tables():
    rows = 8192 // 64
    row = np.repeat(np.arange(rows, dtype=np.float32), 64)
    col = np.tile(np.arange(64, dtype=np.float32), rows)
    inv = (np.float32(10000.0) ** (-np.arange(32, dtype=np.float32) / np.float32(32))).astype(np.float32)
    ang = np.concatenate([row[:, None] * inv, col[:, None] * inv], axis=-1).astype(np.float32)
    c, s_ = np.cos(ang).astype(np.float32), np.sin(ang).astype(np.float32)
    cosf = np.repeat(c, 2, axis=1).T
    sgn = np.tile(np.array([-1.0, 1.0], np.float32), 64)
    sinf = (np.repeat(s_, 2, axis=1) * sgn[None, :]).T
    return np.ascontiguousarray(cosf), np.ascontiguousarray(sinf)


def _fft_tables():
    k1 = np.arange(64, dtype=np.float64)
    t1 = np.arange(64, dtype=np.float64)
    a = 2 * np.pi * np.outer(t1, k1) / 64.0
    wa = np.stack([np.cos(a), -np.sin(a)], -1) / 8.0
    wa = wa.reshape(64, 4, 16 * 2)
    t2 = np.arange(128, dtype=np.float64)
    k2 = np.arange(128, dtype=np.float64)
    kk = k1[:, None] + 64.0 * k2[None, :]
    th = 2 * np.pi * t2[:, None, None] * kk[None, :, :] / 8192.0
    mr, mi = np.cos(th), -np.sin(th)
    mb = np.stack([mr, mi, -mi, mr], 2) / np.sqrt(128.0)
    n = np.arange(256, dtype=np.float64)
    ph = 2 * np.pi * np.outer(n, n) / 256.0
    cs = np.concatenate([np.cos(ph), np.sin(ph)], 1) / 16.0
    csn = np.concatenate([np.cos(ph), -np.sin(ph)], 1) / 16.0
    cd = cs.reshape(2, 128, 512).transpose(1, 0, 2).reshape(128, 1024)
    cdn = csn.reshape(2, 128, 512).transpose(1, 0, 2).reshape(128, 1024)
    f = lambda a_: np.ascontiguousarray(a_.astype(np.float32))
    return f(wa), f(mb.reshape(128, -1)), f(cd), f(cdn)


def kernel(**inp):
    inp = {k: np.asarray(v) for k, v in inp.items()}
    if "nc" not in _CACHE:
        _CACHE["nc"] = build_fused()
    nc = _CACHE["nc"]
    NBM = (2 * 66 * 128 + BLK - 1) // BLK + NE
    sh = dict(ident=np.eye(128, dtype=np.float32), ltri=np.triu(np.ones((128, 128), np.float32), 1),
              iota_b=np.tile((np.arange(NBM, dtype=np.float32) * BLK)[None, :], (128, 1)),
              iota_p=np.arange(128, dtype=np.float32).reshape(128, 1))
    for L in range(4):
        sh[f"w_mod{L}"] = inp["w_mod"][L]
        sh[f"b_mod{L}"] = inp["b_mod"][L][None, :]
        sh[f"ng{L}"] = inp["norm_g"][L]
        sh[f"wr{L}"] = np.ascontiguousarray(np.concatenate([inp["moe_w_group"][L], inp["moe_w_expert"][L]], axis=1))
        sh[f"br{L}"] = np.concatenate([inp["moe_b_group"][L], inp["moe_b_expert"][L]])[None, :]
        sh[f"wg{L}"] = inp["moe_w_gate"][L].reshape(NE * 128, 4096)
        sh[f"wu{L}"] = inp["moe_w_up"][L].reshape(NE * 128, 4096)
        sh[f"wd{L}"] = inp["moe_w_down"][L].reshape(NE * 128, 4096)
    for L, j in ((0, 0), (3, 1)):
        sh[f"w_in{L}"] = inp["conv_w_in"][j]
        sh[f"b_in_fm{L}"] = _fm(inp["conv_b_in"][j], 16)
        sh[f"w_dw_fm{L}"] = np.ascontiguousarray(inp["conv_w_dw"][j].T.reshape(8, 128, 31).transpose(1, 0, 2).reshape(128, 8 * 31))
        sh[f"b_dw_fm{L}"] = _fm(inp["conv_b_dw"][j], 8)
        sh[f"gn_fm{L}"] = _fm(inp["conv_norm_g"][j], 8)
        sh[f"w_out{L}"] = inp["conv_w_out"][j]
        sh[f"b_out{L}"] = inp["conv_b_out"][j][None, :]
    sh["w_out1"] = inp["fnet_w_out"][0]
    sh["b_out1"] = inp["fnet_b_out"][0][None, :]
    sh["wa"], sh["mb"], sh["cd"], sh["cdn"] = _fft_tables()
    cosf, sinf = _rope_tables()
    gtok = (np.arange(64)[:, None] + 64 * np.arange(128)[None, :]).reshape(-1)
    sh["cosg"] = np.ascontiguousarray(cosf[:, gtok])
    sh["sing"] = np.ascontiguousarray(sinf[:, gtok])
    prot = np.zeros((128, 128), np.float32)
    for i in range(64):
        prot[2 * i, 2 * i + 1] = 1
        prot[2 * i + 1, 2 * i] = 1
    sh.update(w_qkv=inp["attn_w_qkv"][0], gq_fm=inp["attn_q_norm_g"][0].reshape(128, 1), gk_fm=inp["attn_k_norm_g"][0].reshape(128, 1),
              w_o=inp["attn_w_out"][0], prot=prot)
    zpad = np.zeros((128, D), np.float32)
    in_maps = []
    for core in range(8):
        b, s = core // 2, core % 2
        m = dict(sh)
        m["xpad"] = np.ascontiguousarray(np.concatenate([zpad, inp["x"][b], zpad], axis=0))
        m["xc"] = np.ascontiguousarray(inp["ctx"][b])
        m["cvec"] = np.stack([inp["c"][b], inp["c_ctx"]])
        tok = np.clip(4096 * s - 128 + np.arange(34 * 128), 0, 8191)
        m["cosq"] = np.ascontiguousarray(cosf[:, tok])
        m["sinq"] = np.ascontiguousarray(sinf[:, tok])
        grow = (tok % 64) * 128 + tok // 64
        m["qidx"] = np.ascontiguousarray(grow.reshape(34, 128).T.astype(np.int32))
        hm = np.ones((128, 2), np.float32)
        if s == 0:
            hm[:, 0] = 0
        else:
            hm[:, 1] = 0
        m["hmask"] = hm
        in_maps.append(m)
    res = run_bass_kernel_spmd(nc, in_maps, core_ids=list(range(8)))
    out = np.empty_like(inp["x"])
    for core in range(8):
        b, s = core // 2, core % 2
        out[b, s * 4096:(s + 1) * 4096] = res.results[core]["xo"]
    return out
```

```python
import numpy as np
import concourse.bass as bass
import concourse.mybir as mybir
from concourse.bass_utils import run_bass_kernel_spmd
from contextlib import ExitStack

F32 = mybir.dt.float32
BF16 = mybir.dt.bfloat16
I32 = mybir.dt.int32
AF = mybir.ActivationFunctionType
ALU = mybir.AluOpType
AX = mybir.AxisListType

ENG = ["pe", "act", "dve", "pool", "sp"]
D = 1024
EPS = 1e-6
NE = 32
BLK = 384
SUB = BLK // 128


class Prog:
    SEM_LIMIT = 20000
    ND = 8

    def __init__(self, nc, es):
        self.nc, self.es = nc, es
        self.q = {e: [] for e in ENG}
        self.cur = {}
        self.waited = {e: {} for e in ENG}
        self.last_w = {}
        self.readers = {}
        self.nsem = 0
        self.dsem = {e: [] for e in ENG}
        self.dn = {e: 0 for e in ENG}
        self.sems = {}
        self.ninst = 0
        self.nbar = 0
        self.full = {}

    def new_sem(self):
        self.nsem += 1
        self.sems[self.nsem] = self.es.enter_context(self.nc.semaphore(f"s{self.nsem}"))
        return self.nsem

    def sb(self, name, shape, dt):
        return self.es.enter_context(self.nc.sbuf_tensor("sb_" + name, list(shape), dt))

    def ps(self, name, shape, dt):
        return self.es.enter_context(self.nc.psum_tensor("ps_" + name, list(shape), dt))

    def _deps(self, eng, reads, writes, skip_same):
        deps = {}

        def add(ev):
            if ev is None:
                return
            s, v, e = ev
            if skip_same and e == eng:
                return
            if deps.get(s, 0) < v:
                deps[s] = v
        for k in reads:
            add(self.last_w.get(k))
        for k in writes:
            add(self.last_w.get(k))
            for ev in self.readers.get(k, ()):
                add(ev)
        out = []
        for s, v in deps.items():
            if self.waited[eng].get(s, 0) < v:
                self.waited[eng][s] = v
                out.append((s, v))
        return out

    def _commit(self, ev, reads, writes):
        for k in writes:
            self.last_w[k] = ev
            self.readers[k] = []
        for k in reads:
            self.readers.setdefault(k, []).append(ev)

    def op(self, eng, fn, reads=(), writes=(), skip_same=False):
        waits = self._deps(eng, reads, writes, skip_same)
        c = self.cur.get(eng)
        if c is None or c[1] >= self.SEM_LIMIT:
            if c is not None:
                self.full[eng] = (c[0], c[1])
            c = [self.new_sem(), 0]
            self.cur[eng] = c
        c[1] += 1
        ev = (c[0], c[1], eng)
        self.q[eng].append((waits, fn, c[0], 1))
        self._commit(ev, reads, writes)
        self.ninst += 1
        return ev

    def dma(self, eng, fn, reads=(), writes=()):
        waits = self._deps(eng, reads, writes, False)
        pool = self.dsem[eng]
        i = self.dn[eng] % self.ND
        self.dn[eng] += 1
        if len(pool) <= i:
            pool.append([self.new_sem(), 0])
        d = pool[i]
        if d[1] > 0 and self.waited[eng].get(d[0], 0) < 16 * d[1]:
            self.waited[eng][d[0]] = 16 * d[1]
            waits.append((d[0], 16 * d[1]))
        d[1] += 1
        ev = (d[0], 16 * d[1], "dma_" + eng)
        self.q[eng].append((waits, fn, d[0], 16))
        self._commit(ev, reads, writes)
        self.ninst += 1
        return ev

    def wait_keys(self, eng, keys):
        waits = self._deps(eng, keys, (), False)
        self.q[eng].append((waits, None, None, 0))

    def barrier(self):
        evs = []
        for e in ENG:
            c = self.cur.get(e)
            if c is not None:
                evs.append((c[0], c[1]))
            if e in self.full:
                evs.append(self.full[e])
            for d in self.dsem[e]:
                if d[1] > 0:
                    evs.append((d[0], 16 * d[1]))
        for e in ENG:
            waits = []
            for s, v in evs:
                if self.waited[e].get(s, 0) < v:
                    self.waited[e][s] = v
                    waits.append((s, v))
            self.q[e].append((waits, None, None, 0))
        self.last_w = {}
        self.readers = {}

    def finish(self):
        nc = self.nc
        engobj = {"pe": "tensor", "act": "scalar", "dve": "vector", "pool": "gpsimd", "sp": "sync"}
        with nc.Block() as block:
            for e in ENG:
                if not self.q[e]:
                    continue

                def body(engine, e=e):
                    for waits, fn, s, inc in self.q[e]:
                        for (ws, wv) in waits:
                            engine.wait_ge(self.sems[ws], wv)
                        if fn is not None:
                            fn(engine).then_inc(self.sems[s], inc)
                getattr(block, engobj[e])(body)

    def act(self, out, in_, func, r, w, bias=None, scale=None, accum=None):
        kw = {}
        if bias is not None:
            kw["bias"] = bias
        if scale is not None:
            kw["scale"] = scale
        if accum is not None:
            kw["accum_out"] = accum
        return self.op("act", lambda e: e.activation(out=out, in_=in_, func=func, **kw), r, w)

    def tt(self, eng, out, in0, in1, op, r, w):
        return self.op(eng, lambda e: e.tensor_tensor(out=out, in0=in0, in1=in1, op=op), r, w)

    def ts(self, eng, out, in0, s1, s2, op0, op1, r, w):
        if op1 is None:
            return self.op(eng, lambda e: e.tensor_scalar(out=out, in0=in0, scalar1=s1, scalar2=None, op0=op0), r, w)
        return self.op(eng, lambda e: e.tensor_scalar(out=out, in0=in0, scalar1=s1, scalar2=s2, op0=op0, op1=op1), r, w)

    def stt(self, out, in0, scalar, in1, op0, op1, r, w):
        return self.op("dve", lambda e: e.scalar_tensor_tensor(out=out, in0=in0, scalar=scalar, in1=in1, op0=op0, op1=op1), r, w)

    def cp(self, eng, out, in_, r, w):
        return self.op(eng, lambda e: e.tensor_copy(out=out, in_=in_), r, w)

    def red(self, out, in_, op, r, w):
        return self.op("dve", lambda e: e.tensor_reduce(out=out, in_=in_, axis=AX.X, op=op), r, w)

    def mm(self, out, lhsT, rhs, start, stop, r, w):
        return self.op("pe", lambda e: e.matmul(out, lhsT=lhsT, rhs=rhs, start=start, stop=stop), r, w, skip_same=True)

    def tr(self, out, in_, ident, r, w):
        return self.op("pe", lambda e: e.transpose(out=out, in_=in_, identity=ident), r, w, skip_same=True)

    def ld(self, q, out, in_, r, w, slow=False):
        if slow:
            return self.dma(q, lambda e: e.dma_start(out=out, in_=in_, allow_slow_non_contiguous=True), r, w)
        return self.dma(q, lambda e: e.dma_start(out=out, in_=in_), r, w)


class Arena:
    def __init__(self, P, name, kbytes):
        self.t = P.sb(name, [128, kbytes * 256], F32)
        self.n = kbytes * 256
        self.off = 0
        self.base = 0

    def reset(self):
        self.off = self.base

    def get(self, shape, dt):
        n = int(np.prod(shape[1:]))
        words = n if dt in (F32, I32) else (n + 1) // 2
        assert self.off + words <= self.n, ("arena overflow", self.off + words, self.n)
        v = self.t[0:shape[0], self.off:self.off + words]
        self.off += words
        if dt != F32:
            v = v.bitcast(dt)
            if dt == BF16 and n % 2:
                v = v[:, 0:n]
        if len(shape) == 3:
            v = v.rearrange("p (a b) -> p a b", a=shape[1])
        elif len(shape) == 4:
            v = v.rearrange("p (a b c) -> p a b c", a=shape[1], b=shape[2])
        return v


def dram(nc, name, shape, dt, kind):
    return nc.dram_tensor(name, list(shape), dt, kind=kind).ap()


def moe_phase(nc, P, C, ar, tiles, NT, NB, dd, psum):
    ar.reset()
    assert len(tiles) == NT
    tp, pab, pmm = psum["big"], psum["a"], psum["b"]
    wr = ar.get([128, 8, 36], F32)
    brb = ar.get([128, 36], F32)
    lg = ar.get([128, NT, 36], F32)
    iotab = ar.get([128, NB], F32)
    iotap = ar.get([128, 1], F32)
    P.ld("sp", wr, dd["wr"].rearrange("(c p) n -> p c n", p=128), [], ["wr"], slow=True)
    P.ld("sp", brb, dd["br"].partition_broadcast(128).rearrange("p o n -> p (o n)"), [], ["brb"])
    P.ld("sp", iotab, dd["iotab"][:, 0:NB], [], ["iotab"])
    P.ld("sp", iotap, dd["iotap"], [], ["iotap"])
    g = lambda shape, dt=F32: ar.get(shape, dt)
    ga1 = g([128, NT]); ga2 = g([128, NT]); d1 = g([128, NT], I32); d2 = g([128, NT], I32); widx = g([128, NB], I32)
    h2b = [ar.get([128, D], BF16) for _ in range(2)]
    mark = ar.off
    xt = [ar.get([128, D], F32) for _ in range(4)]
    h2 = [ar.get([128, D], F32) for _ in range(2)]
    h2T = [ar.get([128, 8, 128], F32) for _ in range(2)]
    junk = ar.get([128, D], BF16)
    ssq = ar.get([128, 4], F32)
    rsq = ar.get([128, 4], F32)
    gmax = g([128, NT]); gmask = g([128, NT, 4]); gex = g([128, NT, 4]); gsum = g([128, NT]); gtop = g([128, NT])
    pen = g([128, NT, 4]); em = g([128, NT, 32]); m1 = g([128, NT]); oh1 = g([128, NT, 32]); em2 = g([128, NT, 32])
    m2 = g([128, NT]); oh2 = g([128, NT, 32]); dd_ = g([128, NT])
    S = em; pre = g([128, NT, 32]); tot = g([128, NT, 32]); base = g([128, NT, 32]); tmp32 = em2
    cnt = g([128, 32]); padded = g([128, 32]); pst = [g([128, 32]) for _ in range(2)]; pend = g([128, 32]); pstart = g([128, 32])
    d1f = g([128, NT]); d2f = g([128, NT])
    cmpb = g([128, NB, 32]); bef = g([128, NB]); wif = g([128, NB])
    tpf = tp.rearrange("p a b -> p (a b)")
    trp = [tpf[:, 0:1024].rearrange("p (j q) -> p j q", j=8), tpf[:, 1024:2048].rearrange("p (j q) -> p j q", j=8)]
    lgp = [pab[0].rearrange("p a b -> p (a b)"), pab[1].rearrange("p a b -> p (a b)")]

    def A0(i):
        w, xd, t = tiles[i]
        P.ld("sp", xt[i % 4], xd[t * 128:(t + 1) * 128, :], [], [("xt", i % 4)])

    def A1(i):
        xk = ("xt", i % 4)
        k = i % 4
        P.act(junk, xt[k], AF.Square, [xk], ["junk", ("ssq", k)], accum=ssq[:, k:k + 1])
        P.act(rsq[:, k:k + 1], ssq[:, k:k + 1], AF.Sqrt, [("ssq", k), "epsc"], [("rsq", k)], bias=C.G.epsc[:, 0:1], scale=1.0 / D)
        P.op("dve", lambda e: e.reciprocal(out=rsq[:, k:k + 1], in_=rsq[:, k:k + 1]), [("rsq", k)], [("rsq", k)])

    def B1(i):
        w, xd, t = tiles[i]
        k = i % 4
        hb, hk = h2[i % 2], ("h2", i % 2)
        P.stt(hb, xt[k], rsq[:, k:k + 1], C.bc[("A2", w)], ALU.mult, ALU.mult, [("xt", k), ("rsq", k), ("A2", w)], [hk])
        P.tt("pool", hb, hb, C.bc[("sh2", w)], ALU.add, [hk, ("sh2", w)], [hk])
        bb, bk = h2b[i % 2], ("h2b", i % 2)
        P.act(bb, hb, AF.Copy, [hk], [bk])
        P.ld("sp", dd["H2s"][i * 128:(i + 1) * 128, :], bb, [bk], [])

    def C1(i):
        hb, hk = h2[i % 2], ("h2", i % 2)
        tb, tk = trp[i % 2], ("trp", i % 2)
        for j in range(8):
            P.tr(tb[:, j, :], hb[:, j * 128:(j + 1) * 128], C.ident[:], [hk, "ident"], [tk])
        P.cp("dve", h2T[i % 2], tb, [tk], [("h2T", i % 2)])

    def D1(i):
        hT, hTk = h2T[i % 2], ("h2T", i % 2)
        lp, lk = lgp[i % 2], ("lgp", i % 2)
        for j in range(8):
            P.mm(lp[:, 0:36], hT[:, j, :], wr[:, j, :], j == 0, j == 7, [hTk, "wr"], [lk])
        P.tt("dve", lg[:, i, :], lp[:, 0:36], brb, ALU.add, [lk, "brb"], ["lg"])

    A0(0)
    if NT > 1:
        A0(1)
    for s_ in range(NT + 3):
        if s_ + 2 < NT:
            A0(s_ + 2)
        if s_ < NT:
            A1(s_)
        if 0 <= s_ - 1 < NT:
            B1(s_ - 1)
        if 0 <= s_ - 2 < NT:
            C1(s_ - 2)
        if 0 <= s_ - 3 < NT:
            D1(s_ - 3)

    glv, elv = lg[:, :, 0:4], lg[:, :, 4:36]
    bc3 = lambda a, n: a.unsqueeze(2).to_broadcast([128, NT, n])
    P.red(gmax, glv, ALU.max, ["lg"], ["gmax"])
    P.tt("dve", gmask, glv, bc3(gmax, 4), ALU.is_equal, ["lg", "gmax"], ["gmask"])
    P.tt("dve", gex, glv, bc3(gmax, 4), ALU.subtract, ["lg", "gmax"], ["gex"])
    P.act(gex, gex, AF.Exp, ["gex"], ["gex"])
    P.red(gsum, gex, ALU.add, ["gex"], ["gsum"])
    P.op("dve", lambda e: e.reciprocal(out=gtop, in_=gsum), ["gsum"], ["gtop"])
    P.ts("dve", pen, gmask, 1.0, 1e30, ALU.subtract, ALU.mult, ["gmask"], ["pen"])
    P.tt("dve", em.rearrange("p t (a b) -> p t a b", a=4), elv.rearrange("p t (a b) -> p t a b", a=4),
         pen.unsqueeze(3).to_broadcast([128, NT, 4, 8]), ALU.add, ["lg", "pen"], ["em"])
    P.red(m1, em, ALU.max, ["em"], ["m1"])
    P.tt("dve", oh1, em, bc3(m1, 32), ALU.is_equal, ["em", "m1"], ["oh1"])
    P.ts("dve", em2, oh1, -1e30, None, ALU.mult, None, ["oh1"], ["em2"])
    P.tt("dve", em2, em2, em, ALU.add, ["em2", "em"], ["em2"])
    P.red(m2, em2, ALU.max, ["em2"], ["m2"])
    P.tt("dve", oh2, em2, bc3(m2, 32), ALU.is_equal, ["em2", "m2"], ["oh2"])
    P.tt("dve", dd_, m2, m1, ALU.subtract, ["m1", "m2"], ["dd"])
    P.act(dd_, dd_, AF.Exp, ["dd"], ["dd"])
    P.ts("dve", dd_, dd_, 1.0, None, ALU.add, None, ["dd"], ["dd"])
    P.op("dve", lambda e: e.reciprocal(out=dd_, in_=dd_), ["dd"], ["dd"])
    P.tt("dve", ga1, gtop, dd_, ALU.mult, ["gtop", "dd"], ["ga1"])
    P.tt("dve", ga2, gtop, ga1, ALU.subtract, ["gtop", "ga1"], ["ga2"])
    P.tt("dve", S, oh1, oh2, ALU.add, ["oh1", "oh2"], ["S", "em"])
    Sf = S.rearrange("p t e -> p (t e)")
    pref = pre.rearrange("p t e -> p (t e)")
    totf = tot.rearrange("p t e -> p (t e)")
    NW = NT * 32
    c0 = 0
    k = 0
    while c0 < NW:
        wdt = min(512, NW - c0)
        for (lh, dst, dk) in ((C.ltri, pref, "pre"), (C.onesf, totf, "tot")):
            pp, pk = pmm[k % 2], ("pmm", k % 2)
            k += 1
            P.mm(pp[:, 0:wdt], lh[:], Sf[:, c0:c0 + wdt], True, True, ["S", "ltri", "onesf"], [pk])
            P.cp("dve", dst[:, c0:c0 + wdt], pp[:, 0:wdt], [pk], [dk])
        c0 += wdt
    P.op("dve", lambda e: e.memset(base[:, 0, :], 0.0), [], ["base"])
    for i in range(1, NT):
        P.tt("dve", base[:, i, :], base[:, i - 1, :], tot[:, i - 1, :], ALU.add, ["base", "tot"], ["base"])
    P.tt("dve", cnt, base[:, NT - 1, :], tot[:, NT - 1, :], ALU.add, ["base", "tot"], ["cnt"])
    P.tt("dve", pre, pre, base, ALU.add, ["pre", "base"], ["pre"])
    cmp2 = cmpb.rearrange("p b e -> p (b e)").rearrange("p (e b) -> p e b", e=32)
    P.tt("dve", cmp2, cnt.unsqueeze(2).to_broadcast([128, 32, NB]), iotab.unsqueeze(1).to_broadcast([128, 32, NB]), ALU.is_gt, ["cnt", "iotab"], ["cmpb"])
    P.red(padded, cmp2, ALU.add, ["cmpb"], ["padded"])
    P.ts("dve", padded, padded, float(BLK), None, ALU.mult, None, ["padded"], ["padded"])
    P.cp("dve", pst[0], padded, ["padded"], [("pst", 0)])
    cur = 0
    sh = 1
    while sh < 32:
        a, b = pst[cur], pst[1 - cur]
        P.cp("dve", b[:, 0:sh], a[:, 0:sh], [("pst", cur)], [("pst", 1 - cur)])
        P.tt("dve", b[:, sh:32], a[:, sh:32], a[:, 0:32 - sh], ALU.add, [("pst", cur)], [("pst", 1 - cur)])
        cur = 1 - cur
        sh *= 2
    P.cp("dve", pend, pst[cur], [("pst", cur)], ["pend"])
    P.tt("dve", pstart, pend, padded, ALU.subtract, ["pend", "padded"], ["pstart"])
    P.tt("dve", pre, pre, pstart.unsqueeze(1).to_broadcast([128, NT, 32]), ALU.add, ["pre", "pstart"], ["pre"])
    for (oh, ohk, df, dfk, di_, dik) in ((oh1, "oh1", d1f, "d1f", d1, "d1"), (oh2, "oh2", d2f, "d2f", d2, "d2")):
        P.tt("dve", tmp32, pre, oh, ALU.mult, ["pre", ohk], ["tmp32", "em2"])
        P.red(df, tmp32, ALU.add, ["tmp32"], [dfk])
        P.cp("dve", di_, df, [dfk], [dik])
    P.tt("dve", cmpb, pend.unsqueeze(1).to_broadcast([128, NB, 32]), iotab.unsqueeze(2).to_broadcast([128, NB, 32]), ALU.is_le, ["pend", "iotab"], ["cmpb"])
    P.red(bef, cmpb, ALU.add, ["cmpb"], ["bef"])
    P.ts("dve", bef, bef, float(NE - 1), None, ALU.min, None, ["bef"], ["bef"])
    P.ts("dve", wif, bef, 128.0, iotap[:, 0:1], ALU.mult, ALU.add, ["bef", "iotap"], ["wif"])
    P.cp("dve", widx, wif, ["wif"], ["widx"])

    P.barrier()
    ar.off = mark
    hb4 = [ar.get([128, D], BF16) for _ in range(4)]
    for i in range(NT):
        bb, bk = hb4[i % 4], ("hb4", i % 4)
        P.ld("sp", bb, dd["H2s"][i * 128:(i + 1) * 128, :], [], [bk])
        for (di_, dik) in ((d1, "d1"), (d2, "d2")):
            P.dma("pool", lambda e, bb=bb, di_=di_, i=i: e.indirect_dma_start(
                out=dd["Xs"], out_offset=bass.IndirectOffsetOnAxis(ap=di_[:, i:i + 1], axis=0), in_=bb, in_offset=None),
                [bk, dik], [])
    P.barrier()

    ar.off = mark
    NWB = 3
    wgb = [ar.get([128, 8, 512], BF16) for _ in range(NWB)]
    wub = [ar.get([128, 8, 512], BF16) for _ in range(NWB)]
    wdb = [ar.get([128, 4, D], BF16) for _ in range(NWB)]
    Xb = [ar.get([128, D], BF16) for _ in range(4)]
    XbT = [ar.get([128, 8, 128], BF16) for _ in range(2)]
    sgl = [ar.get([128, 512], F32) for _ in range(2)]
    actb = [ar.get([128, 512], BF16) for _ in range(2)]
    actT = [ar.get([128, 4, 128], BF16) for _ in range(2)]
    Yb = [ar.get([128, D], F32) for _ in range(4)]
    tpf2 = tp.rearrange("p a b -> p (a b)")
    tpb = tpf2.bitcast(BF16)
    xTp = tpb[:, 0:1024].rearrange("p (j q) -> p j q", j=8)
    aTp1 = tpb[:, 1024:1536].rearrange("p (j q) -> p j q", j=4)
    gup = [[pab[0].rearrange("p a b -> p (a b)"), pab[1].rearrange("p a b -> p (a b)")], [tpf2[:, 1024:1536], tpf2[:, 1536:2048]]]
    dn = pmm
    NS = NB * SUB

    def wload(b, which):
        wb = b % NWB
        for (buf, src, nm) in which:
            dst = buf[wb].rearrange("p a b -> p (a b)")
            P.dma("pool", lambda e, dst=dst, src=src, b=b: e.indirect_dma_start(
                out=dst, out_offset=None, in_=src, in_offset=bass.IndirectOffsetOnAxis(ap=widx[:, b:b + 1], axis=0)),
                ["widx"], [(nm, wb)])
    WGU = ((wgb, dd["wg"], "wg"), (wub, dd["wu"], "wu"))
    WD = ((wdb, dd["wd"], "wd"),)

    def S0(n):
        r0 = (n // SUB) * BLK + (n % SUB) * 128
        P.ld("sp", Xb[n % 4], dd["Xs"][r0:r0 + 128, :], [], [("Xb", n % 4)])

    def S1(n):
        xb_, xk = Xb[n % 4], ("Xb", n % 4)
        for c in range(8):
            P.tr(xTp[:, c, :], xb_[:, c::8], C.identb[:], [xk, "identb"], ["xTp"])
        P.cp("dve", XbT[n % 2], xTp, ["xTp"], [("XbT", n % 2)])

    def S2a(n):
        wb = (n // SUB) % NWB
        xT, xTk = XbT[n % 2], ("XbT", n % 2)
        g_, u_ = gup[n % 2]
        gk_, uk_ = ("gp", n % 2), ("up", n % 2)
        for c in range(8):
            P.mm(g_, xT[:, c, :], wgb[wb][:, c, :], c == 0, c == 7, [xTk, ("wg", wb)], [gk_])
        for c in range(8):
            P.mm(u_, xT[:, c, :], wub[wb][:, c, :], c == 0, c == 7, [xTk, ("wu", wb)], [uk_])
        P.act(sgl[n % 2], g_, AF.Silu, [gk_], [("sgl", n % 2)])
        P.tt("dve", actb[n % 2], u_, sgl[n % 2], ALU.mult, [uk_, ("sgl", n % 2)], [("actb", n % 2)])

    def S2b(n):
        ab, ak = actb[n % 2], ("actb", n % 2)
        ap_, apk = aTp1, "aTp"
        for c in range(4):
            P.tr(ap_[:, c, :], ab[:, c::4], C.identb[:], [ak, "identb"], [apk])
        P.act(actT[n % 2], ap_, AF.Copy, [apk], [("actT", n % 2)])

    def S3(n):
        wb = (n // SUB) % NWB
        r0 = (n // SUB) * BLK + (n % SUB) * 128
        aT, aTk = actT[n % 2], ("actT", n % 2)
        yb, yk = Yb[n % 4], ("Yb", n % 4)
        for half in range(2):
            pp, pk = dn[half], ("dn", half)
            for c in range(4):
                P.mm(pp[:], aT[:, c, :], wdb[wb][:, c, half * 512:(half + 1) * 512], c == 0, c == 3, [aTk, ("wd", wb)], [pk])
            if half == 0:
                P.act(yb[:, 0:512], pp[:], AF.Copy, [pk], [yk])
            else:
                P.cp("dve", yb[:, 512:1024], pp[:], [pk], [yk])
        P.ld("sp", dd["Ys"][r0:r0 + 128, :], yb, [yk], [])

    for b0 in range(NWB):
        wload(b0, WGU)
        wload(b0, WD)
    S0(0)
    S0(1)
    for step in range(NS + 3):
        if step + 2 < NS:
            S0(step + 2)
        if step < NS:
            S1(step)
        if 0 <= step - 1 < NS:
            S2a(step - 1)
            n2 = step - 1
            if n2 % SUB == SUB - 1 and n2 // SUB + NWB < NB:
                wload(n2 // SUB + NWB, WGU)
        if 0 <= step - 2 < NS:
            S2b(step - 2)
        if 0 <= step - 3 < NS:
            S3(step - 3)
            n3 = step - 3
            if n3 % SUB == SUB - 1 and n3 // SUB + NWB < NB:
                wload(n3 // SUB + NWB, WD)
    P.barrier()

    ar.off = mark
    xt = [ar.get([128, D], F32) for _ in range(4)]
    Y1 = [ar.get([128, D], F32) for _ in range(4)]
    Y2 = [ar.get([128, D], F32) for _ in range(4)]

    def L3(i):
        w, xd, t = tiles[i]
        k = i % 4
        P.ld("sp", xt[k], xd[t * 128:(t + 1) * 128, :], [], [("xt", k)])
        for (Y, nm, di_) in ((Y1, "Y1", d1), (Y2, "Y2", d2)):
            P.dma("pool", lambda e, yb=Y[k], di_=di_, i=i: e.indirect_dma_start(
                out=yb, out_offset=None, in_=dd["Ys"], in_offset=bass.IndirectOffsetOnAxis(ap=di_[:, i:i + 1], axis=0)),
                [], [(nm, k)])

    def C3(i):
        w, xd, t = tiles[i]
        k = i % 4
        ya, yak = Y1[k], ("Y1", k)
        yb2, ybk = Y2[k], ("Y2", k)
        xb_, xk = xt[k], ("xt", k)
        P.ts("dve", ya, ya, ga1[:, i:i + 1], None, ALU.mult, None, [yak], [yak])
        P.stt(ya, yb2, ga2[:, i:i + 1], ya, ALU.mult, ALU.add, [yak, ybk], [yak])
        P.tt("pool", ya, ya, C.bc[("g2", w)], ALU.mult, [yak, ("g2", w)], [yak])
        P.tt("dve", xb_, ya, xb_, ALU.add, [yak, xk], [xk])
        P.ld("sp", xd[t * 128:(t + 1) * 128, :], xb_, [xk], [])

    L3(0)
    if NT > 1:
        L3(1)
    for i in range(NT):
        if i + 2 < NT:
            L3(i + 2)
        C3(i)


class Ctx:
    pass


def setup_consts(nc, P, G):
    I = lambda n, s, dt=F32: dram(nc, n, s, dt, "ExternalInput")
    G.d_ident = I("ident", [128, 128])
    G.d_ltri = I("ltri", [128, 128])
    G.d_cvec = I("cvec", [2, D])
    G.ident = P.sb("ident", [128, 128], F32)
    G.identb = P.sb("identb", [128, 128], BF16)
    G.onesb = P.sb("onesb", [128, 128], BF16)
    G.onesf = P.sb("onesf", [128, 128], F32)
    G.ltri = P.sb("ltri", [128, 128], F32)
    G.negh = P.sb("negh", [128, 512], F32)
    G.epsc = P.sb("epsc", [128, 1], F32)
    P.ld("sp", G.ident[:], G.d_ident, [], ["ident"])
    P.ld("sp", G.ltri[:], G.d_ltri, [], ["ltri"])
    P.cp("dve", G.identb[:], G.ident[:], ["ident"], ["identb"])
    P.op("dve", lambda e: e.memset(G.onesb[:], 1.0), [], ["onesb"])
    P.op("dve", lambda e: e.memset(G.onesf[:], 1.0), [], ["onesf"])
    P.op("dve", lambda e: e.memset(G.negh[:], -0.5), [], ["negh"])
    P.op("dve", lambda e: e.memset(G.epsc[:], EPS), [], ["epsc"])
    G.tp = P.ps("tp", [128, 8, 256], F32)
    G.pm = [P.ps(f"pm{i}", [128, 512], F32) for i in range(2)]
    G.pb6 = P.ps("pb6", [128, 512], F32)
    G.pb7 = P.ps("pb7", [128, 512], F32)


class Layer:
    def __init__(self, nc, P, G, ar, L, nwhich, ctx_bc, bc1):
        self.nc, self.P, self.G = nc, P, G
        self.nwhich, self.ctx_bc, self.bc1 = nwhich, ctx_bc, bc1
        self.ident, self.identb, self.onesb, self.onesf, self.ltri, self.negh = G.ident, G.identb, G.onesb, G.onesf, G.ltri, G.negh
        I = lambda n, s, dt=F32: dram(nc, n, s, dt, "ExternalInput")
        self.d_wmod = I(f"w_mod{L}", [D, 6 * D])
        self.d_bmod = I(f"b_mod{L}", [1, 6 * D])
        self.d_ng = I(f"ng{L}", [2, D])
        self.d_cvec = G.d_cvec
        P.barrier()
        ar.base = 0
        ar.off = 0
        self.bc = {}
        for wch in range(nwhich if ctx_bc else 1):
            for nm in ["g1", "A2", "sh2", "g2"]:
                self.bc[(nm, wch)] = ar.get([128, D], F32)
        if bc1:
            self.bc[("A1", 0)] = ar.get([128, D], F32)
            self.bc[("sh1", 0)] = ar.get([128, D], F32)
        self.fm = ar.get([128, 2, 2, 8], F32)
        ar.base = ar.off
        self.mod_psum = G.pm
        self.mod_phase(ar)

    def mod_phase(self, ar):
        P = self.P
        ar.reset()
        nw = self.nwhich
        cfm = ar.get([128, 2, 8], F32)
        scb = ar.get([128, 2, 8], F32)
        Lb = ar.get([128, 2 * 8, 128], F32)
        modbc = ar.get([128, nw, 6 * D], F32)
        bmb = ar.get([128, 6 * D], F32)
        ngb = ar.get([128, 2, D], F32)
        wm = [ar.get([128, 8, 512], F32) for _ in range(2)]
        tmp = ar.get([128, 8, 128], F32)
        pm = self.G.pm
        P.ld("sp", cfm, self.d_cvec.rearrange("r (j p) -> p r j", p=128), [], ["cfm"], slow=True)
        P.ld("sp", bmb, self.d_bmod.partition_broadcast(128).rearrange("p o n -> p (o n)"), [], ["bmb"])
        P.ld("sp", ngb, self.d_ng.partition_broadcast(128), [], ["ngb"])
        P.act(scb, cfm, AF.Silu, ["cfm"], ["scb"])
        for wch in range(nw):
            for j in range(8):
                P.cp("dve", Lb[:, wch * 8 + j, :], scb[:, wch, j:j + 1].to_broadcast([128, 128]), ["scb"], [("Lb", wch, j)])
        wmv = self.d_wmod.rearrange("(j p) n -> p j n", p=128)
        for n in range(12):
            w = wm[n % 2]
            P.ld("sp", w, wmv[:, :, n * 512:(n + 1) * 512], [], [("wm", n % 2)])
            for wch in range(nw):
                pp = pm[(n * nw + wch) % 2]
                pk = ("pm", (n * nw + wch) % 2)
                for j in range(8):
                    P.mm(pp[:], Lb[:, wch * 8 + j, :], w[:, j, :], j == 0, j == 7, [("Lb", wch, j), ("wm", n % 2)], [pk])
                P.tt("dve", modbc[:, wch, n * 512:(n + 1) * 512], pp[:], bmb[:, n * 512:(n + 1) * 512], ALU.add, [pk, "bmb"], [("mod", wch)])
        for wch in range(nw):
            m = lambda i: modbc[:, wch, i * D:(i + 1) * D]
            mk = ("mod", wch)
            if wch == 0 or self.ctx_bc:
                P.cp("pool", self.bc[("g1", wch)], m(2), [mk], [("g1", wch)])
                P.cp("pool", self.bc[("sh2", wch)], m(3), [mk], [("sh2", wch)])
                P.cp("pool", self.bc[("g2", wch)], m(5), [mk], [("g2", wch)])
                P.stt(self.bc[("A2", wch)], m(4), 1.0, ngb[:, 1, :], ALU.add, ALU.mult, [mk, "ngb"], [("A2", wch)])
            P.stt(m(1), m(1), 1.0, ngb[:, 0, :], ALU.add, ALU.mult, [mk, "ngb"], [mk])
            if self.bc1 and wch == 0:
                P.cp("pool", self.bc[("A1", 0)], m(1), [mk], [("A1", 0)])
                P.cp("pool", self.bc[("sh1", 0)], m(0), [mk], [("sh1", 0)])
            for k, src in enumerate([m(1), m(0)]):
                P.tt("dve", tmp, src.rearrange("p (j q) -> p j q", j=8), self.ident[:].unsqueeze(1).to_broadcast([128, 8, 128]), ALU.mult, [mk, "ident"], ["fmtmp"])
                P.red(self.fm[:, wch, k, :], tmp, ALU.add, ["fmtmp"], [("fm", wch)])
        P.barrier()

    def norm_tile(self, xt, xk, xh, xhk, ss_name):
        P = self.P
        junk, ss, rs = self.nt_junk, self.nt_ss, self.nt_rs
        P.act(junk[:], xt, AF.Square, [xk], ["nt_junk", "nt_ss"], accum=ss[:])
        P.act(rs[:], ss[:], AF.Sqrt, ["nt_ss", "epsc"], ["nt_rs"], bias=self.G.epsc[:, 0:1], scale=1.0 / D)
        P.op("dve", lambda e: e.reciprocal(out=rs[:], in_=rs[:]), ["nt_rs"], ["nt_rs"])
        P.ts("dve", xh, xt, rs[:, 0:1], None, ALU.mult, None, [xk, "nt_rs"], [xhk])

    def alloc_norm(self, ar):
        self.nt_junk = ar.get([128, D], BF16)
        self.nt_ss = ar.get([128, 1], F32)
        self.nt_rs = ar.get([128, 1], F32)


def moe_inputs(nc, L):
    I = lambda n, s, dt=F32: dram(nc, n, s, dt, "ExternalInput")
    return dict(wr=I(f"wr{L}", [D, 36]), br=I(f"br{L}", [1, 36]), wg=I(f"wg{L}", [NE * 128, 4096]),
                wu=I(f"wu{L}", [NE * 128, 4096]), wd=I(f"wd{L}", [NE * 128, 4096]))


def run_moe(nc, P, G, C, ar, tiles, mi):
    NT = len(tiles)
    NB = (2 * NT * 128 + BLK - 1) // BLK + NE
    dd = dict(mi)
    dd.update(iotab=G.d_iotab, iotap=G.d_iotap, Xs=G.d_Xs, Ys=G.d_Ys, H2s=G.d_H2s)
    pab = [G.pm[0].rearrange("p (a b) -> p a b", a=2), G.pm[1].rearrange("p (a b) -> p a b", a=2)]
    P.barrier()
    moe_phase(nc, P, C, ar, tiles, NT, NB, dd, psum=dict(big=G.tp, a=pab, b=[G.pb6, G.pb7]))
    P.barrier()


def emit_conv(nc, P, G, C, ar, L, wins, ctxseg):
    I = lambda n, s, dt=F32: dram(nc, n, s, dt, "ExternalInput")
    d_win = I(f"w_in{L}", [D, 2 * D])
    d_binfm = I(f"b_in_fm{L}", [128, 16])
    d_wdwfm = I(f"w_dw_fm{L}", [128, 8 * 31])
    d_bdwfm = I(f"b_dw_fm{L}", [128, 8])
    d_gnfm = I(f"gn_fm{L}", [128, 8])
    d_wout = I(f"w_out{L}", [D, D])
    d_bout = I(f"b_out{L}", [1, D])
    tp, pm = G.tp, G.pm
    NTm = 32
    VW = (NTm + 2) * 128
    ar.off = ar.base
    binfm = ar.get([128, 16], F32)
    wdwfm = ar.get([128, 8, 31], F32)
    bdwfm = ar.get([128, 8], F32)
    gnfm = ar.get([128, 8], F32)
    hmask = ar.get([128, 2], F32)
    bog = [ar.get([128, D], F32) for _ in range(C.nwhich)]
    ar.base = ar.off
    P.ld("sp", binfm, d_binfm, [], ["binfm"])
    P.ld("sp", wdwfm, d_wdwfm.rearrange("p (j t) -> p j t", j=8), [], ["wdwfm"])
    P.ld("sp", bdwfm, d_bdwfm, [], ["bdwfm"])
    P.ld("sp", gnfm, d_gnfm, [], ["gnfm"])
    if any(w_.get("mask") is not None for w_ in wins):
        P.ld("sp", hmask, [w_["mask"] for w_ in wins if w_.get("mask") is not None][0], [], ["hmask"])
    for w in range(C.nwhich):
        P.ld("sp", bog[w], d_bout.partition_broadcast(128).rearrange("p o n -> p (o n)"), [], [("bog", w)])
        P.tt("pool", bog[w], bog[w], C.bc[("g1", w)], ALU.mult, [("bog", w), ("g1", w)], [("bog", w)])
    P.barrier()
    for wi, win_ in enumerate(wins):
        segs = [dict(w=0, x=win_["x"], nt=NTm + 2, own0=1, nown=NTm, out=win_["out"], halo=True)]
        if ctxseg is not None and wi == 0:
            segs.append(dict(w=1, x=ctxseg["x"], nt=2, own0=0, nown=2, out=ctxseg["out"], halo=False))
        has_ctx = len(segs) > 1
        ar.reset()
        vT = ar.get([128, 8, VW], BF16)
        vTc = ar.get([128, 8, 16 + 256 + 16], BF16)
        c2mark = ar.off
        win = ar.get([128, 8, 2 * D], BF16)
        hT = [ar.get([128, 8, 256], BF16) for _ in range(2)]
        xt = [ar.get([128, D], F32) for _ in range(2)]
        xh_ = [ar.get([128, D], F32) for _ in range(2)]
        sig = [ar.get([128, 256], F32) for _ in range(2)]
        C.alloc_norm(ar)
        pab = [pm[0].rearrange("p (a b) -> p a b", a=2), pm[1].rearrange("p (a b) -> p a b", a=2)]
        P.ld("pool", win, d_win.rearrange("(c p) n -> p c n", p=128), [], ["win"])
        if has_ctx:
            P.op("pool", lambda e: e.memset(vTc, 0.0), [], ["vTc"])
        gi = 0
        ti = 0
        for sg in segs:
            w = sg["w"]
            for g in range(sg["nt"] // 2):
                hb = hT[gi % 2]
                hk = ("hT", gi % 2)
                for tl in range(2):
                    t = g * 2 + tl
                    xb_, xk = xt[ti % 2], ("xt", ti % 2)
                    xhb, xhk = xh_[ti % 2], ("xh", ti % 2)
                    ti += 1
                    P.ld("sp", xb_, sg["x"][t * 128:(t + 1) * 128, :], [], [xk])
                    C.norm_tile(xb_, xk, xhb, xhk, None)
                    for j in range(8):
                        P.tr(tp[:, j, tl * 128:(tl + 1) * 128], xhb[:, j * 128:(j + 1) * 128], C.ident[:], [xhk, "ident"], [("tp", j // 2)])
                for j in range(8):
                    P.act(hb[:, j, :], tp[:, j, :], AF.Identity, [("tp", j // 2), ("fm", w)], [hk],
                          bias=C.fm[:, w, 1, j:j + 1], scale=C.fm[:, w, 0, j:j + 1])
                for jo in range(8):
                    pb_ = pab[jo % 2]
                    pk = ("pab", jo % 2)
                    for half in range(2):
                        for c in range(8):
                            P.mm(pb_[:, half, :], win[:, c, half * D + jo * 128: half * D + (jo + 1) * 128], hb[:, c, :], c == 0, c == 7, ["win", hk], [pk])
                    sg_ = sig[jo % 2]
                    P.act(sg_, pb_[:, 1, :], AF.Sigmoid, [pk, "binfm"], [("sig", jo % 2)], bias=binfm[:, 8 + jo:9 + jo])
                    if sg["halo"]:
                        dst = vT[:, jo, g * 256:(g + 1) * 256]
                        dk = "vT"
                    else:
                        dst = vTc[:, jo, 16 + g * 256:16 + (g + 1) * 256]
                        dk = "vTc"
                    P.stt(dst, pb_[:, 0, :], binfm[:, jo:jo + 1], sg_, ALU.add, ALU.mult, [pk, ("sig", jo % 2), "binfm"], [dk])
                if sg["halo"] and g == 0:
                    if win_.get("mask") is not None:
                        P.ts("pool", vT[:, :, 0:128], vT[:, :, 0:128], hmask[:, 0:1], None, ALU.mult, None, ["vT", "hmask"], ["vT"])
                    elif win_["zlo"]:
                        P.op("pool", lambda e: e.memset(vT[:, :, 0:128], 0.0), [], ["vT"])
                if sg["halo"] and g == sg["nt"] // 2 - 1:
                    if win_.get("mask") is not None:
                        P.ts("pool", vT[:, :, VW - 128:VW], vT[:, :, VW - 128:VW], hmask[:, 1:2], None, ALU.mult, None, ["vT", "hmask"], ["vT"])
                    elif win_["zhi"]:
                        P.op("pool", lambda e: e.memset(vT[:, :, VW - 128:VW], 0.0), [], ["vT"])
                gi += 1
        P.barrier()
        ar.off = c2mark
        wout = ar.get([128, 8, D], BF16)
        Dg = [ar.get([128, 31, 128], BF16) for _ in range(2)]
        vc = ar.get([128, 8, 256], F32)
        sq = ar.get([128, 8, 256], BF16)
        t1 = ar.get([128, 256], F32)
        rsb = ar.get([128, 256], F32)
        tmpv = [ar.get([128, 256], F32) for _ in range(2)]
        uT = ar.get([128, 8, 256], BF16)
        xt2 = [ar.get([128, D], F32) for _ in range(2)]
        xn = [ar.get([128, D], F32) for _ in range(2)]
        cv = [tp[:, 0:2, :].rearrange("p a b -> p (a b)"), tp[:, 2:4, :].rearrange("p a b -> p (a b)")]
        ssb = tp[:, 4:6, :].rearrange("p a b -> p (a b)")
        po = [pm[0], pm[1]]
        P.ld("pool", wout, d_wout.rearrange("(c p) n -> p c n", p=128), [], ["wout"])
        di = 0
        xi = 0
        pi = 0
        for sg in segs:
            w = sg["w"]
            ntok = sg["nown"] * 128
            W = 256
            for tb in range(ntok // W):
                for j in range(8):
                    dg, dgk = Dg[di % 2], ("Dg", di % 2)
                    di += 1
                    P.tt("pool" if j % 4 == 3 else "dve", dg, C.identb[:].unsqueeze(1).to_broadcast([128, 31, 128]),
                         wdwfm[:, j, :].unsqueeze(2).to_broadcast([128, 31, 128]), ALU.mult, ["identb", "wdwfm"], [dgk])
                    cvb, cvk = cv[j % 2], ("cv", j % 2)
                    for tau in range(31):
                        if sg["halo"]:
                            c0 = 128 + tb * W + tau - 15
                            rhs = vT[:, j, c0:c0 + W]
                            rk = "vT"
                        else:
                            c0 = 16 + tb * W + tau - 15
                            rhs = vTc[:, j, c0:c0 + W]
                            rk = "vTc"
                        P.mm(cvb[:, 0:W], dg[:, tau, :], rhs, tau == 0, tau == 30, [dgk, rk], [cvk])
                    P.act(vc[:, j, 0:W], cvb[:, 0:W], AF.Identity, [cvk, "bdwfm"], [("vc", j)], bias=bdwfm[:, j:j + 1])
                    P.act(sq[:, j, 0:W], cvb[:, 0:W], AF.Square, [cvk, "bdwfm"], [("sq", j)], bias=bdwfm[:, j:j + 1])
                for j in range(8):
                    P.mm(ssb[:, 0:W], C.onesb[:], sq[:, j, 0:W], j == 0, j == 7, [("sq", j), "onesb"], ["ssb"])
                P.act(t1[:, 0:W], ssb[:, 0:W], AF.Sqrt, ["ssb", "epsc"], ["t1"], bias=G.epsc[:, 0:1], scale=1.0 / D)
                P.op("dve", lambda e, W=W: e.reciprocal(out=rsb[:, 0:W], in_=t1[:, 0:W]), ["t1"], ["rsb"])
                for j in range(8):
                    tv, tvk = tmpv[j % 2], ("tmpv", j % 2)
                    P.tt("dve", tv[:, 0:W], vc[:, j, 0:W], rsb[:, 0:W], ALU.mult, [("vc", j), "rsb"], [tvk])
                    P.act(uT[:, j, 0:W], tv[:, 0:W], AF.Silu, [tvk, "gnfm"], [("uT", j)], scale=gnfm[:, j:j + 1])
                for s in range(W // 128):
                    t = sg["own0"] + (tb * W) // 128 + s
                    xb_, xk = xt2[xi % 2], ("xt2", xi % 2)
                    xnb, xnk = xn[xi % 2], ("xn", xi % 2)
                    xi += 1
                    P.ld("sp", xb_, sg["x"][t * 128:(t + 1) * 128, :], [], [xk])
                    P.tt("pool", xb_, xb_, bog[w], ALU.add, [xk, ("bog", w)], [xk])
                    for half in range(2):
                        pp, pk = po[pi % 2], ("po", pi % 2)
                        pi += 1
                        for j in range(8):
                            P.mm(pp[:], uT[:, j, s * 128:(s + 1) * 128], wout[:, j, half * 512:(half + 1) * 512], j == 0, j == 7, [("uT", j), "wout"], [pk])
                        P.tt("dve", xnb[:, half * 512:(half + 1) * 512], pp[:], C.bc[("g1", w)][:, half * 512:(half + 1) * 512], ALU.mult, [pk, ("g1", w)], [xnk])
                        P.tt("pool", xnb[:, half * 512:(half + 1) * 512], xnb[:, half * 512:(half + 1) * 512], xb_[:, half * 512:(half + 1) * 512], ALU.add, [xnk, xk], [xnk])
                    to = t - sg["own0"]
                    P.ld("sp", sg["out"][to * 128:(to + 1) * 128, :], xnb, [xnk], [])
        P.barrier()


def emit_fft(nc, P, G, C, ar, L, d_x1, d_xc1, d_x2g, d_xc2):
    I = lambda n, s, dt=F32: dram(nc, n, s, dt, "ExternalInput")
    NPASS = 4
    KP = 64 // NPASS
    CB = 512 // (KP * 2)
    d_wa = I("wa", [64, NPASS, KP * 2])
    d_mb = I("mb", [128, 64 * 4 * 128])
    d_cd = I("cd", [128, 2 * 512])
    d_cdn = I("cdn", [128, 2 * 512])
    d_wout = I(f"w_out{L}", [D, D])
    d_bout = I(f"b_out{L}", [1, D])
    d_hl = G.d_hl
    tp, pm, pb6, pb7 = G.tp, G.pm, G.pb6, G.pb7
    tpf = tp.rearrange("p a b -> p (a b)")
    ar.off = ar.base
    bog = [ar.get([128, D], F32) for _ in range(2)]
    ar.base = ar.off
    for w in range(2):
        P.ld("sp", bog[w], d_bout.partition_broadcast(128).rearrange("p o n -> p (o n)"), [], [("bog", w)])
        P.tt("pool", bog[w], bog[w], C.bc[("g1", w)], ALU.mult, [("bog", w), ("g1", w)], [("bog", w)])
    P.barrier()
    ar.reset()
    xt = [ar.get([128, D], F32) for _ in range(2)]
    xh_ = [ar.get([128, D], F32) for _ in range(2)]
    hb = [ar.get([128, D], BF16) for _ in range(2)]
    C.alloc_norm(ar)
    for t in range(64):
        xb_, xk = xt[t % 2], ("xt", t % 2)
        xhb, xhk = xh_[t % 2], ("xh", t % 2)
        P.ld("sp", xb_, d_x1[t * 128:(t + 1) * 128, :], [], [xk])
        C.norm_tile(xb_, xk, xhb, xhk, None)
        P.tt("dve", xhb, xhb, C.bc[("A1", 0)], ALU.mult, [xhk, ("A1", 0)], [xhk])
        P.tt("pool", hb[t % 2], xhb, C.bc[("sh1", 0)], ALU.add, [xhk, ("sh1", 0)], [("hb", t % 2)])
        P.ld("sp", d_hl[t * 128:(t + 1) * 128, :], hb[t % 2], [("hb", t % 2)], [])
    P.barrier()
    ar.reset()
    wout = ar.get([128, 8, D], BF16)
    cd = ar.get([128, 2, 512], BF16)
    cdn = ar.get([128, 2, 512], BF16)
    wa = ar.get([64, NPASS, KP * 2], BF16)
    fT = ar.get([128, 8, KP * 128], BF16)
    XA = ar.get([64, 128, 128], BF16)
    yg_off = ar.off
    Yg = ar.get([128, 2, KP, 256], BF16)
    MB4 = [ar.get([128, 4, 4, 128], BF16) for _ in range(2)]
    ZT4 = [ar.get([128, 2, 2, 512], BF16) for _ in range(2)]
    xt = [ar.get([128, D], F32) for _ in range(2)]
    xn = [ar.get([128, 512], F32) for _ in range(2)]
    C.alloc_norm(ar)
    print('fft arena words', ar.off, 'of', ar.n)
    P.ld("pool", wout, d_wout.rearrange("(c p) n -> p c n", p=128), [], ["wout"])
    P.ld("pool", cd, d_cd.rearrange("p (a b) -> p a b", a=2), [], ["cd"])
    P.ld("pool", cdn, d_cdn.rearrange("p (a b) -> p a b", a=2), [], ["cdn"])
    P.ld("pool", wa, d_wa, [], ["wa"])
    hlv = d_hl.rearrange("(t1 t2) c -> t1 t2 c", t2=128)
    mbv = d_mb.rearrange("p (k a q) -> p k a q", k=64, a=4)
    x1v = d_x1.rearrange("(k2 k1) d -> k1 k2 d", k1=64)
    pA = [tpf[:, 0:512], tpf[:, 512:1024]]
    pZ = [tpf[:, 1024:1280], tpf[:, 1536:1792]]
    pF = [pm[0], pm[1]]
    pO2 = [pb6, pb7]

    def out_proj(src, ntile, xsrc, w, outap):
        for tl in range(ntile):
            xb_, xk = xt[tl % 2], ("xt", tl % 2)
            P.ld("sp", xb_, xsrc(tl), [], [xk])
            P.tt("pool", xb_, xb_, bog[w], ALU.add, [xk, ("bog", w)], [xk])
            for half in range(2):
                pp, pk = pO2[half], ("pO2", half)
                tb, tk = xn[half], ("xn", half)
                for c in range(8):
                    P.mm(pp[:], src[:, c, tl * 128:(tl + 1) * 128], wout[:, c, half * 512:(half + 1) * 512], c == 0, c == 7, ["fT", "wout"], [pk])
                P.tt("dve", tb, pp[:], C.bc[("g1", w)][:, half * 512:(half + 1) * 512], ALU.mult, [pk, ("g1", w)], [tk])
                P.tt("pool", xb_[:, half * 512:(half + 1) * 512], xb_[:, half * 512:(half + 1) * 512], tb, ALU.add, [tk, xk], [xk])
            P.ld("sp", outap[tl * 128:(tl + 1) * 128, :], xb_, [xk], [])

    an = 0
    zn = 0
    fn_ = 0
    mn = 0
    for hh in range(NPASS):
        for g in range(4):
            for nch in range(2):
                ch0 = g * 256 + nch * 128
                for q4 in range(4):
                    P.ld("sp", XA[:, q4 * 32:(q4 + 1) * 32, :], hlv[:, q4 * 32:(q4 + 1) * 32, ch0:ch0 + 128], [], ["XA"])
                for cb in range(128 // CB):
                    pa, pak = pA[an % 2], ("pA", an % 2)
                    an += 1
                    for cc in range(CB):
                        ch = cb * CB + cc
                        P.mm(pa[:, cc * KP * 2:(cc + 1) * KP * 2], XA[:, :, ch], wa[:, hh, :], True, True, ["XA", "wa"], [pak])
                    dst = Yg[:, :, :, nch * 128 + cb * CB: nch * 128 + (cb + 1) * CB].rearrange("p r k c -> p c k r")
                    srcv = pa.rearrange("p (c k r) -> p c k r", c=CB, k=KP)
                    if cb % 2 == 0:
                        P.cp("dve", dst, srcv, [pak], ["Yg"])
                    else:
                        P.act(dst, srcv, AF.Copy, [pak], ["Yg"])
            for kb in range(KP // 4):
                mb_, mbk = MB4[mn % 2], ("MB4", mn % 2)
                mn += 1
                k0 = hh * KP + kb * 4
                P.ld("pool", mb_, mbv[:, k0:k0 + 4, :, :], [], [mbk])
                zt, ztk = ZT4[fn_ % 2], ("ZT4", fn_ % 2)
                for q in range(4):
                    kl = kb * 4 + q
                    for nch in range(2):
                        pz, pzk = pZ[zn % 2], ("pZ", zn % 2)
                        zn += 1
                        P.mm(pz, Yg[:, 0, kl, nch * 128:(nch + 1) * 128], mb_[:, q, 0:2, :].rearrange("p a b -> p (a b)"), True, False, ["Yg", mbk], [pzk])
                        P.mm(pz, Yg[:, 1, kl, nch * 128:(nch + 1) * 128], mb_[:, q, 2:4, :].rearrange("p a b -> p (a b)"), False, True, ["Yg", mbk], [pzk])
                        P.cp("dve", zt[:, nch, :, q * 128:(q + 1) * 128], pz.rearrange("p (a b) -> p a b", a=2), [pzk], [ztk])
                for mch in range(2):
                    pf, pfk = pF[mch], ("pF", mch)
                    i4 = 0
                    for nch in range(2):
                        for ri in range(2):
                            P.mm(pf[:], cd[:, nch, ri * 256 + mch * 128: ri * 256 + (mch + 1) * 128], zt[:, nch, ri, :], i4 == 0, i4 == 3, ["cd", ztk], [pfk])
                            i4 += 1
                    P.act(fT[:, 2 * g + mch, kb * 512:(kb + 1) * 512], pf[:], AF.Copy, [pfk], ["fT"])
                fn_ += 1
        out_proj(fT, KP, lambda tl, hh=hh: x1v[hh * KP + tl], 0, d_x2g[hh * KP * 128:(hh + 1) * KP * 128, :])
    P.barrier()
    save_off = ar.off
    ar.off = yg_off
    fTc = ar.get([128, 8, 256], BF16)
    hTc = ar.get([128, 8, 256], BF16)
    Hcs = ar.get([128, 2, 512], BF16)
    xh_ = [ar.get([128, D], F32)]
    assert ar.off <= yg_off + (2 * KP * 256) // 2, "ctx buffers exceed Yg region"
    ar.off = save_off
    for tl in range(2):
        xb_, xk = xt[tl % 2], ("xt", tl % 2)
        xhb, xhk = xh_[0], ("xh", 0)
        P.ld("sp", xb_, d_xc1[tl * 128:(tl + 1) * 128, :], [], [xk])
        C.norm_tile(xb_, xk, xhb, xhk, None)
        for j in range(8):
            P.tr(tp[:, j, tl * 128:(tl + 1) * 128], xhb[:, j * 128:(j + 1) * 128], C.ident[:], [xhk, "ident"], [("tpc", j // 2)])
    for j in range(8):
        P.act(hTc[:, j, :], tp[:, j, :], AF.Identity, [("tpc", j // 2), ("fm", 1)], ["hTc"], bias=C.fm[:, 1, 1, j:j + 1], scale=C.fm[:, 1, 0, j:j + 1])
    for g in range(4):
        for tl in range(2):
            pf, pfk = pF[tl], ("pF", tl)
            for nch in range(2):
                P.mm(pf[:], hTc[:, 2 * g + nch, tl * 128:(tl + 1) * 128], cd[:, nch, :], nch == 0, nch == 1, ["hTc", "cd"], [pfk])
            P.cp("dve", Hcs[:, tl, :], pf[:], [pfk], ["Hcs"])
        for mch in range(2):
            pf, pfk = pF[mch], ("pF", mch)
            i4 = 0
            for tl in range(2):
                for ri in range(2):
                    P.mm(pf[:, 0:256], Hcs[:, tl, ri * 256 + mch * 128: ri * 256 + (mch + 1) * 128], cdn[:, tl, ri * 256:(ri + 1) * 256], i4 == 0, i4 == 3, ["Hcs", "cdn"], [pfk])
                    i4 += 1
            P.act(fTc[:, 2 * g + mch, :], pf[:, 0:256], AF.Copy, [pfk], ["fT"])
    out_proj(fTc, 2, lambda tl: d_xc1[tl * 128:(tl + 1) * 128, :], 1, d_xc2)
    P.barrier()


def emit_attn(nc, P, G, C, ar, d_x2g, d_xc2, d_x3h):
    I = lambda n, s, dt=F32: dram(nc, n, s, dt, "ExternalInput")
    NQT = 34
    NKT = 66
    d_cosg = I("cosg", [128, 8192])
    d_sing = I("sing", [128, 8192])
    d_cosq = I("cosq", [128, NQT * 128])
    d_sinq = I("sinq", [128, NQT * 128])
    d_qidx = I("qidx", [128, NQT], I32)
    d_wqkv = I("w_qkv", [D, 1536])
    d_gq = I("gq_fm", [128, 1])
    d_gk = I("gk_fm", [128, 1])
    d_wo = I("w_o", [D, D])
    d_prot = I("prot", [128, 128])
    SCALE = 128.0 ** -0.5
    tp, pm, pb6, pb7 = G.tp, G.pm, G.pb6, G.pb7
    tpf = tp.rearrange("p a b -> p (a b)")
    ar.off = ar.base
    gq = ar.get([128, 1], F32)
    gk = ar.get([128, 1], F32)
    prot = ar.get([128, 128], F32)
    protb = ar.get([128, 128], BF16)
    qidx = ar.get([128, NQT], I32)
    ar.base = ar.off
    P.ld("sp", gq, d_gq, [], ["gq"])
    P.ld("sp", gk, d_gk, [], ["gk"])
    P.ld("sp", prot, d_prot, [], ["prot"])
    P.ld("sp", qidx, d_qidx, [], ["qidx"])
    P.cp("dve", protb, prot, ["prot"], ["protb"])
    P.barrier()
    ar.reset()
    KT = ar.get([128, 2, NKT * 128], BF16)
    Vx = ar.get([128, NKT, 2, 130], BF16)
    wqkv = ar.get([128, 8, 1536], BF16)
    wo = ar.get([128, 8, D], BF16)
    hT = ar.get([128, 8, 512], BF16)
    xt = [ar.get([128, D], F32) for _ in range(2)]
    xh_ = [ar.get([128, D], F32) for _ in range(2)]
    C.alloc_norm(ar)
    sqb = ar.get([128, 512], BF16)
    t1 = ar.get([128, 512], F32)
    rs = ar.get([128, 512], F32)
    kn = ar.get([128, 512], F32)
    knb = ar.get([128, 512], BF16)
    cs = ar.get([128, 512], F32)
    sn = ar.get([128, 512], F32)
    t2 = ar.get([128, 512], F32)
    QT = [ar.get([128, 512], BF16) for _ in range(2)]
    PT = [ar.get([128, 512], BF16) for _ in range(3)]
    rec = ar.get([128, 4], F32)
    Ob = [ar.get([128, 128], BF16) for _ in range(2)]
    OT = ar.get([128, 8, 512], BF16)
    oacc = [ar.get([128, 4, 130], F32)] * 2
    print('attn arena words', ar.off, 'of', ar.n)
    pb4, pb5 = pm
    pS = [pb4, pb5]
    pO = [tpf[:, i * 512:i * 512 + 130] for i in range(4)]
    pb7b = pb7[:, 0:64].bitcast(BF16)
    P.ld("pool", wqkv, d_wqkv.rearrange("(c p) n -> p c n", p=128), [], ["wqkv"])
    P.ld("pool", wo, d_wo.rearrange("(c p) n -> p c n", p=128), [], ["wo"])
    P.op("pool", lambda e: e.memset(Vx, 1.0), [], ["Vx"])
    cnt = {"x": 0}

    def load_tile(dst, dk, src):
        if src[0] == "rows":
            P.ld("sp", dst, src[1], [], [dk])
        else:
            t = src[1]
            P.dma("pool", lambda e: e.indirect_dma_start(out=dst, out_offset=None, in_=d_x2g,
                                                         in_offset=bass.IndirectOffsetOnAxis(ap=qidx[:, t:t + 1], axis=0)), ["qidx"], [dk])

    def pro_group(srcs, w, c0):
        ntl = len(srcs)
        for tl in range(ntl):
            n = cnt["x"]
            cnt["x"] += 1
            xb_, xk = xt[n % 2], ("xt", n % 2)
            xhb, xhk = xh_[n % 2], ("xh", n % 2)
            load_tile(xb_, xk, srcs[tl])
            C.norm_tile(xb_, xk, xhb, xhk, None)
            for j in range(8):
                P.tr(tp[:, j, tl * 128:(tl + 1) * 128], xhb[:, j * 128:(j + 1) * 128], C.ident[:], [xhk, "ident"], [("bank", j // 2)])
        for j in range(8):
            P.act(hT[:, j, c0:c0 + ntl * 128], tp[:, j, 0:ntl * 128], AF.Identity, [("bank", j // 2), ("fm", w)], ["hT"],
                  bias=C.fm[:, w, 1, j:j + 1], scale=C.fm[:, w, 0, j:j + 1])

    def qk_steps(mm_fn, ps, psk, W, gfm, gk_, rope, out, outk):
        mm_fn()
        yield
        P.act(sqb[:, 0:W], ps[:, 0:W], AF.Square, [psk], ["sqb"])
        yield
        P.mm(pb7[:, 0:W], C.onesb[:], sqb[:, 0:W], True, True, ["sqb", "onesb"], ["pb7"])
        yield
        P.act(t1[:, 0:W], pb7[:, 0:W], AF.Sqrt, ["pb7", "epsc"], ["t1"], bias=G.epsc[:, 0:1], scale=1.0 / 128)
        yield
        P.op("dve", lambda e: e.reciprocal(out=rs[:, 0:W], in_=t1[:, 0:W]), ["t1"], ["rs"])
        yield
        P.stt(kn[:, 0:W], ps[:, 0:W], gfm[:, 0:1], rs[:, 0:W], ALU.mult, ALU.mult, [psk, gk_, "rs"], ["kn"])
        yield
        if rope:
            P.act(knb[:, 0:W], kn[:, 0:W], AF.Copy, ["kn"], ["knb"])
            yield
            P.mm(pb7[:, 0:W], protb, knb[:, 0:W], True, True, ["knb", "protb"], ["pb7"])
            yield
            P.tt("dve", t2[:, 0:W], pb7[:, 0:W], sn[:, 0:W], ALU.mult, ["pb7", "sn"], ["t2"])
            yield
            P.tt("pool", kn[:, 0:W], kn[:, 0:W], cs[:, 0:W], ALU.mult, ["kn", "cs"], ["kn"])
            yield
            P.tt("dve", out, kn[:, 0:W], t2[:, 0:W], ALU.add, ["kn", "t2"], [outk])
        else:
            P.act(out, kn[:, 0:W], AF.Copy, ["kn"], [outk])
        yield

    def run_all(gen):
        if gen is not None:
            for _ in gen:
                pass

    def step(gen):
        if gen is None:
            return None
        try:
            next(gen)
            return gen
        except StopIteration:
            return None

    rows = lambda ap, t: ("rows", ap[t * 128:(t + 1) * 128, :])
    groups = [(d_xc2, 0, 2, 1, 0, False, None)]
    for g in range(16):
        groups.append((d_x2g, g * 4, 4, 0, 2 + g * 4, True, g))
    for (xap, t0, ntl, w, kt0, rope, g) in groups:
        W = ntl * 128
        for r in range(0, ntl, 2):
            pro_group([rows(xap, t0 + r), rows(xap, t0 + r + 1)], w, r * 128)
        if rope:
            P.ld("sp", cs[:, 0:W], d_cosg[:, g * 512:g * 512 + W], [], ["cs"])
            P.ld("sp", sn[:, 0:W], d_sing[:, g * 512:g * 512 + W], [], ["sn"])
        for j in range(2):
            def kmm(j=j, W=W):
                for c in range(8):
                    P.mm(pb6[:, 0:W], wqkv[:, c, 1024 + j * 128:1024 + (j + 1) * 128], hT[:, c, 0:W], c == 0, c == 7, ["wqkv", "hT"], ["pb6"])
            run_all(qk_steps(kmm, pb6, "pb6", W, gk, "gk", rope, KT[:, j, kt0 * 128:kt0 * 128 + W], "KT"))
        for tl in range(ntl):
            pv = pS[tl % 2]
            pvk = ("pS", tl % 2)
            for c in range(8):
                P.mm(pv[:, 0:256], hT[:, c, tl * 128:(tl + 1) * 128], wqkv[:, c, 1280:1536], c == 0, c == 7, ["wqkv", "hT"], [pvk])
            P.cp("dve", Vx[:, kt0 + tl, :, 0:128], pv[:, 0:256].rearrange("p (a b) -> p a b", a=2), [pvk], ["Vx"])

    pn = 0
    qchunks = [(i * 4, 4) for i in range(8)] + [(32, 2)]
    for (qt0, nqt) in qchunks:
        W = nqt * 128
        for r in range(0, nqt, 2):
            pro_group([("idx", qt0 + r), ("idx", qt0 + r + 1)], 0, r * 128)
        P.ld("sp", cs[:, 0:W], d_cosq[:, qt0 * 128:qt0 * 128 + W], [], ["cs"])
        P.ld("sp", sn[:, 0:W], d_sinq[:, qt0 * 128:qt0 * 128 + W], [], ["sn"])
        def qgen(h, W=W):
            def qmm():
                for c in range(8):
                    P.mm(pb6[:, 0:W], wqkv[:, c, h * 128:(h + 1) * 128], hT[:, c, 0:W], c == 0, c == 7, ["wqkv", "hT"], ["pb6"])
            return qk_steps(qmm, pb6, "pb6", W, gq, "gq", True, QT[h % 2][:, 0:W], ("QT", h % 2))

        def fin_steps(h, nqt=nqt):
            oa, oak = oacc[0], "oacc"
            for qs in range(nqt):
                P.op("dve", lambda e, qs=qs: e.reciprocal(out=rec[:, qs:qs + 1], in_=oa[:, qs, 128:129]), [oak], ["rec"])
                ob, obk = Ob[qs % 2], ("Ob", qs % 2)
                P.ts("dve", ob, oa[:, qs, 0:128], rec[:, qs:qs + 1], None, ALU.mult, None, [oak, "rec"], [obk])
                yield
                P.tr(pb7b, ob, C.identb[:], [obk, "identb"], ["pb7"])
                yield
                P.act(OT[:, h, qs * 128:(qs + 1) * 128], pb7b, AF.Copy, ["pb7"], ["OT"])
                yield

        run_all(qgen(0))
        deferred = None
        for h in range(8):
            j = h // 4
            qt, qk_ = QT[h % 2], ("QT", h % 2)
            gen = qgen(h + 1) if h + 1 < 8 else None

            def S(kt):
                P.mm(pS[kt % 2][:, 0:W], KT[:, j, kt * 128:(kt + 1) * 128], qt[:, 0:W], True, True, ["KT", qk_], [("pS", kt % 2)])
            def PV(kt, pt, ptk):
                for qs in range(nqt):
                    P.mm(pO[qs], pt[:, qs * 128:(qs + 1) * 128], Vx[:, kt, j, :], kt == 0, kt == NKT - 1, [ptk, "Vx"], [("bank", qs)])
            S(0)
            prev = None
            for kt in range(NKT):
                if kt + 1 < NKT:
                    S(kt + 1)
                pt, ptk = PT[pn % 3], ("PT", pn % 3)
                pn += 1
                P.act(pt[:, 0:W], pS[kt % 2][:, 0:W], AF.Exp, [("pS", kt % 2)], [ptk], scale=SCALE)
                if prev is not None:
                    PV(*prev)
                prev = (kt, pt, ptk)
                if kt % 2 == 1:
                    if deferred is not None:
                        deferred = step(deferred)
                    else:
                        gen = step(gen)
            PV(*prev)
            run_all(deferred)
            run_all(gen)
            oa, oak = oacc[0], "oacc"
            for qs in range(nqt):
                P.cp("dve", oa[:, qs, :], pO[qs], [("bank", qs)], [oak])
            deferred = fin_steps(h)
        run_all(deferred)
        for qs in range(nqt):
            t = qt0 + qs
            n = cnt["x"]
            cnt["x"] += 1
            xb_, xk = xt[n % 2], ("xt", n % 2)
            load_tile(xb_, xk, ("idx", t))
            xnb, xnk = xh_[n % 2], ("xh", n % 2)
            for half in range(2):
                pp, pk = pS[half], ("pS", half)
                for h in range(8):
                    P.mm(pp[:], OT[:, h, qs * 128:(qs + 1) * 128], wo[:, h, half * 512:(half + 1) * 512], h == 0, h == 7, ["OT", "wo"], [pk])
                P.tt("dve", xnb[:, half * 512:(half + 1) * 512], pp[:], C.bc[("g1", 0)][:, half * 512:(half + 1) * 512], ALU.mult, [pk, ("g1", 0)], [xnk])
                P.tt("pool", xnb[:, half * 512:(half + 1) * 512], xnb[:, half * 512:(half + 1) * 512], xb_[:, half * 512:(half + 1) * 512], ALU.add, [xnk, xk], [xnk])
            P.ld("sp", d_x3h[t * 128:(t + 1) * 128, :], xnb, [xnk], [])
    P.barrier()


def build_fused():
    nc = bass.Bass("TRN2", target_bir_lowering=False)
    I = lambda n, s, dt=F32: dram(nc, n, s, dt, "ExternalInput")
    N = lambda n, s, dt=F32: dram(nc, n, s, dt, "Internal")
    NBM = (2 * 66 * 128 + BLK - 1) // BLK + NE
    d_xpad = I("xpad", [66 * 128, D])
    d_xc = I("xc", [256, D])
    d_hmask = I("hmask", [128, 2])
    d_xo = dram(nc, "xo", [32 * 128, D], F32, "ExternalOutput")
    d_x1 = N("x1", [8192, D])
    d_xc1 = N("xc1", [256, D])
    d_x2g = N("x2g", [8192, D])
    d_xc2 = N("xc2", [256, D])
    d_x3h = N("x3h", [34 * 128, D])
    with ExitStack() as es:
        P = Prog(nc, es)
        G = Ctx()
        setup_consts(nc, P, G)
        G.d_iotab = I("iota_b", [128, NBM])
        G.d_iotap = I("iota_p", [128, 1])
        G.d_Xs = N("Xs", [NBM * BLK, D], BF16)
        G.d_Ys = N("Ys", [NBM * BLK, D])
        G.d_hl = N("hl", [8192, D], BF16)
        G.d_H2s = N("H2s", [66 * 128, D], BF16)
        ar = Arena(P, "arena", 182)
        ar.base = 0
        tl = lambda ap, a, b, w=0: [(w, ap, t) for t in range(a, b)]
        C = Layer(nc, P, G, ar, 0, 2, True, False)
        mi = moe_inputs(nc, 0)
        wins = [dict(x=d_xpad[0:34 * 128, :], out=d_x1[0:4096, :], zlo=True, zhi=False),
                dict(x=d_xpad[32 * 128:66 * 128, :], out=d_x1[4096:8192, :], zlo=False, zhi=True)]
        emit_conv(nc, P, G, C, ar, 0, wins, dict(x=d_xc, out=d_xc1))
        run_moe(nc, P, G, C, ar, tl(d_x1, 0, 64) + tl(d_xc1, 0, 2, 1), mi)
        C = Layer(nc, P, G, ar, 1, 2, True, True)
        mi = moe_inputs(nc, 1)
        emit_fft(nc, P, G, C, ar, 1, d_x1, d_xc1, d_x2g, d_xc2)
        run_moe(nc, P, G, C, ar, tl(d_x2g, 0, 64) + tl(d_xc2, 0, 2, 1), mi)
        C = Layer(nc, P, G, ar, 2, 2, False, False)
        mi = moe_inputs(nc, 2)
        emit_attn(nc, P, G, C, ar, d_x2g, d_xc2, d_x3h)
        run_moe(nc, P, G, C, ar, tl(d_x3h, 0, 34), mi)
        C = Layer(nc, P, G, ar, 3, 1, False, False)
        mi = moe_inputs(nc, 3)
        emit_conv(nc, P, G, C, ar, 3, [dict(x=d_x3h, out=d_xo, zlo=False, zhi=False, mask=d_hmask)], None)
        run_moe(nc, P, G, C, ar, tl(d_xo, 0, 32), mi)
        P.barrier()
        P.finish()
        print("fused program instructions:", P.ninst, "sems:", P.nsem)
    return nc


_CACHE = {}


def _fm(v, n):
    return np.ascontiguousarray(v.reshape(n, 128).T)


def _rope_tables():
    rows = 8192 // 64
    row = np.repeat(np.arange(rows, dtype=np.float32), 64)
    col = np.tile(np.arange(64, dtype=np.float32), rows)
    inv = (np.float32(10000.0) ** (-np.arange(32, dtype=np.float32) / np.float32(32))).astype(np.float32)
    ang = np.concatenate([row[:, None] * inv, col[:, None] * inv], axis=-1).astype(np.float32)
    c, s_ = np.cos(ang).astype(np.float32), np.sin(ang).astype(np.float32)
    cosf = np.repeat(c, 2, axis=1).T
    sgn = np.tile(np.array([-1.0, 1.0], np.float32), 64)
    sinf = (np.repeat(s_, 2, axis=1) * sgn[None, :]).T
    return np.ascontiguousarray(cosf), np.ascontiguousarray(sinf)


def _fft_tables():
    k1 = np.arange(64, dtype=np.float64)
    t1 = np.arange(64, dtype=np.float64)
    a = 2 * np.pi * np.outer(t1, k1) / 64.0
    wa = np.stack([np.cos(a), -np.sin(a)], -1) / 8.0
    wa = wa.reshape(64, 4, 16 * 2)
    t2 = np.arange(128, dtype=np.float64)
    k2 = np.arange(128, dtype=np.float64)
    kk = k1[:, None] + 64.0 * k2[None, :]
    th = 2 * np.pi * t2[:, None, None] * kk[None, :, :] / 8192.0
    mr, mi = np.cos(th), -np.sin(th)
    mb = np.stack([mr, mi, -mi, mr], 2) / np.sqrt(128.0)
    n = np.arange(256, dtype=np.float64)
    ph = 2 * np.pi * np.outer(n, n) / 256.0
    cs = np.concatenate([np.cos(ph), np.sin(ph)], 1) / 16.0
    csn = np.concatenate([np.cos(ph), -np.sin(ph)], 1) / 16.0
    cd = cs.reshape(2, 128, 512).transpose(1, 0, 2).reshape(128, 1024)
    cdn = csn.reshape(2, 128, 512).transpose(1, 0, 2).reshape(128, 1024)
    f = lambda a_: np.ascontiguousarray(a_.astype(np.float32))
    return f(wa), f(mb.reshape(128, -1)), f(cd), f(cdn)


def kernel(**inp):
    inp = {k: np.asarray(v) for k, v in inp.items()}
    if "nc" not in _CACHE:
        _CACHE["nc"] = build_fused()
    nc = _CACHE["nc"]
    NBM = (2 * 66 * 128 + BLK - 1) // BLK + NE
    sh = dict(ident=np.eye(128, dtype=np.float32), ltri=np.triu(np.ones((128, 128), np.float32), 1),
              iota_b=np.tile((np.arange(NBM, dtype=np.float32) * BLK)[None, :], (128, 1)),
              iota_p=np.arange(128, dtype=np.float32).reshape(128, 1))
    for L in range(4):
        sh[f"w_mod{L}"] = inp["w_mod"][L]
        sh[f"b_mod{L}"] = inp["b_mod"][L][None, :]
        sh[f"ng{L}"] = inp["norm_g"][L]
        sh[f"wr{L}"] = np.ascontiguousarray(np.concatenate([inp["moe_w_group"][L], inp["moe_w_expert"][L]], axis=1))
        sh[f"br{L}"] = np.concatenate([inp["moe_b_group"][L], inp["moe_b_expert"][L]])[None, :]
        sh[f"wg{L}"] = inp["moe_w_gate"][L].reshape(NE * 128, 4096)
        sh[f"wu{L}"] = inp["moe_w_up"][L].reshape(NE * 128, 4096)
        sh[f"wd{L}"] = inp["moe_w_down"][L].reshape(NE * 128, 4096)
    for L, j in ((0, 0), (3, 1)):
        sh[f"w_in{L}"] = inp["conv_w_in"][j]
        sh[f"b_in_fm{L}"] = _fm(inp["conv_b_in"][j], 16)
        sh[f"w_dw_fm{L}"] = np.ascontiguousarray(inp["conv_w_dw"][j].T.reshape(8, 128, 31).transpose(1, 0, 2).reshape(128, 8 * 31))
        sh[f"b_dw_fm{L}"] = _fm(inp["conv_b_dw"][j], 8)
        sh[f"gn_fm{L}"] = _fm(inp["conv_norm_g"][j], 8)
        sh[f"w_out{L}"] = inp["conv_w_out"][j]
        sh[f"b_out{L}"] = inp["conv_b_out"][j][None, :]
    sh["w_out1"] = inp["fnet_w_out"][0]
    sh["b_out1"] = inp["fnet_b_out"][0][None, :]
    sh["wa"], sh["mb"], sh["cd"], sh["cdn"] = _fft_tables()
    cosf, sinf = _rope_tables()
    gtok = (np.arange(64)[:, None] + 64 * np.arange(128)[None, :]).reshape(-1)
    sh["cosg"] = np.ascontiguousarray(cosf[:, gtok])
    sh["sing"] = np.ascontiguousarray(sinf[:, gtok])
    prot = np.zeros((128, 128), np.float32)
    for i in range(64):
        prot[2 * i, 2 * i + 1] = 1
        prot[2 * i + 1, 2 * i] = 1
    sh.update(w_qkv=inp["attn_w_qkv"][0], gq_fm=inp["attn_q_norm_g"][0].reshape(128, 1), gk_fm=inp["attn_k_norm_g"][0].reshape(128, 1),
              w_o=inp["attn_w_out"][0], prot=prot)
    zpad = np.zeros((128, D), np.float32)
    in_maps = []
    for core in range(8):
        b, s = core // 2, core % 2
        m = dict(sh)
        m["xpad"] = np.ascontiguousarray(np.concatenate([zpad, inp["x"][b], zpad], axis=0))
        m["xc"] = np.ascontiguousarray(inp["ctx"][b])
        m["cvec"] = np.stack([inp["c"][b], inp["c_ctx"]])
        tok = np.clip(4096 * s - 128 + np.arange(34 * 128), 0, 8191)
        m["cosq"] = np.ascontiguousarray(cosf[:, tok])
        m["sinq"] = np.ascontiguousarray(sinf[:, tok])
        grow = (tok % 64) * 128 + tok // 64
        m["qidx"] = np.ascontiguousarray(grow.reshape(34, 128).T.astype(np.int32))
        hm = np.ones((128, 2), np.float32)
        if s == 0:
            hm[:, 0] = 0
        else:
            hm[:, 1] = 0
        m["hmask"] = hm
        in_maps.append(m)
    res = run_bass_kernel_spmd(nc, in_maps, core_ids=list(range(8)))
    out = np.empty_like(inp["x"])
    for core in range(8):
        b, s = core // 2, core % 2
        out[b, s * 4096:(s + 1) * 4096] = res.results[core]["xo"]
    return out
```
